# Optimizing a Trainium2 kernel written in Bass

```python
import jax, jax.numpy as jnp
from jax import lax
import numpy as np

D_MODEL = 1024
BATCH = 2
SEQ = 8192
DEPTH = 1

CHUNK = 64
Q_BLOCK = 128
CONV_DIM = D_MODEL // 2
CONV_WIDTH = 3
N_HEADS = 8
NOPE_DIM = 64
ROPE_DIM = 32
V_DIM = 64
Q_LORA = 384
KV_LORA = 256
ROPE_THETA = 10000.0
D_FF = 2816
PLE_DIM = 256
EPS = 1e-6

QK_DIM = NOPE_DIM + ROPE_DIM
SPLIT_SIZES = (CONV_DIM, CONV_DIM, CONV_DIM, Q_LORA, KV_LORA, ROPE_DIM, D_MODEL, D_MODEL)
N_IN = CONV_DIM * 3 + Q_LORA + KV_LORA + ROPE_DIM + 2 * D_MODEL

kernel_name = "hybrid_conv_mla_gated_stream_block"


def _rmsnorm(x, g):
    xf = x.astype(jnp.float32)
    y = xf * lax.rsqrt(jnp.mean(xf * xf, axis=-1, keepdims=True) + EPS)
    return (y * g.astype(jnp.float32)).astype(x.dtype)


def _causal_dwconv3(x, w):
    s = x.shape[1]
    xp = jnp.pad(x, ((0, 0), (CONV_WIDTH - 1, 0), (0, 0)))
    return xp[:, 0:s] * w[0] + xp[:, 1:s + 1] * w[1] + xp[:, 2:s + 2] * w[2]


def _rope(x, positions):
    half = ROPE_DIM // 2
    inv = 1.0 / (ROPE_THETA ** (jnp.arange(half, dtype=jnp.float32) * (2.0 / ROPE_DIM)))
    ang = positions.astype(jnp.float32)[..., None] * inv
    cos = jnp.cos(ang)[:, :, None, :]
    sin = jnp.sin(ang)[:, :, None, :]
    xf = x.astype(jnp.float32)
    x1, x2 = xf[..., :half], xf[..., half:]
    return jnp.concatenate([x1 * cos - x2 * sin, x2 * cos + x1 * sin], axis=-1).astype(x.dtype)


def _chunk_causal_attention(q, k, v):
    bsz, s, h, dqk = q.shape
    nblk = s // Q_BLOCK
    scale = dqk ** -0.5
    k_chunk = jnp.arange(s) // CHUNK
    qb = q.reshape(bsz, nblk, Q_BLOCK, h, dqk).transpose(1, 0, 2, 3, 4)
    kf = k.astype(jnp.float32)
    vf = v.astype(jnp.float32)
    neg = jnp.finfo(jnp.float32).min

    def one_block(args):
        qi, bi = args
        q_chunk = (bi * Q_BLOCK + jnp.arange(Q_BLOCK)) // CHUNK
        mask = k_chunk[None, :] <= q_chunk[:, None]
        sc = jnp.einsum('bqhd,bkhd->bhqk', qi.astype(jnp.float32), kf) * scale
        sc = jnp.where(mask[None, None], sc, neg)
        pr = jax.nn.softmax(sc, axis=-1)
        return jnp.einsum('bhqk,bkhd->bqhd', pr, vf)

    out = lax.map(one_block, (qb, jnp.arange(nblk)))
    return out.transpose(1, 0, 2, 3, 4).reshape(bsz, s, h, v.shape[-1]).astype(v.dtype)


def setup_inputs(seed: int = 0) -> dict:
    key = jax.random.key(seed)
    ks = jax.random.split(key, 32)
    L = DEPTH
    f32 = jnp.float32

    def w(k, shape, fan_in):
        return jax.random.normal(k, shape, f32) * (fan_in ** -0.5)

    def gain(k, shape):
        return 1.0 + 0.05 * jax.random.normal(k, shape, f32)

    x = jax.random.normal(ks[0], (BATCH, SEQ, D_MODEL), f32)
    p = jax.random.normal(ks[1], (DEPTH, BATCH, SEQ, PLE_DIM), f32)
    offset = jax.random.randint(ks[2], (BATCH, 1), 0, 4096, dtype=jnp.int32)
    positions = offset + jnp.arange(SEQ, dtype=jnp.int32)[None, :]
    return {
        "x": x,
        "p": p,
        "positions": positions,
        "g_mix_pre": gain(ks[3], (L, D_MODEL)),
        "w_in": w(ks[4], (L, D_MODEL, N_IN), D_MODEL),
        "conv_a_w": w(ks[5], (L, CONV_WIDTH, CONV_DIM), CONV_WIDTH),
        "w_a_out": w(ks[6], (L, CONV_DIM, D_MODEL), CONV_DIM),
        "g_q_lat": gain(ks[7], (L, Q_LORA)),
        "w_q_up": w(ks[8], (L, Q_LORA, N_HEADS * QK_DIM), Q_LORA),
        "g_kv_lat": gain(ks[9], (L, KV_LORA)),
        "w_kv_up": w(ks[10], (L, KV_LORA, N_HEADS * (NOPE_DIM + V_DIM)), KV_LORA),
        "w_b_out": w(ks[11], (L, N_HEADS * V_DIM, D_MODEL), N_HEADS * V_DIM),
        "w_o": w(ks[12], (L, D_MODEL, D_MODEL), D_MODEL),
        "g_mix_post": gain(ks[13], (L, D_MODEL)),
        "g_ffn_pre": gain(ks[14], (L, D_MODEL)),
        "w_ffn_up": w(ks[15], (L, D_MODEL, 2 * D_FF), D_MODEL),
        "conv_ffn_w": w(ks[16], (L, CONV_WIDTH, 2 * D_FF), CONV_WIDTH),
        "b_ffn_conv": 0.01 * jax.random.normal(ks[17], (L, 2 * D_FF), f32),
        "w_ffn_down": w(ks[18], (L, D_FF, D_MODEL), D_FF),
        "g_ffn_post": gain(ks[19], (L, D_MODEL)),
        "w_ple_proj": w(ks[20], (L, PLE_DIM, D_MODEL), PLE_DIM),
        "w_ple_gate": w(ks[21], (L, D_MODEL, D_MODEL), D_MODEL),
        "g_ple_post": gain(ks[22], (L, D_MODEL)),
    }


def reference(x, p, positions, g_mix_pre, w_in, conv_a_w, w_a_out, g_q_lat, w_q_up,
              g_kv_lat, w_kv_up, w_b_out, w_o, g_mix_post, g_ffn_pre, w_ffn_up,
              conv_ffn_w, b_ffn_conv, w_ffn_down, g_ffn_post, w_ple_proj, w_ple_gate,
              g_ple_post):
    bsz, s, _ = x.shape
    split_pts = list(np.cumsum(SPLIT_SIZES)[:-1])
    h = x
    for i in range(DEPTH):
        u = _rmsnorm(h, g_mix_pre[i])
        z = u @ w_in[i]
        a_b, a_c, a_x, q_lat, kv_lat, k_rope, gate_a, gate_b = jnp.split(z, split_pts, axis=-1)

        ya = a_b * _causal_dwconv3(a_c * a_x, conv_a_w[i])
        ya = ya @ w_a_out[i]

        q = (_rmsnorm(q_lat, g_q_lat[i]) @ w_q_up[i]).reshape(bsz, s, N_HEADS, QK_DIM)
        kv = (_rmsnorm(kv_lat, g_kv_lat[i]) @ w_kv_up[i]).reshape(bsz, s, N_HEADS, NOPE_DIM + V_DIM)
        q_nope, q_pe = q[..., :NOPE_DIM], q[..., NOPE_DIM:]
        k_nope, v = kv[..., :NOPE_DIM], kv[..., NOPE_DIM:]
        q_pe = _rope(q_pe, positions)
        k_pe = _rope(k_rope[:, :, None, :], positions)
        q_full = jnp.concatenate([q_nope, q_pe], axis=-1)
        k_full = jnp.concatenate(
            [k_nope, jnp.broadcast_to(k_pe, (bsz, s, N_HEADS, ROPE_DIM))], axis=-1)
        ob = _chunk_causal_attention(q_full, k_full, v).reshape(bsz, s, N_HEADS * V_DIM)
        yb = ob @ w_b_out[i]

        mixed = jax.nn.sigmoid(gate_a) * ya + jax.nn.sigmoid(gate_b) * yb
        h = h + _rmsnorm(mixed @ w_o[i], g_mix_post[i])

        u2 = _rmsnorm(h, g_ffn_pre[i])
        up = _causal_dwconv3(u2 @ w_ffn_up[i], conv_ffn_w[i]) + b_ffn_conv[i]
        f_gate, f_val = up[..., :D_FF], up[..., D_FF:]
        f = (jax.nn.gelu(f_gate, approximate=True) * f_val) @ w_ffn_down[i]
        h = h + _rmsnorm(f, g_ffn_post[i])

        e = p[i] @ w_ple_proj[i]
        g = jax.nn.sigmoid(h @ w_ple_gate[i])
        h = h + _rmsnorm(e * g, g_ple_post[i])
    return h
```

```python
import contextlib
import numpy as np
import concourse.bass as bass
import concourse.mybir as mybir
from concourse.bass_utils import run_bass_kernel_spmd

F32, BF16, I32 = mybir.dt.float32, mybir.dt.bfloat16, mybir.dt.int32
AF = mybir.ActivationFunctionType
ALU = mybir.AluOpType

D = 1024
S = 8192
NCORE = 8
CPB = 4
NG = 16
GW = 132
HALO = 4
NT = NG * GW
NH = 8
DFF = 2816
NPAIR = 22
SCALE = 96.0 ** -0.5
EPS = 1e-6
TWO_PI = 6.283185307179586

V_GMIX, V_GQ, V_GKV, V_GMPOST, V_GFPRE, V_GFPOST, V_GPPOST = 0, 8, 11, 13, 21, 29, 37
V_CAW, V_CFW, V_BF, V_INV, V_SGN, V_EPS = 45, 57, 189, 233, 234, 235
NV = 240


class Buf:
    __slots__ = ("name", "w", "readers", "dsem")

    def __init__(self, name):
        self.name = name
        self.w = None
        self.readers = {}
        self.dsem = None


class Ev:
    __slots__ = ("sem", "val", "eng")

    def __init__(self, sem, val, eng):
        self.sem = sem
        self.val = val
        self.eng = eng


class SemRec:
    __slots__ = ("h", "cnt", "key")

    def __init__(self, h, key):
        self.h = h
        self.cnt = 0
        self.key = key


class FW:
    EPOCH = 30000

    def __init__(self, nc, stack):
        self.nc = nc
        self.stack = stack
        self.engs = {"pe": nc.tensor, "act": nc.scalar, "dve": nc.vector,
                     "pool": nc.gpsimd, "sp": nc.sync}
        self.nsem = 0
        self.esem = {}
        for e in ("pe", "act", "dve", "pool"):
            self.esem[e] = self._newsem(e)
        self.seen = {e: {} for e in self.engs}
        self.pending = {e: False for e in self.engs}
        self.dsems = []
        self.nwaits = 0
        self.nops = {e: 0 for e in self.engs}

    def _newsem(self, name):
        self.nsem += 1
        h = self.stack.enter_context(self.nc.semaphore(f"s{self.nsem}_{name}"))
        return SemRec(h, self.nsem)

    def _deps(self, eng, reads, writes):
        out = []
        for b in reads:
            if b.w is not None:
                out.append(b.w)
        for b in writes:
            if b.w is not None:
                out.append(b.w)
            for ev in b.readers.values():
                if ev.eng == eng:
                    continue
                out.append(ev)
        return out

    def _wait(self, eng, evs):
        best = {}
        for ev in evs:
            if ev.eng == "pe" and eng == "pe":
                continue
            k = ev.sem.key
            if self.seen[eng].get(k, 0) >= ev.val:
                continue
            if k not in best or best[k].val < ev.val:
                best[k] = ev
        e = self.engs[eng]
        for k, ev in best.items():
            e.wait_ge(ev.sem.h, ev.val)
            self.seen[eng][k] = ev.val
            self.nwaits += 1

    def _record(self, ev, reads, writes):
        for b in reads:
            key = ev.eng if ev.eng != "dma" else ("dma", ev.sem.key)
            b.readers[key] = ev
        for b in writes:
            b.w = ev
            b.readers = {}

    def op(self, eng, fn, reads=(), writes=(), signal=True):
        self._wait(eng, self._deps(eng, reads, writes))
        ins = fn(self.engs[eng])
        self.nops[eng] += 1
        s = self.esem[eng]
        if signal:
            if s.cnt >= self.EPOCH:
                s = self.esem[eng] = self._newsem(eng)
            s.cnt += 1
            ins.then_inc(s.h, 1)
            ev = Ev(s, s.cnt, eng)
            self.pending[eng] = False
        else:
            ev = Ev(s, s.cnt + 1, eng)
            self.pending[eng] = True
        self._record(ev, reads, writes)
        return ev

    def dma(self, queue, out, in_, reads=(), writes=(), key=None):
        self._wait(queue, self._deps(queue, reads, writes))
        ins = self.engs[queue].dma_start(out=out, in_=in_)
        if key.dsem is None:
            key.dsem = self._newsem("d")
            self.dsems.append(key.dsem)
        s = key.dsem
        s.cnt += 16
        ins.then_inc(s.h, 16)
        ev = Ev(s, s.cnt, "dma")
        self._record(ev, reads, writes)
        return ev

    def barrier(self):
        assert not self.pending["pe"], "PE has unsignaled tail"
        evs = []
        for e in ("pe", "act", "dve", "pool"):
            s = self.esem[e]
            if s.cnt > 0:
                evs.append(Ev(s, s.cnt, e))
        for s in self.dsems:
            if s.cnt > 0:
                evs.append(Ev(s, s.cnt, "dma"))
        for e in self.engs:
            self._wait(e, [ev for ev in evs if ev.eng != e])


def MM(out, lhsT, rhs, start, stop):
    return lambda e: e.matmul(out, lhsT=lhsT, rhs=rhs, start=start, stop=stop)


def ACTF(out, in_, func, scale=1.0, bias=None):
    if bias is None:
        return lambda e: e.activation(out=out, in_=in_, func=func, scale=scale)
    return lambda e: e.activation(out=out, in_=in_, func=func, scale=scale, bias=bias)


def TT(out, a, b, op):
    return lambda e: e.tensor_tensor(out=out, in0=a, in1=b, op=op)


def TS(out, a, s1, op0, s2=None, op1=None):
    if op1 is None:
        return lambda e: e.tensor_scalar(out=out, in0=a, scalar1=s1, scalar2=None, op0=op0)
    return lambda e: e.tensor_scalar(out=out, in0=a, scalar1=s1, scalar2=s2, op0=op0, op1=op1)


def STT(out, in0, scalar, in1, op0, op1):
    return lambda e: e.scalar_tensor_tensor(out=out, in0=in0, scalar=scalar, in1=in1, op0=op0, op1=op1)


def CP(out, in_):
    return lambda e: e.tensor_copy(out=out, in_=in_)


def MSET(ap, val):
    return lambda e: e.memset(ap, val)


def RCP(out, in_):
    return lambda e: e.reciprocal(out=out, in_=in_)


def pieces_of(total, width):
    out = []
    a = 0
    while a < total:
        out.append((a, min(width, total - a)))
        a += width
    return out


def build_program(stage=99):
    nc = bass.Bass("TRN2", target_bir_lowering=False)

    def din(name, shape, dt=F32):
        return nc.dram_tensor(name, list(shape), dt, kind="ExternalInput").ap()

    xT_all = din("xT_all", [16, 128, 8 * 512])
    xT_loc = din("xT_loc", [128, 8, NT])
    pT_loc = din("pT_loc", [128, 2, NT])
    pos_all = din("pos_all", [1, S], I32)
    pos_loc = din("pos_loc", [1, NT], I32)
    vecs_d = din("vecs", [128, NV])
    mask_d = din("mask", [128, 4 * GW])
    wkv_d = din("wkv", [128, 8 * 448])
    wq_d = din("wq", [128, 8 * 384])
    wab_d = din("wab", [128, 8 * 1536])
    wg_d = din("wg", [128, 8 * 2048])
    wqu_d = din("wqu", [128, 3 * 768])
    wqs_d = din("wqs", [128, 3 * 768])
    wkvu_d = din("wkvu", [128, 2 * 1024])
    wao_d = din("wao", [128, 4 * 1024])
    wbo_d = din("wbo", [128, 4 * 1024])
    wo_d = din("wo", [128, 8 * 1024])
    wup_d = din("wup", [NPAIR, 128, 8 * 256])
    wdn_d = din("wdn", [128, NPAIR * 1024])
    wpp_d = din("wpp", [128, 2 * 1024])
    wpg_d = din("wpg", [128, 8 * 1024])
    out_d = nc.dram_tensor("out", [128, 8, NT], F32, kind="ExternalOutput").ap()
    dbg_d = None
    if stage < 99:
        dbg_d = nc.dram_tensor("dbg", [128, 8 * NT], F32, kind="ExternalOutput").ap()

    with contextlib.ExitStack() as st:
        fw = FW(nc, st)

        def sb(name, shape, dt):
            return st.enter_context(nc.sbuf_tensor("s_" + name, list(shape), dt))

        ps = st.enter_context(nc.psum_tensor("ps", [128, 4096], F32))
        PB = [Buf(f"pb{i}") for i in range(8)]

        def bank(i):
            return ps[:, 512 * i:512 * (i + 1)]

        class Rot:
            def __init__(self, ids):
                self.ids = list(ids)
                self.i = 0

            def next(self):
                b = self.ids[self.i % len(self.ids)]
                self.i += 1
                return b

        vec = sb("vec", [128, NV], F32)
        VEC = Buf("vec")
        fw.dma("sp", vec[:], vecs_d, writes=[VEC], key=VEC)
        ones = sb("ones", [128, 128], BF16)
        ONES = Buf("ones")
        fw.op("pool", MSET(ones[:], 1.0), writes=[ONES])
        mask = sb("mask", [128, 4 * GW], BF16)
        MASK = Buf("mask")
        fw.dma("pool", mask[:], mask_d, writes=[MASK], key=MASK)

        def vcol(i, lo=0, hi=128):
            return vec[lo:hi, i:i + 1]

        def rstd_from_bank(bk_ap, BK, out_ap, OUT, tmp_ap, TMP, npart, inv_d):
            fw.op("act", ACTF(tmp_ap, bk_ap, AF.Ln, scale=inv_d, bias=vcol(V_EPS, 0, npart)),
                  reads=[BK, VEC], writes=[TMP])
            fw.op("act", ACTF(out_ap, tmp_ap, AF.Exp, scale=-0.5), reads=[TMP], writes=[OUT])

        ob = sb("ob", [128, 4, NT], BF16)
        OB = [Buf(f"ob{i}") for i in range(4)]
        stA = contextlib.ExitStack()

        def sbA(name, shape, dt):
            return stA.enter_context(nc.sbuf_tensor("s_" + name, list(shape), dt))

        kvn = sbA("kvn", [128, 2, S], BF16)
        KVN = [Buf(f"kvn{t}") for t in range(16)]
        kbuf = [sbA(f"kbuf{i}", [96, S], BF16) for i in range(2)]
        KB_PE = [Buf("kpe0"), Buf("kpe1")]
        KB_NO = [[Buf(f"kno{i}_{t}") for t in range(16)] for i in range(2)]
        wqu = sbA("wqu", [128, 3, 768], BF16)
        wqs = sbA("wqs", [128, 3, 768], BF16)
        wkvu = sbA("wkvu", [128, 2, 1024], BF16)
        WQU, WQS, WKVU = Buf("wqu"), Buf("wqs"), Buf("wkvu")
        fw.dma("pool", wqu[:].rearrange("p a b -> p (a b)"), wqu_d, writes=[WQU], key=WQU)
        fw.dma("pool", wqs[:].rearrange("p a b -> p (a b)"), wqs_d, writes=[WQS], key=WQS)
        fw.dma("pool", wkvu[:].rearrange("p a b -> p (a b)"), wkvu_d, writes=[WKVU], key=WKVU)

        def rope_tables(posi_ap, POSI, n, cos_ap, COS, sin_ap, SIN, ta, TA, tb, TB, ti, TI):
            R = slice(64, 96)
            fw.op("pool", CP(ta[R, :n], posi_ap), reads=[POSI], writes=[TA])
            fw.op("pool", TS(ta[R, :n], ta[R, :n], vcol(V_INV, 64, 96), ALU.mult), reads=[TA, VEC], writes=[TA])
            fw.op("pool", CP(ti[R, :n], ta[R, :n]), reads=[TA], writes=[TI])
            fw.op("pool", CP(tb[R, :n], ti[R, :n]), reads=[TI], writes=[TB])
            fw.op("pool", TT(ta[R, :n], ta[R, :n], tb[R, :n], ALU.subtract), reads=[TA, TB], writes=[TA])
            fw.op("pool", TS(tb[R, :n], ta[R, :n], 0.5, ALU.is_gt), reads=[TA], writes=[TB])
            fw.op("pool", TT(ta[R, :n], ta[R, :n], tb[R, :n], ALU.subtract), reads=[TA, TB], writes=[TA])
            fw.op("pool", TS(tb[R, :n], ta[R, :n], -0.5, ALU.is_lt), reads=[TA], writes=[TB])
            fw.op("pool", TT(ta[R, :n], ta[R, :n], tb[R, :n], ALU.add), reads=[TA, TB], writes=[TA])
            yield
            fw.op("act", ACTF(sin_ap, ta[R, :n], AF.Sin, scale=vcol(V_SGN, 64, 96)), reads=[TA, VEC], writes=[SIN])
            fw.op("pool", TS(tb[R, :n], ta[R, :n], 0.25, ALU.add), reads=[TA], writes=[TB])
            fw.op("pool", TS(ti[R, :n].bitcast(F32), tb[R, :n], 0.5, ALU.is_gt), reads=[TB], writes=[TI])
            fw.op("pool", TT(tb[R, :n], tb[R, :n], ti[R, :n].bitcast(F32), ALU.subtract), reads=[TI, TB], writes=[TB])
            yield
            fw.op("act", ACTF(cos_ap, tb[R, :n], AF.Sin, scale=TWO_PI), reads=[TB], writes=[COS])
            yield

        with contextlib.ExitStack() as st1:
            def sb1(name, shape, dt):
                return st1.enter_context(nc.sbuf_tensor("s_" + name, list(shape), dt))

            TBK = 1024
            TPT = TBK // 512
            wkv = sb1("wkv_bf", [128, 8, 448], BF16)
            WKVST, WKV = Buf("wkvst"), Buf("wkv")
            with nc.sbuf_tensor("s_wkv_st", [128, 8, 448], F32) as wkv_st:
                fw.dma("sp", wkv_st[:].rearrange("p a b -> p (a b)"), wkv_d, writes=[WKVST], key=WKVST)
                for c in range(8):
                    fw.op("dve", TS(wkv[:, c, :], wkv_st[:, c, :], vcol(V_GMIX + c), ALU.mult),
                          reads=[WKVST, VEC], writes=[WKV])
                fw.barrier()
            xb = [sb1(f"xb{i}", [128, 8 * 512], BF16) for i in range(2)]
            XB = [Buf("xb0"), Buf("xb1")]
            sq = [sb1(f"sq{i}", [128, 8 * 512], BF16) for i in range(2)]
            SQ = [Buf("sq0"), Buf("sq1")]
            rs = [sb1(f"rs{i}", [128, 512], F32) for i in range(2)]
            RS = [Buf("rs0"), Buf("rs1")]
            lnt = [sb1(f"lnt{i}", [128, 512], F32) for i in range(2)]
            LNT = [Buf("lnt0"), Buf("lnt1")]
            kvl = [sb1(f"kvl{i}", [128, 2, 512], F32) for i in range(2)]
            KVL = [Buf("kvl0"), Buf("kvl1")]
            sq2 = [sb1(f"sq2{i}", [128, 1024], BF16) for i in range(2)]
            SQ2 = [Buf("sq20"), Buf("sq21")]
            rs2 = [sb1(f"rs2{i}", [128, 512], F32) for i in range(2)]
            RS2 = [Buf("rs20"), Buf("rs21")]
            tpa = [sb1(f"tpa{i}", [96, 512], F32) for i in range(2)]
            tpb = [sb1(f"tpb{i}", [96, 512], F32) for i in range(2)]
            TPA = [Buf("tpa0"), Buf("tpa1")]
            TPB = [Buf("tpb0"), Buf("tpb1")]
            posk = [sb1(f"posk{i}", [96, TBK], I32) for i in range(2)]
            POSK = [Buf("posk0"), Buf("posk1")]
            cosk = [sb1(f"cosk{i}", [96, TBK], F32) for i in range(2)]
            sink = [sb1(f"sink{i}", [96, TBK], F32) for i in range(2)]
            COSK = [Buf("cosk0"), Buf("cosk1")]
            SINK = [Buf("sink0"), Buf("sink1")]
            tta = sb1("tta", [96, TBK], F32)
            ttb = sb1("ttb", [96, TBK], F32)
            tti = sb1("tti", [96, TBK], I32)
            TTA, TTB, TTI = Buf("tta"), Buf("ttb"), Buf("tti")

            rot = Rot(range(8))

            def tables_batch(tb_i):
                i = tb_i % 2
                fw.dma("sp", posk[i][64:96, :],
                       pos_all[0:1, tb_i * TBK:(tb_i + 1) * TBK].partition_broadcast(32),
                       writes=[POSK[i]], key=POSK[i])
                return rope_tables(posk[i][64:96, :], POSK[i], TBK, cosk[i][64:96, :], COSK[i],
                                   sink[i][64:96, :], SINK[i], tta, TTA, ttb, TTB, tti, TTI)

            g0 = tables_batch(0)
            for _ in g0:
                pass
            gen = None
            for t in range(16):
                i = t % 2
                if t % TPT == 0 and t // TPT + 1 < 16 // TPT:
                    gen = tables_batch(t // TPT + 1)
                    next(gen)
                    gcnt = 0
                tbi = (t // TPT) % 2
                tcol = (t % TPT) * 512
                fw.dma("pool", xb[i][:], xT_all[t], writes=[XB[i]], key=XB[i])
                fw.op("act", ACTF(sq[i][:], xb[i][:], AF.Square), reads=[XB[i]], writes=[SQ[i]])
                if gen is not None and t % TPT == 0:
                    next(gen)
                bA = rot.next()
                for c in range(8):
                    fw.op("pe", MM(bank(bA), ones[:, :], sq[i][:, c * 512:(c + 1) * 512], c == 0, c == 7),
                          reads=[ONES, SQ[i]], writes=[PB[bA]], signal=(c == 7))
                rstd_from_bank(bank(bA), PB[bA], rs[i][:], RS[i], lnt[i][:], LNT[i], 128, 1.0 / D)
                for m in range(2):
                    bk = rot.next()
                    for c in range(8):
                        fw.op("pe", MM(bank(bk), wkv[:, c, m * 128:(m + 1) * 128], xb[i][:, c * 512:(c + 1) * 512],
                                       c == 0, c == 7), reads=[WKV, XB[i]], writes=[PB[bk]], signal=(c == 7))
                    fw.op("dve", TT(kvl[i][:, m, :], bank(bk), rs[i][:], ALU.mult),
                          reads=[PB[bk], RS[i]], writes=[KVL[i]])
                fw.op("act", ACTF(sq2[i][:], kvl[i][:].rearrange("p a b -> p (a b)"), AF.Square),
                      reads=[KVL[i]], writes=[SQ2[i]])
                bD = rot.next()
                for c in range(8):
                    fw.op("pe", MM(bank(bD)[0:96, :], wkv[:, c, 256:352], xb[i][:, c * 512:(c + 1) * 512],
                                   c == 0, c == 7), reads=[WKV, XB[i]], writes=[PB[bD]], signal=(c == 7))
                bE = rot.next()
                for c in range(8):
                    fw.op("pe", MM(bank(bE)[0:96, :], wkv[:, c, 352:448], xb[i][:, c * 512:(c + 1) * 512],
                                   c == 0, c == 7), reads=[WKV, XB[i]], writes=[PB[bE]], signal=(c == 7))
                bC = rot.next()
                for m in range(2):
                    fw.op("pe", MM(bank(bC), ones[:, :], sq2[i][:, m * 512:(m + 1) * 512], m == 0, m == 1),
                          reads=[ONES, SQ2[i]], writes=[PB[bC]], signal=(m == 1))
                rstd_from_bank(bank(bC), PB[bC], rs2[i][:], RS2[i], lnt[i][:], LNT[i], 128, 1.0 / 256)
                for m in range(2):
                    fw.op("dve", STT(kvn[:, m, t * 512:(t + 1) * 512], kvl[i][:, m, :], vcol(V_GKV + m),
                                     rs2[i][:], ALU.mult, ALU.mult),
                          reads=[KVL[i], RS2[i], VEC], writes=[KVN[t]])
                R = slice(64, 96)
                fw.op("dve", TT(tpa[i][R, :], bank(bD)[R, :], cosk[tbi][R, tcol:tcol + 512], ALU.mult),
                      reads=[PB[bD], COSK[tbi]], writes=[TPA[i]])
                fw.op("dve", TT(tpb[i][R, :], bank(bE)[R, :], sink[tbi][R, tcol:tcol + 512], ALU.mult),
                      reads=[PB[bE], SINK[tbi]], writes=[TPB[i]])
                fw.op("pool", TT(tpa[i][R, :], tpa[i][R, :], tpb[i][R, :], ALU.add),
                      reads=[TPA[i], TPB[i]], writes=[TPA[i]])
                fw.op("pool", TT(kbuf[0][R, t * 512:(t + 1) * 512], tpa[i][R, :], rs[i][R, :], ALU.mult),
                      reads=[TPA[i], RS[i]], writes=[KB_PE[0]])
                if gen is not None and t % TPT == 0:
                    next(gen)
                    gen = None
            fw.dma("sp", kbuf[1][64:96, :], kbuf[0][64:96, :], reads=[KB_PE[0]], writes=[KB_PE[1]], key=KB_PE[1])
            fw.barrier()

        if stage == 1:
            with contextlib.ExitStack() as std:
                dt_ = std.enter_context(nc.sbuf_tensor("dbgt", [128, 8 * NT], F32))
                DT = Buf("dbgt")
                fw.op("dve", MSET(dt_[:], 0.0), writes=[DT])
                fw.op("dve", CP(dt_[:, 0:8192], kvn[:, 0, :]), reads=KVN, writes=[DT])
                fw.op("dve", CP(dt_[64:96, 8192:16384], kbuf[1][64:96, :]), reads=[KB_PE[1]], writes=[DT])
                fw.dma("sp", dbg_d, dt_[:], reads=[DT], key=DT)
                fw.barrier()
            return nc

        qn = sbA("qn", [128, 3, NT], BF16)
        QN = [Buf(f"qn{i}") for i in range(6)]
        cos_l = sbA("cos_l", [96, NT], F32)
        sin_l = sbA("sin_l", [96, NT], F32)
        COSL, SINL = Buf("cosl"), Buf("sinl")
        PW = 352
        PCS = pieces_of(NT, PW)

        with contextlib.ExitStack() as st2t:
            def sb2(name, shape, dt):
                return st2t.enter_context(nc.sbuf_tensor("s_" + name, list(shape), dt))
            posl = sb2("posl", [96, NT], I32)
            POSL = Buf("posl")
            lta = sb2("lta", [96, NT], F32)
            ltb = sb2("ltb", [96, NT], F32)
            lti = sb2("lti", [96, NT], I32)
            LTA, LTB, LTI = Buf("lta"), Buf("ltb"), Buf("lti")
            fw.dma("sp", posl[64:96, :], pos_loc[0:1, :].partition_broadcast(32), writes=[POSL], key=POSL)
            for _ in rope_tables(posl[64:96, :], POSL, NT, cos_l[64:96, :], COSL, sin_l[64:96, :], SINL,
                                 lta, LTA, ltb, LTB, lti, LTI):
                pass
            fw.barrier()
        with contextlib.ExitStack() as st2:
            def sb2(name, shape, dt):
                return st2.enter_context(nc.sbuf_tensor("s_" + name, list(shape), dt))

            wq = sb2("wq_bf", [128, 8, 384], BF16)
            WQ = Buf("wq")
            fw.dma("pool", wq[:].rearrange("p a b -> p (a b)"), wq_d, writes=[WQ], key=WQ)
            xl = [sb2(f"xl{i}", [128, 8, PW], F32) for i in range(2)]
            XL = [Buf("xl0"), Buf("xl1")]
            sqx = [sb2(f"sqx{i}", [128, 8, PW], BF16) for i in range(2)]
            SQX = [Buf("sqx0"), Buf("sqx1")]
            up = [sb2(f"up{i}", [128, 8, PW], BF16) for i in range(2)]
            UP = [Buf("up0"), Buf("up1")]
            rsl = [sb2(f"rsl{i}", [128, PW], F32) for i in range(2)]
            RSL = [Buf("rsl0"), Buf("rsl1")]
            lnl = [sb2(f"lnl{i}", [128, PW], F32) for i in range(2)]
            LNL = [Buf("lnl0"), Buf("lnl1")]
            sq3 = [sb2(f"sq3{i}", [128, 3, PW], BF16) for i in range(2)]
            SQ3 = [Buf("sq30"), Buf("sq31")]
            rsq = [sb2(f"rsq{i}", [128, PW], F32) for i in range(2)]
            RSQ = [Buf("rsq0"), Buf("rsq1")]
            rot = Rot(range(8))
            for pi, (a, w) in enumerate(PCS):
                i = pi % 2
                fw.dma("sp", xl[i][:, :, :w], xT_loc[:, :, a:a + w], writes=[XL[i]], key=XL[i])
                fw.op("act", ACTF(sqx[i][:, :, :w], xl[i][:, :, :w], AF.Square), reads=[XL[i]], writes=[SQX[i]])
                bA = rot.next()
                for c in range(8):
                    fw.op("pe", MM(bank(bA)[:, :w], ones[:, :], sqx[i][:, c, :w], c == 0, c == 7),
                          reads=[ONES, SQX[i]], writes=[PB[bA]], signal=(c == 7))
                rstd_from_bank(bank(bA)[:, :w], PB[bA], rsl[i][:, :w], RSL[i], lnl[i][:, :w], LNL[i], 128, 1.0 / D)
                for c in range(8):
                    fw.op("dve", STT(up[i][:, c, :w], xl[i][:, c, :w], vcol(V_GMIX + c), rsl[i][:, :w],
                                     ALU.mult, ALU.mult), reads=[XL[i], RSL[i], VEC], writes=[UP[i]])
                bq = []
                for m in range(3):
                    bk = rot.next()
                    bq.append(bk)
                    for c in range(8):
                        fw.op("pe", MM(bank(bk)[:, :w], wq[:, c, m * 128:(m + 1) * 128], up[i][:, c, :w],
                                       c == 0, c == 7), reads=[WQ, UP[i]], writes=[PB[bk]], signal=(c == 7))
                    fw.op("act", ACTF(sq3[i][:, m, :w], bank(bk)[:, :w], AF.Square), reads=[PB[bk]], writes=[SQ3[i]])
                bS = rot.next()
                for m in range(3):
                    fw.op("pe", MM(bank(bS)[:, :w], ones[:, :], sq3[i][:, m, :w], m == 0, m == 2),
                          reads=[ONES, SQ3[i]], writes=[PB[bS]], signal=(m == 2))
                rstd_from_bank(bank(bS)[:, :w], PB[bS], rsq[i][:, :w], RSQ[i], lnl[i][:, :w], LNL[i], 128, 1.0 / 384)
                for m in range(3):
                    fw.op("dve", STT(qn[:, m, a:a + w], bank(bq[m])[:, :w], vcol(V_GQ + m), rsq[i][:, :w],
                                     ALU.mult, ALU.mult), reads=[PB[bq[m]], RSQ[i], VEC], writes=[QN[pi]])
            fw.barrier()

        with contextlib.ExitStack() as st3:
            def sb3(name, shape, dt):
                return st3.enter_context(nc.sbuf_tensor("s_" + name, list(shape), dt))

            vbuf = [sb3(f"vbuf{i}", [128, 64, 128], BF16) for i in range(2)]
            VB = [[Buf(f"vb{i}_{u}") for u in range(8)] for i in range(2)]
            VONE = [Buf("vone0"), Buf("vone1")]
            fw.op("pool", MSET(vbuf[0][:, :, 64:128], 1.0), writes=[VONE[0]])
            fw.op("pool", MSET(vbuf[1][:, :, 0:64], 1.0), writes=[VONE[1]])
            qbuf = [sb3(f"qbuf{i}", [96, NT], BF16) for i in range(2)]
            QB = [Buf("qb0"), Buf("qb1")]
            NP = 3
            pbuf = [sb3(f"pbuf{i}", [128, 512], BF16) for i in range(NP)]
            PBUF = [Buf(f"pbuf{i}") for i in range(NP)]
            dtmp = sb3("dtmp", [128, NT], F32)
            DTMP = Buf("dtmp")
            qta = sb3("qta", [96, 512], F32)
            qtb = sb3("qtb", [96, 512], F32)
            QTA, QTB = Buf("qta"), Buf("qtb")
            APC = pieces_of(NT, 512)

            def build_units(h, bank_rot):
                hb = h % 2
                us = []

                def k_unit(t):
                    def f():
                        bk = bank_rot.next()
                        for c in range(2):
                            fw.op("pe", MM(bank(bk)[0:64, :], wkvu[:, c, h * 128:h * 128 + 64],
                                           kvn[:, c, t * 512:(t + 1) * 512], c == 0, c == 1),
                                  reads=[WKVU, KVN[t]], writes=[PB[bk]], signal=(c == 1))
                        fw.op("dve", CP(kbuf[hb][0:64, t * 512:(t + 1) * 512], bank(bk)[0:64, :]),
                              reads=[PB[bk]], writes=[KB_NO[hb][t]])
                    return f

                def v_unit(u):
                    def f():
                        bk = bank_rot.next()
                        for tt in range(8):
                            tile = 8 * u + tt
                            for c in range(2):
                                fw.op("pe", MM(bank(bk)[:, tt * 64:(tt + 1) * 64],
                                               kvn[:, c, tile * 128:(tile + 1) * 128],
                                               wkvu[:, c, h * 128 + 64:h * 128 + 128], c == 0, c == 1),
                                      reads=[WKVU, KVN[tile // 4]], writes=[PB[bk]],
                                      signal=(c == 1 and tt == 7))
                        voff = 0 if hb == 0 else 64
                        fw.op("dve", CP(vbuf[hb][:, 8 * u:8 * u + 8, voff:voff + 64],
                                        bank(bk).rearrange("p (a b) -> p a b", b=64)),
                              reads=[PB[bk]], writes=[VB[hb][u]])
                    return f

                def q_unit(pi, a, w):
                    def f():
                        bk = bank_rot.next()
                        for c in range(3):
                            fw.op("pe", MM(bank(bk)[0:96, :w], wqu[:, c, h * 96:(h + 1) * 96], qn[:, c, a:a + w],
                                           c == 0, c == 2), reads=[WQU] + QN, writes=[PB[bk]], signal=(c == 2))
                        fw.op("dve", CP(qbuf[hb][0:64, a:a + w], bank(bk)[0:64, :w]), reads=[PB[bk]], writes=[QB[hb]])
                        fw.op("dve", TT(qta[64:96, :w], bank(bk)[64:96, :w], cos_l[64:96, a:a + w], ALU.mult),
                              reads=[PB[bk], COSL], writes=[QTA])
                        bk2 = bank_rot.next()
                        for c in range(3):
                            fw.op("pe", MM(bank(bk2)[0:96, :w], wqs[:, c, h * 96:(h + 1) * 96], qn[:, c, a:a + w],
                                           c == 0, c == 2), reads=[WQS] + QN, writes=[PB[bk2]], signal=(c == 2))
                        fw.op("dve", TT(qtb[64:96, :w], bank(bk2)[64:96, :w], sin_l[64:96, a:a + w], ALU.mult),
                              reads=[PB[bk2], SINL], writes=[QTB])
                        fw.op("dve", TT(qbuf[hb][64:96, a:a + w], qta[64:96, :w], qtb[64:96, :w], ALU.add),
                              reads=[QTA, QTB], writes=[QB[hb]])
                    return f

                for pi, (a, w) in enumerate(APC):
                    us.append(q_unit(pi, a, w))
                kk = [k_unit(t) for t in range(16)]
                vv = [v_unit(u) for u in range(8)]
                for u in range(8):
                    us.append(kk[2 * u])
                    us.append(kk[2 * u + 1])
                    us.append(vv[u])
                return us

            def main_units():
                out = []
                for kb in range(64):
                    G = kb // 4
                    a0 = GW * G
                    for p, (pa, pw) in enumerate(APC):
                        lo = max(pa, a0)
                        hi = pa + pw
                        if lo < hi:
                            out.append((kb, p, lo, hi - lo))
                return out

            LASTKB = {}
            for p, (pa, pw) in enumerate(APC):
                LASTKB[p] = 4 * min(15, (pa + pw - 1) // GW) + 3

            for un in build_units(0, Rot(range(8))):
                un()
            for h in range(NH):
                hb = h % 2
                units = main_units()
                builds = build_units(h + 1, Rot([7])) if h + 1 < NH else []
                bi = 0

                def qk(k):
                    kb, p, a, w = units[k]
                    sbk = 5 + k % 2
                    fw.op("pe", MM(bank(sbk)[:, :w], kbuf[hb][0:96, kb * 128:(kb + 1) * 128], qbuf[hb][0:96, a:a + w],
                                   True, True), reads=[KB_NO[hb][kb // 4], KB_PE[hb], QB[hb]], writes=[PB[sbk]])

                qk(0)
                for k in range(len(units)):
                    kb, p, a, w = units[k]
                    G, r = kb // 4, kb % 4
                    sbk = 5 + k % 2
                    pb = k % NP
                    if k + 1 < len(units):
                        qk(k + 1)
                    fw.op("act", ACTF(pbuf[pb][:, :w], bank(sbk)[:, :w], AF.Exp, scale=SCALE),
                          reads=[PB[sbk]], writes=[PBUF[pb]])
                    lo = max(a, GW * G)
                    hi = min(a + w, GW * G + GW)
                    if lo < hi:
                        fw.op("pool", TT(pbuf[pb][:, lo - a:hi - a], pbuf[pb][:, lo - a:hi - a],
                                         mask[:, r * GW + lo - GW * G:r * GW + hi - GW * G], ALU.mult),
                              reads=[PBUF[pb], MASK], writes=[PBUF[pb]])
                    fw.op("pe", MM(ps[:, a:a + w], vbuf[hb][:, kb, :], pbuf[pb][:, :w], kb == 0, kb == LASTKB[p]),
                          reads=[VB[hb][kb // 8], VONE[hb], PBUF[pb]], writes=[PB[p]])
                    if k % 6 == 5 and bi < len(builds):
                        builds[bi]()
                        bi += 1
                while bi < len(builds):
                    builds[bi]()
                    bi += 1
                lo_r, hi_r = (slice(0, 64), slice(64, 128)) if hb == 0 else (slice(64, 128), slice(0, 64))
                for p, (pa, pw) in enumerate(APC):
                    cs = slice(pa, pa + pw)
                    fw.op("dve", TS(dtmp[lo_r, cs], ps[hi_r, cs], 1e-30, ALU.max), reads=[PB[p]], writes=[DTMP])
                    fw.op("dve", RCP(dtmp[lo_r, cs], dtmp[lo_r, cs]), reads=[DTMP], writes=[DTMP])
                    fw.op("dve", TT(ob[lo_r, h // 2, cs], ps[lo_r, cs], dtmp[lo_r, cs], ALU.mult),
                          reads=[PB[p], DTMP], writes=[OB[h // 2]])
            fw.barrier()

        if stage == 2:
            with contextlib.ExitStack() as std:
                dt_ = std.enter_context(nc.sbuf_tensor("dbgt", [128, 8 * NT], F32))
                DT = Buf("dbgt")
                fw.op("dve", MSET(dt_[:], 0.0), writes=[DT])
                fw.op("dve", CP(dt_[:, 0:4 * NT], ob[:].rearrange("p a b -> p (a b)")), reads=OB, writes=[DT])
                fw.op("dve", CP(dt_[:, 4 * NT:7 * NT], qn[:].rearrange("p a b -> p (a b)")), reads=QN, writes=[DT])
                fw.dma("sp", dbg_d, dt_[:], reads=[DT], key=DT)
                fw.barrier()
            return nc

        stA.close()

        NTH = NT // 2
        HPC = pieces_of(NTH, PW)
        rot = Rot(range(8))
        OUTB = Buf("outdma")

        def sq_stat_rs(src_fn, SRC, nch, w, sqt, SQT, rst, RST, lnb, LNB, inv_d, sq_eng="pool"):
            for c in range(nch):
                if sq_eng == "act":
                    fw.op("act", ACTF(sqt[:, c, :w], src_fn(c), AF.Square), reads=SRC, writes=[SQT])
                else:
                    fw.op("pool", TT(sqt[:, c, :w], src_fn(c), src_fn(c), ALU.mult), reads=SRC, writes=[SQT])
            bS = rot.next()
            for c in range(nch):
                fw.op("pe", MM(bank(bS)[:, :w], ones[:, :], sqt[:, c, :w], c == 0, c == nch - 1),
                      reads=[ONES, SQT], writes=[PB[bS]], signal=(c == nch - 1))
            rstd_from_bank(bank(bS)[:, :w], PB[bS], rst[:, :w], RST, lnb[:, :w], LNB, 128, inv_d)

        for hf in range(2):
            c0 = hf * NTH
            with contextlib.ExitStack() as sth:
                def sbh(name, shape, dt):
                    return sth.enter_context(nc.sbuf_tensor(f"s_{name}_{hf}", list(shape), dt))

                xres = sbh("xres", [128, 8, NTH], F32)
                XRES = [Buf(f"xres{i}") for i in range(3)]
                mixed = sbh("mixed", [128, 8, NTH], BF16)
                MIX = [Buf(f"mix{i}") for i in range(3)]
                u2 = sbh("u2", [128, 8, NTH], BF16)
                U2 = [Buf(f"u2{i}") for i in range(3)]
                sqt = sbh("sqt", [128, 8, PW], BF16)
                SQT = Buf("sqt")
                rst = sbh("rst", [128, PW], F32)
                RST = Buf("rst")
                lnb = sbh("lnb", [128, PW], F32)
                LNB = Buf("lnb")
                for pi, (a, w) in enumerate(HPC):
                    fw.dma("sp", xres[:, :, a:a + w], xT_loc[:, :, c0 + a:c0 + a + w], writes=[XRES[pi]], key=XRES[pi])

                with contextlib.ExitStack() as sab:
                    def sbab(name, shape, dt):
                        return sab.enter_context(nc.sbuf_tensor(f"s_{name}_{hf}", list(shape), dt))

                    uh = sbab("uh", [128, 8, NTH], BF16)
                    UH = [Buf(f"uh{i}") for i in range(3)]
                    yap = sbab("yap", [128, 4, NTH], BF16)
                    YAP = [Buf(f"yap{m}") for m in range(4)]
                    for pi, (a, w) in enumerate(HPC):
                        sq_stat_rs(lambda c: xres[:, c, a:a + w], [XRES[pi]], 8, w, sqt, SQT, rst, RST, lnb, LNB,
                                   1.0 / D, sq_eng="act")
                        for c in range(8):
                            fw.op("dve", STT(uh[:, c, a:a + w], xres[:, c, a:a + w], vcol(V_GMIX + c), rst[:, :w],
                                             ALU.mult, ALU.mult), reads=[XRES[pi], RST, VEC], writes=[UH[pi]])
                    with contextlib.ExitStack() as sa:
                        def sba(name, shape, dt):
                            return sa.enter_context(nc.sbuf_tensor(f"s_{name}_{hf}", list(shape), dt))

                        wab = sba("wab", [128, 8, 1536], BF16)
                        WAB = Buf("wab")
                        fw.dma("pool", wab[:].rearrange("p a b -> p (a b)"), wab_d, writes=[WAB], key=WAB)
                        t1 = [sba(f"t1{i}", [128, PW], F32) for i in range(2)]
                        T1 = [Buf("t10"), Buf("t11")]
                        cx = [sba(f"cx{i}", [128, NTH], F32) for i in range(2)]
                        CX = [Buf("cx0"), Buf("cx1")]
                        cv = [sba(f"cv{i}", [128, NTH], F32) for i in range(2)]
                        CV = [Buf("cv0"), Buf("cv1")]
                        k = 0
                        for m in range(4):
                            i = m % 2
                            for pi, (a, w) in enumerate(HPC):
                                bc = rot.next()
                                for c in range(8):
                                    fw.op("pe", MM(bank(bc)[:, :w], wab[:, c, 512 + m * 128:512 + (m + 1) * 128],
                                                   uh[:, c, a:a + w], c == 0, c == 7),
                                          reads=[WAB, UH[pi]], writes=[PB[bc]], signal=(c == 7))
                                fw.op("act", ACTF(t1[k % 2][:, :w], bank(bc)[:, :w], AF.Copy),
                                      reads=[PB[bc]], writes=[T1[k % 2]])
                                bx = rot.next()
                                for c in range(8):
                                    fw.op("pe", MM(bank(bx)[:, :w], wab[:, c, 1024 + m * 128:1024 + (m + 1) * 128],
                                                   uh[:, c, a:a + w], c == 0, c == 7),
                                          reads=[WAB, UH[pi]], writes=[PB[bx]], signal=(c == 7))
                                fw.op("dve", TT(cx[i][:, a:a + w], bank(bx)[:, :w], t1[k % 2][:, :w], ALU.mult),
                                      reads=[PB[bx], T1[k % 2]], writes=[CX[i]])
                                k += 1
                            w0, w1, w2 = (vcol(V_CAW + 3 * m + kk) for kk in range(3))
                            fw.op("pool", TS(cv[i][:, :], cx[i][:, :], w2, ALU.mult), reads=[CX[i], VEC], writes=[CV[i]])
                            fw.op("dve", STT(cv[i][:, 1:NTH], cx[i][:, 0:NTH - 1], w1, cv[i][:, 1:NTH], ALU.mult, ALU.add),
                                  reads=[CX[i], CV[i], VEC], writes=[CV[i]])
                            fw.op("dve", STT(cv[i][:, 2:NTH], cx[i][:, 0:NTH - 2], w0, cv[i][:, 2:NTH], ALU.mult, ALU.add),
                                  reads=[CX[i], CV[i], VEC], writes=[CV[i]])
                            for pi, (a, w) in enumerate(HPC):
                                bb = rot.next()
                                for c in range(8):
                                    fw.op("pe", MM(bank(bb)[:, :w], wab[:, c, m * 128:(m + 1) * 128],
                                                   uh[:, c, a:a + w], c == 0, c == 7),
                                          reads=[WAB, UH[pi]], writes=[PB[bb]], signal=(c == 7))
                                fw.op("dve", TT(yap[:, m, a:a + w], bank(bb)[:, :w], cv[i][:, a:a + w], ALU.mult),
                                      reads=[PB[bb], CV[i]], writes=[YAP[m]])
                        fw.barrier()
                    with contextlib.ExitStack() as sbb:
                        def sbb_(name, shape, dt):
                            return sbb.enter_context(nc.sbuf_tensor(f"s_{name}_{hf}", list(shape), dt))

                        wg = sbb_("wg", [128, 8, 2048], BF16)
                        wao = sbb_("wao", [128, 4, 1024], BF16)
                        wbo = sbb_("wbo", [128, 4, 1024], BF16)
                        WG, WAO, WBO = Buf("wg"), Buf("wao"), Buf("wbo")
                        fw.dma("pool", wao[:].rearrange("p a b -> p (a b)"), wao_d, writes=[WAO], key=WAO)
                        fw.dma("pool", wbo[:].rearrange("p a b -> p (a b)"), wbo_d, writes=[WBO], key=WBO)
                        fw.dma("pool", wg[:].rearrange("p a b -> p (a b)"), wg_d, writes=[WG], key=WG)
                        sga = [sbb_(f"sga{i}", [128, PW], F32) for i in range(2)]
                        sgb = [sbb_(f"sgb{i}", [128, PW], F32) for i in range(2)]
                        SGA = [Buf("sga0"), Buf("sga1")]
                        SGB = [Buf("sgb0"), Buf("sgb1")]
                        k = 0
                        for mo in range(8):
                            for pi, (a, w) in enumerate(HPC):
                                i = k % 2
                                k += 1
                                b1, b2, b3, b4 = rot.next(), rot.next(), rot.next(), rot.next()
                                for c in range(8):
                                    fw.op("pe", MM(bank(b1)[:, :w], wg[:, c, mo * 128:(mo + 1) * 128], uh[:, c, a:a + w],
                                                   c == 0, c == 7), reads=[WG, UH[pi]], writes=[PB[b1]], signal=(c == 7))
                                fw.op("act", ACTF(sga[i][:, :w], bank(b1)[:, :w], AF.Sigmoid), reads=[PB[b1]], writes=[SGA[i]])
                                for c in range(4):
                                    fw.op("pe", MM(bank(b2)[:, :w], wao[:, c, mo * 128:(mo + 1) * 128], yap[:, c, a:a + w],
                                                   c == 0, c == 3), reads=[WAO] + YAP, writes=[PB[b2]], signal=(c == 3))
                                fw.op("dve", TT(sga[i][:, :w], bank(b2)[:, :w], sga[i][:, :w], ALU.mult),
                                      reads=[PB[b2], SGA[i]], writes=[SGA[i]])
                                for c in range(8):
                                    fw.op("pe", MM(bank(b3)[:, :w], wg[:, c, 1024 + mo * 128:1024 + (mo + 1) * 128],
                                                   uh[:, c, a:a + w], c == 0, c == 7),
                                          reads=[WG, UH[pi]], writes=[PB[b3]], signal=(c == 7))
                                fw.op("act", ACTF(sgb[i][:, :w], bank(b3)[:, :w], AF.Sigmoid), reads=[PB[b3]], writes=[SGB[i]])
                                for c in range(4):
                                    fw.op("pe", MM(bank(b4)[:, :w], wbo[:, c, mo * 128:(mo + 1) * 128],
                                                   ob[:, c, c0 + a:c0 + a + w], c == 0, c == 3),
                                          reads=[WBO] + OB, writes=[PB[b4]], signal=(c == 3))
                                fw.op("dve", TT(sgb[i][:, :w], bank(b4)[:, :w], sgb[i][:, :w], ALU.mult),
                                      reads=[PB[b4], SGB[i]], writes=[SGB[i]])
                                fw.op("pool", TT(mixed[:, mo, a:a + w], sga[i][:, :w], sgb[i][:, :w], ALU.add),
                                      reads=[SGA[i], SGB[i]], writes=[MIX[pi]])
                        fw.barrier()

                with contextlib.ExitStack() as sc_:
                    def sbc(name, shape, dt):
                        return sc_.enter_context(nc.sbuf_tensor(f"s_{name}_{hf}", list(shape), dt))

                    wo = sbc("wo", [128, 8, 1024], BF16)
                    WO = Buf("wo")
                    fw.dma("pool", wo[:].rearrange("p a b -> p (a b)"), wo_d, writes=[WO], key=WO)
                    mos = sbc("mos", [128, 8, PW], F32)
                    MOS = Buf("mos")
                    tmpc = [sbc(f"tmpc{i}", [128, PW], F32) for i in range(2)]
                    TMPC = [Buf("tmpc0"), Buf("tmpc1")]
                    for pi, (a, w) in enumerate(HPC):
                        for m in range(8):
                            bk = rot.next()
                            for c in range(8):
                                fw.op("pe", MM(bank(bk)[:, :w], wo[:, c, m * 128:(m + 1) * 128], mixed[:, c, a:a + w],
                                               c == 0, c == 7), reads=[WO, MIX[pi]], writes=[PB[bk]], signal=(c == 7))
                            fw.op("act", ACTF(mos[:, m, :w], bank(bk)[:, :w], AF.Copy), reads=[PB[bk]], writes=[MOS])
                        sq_stat_rs(lambda c: mos[:, c, :w], [MOS], 8, w, sqt, SQT, rst, RST, lnb, LNB, 1.0 / D)
                        for m in range(8):
                            i = m % 2
                            fw.op("dve", STT(tmpc[i][:, :w], mos[:, m, :w], vcol(V_GMPOST + m), rst[:, :w],
                                             ALU.mult, ALU.mult), reads=[MOS, RST, VEC], writes=[TMPC[i]])
                            fw.op("pool", TT(xres[:, m, a:a + w], xres[:, m, a:a + w], tmpc[i][:, :w], ALU.add),
                                  reads=[XRES[pi], TMPC[i]], writes=[XRES[pi]])
                        sq_stat_rs(lambda c: xres[:, c, a:a + w], [XRES[pi]], 8, w, sqt, SQT, rst, RST, lnb, LNB,
                                   1.0 / D, sq_eng="act")
                        for c in range(8):
                            fw.op("dve", STT(u2[:, c, a:a + w], xres[:, c, a:a + w], vcol(V_GFPRE + c), rst[:, :w],
                                             ALU.mult, ALU.mult), reads=[XRES[pi], RST, VEC], writes=[U2[pi]])
                    fw.barrier()

                with contextlib.ExitStack() as sd:
                    def sbd(name, shape, dt):
                        return sd.enter_context(nc.sbuf_tensor(f"s_{name}_{hf}", list(shape), dt))

                    ff = sbd("ff", [128, NPAIR, NTH], BF16)
                    FF = [Buf(f"ff{m}") for m in range(NPAIR)]
                    with contextlib.ExitStack() as sdd:
                        def sbdd(name, shape, dt):
                            return sdd.enter_context(nc.sbuf_tensor(f"s_{name}_{hf}", list(shape), dt))

                        NWB = 3
                        wup = [sbdd(f"wup{i}", [128, 8, 256], BF16) for i in range(NWB)]
                        WUP = [Buf(f"wup{i}") for i in range(NWB)]
                        rows = [[sbdd(f"row{i}_{r}", [128, NTH], F32) for r in range(4)] for i in range(2)]
                        ROW = [[Buf(f"row{i}_{r}") for r in range(4)] for i in range(2)]

                        def load_wup(m):
                            fw.dma("pool", wup[m % NWB][:].rearrange("p a b -> p (a b)"), wup_d[m],
                                   writes=[WUP[m % NWB]], key=WUP[m % NWB])

                        for m in range(min(NWB - 1, NPAIR)):
                            load_wup(m)
                        for m in range(NPAIR):
                            if m + NWB - 1 < NPAIR:
                                load_wup(m + NWB - 1)
                            i = m % 2
                            wb = m % NWB
                            for half_i, chunk in enumerate((m, NPAIR + m)):
                                gs, t0 = rows[i][2 * half_i], rows[i][2 * half_i + 1]
                                GS, T0 = ROW[i][2 * half_i], ROW[i][2 * half_i + 1]
                                for pi, (a, w) in enumerate(HPC):
                                    bk = rot.next()
                                    for c in range(8):
                                        fw.op("pe", MM(bank(bk)[:, :w], wup[wb][:, c, half_i * 128:(half_i + 1) * 128],
                                                       u2[:, c, a:a + w], c == 0, c == 7),
                                              reads=[WUP[wb], U2[pi]], writes=[PB[bk]], signal=(c == 7))
                                    fw.op("act", ACTF(gs[:, a:a + w], bank(bk)[:, :w], AF.Copy), reads=[PB[bk]], writes=[GS])
                                cw = [vcol(V_CFW + 3 * chunk + kk) for kk in range(3)]
                                fw.op("pool", TS(t0[:, :], gs[:, :], cw[2], ALU.mult, vcol(V_BF + chunk), ALU.add),
                                      reads=[GS, VEC], writes=[T0])
                                fw.op("dve", STT(t0[:, 1:NTH], gs[:, 0:NTH - 1], cw[1], t0[:, 1:NTH], ALU.mult, ALU.add),
                                      reads=[GS, T0, VEC], writes=[T0])
                                fw.op("dve", STT(t0[:, 2:NTH], gs[:, 0:NTH - 2], cw[0], t0[:, 2:NTH], ALU.mult, ALU.add),
                                      reads=[GS, T0, VEC], writes=[T0])
                                if half_i == 0:
                                    fw.op("act", ACTF(t0[:, :], t0[:, :], AF.Gelu_apprx_tanh), reads=[T0], writes=[T0])
                            fw.op("dve", TT(ff[:, m, :], rows[i][1][:, :], rows[i][3][:, :], ALU.mult),
                                  reads=[ROW[i][1], ROW[i][3]], writes=[FF[m]])
                        fw.barrier()
                    with contextlib.ExitStack() as se:
                        def sbe(name, shape, dt):
                            return se.enter_context(nc.sbuf_tensor(f"s_{name}_{hf}", list(shape), dt))

                        wdn = [sbe(f"wdn{i}", [128, NPAIR, 512], BF16) for i in range(2)]
                        WDN = [Buf("wdn0"), Buf("wdn1")]
                        for i in range(2):
                            fw.dma("pool", wdn[i][:], wdn_d.rearrange("p (a b) -> p a b", b=1024)[:, :, i * 512:(i + 1) * 512],
                                   writes=[WDN[i]], key=WDN[i])
                        fds = sbe("fds", [128, 8, PW], F32)
                        FDS = Buf("fds")
                        tmpe = [sbe(f"tmpe{i}", [128, PW], F32) for i in range(2)]
                        TMPE = [Buf("tmpe0"), Buf("tmpe1")]
                        for pi, (a, w) in enumerate(HPC):
                            for m in range(8):
                                bk = rot.next()
                                for c in range(NPAIR):
                                    fw.op("pe", MM(bank(bk)[:, :w], wdn[m // 4][:, c, (m % 4) * 128:(m % 4 + 1) * 128],
                                                   ff[:, c, a:a + w], c == 0, c == NPAIR - 1),
                                          reads=[WDN[m // 4], FF[c]], writes=[PB[bk]], signal=(c == NPAIR - 1))
                                fw.op("act", ACTF(fds[:, m, :w], bank(bk)[:, :w], AF.Copy), reads=[PB[bk]], writes=[FDS])
                            sq_stat_rs(lambda c: fds[:, c, :w], [FDS], 8, w, sqt, SQT, rst, RST, lnb, LNB, 1.0 / D)
                            for m in range(8):
                                i = m % 2
                                fw.op("dve", STT(tmpe[i][:, :w], fds[:, m, :w], vcol(V_GFPOST + m), rst[:, :w],
                                                 ALU.mult, ALU.mult), reads=[FDS, RST, VEC], writes=[TMPE[i]])
                                fw.op("pool", TT(xres[:, m, a:a + w], xres[:, m, a:a + w], tmpe[i][:, :w], ALU.add),
                                      reads=[XRES[pi], TMPE[i]], writes=[XRES[pi]])
                        fw.barrier()

                with contextlib.ExitStack() as sf:
                    def sbf(name, shape, dt):
                        return sf.enter_context(nc.sbuf_tensor(f"s_{name}_{hf}", list(shape), dt))

                    wpg = sbf("wpg", [128, 8, 1024], BF16)
                    wpp = sbf("wpp", [128, 2, 1024], BF16)
                    ptb = sbf("ptb", [128, 2, NTH], BF16)
                    WPG, WPP, PTB = Buf("wpg"), Buf("wpp"), Buf("ptb")
                    fw.dma("pool", wpp[:].rearrange("p a b -> p (a b)"), wpp_d, writes=[WPP], key=WPP)
                    fw.dma("pool", ptb[:], pT_loc[:, :, c0:c0 + NTH], writes=[PTB], key=PTB)
                    fw.dma("pool", wpg[:].rearrange("p a b -> p (a b)"), wpg_d, writes=[WPG], key=WPG)
                    h2b = mixed
                    H2B = MIX
                    egs = sbf("egs", [128, 8, PW], F32)
                    EGS = Buf("egs")
                    sgp = [sbf(f"sgp{i}", [128, PW], F32) for i in range(2)]
                    SGP = [Buf("sgp0"), Buf("sgp1")]
                    tmpf = [sbf(f"tmpf{i}", [128, PW], F32) for i in range(2)]
                    TMPF = [Buf("tmpf0"), Buf("tmpf1")]
                    otile = [sbf(f"otile{i}", [128, 8, PW], F32) for i in range(2)]
                    OT = [Buf("ot0"), Buf("ot1")]
                    for pi, (a, w) in enumerate(HPC):
                        for c in range(8):
                            fw.op("act", ACTF(h2b[:, c, a:a + w], xres[:, c, a:a + w], AF.Copy),
                                  reads=[XRES[pi]], writes=[H2B[pi]])
                    for pi, (a, w) in enumerate(HPC):
                        oi = pi % 2
                        for m in range(8):
                            i = m % 2
                            b1, b2 = rot.next(), rot.next()
                            for c in range(8):
                                fw.op("pe", MM(bank(b1)[:, :w], wpg[:, c, m * 128:(m + 1) * 128], h2b[:, c, a:a + w],
                                               c == 0, c == 7), reads=[WPG, H2B[pi]], writes=[PB[b1]], signal=(c == 7))
                            fw.op("act", ACTF(sgp[i][:, :w], bank(b1)[:, :w], AF.Sigmoid), reads=[PB[b1]], writes=[SGP[i]])
                            for c in range(2):
                                fw.op("pe", MM(bank(b2)[:, :w], wpp[:, c, m * 128:(m + 1) * 128], ptb[:, c, a:a + w],
                                               c == 0, c == 1), reads=[WPP, PTB], writes=[PB[b2]], signal=(c == 1))
                            fw.op("dve", TT(egs[:, m, :w], bank(b2)[:, :w], sgp[i][:, :w], ALU.mult),
                                  reads=[PB[b2], SGP[i]], writes=[EGS])
                        sq_stat_rs(lambda c: egs[:, c, :w], [EGS], 8, w, sqt, SQT, rst, RST, lnb, LNB, 1.0 / D)
                        for m in range(8):
                            i = m % 2
                            fw.op("dve", STT(tmpf[i][:, :w], egs[:, m, :w], vcol(V_GPPOST + m), rst[:, :w],
                                             ALU.mult, ALU.mult), reads=[EGS, RST, VEC], writes=[TMPF[i]])
                            fw.op("pool", TT(otile[oi][:, m, :w], xres[:, m, a:a + w], tmpf[i][:, :w], ALU.add),
                                  reads=[XRES[pi], TMPF[i]], writes=[OT[oi]])
                        fw.dma("sp", out_d[:, :, c0 + a:c0 + a + w], otile[oi][:, :, :w], reads=[OT[oi]], key=OT[oi])
                    fw.barrier()
    return nc


def _chunks(w, kc):
    n = w.shape[1]
    return np.ascontiguousarray(w.reshape(kc, 128, n).transpose(1, 0, 2).reshape(128, kc * n))


def col_tokens(j):
    c = np.arange(NT)
    G = c // GW
    o = c % GW
    return 512 * G + 128 * j + (o - HALO)


def prep_inputs(inputs):
    f32 = np.float32
    x = np.asarray(inputs["x"], f32)
    p = np.asarray(inputs["p"], f32)[0]
    positions = np.asarray(inputs["positions"]).astype(np.int32)
    w_in = np.asarray(inputs["w_in"], f32)[0]

    def vec_cols(v, kc):
        return np.asarray(v, f32).reshape(kc, 128).T

    vecs = np.zeros((128, NV), f32)
    vecs[:, V_GMIX:V_GMIX + 8] = vec_cols(inputs["g_mix_pre"][0], 8)
    vecs[:, V_GQ:V_GQ + 3] = vec_cols(inputs["g_q_lat"][0], 3)
    vecs[:, V_GKV:V_GKV + 2] = vec_cols(inputs["g_kv_lat"][0], 2)
    vecs[:, V_GMPOST:V_GMPOST + 8] = vec_cols(inputs["g_mix_post"][0], 8)
    vecs[:, V_GFPRE:V_GFPRE + 8] = vec_cols(inputs["g_ffn_pre"][0], 8)
    vecs[:, V_GFPOST:V_GFPOST + 8] = vec_cols(inputs["g_ffn_post"][0], 8)
    vecs[:, V_GPPOST:V_GPPOST + 8] = vec_cols(inputs["g_ple_post"][0], 8)
    caw = np.asarray(inputs["conv_a_w"], f32)[0]
    for m in range(4):
        for k in range(3):
            vecs[:, V_CAW + 3 * m + k] = caw[k, m * 128:(m + 1) * 128]
    cfw = np.asarray(inputs["conv_ffn_w"], f32)[0]
    bfc = np.asarray(inputs["b_ffn_conv"], f32)[0]
    for m in range(44):
        for k in range(3):
            vecs[:, V_CFW + 3 * m + k] = cfw[k, m * 128:(m + 1) * 128]
        vecs[:, V_BF + m] = bfc[m * 128:(m + 1) * 128]
    inv = 1.0 / (10000.0 ** (np.arange(16, dtype=np.float64) * (2.0 / 32)))
    for i in range(32):
        vecs[64 + i, V_INV] = np.float32(inv[i % 16] / TWO_PI)
        vecs[64 + i, V_SGN] = np.float32(-TWO_PI if i < 16 else TWO_PI)
    vecs[:, V_EPS] = EPS

    kv_lat = w_in[:, 1920:2176]
    k_rope = w_in[:, 2176:2208]
    z64 = np.zeros((D, 64), f32)
    wkv = np.concatenate([kv_lat, z64, k_rope, z64, k_rope[:, 16:32], k_rope[:, 0:16]], axis=1)
    wq = w_in[:, 1536:1920]
    wab = w_in[:, 0:1536]
    wg = w_in[:, 2208:4256]
    wqu = np.asarray(inputs["w_q_up"], f32)[0]
    wqs = np.zeros_like(wqu)
    for h in range(NH):
        b0 = h * 96
        wqs[:, b0 + 64:b0 + 80] = wqu[:, b0 + 80:b0 + 96]
        wqs[:, b0 + 80:b0 + 96] = wqu[:, b0 + 64:b0 + 80]
    wup = np.asarray(inputs["w_ffn_up"], f32)[0]
    wup_p = np.empty((NPAIR, 128, 8 * 256), f32)
    for m in range(NPAIR):
        blk = np.concatenate([wup[:, m * 128:(m + 1) * 128], wup[:, DFF + m * 128:DFF + (m + 1) * 128]], axis=1)
        wup_p[m] = _chunks(blk, 8)
    shared = {
        "vecs": vecs,
        "wkv": _chunks(wkv, 8), "wq": _chunks(wq, 8), "wab": _chunks(wab, 8), "wg": _chunks(wg, 8),
        "wqu": _chunks(wqu, 3), "wqs": _chunks(wqs, 3),
        "wkvu": _chunks(np.asarray(inputs["w_kv_up"], f32)[0], 2),
        "wao": _chunks(np.asarray(inputs["w_a_out"], f32)[0], 4),
        "wbo": _chunks(np.asarray(inputs["w_b_out"], f32)[0], 4),
        "wo": _chunks(np.asarray(inputs["w_o"], f32)[0], 8),
        "wup": wup_p,
        "wdn": _chunks(np.asarray(inputs["w_ffn_down"], f32)[0], NPAIR),
        "wpp": _chunks(np.asarray(inputs["w_ple_proj"], f32)[0], 2),
        "wpg": _chunks(np.asarray(inputs["w_ple_gate"], f32)[0], 8),
    }
    in_maps = []
    per_batch = {}
    for b in range(2):
        xT = x[b].T
        xa = xT.reshape(8, 128, 16, 512).transpose(2, 1, 0, 3).reshape(16, 128, 8 * 512)
        per_batch[b] = (np.ascontiguousarray(xa), np.ascontiguousarray(positions[b][None, :]))
    for core in range(NCORE):
        b, j = core // CPB, core % CPB
        tok = col_tokens(j)
        valid = tok >= 0
        tk = np.where(valid, tok, 0)
        xl = x[b][tk] * valid[:, None].astype(f32)
        xl = np.ascontiguousarray(xl.T.reshape(8, 128, NT).transpose(1, 0, 2))
        pl = p[b][tk] * valid[:, None].astype(f32)
        pl = np.ascontiguousarray(pl.T.reshape(2, 128, NT).transpose(1, 0, 2))
        posl = np.where(valid, positions[b][tk], 0).astype(np.int32)[None, :]
        mk = np.zeros((128, 4, GW), f32)
        kk = np.arange(128)[:, None]
        qq = np.arange(128)[None, :]
        for r in range(4):
            if r < j:
                mk[:, r, :] = 1.0
            elif r == j:
                mk[:, r, HALO:] = ((kk // 64) <= (qq // 64)).astype(f32)
                mk[:, r, :HALO] = 0.0
        m = dict(shared)
        m.update({"xT_all": per_batch[b][0], "pos_all": per_batch[b][1], "xT_loc": xl, "pT_loc": pl,
                  "pos_loc": posl, "mask": mk.reshape(128, 4 * GW)})
        in_maps.append(m)
    return in_maps


def assemble(results):
    out = np.empty((2, S, D), np.float32)
    for core in range(NCORE):
        b, j = core // CPB, core % CPB
        o = results[core]["out"]
        tok = col_tokens(j)
        own = (np.arange(NT) % GW) >= HALO
        oT = o.transpose(2, 1, 0).reshape(NT, D)
        out[b, tok[own]] = oT[own]
    return out


_NC_CACHE = {}


def kernel(**inputs):
    in_maps = prep_inputs(inputs)
    if "nc" not in _NC_CACHE:
        _NC_CACHE["nc"] = build_program()
    nc = _NC_CACHE["nc"]
    res = run_bass_kernel_spmd(nc, in_maps, core_ids=list(range(NCORE)))
    return assemble(res.results)
```

```python
import contextlib
import numpy as np
import concourse.bass as bass
import concourse.mybir as mybir
from concourse.bass_utils import run_bass_kernel_spmd

F32, BF16, I32 = mybir.dt.float32, mybir.dt.bfloat16, mybir.dt.int32
AF = mybir.ActivationFunctionType
ALU = mybir.AluOpType

D = 1024
S = 8192
NCORE = 8
CPB = 4
NG = 16
GW = 132
HALO = 4
NT = NG * GW
NH = 8
DFF = 2816
NPAIR = 22
SCALE = 96.0 ** -0.5
EPS = 1e-6
TWO_PI = 6.283185307179586

V_GMIX, V_GQ, V_GKV, V_GMPOST, V_GFPRE, V_GFPOST, V_GPPOST = 0, 8, 11, 13, 21, 29, 37
V_CAW, V_CFW, V_BF, V_INV, V_SGN, V_EPS = 45, 57, 189, 233, 234, 235
NV = 240


class Buf:
    __slots__ = ("name", "w", "readers", "dsem")

    def __init__(self, name):
        self.name = name
        self.w = None
        self.readers = {}
        self.dsem = None


class Ev:
    __slots__ = ("sem", "val", "eng")

    def __init__(self, sem, val, eng):
        self.sem = sem
        self.val = val
        self.eng = eng


class SemRec:
    __slots__ = ("h", "cnt", "key")

    def __init__(self, h, key):
        self.h = h
        self.cnt = 0
        self.key = key


class FW:
    EPOCH = 30000

    def __init__(self, nc, stack):
        self.nc = nc
        self.stack = stack
        self.engs = {"pe": nc.tensor, "act": nc.scalar, "dve": nc.vector,
                     "pool": nc.gpsimd, "sp": nc.sync}
        self.nsem = 0
        self.esem = {}
        for e in ("pe", "act", "dve", "pool"):
            self.esem[e] = self._newsem(e)
        self.seen = {e: {} for e in self.engs}
        self.pending = {e: False for e in self.engs}
        self.dsems = []
        self.nwaits = 0
        self.nops = {e: 0 for e in self.engs}

    def _newsem(self, name):
        self.nsem += 1
        h = self.stack.enter_context(self.nc.semaphore(f"s{self.nsem}_{name}"))
        return SemRec(h, self.nsem)

    def _deps(self, eng, reads, writes):
        out = []
        for b in reads:
            if b.w is not None:
                out.append(b.w)
        for b in writes:
            if b.w is not None:
                out.append(b.w)
            for ev in b.readers.values():
                if ev.eng == eng:
                    continue
                out.append(ev)
        return out

    def _wait(self, eng, evs):
        best = {}
        for ev in evs:
            if ev.eng == "pe" and eng == "pe":
                continue
            k = ev.sem.key
            if self.seen[eng].get(k, 0) >= ev.val:
                continue
            if k not in best or best[k].val < ev.val:
                best[k] = ev
        e = self.engs[eng]
        for k, ev in best.items():
            e.wait_ge(ev.sem.h, ev.val)
            self.seen[eng][k] = ev.val
            self.nwaits += 1

    def _record(self, ev, reads, writes):
        for b in reads:
            key = ev.eng if ev.eng != "dma" else ("dma", ev.sem.key)
            b.readers[key] = ev
        for b in writes:
            b.w = ev
            b.readers = {}

    def op(self, eng, fn, reads=(), writes=(), signal=True):
        self._wait(eng, self._deps(eng, reads, writes))
        ins = fn(self.engs[eng])
        self.nops[eng] += 1
        s = self.esem[eng]
        if signal:
            if s.cnt >= self.EPOCH:
                s = self.esem[eng] = self._newsem(eng)
            s.cnt += 1
            ins.then_inc(s.h, 1)
            ev = Ev(s, s.cnt, eng)
            self.pending[eng] = False
        else:
            ev = Ev(s, s.cnt + 1, eng)
            self.pending[eng] = True
        self._record(ev, reads, writes)
        return ev

    def dma(self, queue, out, in_, reads=(), writes=(), key=None):
        self._wait(queue, self._deps(queue, reads, writes))
        ins = self.engs[queue].dma_start(out=out, in_=in_)
        if key.dsem is None:
            key.dsem = self._newsem("d")
            self.dsems.append(key.dsem)
        s = key.dsem
        s.cnt += 16
        ins.then_inc(s.h, 16)
        ev = Ev(s, s.cnt, "dma")
        self._record(ev, reads, writes)
        return ev

    def barrier(self):
        assert not self.pending["pe"], "PE has unsignaled tail"
        evs = []
        for e in ("pe", "act", "dve", "pool"):
            s = self.esem[e]
            if s.cnt > 0:
                evs.append(Ev(s, s.cnt, e))
        for s in self.dsems:
            if s.cnt > 0:
                evs.append(Ev(s, s.cnt, "dma"))
        for e in self.engs:
            self._wait(e, [ev for ev in evs if ev.eng != e])


def MM(out, lhsT, rhs, start, stop):
    return lambda e: e.matmul(out, lhsT=lhsT, rhs=rhs, start=start, stop=stop)


def ACTF(out, in_, func, scale=1.0, bias=None):
    if bias is None:
        return lambda e: e.activation(out=out, in_=in_, func=func, scale=scale)
    return lambda e: e.activation(out=out, in_=in_, func=func, scale=scale, bias=bias)


def TT(out, a, b, op):
    return lambda e: e.tensor_tensor(out=out, in0=a, in1=b, op=op)


def TS(out, a, s1, op0, s2=None, op1=None):
    if op1 is None:
        return lambda e: e.tensor_scalar(out=out, in0=a, scalar1=s1, scalar2=None, op0=op0)
    return lambda e: e.tensor_scalar(out=out, in0=a, scalar1=s1, scalar2=s2, op0=op0, op1=op1)


def STT(out, in0, scalar, in1, op0, op1):
    return lambda e: e.scalar_tensor_tensor(out=out, in0=in0, scalar=scalar, in1=in1, op0=op0, op1=op1)


def CP(out, in_):
    return lambda e: e.tensor_copy(out=out, in_=in_)


def MSET(ap, val):
    return lambda e: e.memset(ap, val)


def RCP(out, in_):
    return lambda e: e.reciprocal(out=out, in_=in_)


def pieces_of(total, width):
    out = []
    a = 0
    while a < total:
        out.append((a, min(width, total - a)))
        a += width
    return out


def build_program(stage=99):
    nc = bass.Bass("TRN2", target_bir_lowering=False)

    def din(name, shape, dt=F32):
        return nc.dram_tensor(name, list(shape), dt, kind="ExternalInput").ap()

    xT_all = din("xT_all", [16, 128, 8 * 512])
    xT_loc = din("xT_loc", [128, 8, NT])
    pT_loc = din("pT_loc", [128, 2, NT])
    pos_all = din("pos_all", [1, S], I32)
    pos_loc = din("pos_loc", [1, NT], I32)
    vecs_d = din("vecs", [128, NV])
    mask_d = din("mask", [128, 4 * GW])
    wkv_d = din("wkv", [128, 8 * 448])
    wq_d = din("wq", [128, 8 * 384])
    wab_d = din("wab", [128, 8 * 1536])
    wg_d = din("wg", [128, 8 * 2048])
    wqu_d = din("wqu", [128, 3 * 768])
    wqs_d = din("wqs", [128, 3 * 768])
    wkvu_d = din("wkvu", [128, 2 * 1024])
    wao_d = din("wao", [128, 4 * 1024])
    wbo_d = din("wbo", [128, 4 * 1024])
    wo_d = din("wo", [128, 8 * 1024])
    wup_d = din("wup", [NPAIR, 128, 8 * 256])
    wdn_d = din("wdn", [128, NPAIR * 1024])
    wpp_d = din("wpp", [128, 2 * 1024])
    wpg_d = din("wpg", [128, 8 * 1024])
    out_d = nc.dram_tensor("out", [128, 8, NT], F32, kind="ExternalOutput").ap()
    dbg_d = None
    if stage < 99:
        dbg_d = nc.dram_tensor("dbg", [128, 8 * NT], F32, kind="ExternalOutput").ap()

    with contextlib.ExitStack() as st:
        fw = FW(nc, st)

        def sb(name, shape, dt):
            return st.enter_context(nc.sbuf_tensor("s_" + name, list(shape), dt))

        ps = st.enter_context(nc.psum_tensor("ps", [128, 4096], F32))
        PB = [Buf(f"pb{i}") for i in range(8)]

        def bank(i):
            return ps[:, 512 * i:512 * (i + 1)]

        class Rot:
            def __init__(self, ids):
                self.ids = list(ids)
                self.i = 0

            def next(self):
                b = self.ids[self.i % len(self.ids)]
                self.i += 1
                return b

        vec = sb("vec", [128, NV], F32)
        VEC = Buf("vec")
        fw.dma("sp", vec[:], vecs_d, writes=[VEC], key=VEC)
        ones = sb("ones", [128, 128], BF16)
        ONES = Buf("ones")
        fw.op("pool", MSET(ones[:], 1.0), writes=[ONES])
        mask = sb("mask", [128, 4 * GW], BF16)
        MASK = Buf("mask")
        fw.dma("pool", mask[:], mask_d, writes=[MASK], key=MASK)

        def vcol(i, lo=0, hi=128):
            return vec[lo:hi, i:i + 1]

        def rstd_from_bank(bk_ap, BK, out_ap, OUT, tmp_ap, TMP, npart, inv_d):
            fw.op("act", ACTF(tmp_ap, bk_ap, AF.Ln, scale=inv_d, bias=vcol(V_EPS, 0, npart)),
                  reads=[BK, VEC], writes=[TMP])
            fw.op("act", ACTF(out_ap, tmp_ap, AF.Exp, scale=-0.5), reads=[TMP], writes=[OUT])

        ob = sb("ob", [128, 4, NT], BF16)
        OB = [Buf(f"ob{i}") for i in range(4)]
        stA = contextlib.ExitStack()

        def sbA(name, shape, dt):
            return stA.enter_context(nc.sbuf_tensor("s_" + name, list(shape), dt))

        kvn = sbA("kvn", [128, 2, S], BF16)
        KVN = [Buf(f"kvn{t}") for t in range(16)]
        kbuf = [sbA(f"kbuf{i}", [96, S], BF16) for i in range(2)]
        KB_PE = [Buf("kpe0"), Buf("kpe1")]
        KB_NO = [[Buf(f"kno{i}_{t}") for t in range(16)] for i in range(2)]
        wqu = sbA("wqu", [128, 3, 768], BF16)
        wqs = sbA("wqs", [128, 3, 768], BF16)
        wkvu = sbA("wkvu", [128, 2, 1024], BF16)
        WQU, WQS, WKVU = Buf("wqu"), Buf("wqs"), Buf("wkvu")
        fw.dma("pool", wqu[:].rearrange("p a b -> p (a b)"), wqu_d, writes=[WQU], key=WQU)
        fw.dma("pool", wqs[:].rearrange("p a b -> p (a b)"), wqs_d, writes=[WQS], key=WQS)
        fw.dma("pool", wkvu[:].rearrange("p a b -> p (a b)"), wkvu_d, writes=[WKVU], key=WKVU)

        def rope_tables(posi_ap, POSI, n, cos_ap, COS, sin_ap, SIN, ta, TA, tb, TB, ti, TI):
            R = slice(64, 96)
            fw.op("dve", CP(ta[R, :n], posi_ap), reads=[POSI], writes=[TA])
            fw.op("dve", TS(ta[R, :n], ta[R, :n], vcol(V_INV, 64, 96), ALU.mult), reads=[TA, VEC], writes=[TA])
            fw.op("dve", CP(ti[R, :n], ta[R, :n]), reads=[TA], writes=[TI])
            fw.op("dve", CP(tb[R, :n], ti[R, :n]), reads=[TI], writes=[TB])
            fw.op("dve", TT(ta[R, :n], ta[R, :n], tb[R, :n], ALU.subtract), reads=[TA, TB], writes=[TA])
            fw.op("dve", TS(tb[R, :n], ta[R, :n], 0.5, ALU.is_gt), reads=[TA], writes=[TB])
            fw.op("dve", TT(ta[R, :n], ta[R, :n], tb[R, :n], ALU.subtract), reads=[TA, TB], writes=[TA])
            fw.op("dve", TS(tb[R, :n], ta[R, :n], -0.5, ALU.is_lt), reads=[TA], writes=[TB])
            fw.op("dve", TT(ta[R, :n], ta[R, :n], tb[R, :n], ALU.add), reads=[TA, TB], writes=[TA])
            yield
            fw.op("act", ACTF(sin_ap, ta[R, :n], AF.Sin, scale=vcol(V_SGN, 64, 96)), reads=[TA, VEC], writes=[SIN])
            fw.op("dve", TS(tb[R, :n], ta[R, :n], 0.25, ALU.add), reads=[TA], writes=[TB])
            fw.op("dve", TS(ti[R, :n].bitcast(F32), tb[R, :n], 0.5, ALU.is_gt), reads=[TB], writes=[TI])
            fw.op("dve", TT(tb[R, :n], tb[R, :n], ti[R, :n].bitcast(F32), ALU.subtract), reads=[TI, TB], writes=[TB])
            yield
            fw.op("act", ACTF(cos_ap, tb[R, :n], AF.Sin, scale=TWO_PI), reads=[TB], writes=[COS])
            yield

        with contextlib.ExitStack() as st1:
            def sb1(name, shape, dt):
                return st1.enter_context(nc.sbuf_tensor("s_" + name, list(shape), dt))

            TBK = 1024
            TPT = TBK // 512
            wkv = sb1("wkv_bf", [128, 8, 448], BF16)
            WKVST, WKV = Buf("wkvst"), Buf("wkv")
            with nc.sbuf_tensor("s_wkv_st", [128, 8, 448], F32) as wkv_st:
                fw.dma("sp", wkv_st[:].rearrange("p a b -> p (a b)"), wkv_d, writes=[WKVST], key=WKVST)
                for c in range(8):
                    fw.op("dve", TS(wkv[:, c, :], wkv_st[:, c, :], vcol(V_GMIX + c), ALU.mult),
                          reads=[WKVST, VEC], writes=[WKV])
                fw.barrier()
            xb = [sb1(f"xb{i}", [128, 8 * 512], BF16) for i in range(2)]
            XB = [Buf("xb0"), Buf("xb1")]
            sq = [sb1(f"sq{i}", [128, 8 * 512], BF16) for i in range(2)]
            SQ = [Buf("sq0"), Buf("sq1")]
            rs = [sb1(f"rs{i}", [128, 512], F32) for i in range(2)]
            RS = [Buf("rs0"), Buf("rs1")]
            lnt = [sb1(f"lnt{i}", [128, 512], F32) for i in range(2)]
            LNT = [Buf("lnt0"), Buf("lnt1")]
            kvl = [sb1(f"kvl{i}", [128, 2, 512], F32) for i in range(2)]
            KVL = [Buf("kvl0"), Buf("kvl1")]
            sq2 = [sb1(f"sq2{i}", [128, 1024], BF16) for i in range(2)]
            SQ2 = [Buf("sq20"), Buf("sq21")]
            rs2 = [sb1(f"rs2{i}", [128, 512], F32) for i in range(2)]
            RS2 = [Buf("rs20"), Buf("rs21")]
            tpa = [sb1(f"tpa{i}", [96, 512], F32) for i in range(2)]
            tpb = [sb1(f"tpb{i}", [96, 512], F32) for i in range(2)]
            TPA = [Buf("tpa0"), Buf("tpa1")]
            TPB = [Buf("tpb0"), Buf("tpb1")]
            posk = [sb1(f"posk{i}", [96, TBK], I32) for i in range(2)]
            POSK = [Buf("posk0"), Buf("posk1")]
            cosk = [sb1(f"cosk{i}", [96, TBK], F32) for i in range(2)]
            sink = [sb1(f"sink{i}", [96, TBK], F32) for i in range(2)]
            COSK = [Buf("cosk0"), Buf("cosk1")]
            SINK = [Buf("sink0"), Buf("sink1")]
            tta = sb1("tta", [96, TBK], F32)
            ttb = sb1("ttb", [96, TBK], F32)
            tti = sb1("tti", [96, TBK], I32)
            TTA, TTB, TTI = Buf("tta"), Buf("ttb"), Buf("tti")

            rot = Rot(range(8))

            def tables_batch(tb_i):
                i = tb_i % 2
                fw.dma("sp", posk[i][64:96, :],
                       pos_all[0:1, tb_i * TBK:(tb_i + 1) * TBK].partition_broadcast(32),
                       writes=[POSK[i]], key=POSK[i])
                return rope_tables(posk[i][64:96, :], POSK[i], TBK, cosk[i][64:96, :], COSK[i],
                                   sink[i][64:96, :], SINK[i], tta, TTA, ttb, TTB, tti, TTI)

            g0 = tables_batch(0)
            for _ in g0:
                pass
            gen = None
            for t in range(16):
                i = t % 2
                if t % TPT == 0 and t // TPT + 1 < 16 // TPT:
                    gen = tables_batch(t // TPT + 1)
                    next(gen)
                    gcnt = 0
                tbi = (t // TPT) % 2
                tcol = (t % TPT) * 512
                fw.dma("pool", xb[i][:], xT_all[t], writes=[XB[i]], key=XB[i])
                fw.op("act", ACTF(sq[i][:], xb[i][:], AF.Square), reads=[XB[i]], writes=[SQ[i]])
                if gen is not None and t % TPT == 0:
                    next(gen)
                bA = rot.next()
                for c in range(8):
                    fw.op("pe", MM(bank(bA), ones[:, :], sq[i][:, c * 512:(c + 1) * 512], c == 0, c == 7),
                          reads=[ONES, SQ[i]], writes=[PB[bA]], signal=(c == 7))
                rstd_from_bank(bank(bA), PB[bA], rs[i][:], RS[i], lnt[i][:], LNT[i], 128, 1.0 / D)
                for m in range(2):
                    bk = rot.next()
                    for c in range(8):
                        fw.op("pe", MM(bank(bk), wkv[:, c, m * 128:(m + 1) * 128], xb[i][:, c * 512:(c + 1) * 512],
                                       c == 0, c == 7), reads=[WKV, XB[i]], writes=[PB[bk]], signal=(c == 7))
                    fw.op("dve", TT(kvl[i][:, m, :], bank(bk), rs[i][:], ALU.mult),
                          reads=[PB[bk], RS[i]], writes=[KVL[i]])
                fw.op("act", ACTF(sq2[i][:], kvl[i][:].rearrange("p a b -> p (a b)"), AF.Square),
                      reads=[KVL[i]], writes=[SQ2[i]])
                bD = rot.next()
                for c in range(8):
                    fw.op("pe", MM(bank(bD)[0:96, :], wkv[:, c, 256:352], xb[i][:, c * 512:(c + 1) * 512],
                                   c == 0, c == 7), reads=[WKV, XB[i]], writes=[PB[bD]], signal=(c == 7))
                bE = rot.next()
                for c in range(8):
                    fw.op("pe", MM(bank(bE)[0:96, :], wkv[:, c, 352:448], xb[i][:, c * 512:(c + 1) * 512],
                                   c == 0, c == 7), reads=[WKV, XB[i]], writes=[PB[bE]], signal=(c == 7))
                bC = rot.next()
                for m in range(2):
                    fw.op("pe", MM(bank(bC), ones[:, :], sq2[i][:, m * 512:(m + 1) * 512], m == 0, m == 1),
                          reads=[ONES, SQ2[i]], writes=[PB[bC]], signal=(m == 1))
                rstd_from_bank(bank(bC), PB[bC], rs2[i][:], RS2[i], lnt[i][:], LNT[i], 128, 1.0 / 256)
                for m in range(2):
                    fw.op("dve", STT(kvn[:, m, t * 512:(t + 1) * 512], kvl[i][:, m, :], vcol(V_GKV + m),
                                     rs2[i][:], ALU.mult, ALU.mult),
                          reads=[KVL[i], RS2[i], VEC], writes=[KVN[t]])
                R = slice(64, 96)
                fw.op("dve", TT(tpa[i][R, :], bank(bD)[R, :], cosk[tbi][R, tcol:tcol + 512], ALU.mult),
                      reads=[PB[bD], COSK[tbi]], writes=[TPA[i]])
                fw.op("dve", TT(tpb[i][R, :], bank(bE)[R, :], sink[tbi][R, tcol:tcol + 512], ALU.mult),
                      reads=[PB[bE], SINK[tbi]], writes=[TPB[i]])
                fw.op("pool", TT(tpa[i][R, :], tpa[i][R, :], tpb[i][R, :], ALU.add),
                      reads=[TPA[i], TPB[i]], writes=[TPA[i]])
                fw.op("pool", TT(kbuf[0][R, t * 512:(t + 1) * 512], tpa[i][R, :], rs[i][R, :], ALU.mult),
                      reads=[TPA[i], RS[i]], writes=[KB_PE[0]])
                if gen is not None and t % TPT == 0:
                    next(gen)
                    gen = None
            fw.dma("sp", kbuf[1][64:96, :], kbuf[0][64:96, :], reads=[KB_PE[0]], writes=[KB_PE[1]], key=KB_PE[1])
            fw.barrier()

        if stage == 1:
            with contextlib.ExitStack() as std:
                dt_ = std.enter_context(nc.sbuf_tensor("dbgt", [128, 8 * NT], F32))
                DT = Buf("dbgt")
                fw.op("dve", MSET(dt_[:], 0.0), writes=[DT])
                fw.op("dve", CP(dt_[:, 0:8192], kvn[:, 0, :]), reads=KVN, writes=[DT])
                fw.op("dve", CP(dt_[64:96, 8192:16384], kbuf[1][64:96, :]), reads=[KB_PE[1]], writes=[DT])
                fw.dma("sp", dbg_d, dt_[:], reads=[DT], key=DT)
                fw.barrier()
            return nc

        qn = sbA("qn", [128, 3, NT], BF16)
        QN = [Buf(f"qn{i}") for i in range(6)]
        cos_l = sbA("cos_l", [96, NT], F32)
        sin_l = sbA("sin_l", [96, NT], F32)
        COSL, SINL = Buf("cosl"), Buf("sinl")
        PW = 352
        PCS = pieces_of(NT, PW)

        with contextlib.ExitStack() as st2t:
            def sb2(name, shape, dt):
                return st2t.enter_context(nc.sbuf_tensor("s_" + name, list(shape), dt))
            posl = sb2("posl", [96, NT], I32)
            POSL = Buf("posl")
            lta = sb2("lta", [96, NT], F32)
            ltb = sb2("ltb", [96, NT], F32)
            lti = sb2("lti", [96, NT], I32)
            LTA, LTB, LTI = Buf("lta"), Buf("ltb"), Buf("lti")
            fw.dma("sp", posl[64:96, :], pos_loc[0:1, :].partition_broadcast(32), writes=[POSL], key=POSL)
            for _ in rope_tables(posl[64:96, :], POSL, NT, cos_l[64:96, :], COSL, sin_l[64:96, :], SINL,
                                 lta, LTA, ltb, LTB, lti, LTI):
                pass
            fw.barrier()
        with contextlib.ExitStack() as st2:
            def sb2(name, shape, dt):
                return st2.enter_context(nc.sbuf_tensor("s_" + name, list(shape), dt))

            wq = sb2("wq_bf", [128, 8, 384], BF16)
            WQ = Buf("wq")
            fw.dma("pool", wq[:].rearrange("p a b -> p (a b)"), wq_d, writes=[WQ], key=WQ)
            xl = [sb2(f"xl{i}", [128, 8, PW], F32) for i in range(2)]
            XL = [Buf("xl0"), Buf("xl1")]
            sqx = [sb2(f"sqx{i}", [128, 8, PW], BF16) for i in range(2)]
            SQX = [Buf("sqx0"), Buf("sqx1")]
            up = [sb2(f"up{i}", [128, 8, PW], BF16) for i in range(2)]
            UP = [Buf("up0"), Buf("up1")]
            rsl = [sb2(f"rsl{i}", [128, PW], F32) for i in range(2)]
            RSL = [Buf("rsl0"), Buf("rsl1")]
            lnl = [sb2(f"lnl{i}", [128, PW], F32) for i in range(2)]
            LNL = [Buf("lnl0"), Buf("lnl1")]
            sq3 = [sb2(f"sq3{i}", [128, 3, PW], BF16) for i in range(2)]
            SQ3 = [Buf("sq30"), Buf("sq31")]
            rsq = [sb2(f"rsq{i}", [128, PW], F32) for i in range(2)]
            RSQ = [Buf("rsq0"), Buf("rsq1")]
            rot = Rot(range(8))
            for pi, (a, w) in enumerate(PCS):
                i = pi % 2
                fw.dma("sp", xl[i][:, :, :w], xT_loc[:, :, a:a + w], writes=[XL[i]], key=XL[i])
                fw.op("act", ACTF(sqx[i][:, :, :w], xl[i][:, :, :w], AF.Square), reads=[XL[i]], writes=[SQX[i]])
                bA = rot.next()
                for c in range(8):
                    fw.op("pe", MM(bank(bA)[:, :w], ones[:, :], sqx[i][:, c, :w], c == 0, c == 7),
                          reads=[ONES, SQX[i]], writes=[PB[bA]], signal=(c == 7))
                rstd_from_bank(bank(bA)[:, :w], PB[bA], rsl[i][:, :w], RSL[i], lnl[i][:, :w], LNL[i], 128, 1.0 / D)
                for c in range(8):
                    fw.op("dve", STT(up[i][:, c, :w], xl[i][:, c, :w], vcol(V_GMIX + c), rsl[i][:, :w],
                                     ALU.mult, ALU.mult), reads=[XL[i], RSL[i], VEC], writes=[UP[i]])
                bq = []
                for m in range(3):
                    bk = rot.next()
                    bq.append(bk)
                    for c in range(8):
                        fw.op("pe", MM(bank(bk)[:, :w], wq[:, c, m * 128:(m + 1) * 128], up[i][:, c, :w],
                                       c == 0, c == 7), reads=[WQ, UP[i]], writes=[PB[bk]], signal=(c == 7))
                    fw.op("act", ACTF(sq3[i][:, m, :w], bank(bk)[:, :w], AF.Square), reads=[PB[bk]], writes=[SQ3[i]])
                bS = rot.next()
                for m in range(3):
                    fw.op("pe", MM(bank(bS)[:, :w], ones[:, :], sq3[i][:, m, :w], m == 0, m == 2),
                          reads=[ONES, SQ3[i]], writes=[PB[bS]], signal=(m == 2))
                rstd_from_bank(bank(bS)[:, :w], PB[bS], rsq[i][:, :w], RSQ[i], lnl[i][:, :w], LNL[i], 128, 1.0 / 384)
                for m in range(3):
                    fw.op("dve", STT(qn[:, m, a:a + w], bank(bq[m])[:, :w], vcol(V_GQ + m), rsq[i][:, :w],
                                     ALU.mult, ALU.mult), reads=[PB[bq[m]], RSQ[i], VEC], writes=[QN[pi]])
            fw.barrier()

        with contextlib.ExitStack() as st3:
            def sb3(name, shape, dt):
                return st3.enter_context(nc.sbuf_tensor("s_" + name, list(shape), dt))

            vbuf = [sb3(f"vbuf{i}", [128, 64, 128], BF16) for i in range(2)]
            VB = [[Buf(f"vb{i}_{u}") for u in range(8)] for i in range(2)]
            VONE = [Buf("vone0"), Buf("vone1")]
            fw.op("pool", MSET(vbuf[0][:, :, 64:128], 1.0), writes=[VONE[0]])
            fw.op("pool", MSET(vbuf[1][:, :, 0:64], 1.0), writes=[VONE[1]])
            qbuf = [sb3(f"qbuf{i}", [96, NT], BF16) for i in range(2)]
            QB = [Buf("qb0"), Buf("qb1")]
            NP = 3
            pbuf = [sb3(f"pbuf{i}", [128, 512], BF16) for i in range(NP)]
            PBUF = [Buf(f"pbuf{i}") for i in range(NP)]
            dtmp = sb3("dtmp", [128, NT], F32)
            DTMP = Buf("dtmp")
            qta = sb3("qta", [96, 512], F32)
            qtb = sb3("qtb", [96, 512], F32)
            QTA, QTB = Buf("qta"), Buf("qtb")
            APC = pieces_of(NT, 512)

            def build_units(h, bank_rot):
                hb = h % 2
                us = []

                def k_unit(t):
                    def f():
                        bk = bank_rot.next()
                        for c in range(2):
                            fw.op("pe", MM(bank(bk)[0:64, :], wkvu[:, c, h * 128:h * 128 + 64],
                                           kvn[:, c, t * 512:(t + 1) * 512], c == 0, c == 1),
                                  reads=[WKVU, KVN[t]], writes=[PB[bk]], signal=(c == 1))
                        fw.op("dve", CP(kbuf[hb][0:64, t * 512:(t + 1) * 512], bank(bk)[0:64, :]),
                              reads=[PB[bk]], writes=[KB_NO[hb][t]])
                    return f

                def v_unit(u):
                    def f():
                        bk = bank_rot.next()
                        for tt in range(8):
                            tile = 8 * u + tt
                            for c in range(2):
                                fw.op("pe", MM(bank(bk)[:, tt * 64:(tt + 1) * 64],
                                               kvn[:, c, tile * 128:(tile + 1) * 128],
                                               wkvu[:, c, h * 128 + 64:h * 128 + 128], c == 0, c == 1),
                                      reads=[WKVU, KVN[tile // 4]], writes=[PB[bk]],
                                      signal=(c == 1 and tt == 7))
                        voff = 0 if hb == 0 else 64
                        fw.op("dve", CP(vbuf[hb][:, 8 * u:8 * u + 8, voff:voff + 64],
                                        bank(bk).rearrange("p (a b) -> p a b", b=64)),
                              reads=[PB[bk]], writes=[VB[hb][u]])
                    return f

                def q_unit(pi, a, w):
                    def f():
                        bk = bank_rot.next()
                        for c in range(3):
                            fw.op("pe", MM(bank(bk)[0:96, :w], wqu[:, c, h * 96:(h + 1) * 96], qn[:, c, a:a + w],
                                           c == 0, c == 2), reads=[WQU] + QN, writes=[PB[bk]], signal=(c == 2))
                        fw.op("dve", CP(qbuf[hb][0:64, a:a + w], bank(bk)[0:64, :w]), reads=[PB[bk]], writes=[QB[hb]])
                        fw.op("dve", TT(qta[64:96, :w], bank(bk)[64:96, :w], cos_l[64:96, a:a + w], ALU.mult),
                              reads=[PB[bk], COSL], writes=[QTA])
                        bk2 = bank_rot.next()
                        for c in range(3):
                            fw.op("pe", MM(bank(bk2)[0:96, :w], wqs[:, c, h * 96:(h + 1) * 96], qn[:, c, a:a + w],
                                           c == 0, c == 2), reads=[WQS] + QN, writes=[PB[bk2]], signal=(c == 2))
                        fw.op("dve", TT(qtb[64:96, :w], bank(bk2)[64:96, :w], sin_l[64:96, a:a + w], ALU.mult),
                              reads=[PB[bk2], SINL], writes=[QTB])
                        fw.op("dve", TT(qbuf[hb][64:96, a:a + w], qta[64:96, :w], qtb[64:96, :w], ALU.add),
                              reads=[QTA, QTB], writes=[QB[hb]])
                    return f

                for pi, (a, w) in enumerate(APC):
                    us.append(q_unit(pi, a, w))
                kk = [k_unit(t) for t in range(16)]
                vv = [v_unit(u) for u in range(8)]
                for u in range(8):
                    us.append(kk[2 * u])
                    us.append(kk[2 * u + 1])
                    us.append(vv[u])
                return us

            def main_units():
                out = []
                for kb in range(64):
                    G = kb // 4
                    a0 = GW * G
                    for p, (pa, pw) in enumerate(APC):
                        lo = max(pa, a0)
                        hi = pa + pw
                        if lo < hi:
                            out.append((kb, p, lo, hi - lo))
                return out

            LASTKB = {}
            for p, (pa, pw) in enumerate(APC):
                LASTKB[p] = 4 * min(15, (pa + pw - 1) // GW) + 3

            for un in build_units(0, Rot(range(8))):
                un()
            for h in range(NH):
                hb = h % 2
                units = main_units()
                builds = build_units(h + 1, Rot([7])) if h + 1 < NH else []
                bi = 0

                def qk(k):
                    kb, p, a, w = units[k]
                    sbk = 5 + k % 2
                    fw.op("pe", MM(bank(sbk)[:, :w], kbuf[hb][0:96, kb * 128:(kb + 1) * 128], qbuf[hb][0:96, a:a + w],
                                   True, True), reads=[KB_NO[hb][kb // 4], KB_PE[hb], QB[hb]], writes=[PB[sbk]])

                qk(0)
                for k in range(len(units)):
                    kb, p, a, w = units[k]
                    G, r = kb // 4, kb % 4
                    sbk = 5 + k % 2
                    pb = k % NP
                    if k + 1 < len(units):
                        qk(k + 1)
                    fw.op("act", ACTF(pbuf[pb][:, :w], bank(sbk)[:, :w], AF.Exp, scale=SCALE),
                          reads=[PB[sbk]], writes=[PBUF[pb]])
                    lo = max(a, GW * G)
                    hi = min(a + w, GW * G + GW)
                    if lo < hi:
                        fw.op("pool", TT(pbuf[pb][:, lo - a:hi - a], pbuf[pb][:, lo - a:hi - a],
                                         mask[:, r * GW + lo - GW * G:r * GW + hi - GW * G], ALU.mult),
                              reads=[PBUF[pb], MASK], writes=[PBUF[pb]])
                    fw.op("pe", MM(ps[:, a:a + w], vbuf[hb][:, kb, :], pbuf[pb][:, :w], kb == 0, kb == LASTKB[p]),
                          reads=[VB[hb][kb // 8], VONE[hb], PBUF[pb]], writes=[PB[p]])
                    if k % 6 == 5 and bi < len(builds):
                        builds[bi]()
                        bi += 1
                while bi < len(builds):
                    builds[bi]()
                    bi += 1
                lo_r, hi_r = (slice(0, 64), slice(64, 128)) if hb == 0 else (slice(64, 128), slice(0, 64))
                for p, (pa, pw) in enumerate(APC):
                    cs = slice(pa, pa + pw)
                    fw.op("dve", TS(dtmp[lo_r, cs], ps[hi_r, cs], 1e-30, ALU.max), reads=[PB[p]], writes=[DTMP])
                    fw.op("dve", RCP(dtmp[lo_r, cs], dtmp[lo_r, cs]), reads=[DTMP], writes=[DTMP])
                    fw.op("dve", TT(ob[lo_r, h // 2, cs], ps[lo_r, cs], dtmp[lo_r, cs], ALU.mult),
                          reads=[PB[p], DTMP], writes=[OB[h // 2]])
            fw.barrier()

        if stage == 2:
            with contextlib.ExitStack() as std:
                dt_ = std.enter_context(nc.sbuf_tensor("dbgt", [128, 8 * NT], F32))
                DT = Buf("dbgt")
                fw.op("dve", MSET(dt_[:], 0.0), writes=[DT])
                fw.op("dve", CP(dt_[:, 0:4 * NT], ob[:].rearrange("p a b -> p (a b)")), reads=OB, writes=[DT])
                fw.op("dve", CP(dt_[:, 4 * NT:7 * NT], qn[:].rearrange("p a b -> p (a b)")), reads=QN, writes=[DT])
                fw.dma("sp", dbg_d, dt_[:], reads=[DT], key=DT)
                fw.barrier()
            return nc

        stA.close()

        NTH = NT // 2
        HPC = pieces_of(NTH, PW)
        rot = Rot(range(8))
        OUTB = Buf("outdma")

        def sq_stat_rs(src_fn, SRC, nch, w, sqt, SQT, rst, RST, lnb, LNB, inv_d, sq_eng="pool"):
            for c in range(nch):
                if sq_eng == "act":
                    fw.op("act", ACTF(sqt[:, c, :w], src_fn(c), AF.Square), reads=SRC, writes=[SQT])
                else:
                    fw.op("pool", TT(sqt[:, c, :w], src_fn(c), src_fn(c), ALU.mult), reads=SRC, writes=[SQT])
            bS = rot.next()
            for c in range(nch):
                fw.op("pe", MM(bank(bS)[:, :w], ones[:, :], sqt[:, c, :w], c == 0, c == nch - 1),
                      reads=[ONES, SQT], writes=[PB[bS]], signal=(c == nch - 1))
            rstd_from_bank(bank(bS)[:, :w], PB[bS], rst[:, :w], RST, lnb[:, :w], LNB, 128, inv_d)

        for hf in range(2):
            c0 = hf * NTH
            with contextlib.ExitStack() as sth:
                def sbh(name, shape, dt):
                    return sth.enter_context(nc.sbuf_tensor(f"s_{name}_{hf}", list(shape), dt))

                xres = sbh("xres", [128, 8, NTH], F32)
                XRES = [Buf(f"xres{i}") for i in range(3)]
                mixed = sbh("mixed", [128, 8, NTH], BF16)
                MIX = [Buf(f"mix{i}") for i in range(3)]
                u2 = sbh("u2", [128, 8, NTH], BF16)
                U2 = [Buf(f"u2{i}") for i in range(3)]
                sqt = sbh("sqt", [128, 8, PW], BF16)
                SQT = Buf("sqt")
                rst = sbh("rst", [128, PW], F32)
                RST = Buf("rst")
                lnb = sbh("lnb", [128, PW], F32)
                LNB = Buf("lnb")
                for pi, (a, w) in enumerate(HPC):
                    fw.dma("sp", xres[:, :, a:a + w], xT_loc[:, :, c0 + a:c0 + a + w], writes=[XRES[pi]], key=XRES[pi])

                with contextlib.ExitStack() as sab:
                    def sbab(name, shape, dt):
                        return sab.enter_context(nc.sbuf_tensor(f"s_{name}_{hf}", list(shape), dt))

                    uh = sbab("uh", [128, 8, NTH], BF16)
                    UH = [Buf(f"uh{i}") for i in range(3)]
                    yap = sbab("yap", [128, 4, NTH], BF16)
                    YAP = [Buf(f"yap{m}") for m in range(4)]
                    for pi, (a, w) in enumerate(HPC):
                        sq_stat_rs(lambda c: xres[:, c, a:a + w], [XRES[pi]], 8, w, sqt, SQT, rst, RST, lnb, LNB,
                                   1.0 / D, sq_eng="act")
                        for c in range(8):
                            fw.op("dve", STT(uh[:, c, a:a + w], xres[:, c, a:a + w], vcol(V_GMIX + c), rst[:, :w],
                                             ALU.mult, ALU.mult), reads=[XRES[pi], RST, VEC], writes=[UH[pi]])
                    with contextlib.ExitStack() as sa:
                        def sba(name, shape, dt):
                            return sa.enter_context(nc.sbuf_tensor(f"s_{name}_{hf}", list(shape), dt))

                        wab = sba("wab", [128, 8, 1536], BF16)
                        WAB = Buf("wab")
                        fw.dma("pool", wab[:].rearrange("p a b -> p (a b)"), wab_d, writes=[WAB], key=WAB)
                        t1 = [sba(f"t1{i}", [128, PW], F32) for i in range(2)]
                        T1 = [Buf("t10"), Buf("t11")]
                        cx = [sba(f"cx{i}", [128, NTH], F32) for i in range(2)]
                        CX = [Buf("cx0"), Buf("cx1")]
                        cv = [sba(f"cv{i}", [128, NTH], F32) for i in range(2)]
                        CV = [Buf("cv0"), Buf("cv1")]
                        k = 0
                        for m in range(4):
                            i = m % 2
                            for pi, (a, w) in enumerate(HPC):
                                bc = rot.next()
                                for c in range(8):
                                    fw.op("pe", MM(bank(bc)[:, :w], wab[:, c, 512 + m * 128:512 + (m + 1) * 128],
                                                   uh[:, c, a:a + w], c == 0, c == 7),
                                          reads=[WAB, UH[pi]], writes=[PB[bc]], signal=(c == 7))
                                fw.op("act", ACTF(t1[k % 2][:, :w], bank(bc)[:, :w], AF.Copy),
                                      reads=[PB[bc]], writes=[T1[k % 2]])
                                bx = rot.next()
                                for c in range(8):
                                    fw.op("pe", MM(bank(bx)[:, :w], wab[:, c, 1024 + m * 128:1024 + (m + 1) * 128],
                                                   uh[:, c, a:a + w], c == 0, c == 7),
                                          reads=[WAB, UH[pi]], writes=[PB[bx]], signal=(c == 7))
                                fw.op("dve", TT(cx[i][:, a:a + w], bank(bx)[:, :w], t1[k % 2][:, :w], ALU.mult),
                                      reads=[PB[bx], T1[k % 2]], writes=[CX[i]])
                                k += 1
                            w0, w1, w2 = (vcol(V_CAW + 3 * m + kk) for kk in range(3))
                            fw.op("pool", TS(cv[i][:, :], cx[i][:, :], w2, ALU.mult, 0.0, ALU.add), reads=[CX[i], VEC], writes=[CV[i]])
                            fw.op("dve", STT(cv[i][:, 1:NTH], cx[i][:, 0:NTH - 1], w1, cv[i][:, 1:NTH], ALU.mult, ALU.add),
                                  reads=[CX[i], CV[i], VEC], writes=[CV[i]])
                            fw.op("dve", STT(cv[i][:, 2:NTH], cx[i][:, 0:NTH - 2], w0, cv[i][:, 2:NTH], ALU.mult, ALU.add),
                                  reads=[CX[i], CV[i], VEC], writes=[CV[i]])
                            for pi, (a, w) in enumerate(HPC):
                                bb = rot.next()
                                for c in range(8):
                                    fw.op("pe", MM(bank(bb)[:, :w], wab[:, c, m * 128:(m + 1) * 128],
                                                   uh[:, c, a:a + w], c == 0, c == 7),
                                          reads=[WAB, UH[pi]], writes=[PB[bb]], signal=(c == 7))
                                fw.op("dve", TT(yap[:, m, a:a + w], bank(bb)[:, :w], cv[i][:, a:a + w], ALU.mult),
                                      reads=[PB[bb], CV[i]], writes=[YAP[m]])
                        fw.barrier()
                    with contextlib.ExitStack() as sbb:
                        def sbb_(name, shape, dt):
                            return sbb.enter_context(nc.sbuf_tensor(f"s_{name}_{hf}", list(shape), dt))

                        wg = sbb_("wg", [128, 8, 2048], BF16)
                        wao = sbb_("wao", [128, 4, 1024], BF16)
                        wbo = sbb_("wbo", [128, 4, 1024], BF16)
                        WG, WAO, WBO = Buf("wg"), Buf("wao"), Buf("wbo")
                        fw.dma("pool", wao[:].rearrange("p a b -> p (a b)"), wao_d, writes=[WAO], key=WAO)
                        fw.dma("pool", wbo[:].rearrange("p a b -> p (a b)"), wbo_d, writes=[WBO], key=WBO)
                        fw.dma("pool", wg[:].rearrange("p a b -> p (a b)"), wg_d, writes=[WG], key=WG)
                        sga = [sbb_(f"sga{i}", [128, PW], F32) for i in range(2)]
                        sgb = [sbb_(f"sgb{i}", [128, PW], F32) for i in range(2)]
                        SGA = [Buf("sga0"), Buf("sga1")]
                        SGB = [Buf("sgb0"), Buf("sgb1")]
                        k = 0
                        for mo in range(8):
                            for pi, (a, w) in enumerate(HPC):
                                i = k % 2
                                k += 1
                                b1, b2, b3, b4 = rot.next(), rot.next(), rot.next(), rot.next()
                                for c in range(8):
                                    fw.op("pe", MM(bank(b1)[:, :w], wg[:, c, mo * 128:(mo + 1) * 128], uh[:, c, a:a + w],
                                                   c == 0, c == 7), reads=[WG, UH[pi]], writes=[PB[b1]], signal=(c == 7))
                                fw.op("act", ACTF(sga[i][:, :w], bank(b1)[:, :w], AF.Sigmoid), reads=[PB[b1]], writes=[SGA[i]])
                                for c in range(4):
                                    fw.op("pe", MM(bank(b2)[:, :w], wao[:, c, mo * 128:(mo + 1) * 128], yap[:, c, a:a + w],
                                                   c == 0, c == 3), reads=[WAO] + YAP, writes=[PB[b2]], signal=(c == 3))
                                fw.op("dve", TT(sga[i][:, :w], bank(b2)[:, :w], sga[i][:, :w], ALU.mult),
                                      reads=[PB[b2], SGA[i]], writes=[SGA[i]])
                                for c in range(8):
                                    fw.op("pe", MM(bank(b3)[:, :w], wg[:, c, 1024 + mo * 128:1024 + (mo + 1) * 128],
                                                   uh[:, c, a:a + w], c == 0, c == 7),
                                          reads=[WG, UH[pi]], writes=[PB[b3]], signal=(c == 7))
                                fw.op("act", ACTF(sgb[i][:, :w], bank(b3)[:, :w], AF.Sigmoid), reads=[PB[b3]], writes=[SGB[i]])
                                for c in range(4):
                                    fw.op("pe", MM(bank(b4)[:, :w], wbo[:, c, mo * 128:(mo + 1) * 128],
                                                   ob[:, c, c0 + a:c0 + a + w], c == 0, c == 3),
                                          reads=[WBO] + OB, writes=[PB[b4]], signal=(c == 3))
                                fw.op("dve", TT(sgb[i][:, :w], bank(b4)[:, :w], sgb[i][:, :w], ALU.mult),
                                      reads=[PB[b4], SGB[i]], writes=[SGB[i]])
                                fw.op("pool", TT(mixed[:, mo, a:a + w], sga[i][:, :w], sgb[i][:, :w], ALU.add),
                                      reads=[SGA[i], SGB[i]], writes=[MIX[pi]])
                        fw.barrier()

                with contextlib.ExitStack() as sc_:
                    def sbc(name, shape, dt):
                        return sc_.enter_context(nc.sbuf_tensor(f"s_{name}_{hf}", list(shape), dt))

                    wo = sbc("wo", [128, 8, 1024], BF16)
                    WO = Buf("wo")
                    fw.dma("pool", wo[:].rearrange("p a b -> p (a b)"), wo_d, writes=[WO], key=WO)
                    mos = sbc("mos", [128, 8, PW], F32)
                    MOS = Buf("mos")
                    tmpc = [sbc(f"tmpc{i}", [128, PW], F32) for i in range(2)]
                    TMPC = [Buf("tmpc0"), Buf("tmpc1")]
                    for pi, (a, w) in enumerate(HPC):
                        for m in range(8):
                            bk = rot.next()
                            for c in range(8):
                                fw.op("pe", MM(bank(bk)[:, :w], wo[:, c, m * 128:(m + 1) * 128], mixed[:, c, a:a + w],
                                               c == 0, c == 7), reads=[WO, MIX[pi]], writes=[PB[bk]], signal=(c == 7))
                            fw.op("act", ACTF(mos[:, m, :w], bank(bk)[:, :w], AF.Copy), reads=[PB[bk]], writes=[MOS])
                        sq_stat_rs(lambda c: mos[:, c, :w], [MOS], 8, w, sqt, SQT, rst, RST, lnb, LNB, 1.0 / D)
                        for m in range(8):
                            i = m % 2
                            fw.op("dve", STT(tmpc[i][:, :w], mos[:, m, :w], vcol(V_GMPOST + m), rst[:, :w],
                                             ALU.mult, ALU.mult), reads=[MOS, RST, VEC], writes=[TMPC[i]])
                            fw.op("pool", TT(xres[:, m, a:a + w], xres[:, m, a:a + w], tmpc[i][:, :w], ALU.add),
                                  reads=[XRES[pi], TMPC[i]], writes=[XRES[pi]])
                        sq_stat_rs(lambda c: xres[:, c, a:a + w], [XRES[pi]], 8, w, sqt, SQT, rst, RST, lnb, LNB,
                                   1.0 / D, sq_eng="act")
                        for c in range(8):
                            fw.op("dve", STT(u2[:, c, a:a + w], xres[:, c, a:a + w], vcol(V_GFPRE + c), rst[:, :w],
                                             ALU.mult, ALU.mult), reads=[XRES[pi], RST, VEC], writes=[U2[pi]])
                    fw.barrier()

                with contextlib.ExitStack() as sd:
                    def sbd(name, shape, dt):
                        return sd.enter_context(nc.sbuf_tensor(f"s_{name}_{hf}", list(shape), dt))

                    ff = sbd("ff", [128, NPAIR, NTH], BF16)
                    FF = [Buf(f"ff{m}") for m in range(NPAIR)]
                    with contextlib.ExitStack() as sdd:
                        def sbdd(name, shape, dt):
                            return sdd.enter_context(nc.sbuf_tensor(f"s_{name}_{hf}", list(shape), dt))

                        NWB = 3
                        wup = [sbdd(f"wup{i}", [128, 8, 256], BF16) for i in range(NWB)]
                        WUP = [Buf(f"wup{i}") for i in range(NWB)]
                        rows = [[sbdd(f"row{i}_{r}", [128, NTH], F32) for r in range(4)] for i in range(2)]
                        ROW = [[Buf(f"row{i}_{r}") for r in range(4)] for i in range(2)]

                        def load_wup(m):
                            fw.dma("pool", wup[m % NWB][:].rearrange("p a b -> p (a b)"), wup_d[m],
                                   writes=[WUP[m % NWB]], key=WUP[m % NWB])

                        for m in range(min(NWB - 1, NPAIR)):
                            load_wup(m)
                        for m in range(NPAIR):
                            if m + NWB - 1 < NPAIR:
                                load_wup(m + NWB - 1)
                            i = m % 2
                            wb = m % NWB
                            for half_i, chunk in enumerate((m, NPAIR + m)):
                                gs, t0 = rows[i][2 * half_i], rows[i][2 * half_i + 1]
                                GS, T0 = ROW[i][2 * half_i], ROW[i][2 * half_i + 1]
                                for pi, (a, w) in enumerate(HPC):
                                    bk = rot.next()
                                    for c in range(8):
                                        fw.op("pe", MM(bank(bk)[:, :w], wup[wb][:, c, half_i * 128:(half_i + 1) * 128],
                                                       u2[:, c, a:a + w], c == 0, c == 7),
                                              reads=[WUP[wb], U2[pi]], writes=[PB[bk]], signal=(c == 7))
                                    fw.op("act", ACTF(gs[:, a:a + w], bank(bk)[:, :w], AF.Copy), reads=[PB[bk]], writes=[GS])
                                cw = [vcol(V_CFW + 3 * chunk + kk) for kk in range(3)]
                                fw.op("pool", TS(t0[:, :], gs[:, :], cw[2], ALU.mult, vcol(V_BF + chunk), ALU.add),
                                      reads=[GS, VEC], writes=[T0])
                                fw.op("dve", STT(t0[:, 1:NTH], gs[:, 0:NTH - 1], cw[1], t0[:, 1:NTH], ALU.mult, ALU.add),
                                      reads=[GS, T0, VEC], writes=[T0])
                                fw.op("dve", STT(t0[:, 2:NTH], gs[:, 0:NTH - 2], cw[0], t0[:, 2:NTH], ALU.mult, ALU.add),
                                      reads=[GS, T0, VEC], writes=[T0])
                                if half_i == 0:
                                    fw.op("act", ACTF(t0[:, :], t0[:, :], AF.Gelu_apprx_tanh), reads=[T0], writes=[T0])
                            fw.op("dve", TT(ff[:, m, :], rows[i][1][:, :], rows[i][3][:, :], ALU.mult),
                                  reads=[ROW[i][1], ROW[i][3]], writes=[FF[m]])
                        fw.barrier()
                    with contextlib.ExitStack() as se:
                        def sbe(name, shape, dt):
                            return se.enter_context(nc.sbuf_tensor(f"s_{name}_{hf}", list(shape), dt))

                        wdn = [sbe(f"wdn{i}", [128, NPAIR, 512], BF16) for i in range(2)]
                        WDN = [Buf("wdn0"), Buf("wdn1")]
                        for i in range(2):
                            fw.dma("pool", wdn[i][:], wdn_d.rearrange("p (a b) -> p a b", b=1024)[:, :, i * 512:(i + 1) * 512],
                                   writes=[WDN[i]], key=WDN[i])
                        fds = sbe("fds", [128, 8, PW], F32)
                        FDS = Buf("fds")
                        tmpe = [sbe(f"tmpe{i}", [128, PW], F32) for i in range(2)]
                        TMPE = [Buf("tmpe0"), Buf("tmpe1")]
                        for pi, (a, w) in enumerate(HPC):
                            for m in range(8):
                                bk = rot.next()
                                for c in range(NPAIR):
                                    fw.op("pe", MM(bank(bk)[:, :w], wdn[m // 4][:, c, (m % 4) * 128:(m % 4 + 1) * 128],
                                                   ff[:, c, a:a + w], c == 0, c == NPAIR - 1),
                                          reads=[WDN[m // 4], FF[c]], writes=[PB[bk]], signal=(c == NPAIR - 1))
                                fw.op("act", ACTF(fds[:, m, :w], bank(bk)[:, :w], AF.Copy), reads=[PB[bk]], writes=[FDS])
                            sq_stat_rs(lambda c: fds[:, c, :w], [FDS], 8, w, sqt, SQT, rst, RST, lnb, LNB, 1.0 / D)
                            for m in range(8):
                                i = m % 2
                                fw.op("dve", STT(tmpe[i][:, :w], fds[:, m, :w], vcol(V_GFPOST + m), rst[:, :w],
                                                 ALU.mult, ALU.mult), reads=[FDS, RST, VEC], writes=[TMPE[i]])
                                fw.op("pool", TT(xres[:, m, a:a + w], xres[:, m, a:a + w], tmpe[i][:, :w], ALU.add),
                                      reads=[XRES[pi], TMPE[i]], writes=[XRES[pi]])
                        fw.barrier()

                with contextlib.ExitStack() as sf:
                    def sbf(name, shape, dt):
                        return sf.enter_context(nc.sbuf_tensor(f"s_{name}_{hf}", list(shape), dt))

                    wpg = sbf("wpg", [128, 8, 1024], BF16)
                    wpp = sbf("wpp", [128, 2, 1024], BF16)
                    ptb = sbf("ptb", [128, 2, NTH], BF16)
                    WPG, WPP, PTB = Buf("wpg"), Buf("wpp"), Buf("ptb")
                    fw.dma("pool", wpp[:].rearrange("p a b -> p (a b)"), wpp_d, writes=[WPP], key=WPP)
                    fw.dma("pool", ptb[:], pT_loc[:, :, c0:c0 + NTH], writes=[PTB], key=PTB)
                    fw.dma("pool", wpg[:].rearrange("p a b -> p (a b)"), wpg_d, writes=[WPG], key=WPG)
                    h2b = mixed
                    H2B = MIX
                    egs = sbf("egs", [128, 8, PW], F32)
                    EGS = Buf("egs")
                    sgp = [sbf(f"sgp{i}", [128, PW], F32) for i in range(2)]
                    SGP = [Buf("sgp0"), Buf("sgp1")]
                    tmpf = [sbf(f"tmpf{i}", [128, PW], F32) for i in range(2)]
                    TMPF = [Buf("tmpf0"), Buf("tmpf1")]
                    otile = [sbf(f"otile{i}", [128, 8, PW], F32) for i in range(2)]
                    OT = [Buf("ot0"), Buf("ot1")]
                    for pi, (a, w) in enumerate(HPC):
                        for c in range(8):
                            fw.op("act", ACTF(h2b[:, c, a:a + w], xres[:, c, a:a + w], AF.Copy),
                                  reads=[XRES[pi]], writes=[H2B[pi]])
                    for pi, (a, w) in enumerate(HPC):
                        oi = pi % 2
                        for m in range(8):
                            i = m % 2
                            b1, b2 = rot.next(), rot.next()
                            for c in range(8):
                                fw.op("pe", MM(bank(b1)[:, :w], wpg[:, c, m * 128:(m + 1) * 128], h2b[:, c, a:a + w],
                                               c == 0, c == 7), reads=[WPG, H2B[pi]], writes=[PB[b1]], signal=(c == 7))
                            fw.op("act", ACTF(sgp[i][:, :w], bank(b1)[:, :w], AF.Sigmoid), reads=[PB[b1]], writes=[SGP[i]])
                            for c in range(2):
                                fw.op("pe", MM(bank(b2)[:, :w], wpp[:, c, m * 128:(m + 1) * 128], ptb[:, c, a:a + w],
                                               c == 0, c == 1), reads=[WPP, PTB], writes=[PB[b2]], signal=(c == 1))
                            fw.op("dve", TT(egs[:, m, :w], bank(b2)[:, :w], sgp[i][:, :w], ALU.mult),
                                  reads=[PB[b2], SGP[i]], writes=[EGS])
                        sq_stat_rs(lambda c: egs[:, c, :w], [EGS], 8, w, sqt, SQT, rst, RST, lnb, LNB, 1.0 / D)
                        for m in range(8):
                            i = m % 2
                            fw.op("dve", STT(tmpf[i][:, :w], egs[:, m, :w], vcol(V_GPPOST + m), rst[:, :w],
                                             ALU.mult, ALU.mult), reads=[EGS, RST, VEC], writes=[TMPF[i]])
                            fw.op("pool", TT(otile[oi][:, m, :w], xres[:, m, a:a + w], tmpf[i][:, :w], ALU.add),
                                  reads=[XRES[pi], TMPF[i]], writes=[OT[oi]])
                        fw.dma("sp", out_d[:, :, c0 + a:c0 + a + w], otile[oi][:, :, :w], reads=[OT[oi]], key=OT[oi])
                    fw.barrier()
    return nc


def _chunks(w, kc):
    n = w.shape[1]
    return np.ascontiguousarray(w.reshape(kc, 128, n).transpose(1, 0, 2).reshape(128, kc * n))


def col_tokens(j):
    c = np.arange(NT)
    G = c // GW
    o = c % GW
    return 512 * G + 128 * j + (o - HALO)


def prep_inputs(inputs):
    f32 = np.float32
    x = np.asarray(inputs["x"], f32)
    p = np.asarray(inputs["p"], f32)[0]
    positions = np.asarray(inputs["positions"]).astype(np.int32)
    w_in = np.asarray(inputs["w_in"], f32)[0]

    def vec_cols(v, kc):
        return np.asarray(v, f32).reshape(kc, 128).T

    vecs = np.zeros((128, NV), f32)
    vecs[:, V_GMIX:V_GMIX + 8] = vec_cols(inputs["g_mix_pre"][0], 8)
    vecs[:, V_GQ:V_GQ + 3] = vec_cols(inputs["g_q_lat"][0], 3)
    vecs[:, V_GKV:V_GKV + 2] = vec_cols(inputs["g_kv_lat"][0], 2)
    vecs[:, V_GMPOST:V_GMPOST + 8] = vec_cols(inputs["g_mix_post"][0], 8)
    vecs[:, V_GFPRE:V_GFPRE + 8] = vec_cols(inputs["g_ffn_pre"][0], 8)
    vecs[:, V_GFPOST:V_GFPOST + 8] = vec_cols(inputs["g_ffn_post"][0], 8)
    vecs[:, V_GPPOST:V_GPPOST + 8] = vec_cols(inputs["g_ple_post"][0], 8)
    caw = np.asarray(inputs["conv_a_w"], f32)[0]
    for m in range(4):
        for k in range(3):
            vecs[:, V_CAW + 3 * m + k] = caw[k, m * 128:(m + 1) * 128]
    cfw = np.asarray(inputs["conv_ffn_w"], f32)[0]
    bfc = np.asarray(inputs["b_ffn_conv"], f32)[0]
    for m in range(44):
        for k in range(3):
            vecs[:, V_CFW + 3 * m + k] = cfw[k, m * 128:(m + 1) * 128]
        vecs[:, V_BF + m] = bfc[m * 128:(m + 1) * 128]
    inv = 1.0 / (10000.0 ** (np.arange(16, dtype=np.float64) * (2.0 / 32)))
    for i in range(32):
        vecs[64 + i, V_INV] = np.float32(inv[i % 16] / TWO_PI)
        vecs[64 + i, V_SGN] = np.float32(-TWO_PI if i < 16 else TWO_PI)
    vecs[:, V_EPS] = EPS

    kv_lat = w_in[:, 1920:2176]
    k_rope = w_in[:, 2176:2208]
    z64 = np.zeros((D, 64), f32)
    wkv = np.concatenate([kv_lat, z64, k_rope, z64, k_rope[:, 16:32], k_rope[:, 0:16]], axis=1)
    wq = w_in[:, 1536:1920]
    wab = w_in[:, 0:1536]
    wg = w_in[:, 2208:4256]
    wqu = np.asarray(inputs["w_q_up"], f32)[0]
    wqs = np.zeros_like(wqu)
    for h in range(NH):
        b0 = h * 96
        wqs[:, b0 + 64:b0 + 80] = wqu[:, b0 + 80:b0 + 96]
        wqs[:, b0 + 80:b0 + 96] = wqu[:, b0 + 64:b0 + 80]
    wup = np.asarray(inputs["w_ffn_up"], f32)[0]
    wup_p = np.empty((NPAIR, 128, 8 * 256), f32)
    for m in range(NPAIR):
        blk = np.concatenate([wup[:, m * 128:(m + 1) * 128], wup[:, DFF + m * 128:DFF + (m + 1) * 128]], axis=1)
        wup_p[m] = _chunks(blk, 8)
    shared = {
        "vecs": vecs,
        "wkv": _chunks(wkv, 8), "wq": _chunks(wq, 8), "wab": _chunks(wab, 8), "wg": _chunks(wg, 8),
        "wqu": _chunks(wqu, 3), "wqs": _chunks(wqs, 3),
        "wkvu": _chunks(np.asarray(inputs["w_kv_up"], f32)[0], 2),
        "wao": _chunks(np.asarray(inputs["w_a_out"], f32)[0], 4),
        "wbo": _chunks(np.asarray(inputs["w_b_out"], f32)[0], 4),
        "wo": _chunks(np.asarray(inputs["w_o"], f32)[0], 8),
        "wup": wup_p,
        "wdn": _chunks(np.asarray(inputs["w_ffn_down"], f32)[0], NPAIR),
        "wpp": _chunks(np.asarray(inputs["w_ple_proj"], f32)[0], 2),
        "wpg": _chunks(np.asarray(inputs["w_ple_gate"], f32)[0], 8),
    }
    in_maps = []
    per_batch = {}
    for b in range(2):
        xT = x[b].T
        xa = xT.reshape(8, 128, 16, 512).transpose(2, 1, 0, 3).reshape(16, 128, 8 * 512)
        per_batch[b] = (np.ascontiguousarray(xa), np.ascontiguousarray(positions[b][None, :]))
    for core in range(NCORE):
        b, j = core // CPB, core % CPB
        tok = col_tokens(j)
        valid = tok >= 0
        tk = np.where(valid, tok, 0)
        xl = x[b][tk] * valid[:, None].astype(f32)
        xl = np.ascontiguousarray(xl.T.reshape(8, 128, NT).transpose(1, 0, 2))
        pl = p[b][tk] * valid[:, None].astype(f32)
        pl = np.ascontiguousarray(pl.T.reshape(2, 128, NT).transpose(1, 0, 2))
        posl = np.where(valid, positions[b][tk], 0).astype(np.int32)[None, :]
        mk = np.zeros((128, 4, GW), f32)
        kk = np.arange(128)[:, None]
        qq = np.arange(128)[None, :]
        for r in range(4):
            if r < j:
                mk[:, r, :] = 1.0
            elif r == j:
                mk[:, r, HALO:] = ((kk // 64) <= (qq // 64)).astype(f32)
                mk[:, r, :HALO] = 0.0
        m = dict(shared)
        m.update({"xT_all": per_batch[b][0], "pos_all": per_batch[b][1], "xT_loc": xl, "pT_loc": pl,
                  "pos_loc": posl, "mask": mk.reshape(128, 4 * GW)})
        in_maps.append(m)
    return in_maps


def assemble(results):
    out = np.empty((2, S, D), np.float32)
    for core in range(NCORE):
        b, j = core // CPB, core % CPB
        o = results[core]["out"]
        tok = col_tokens(j)
        own = (np.arange(NT) % GW) >= HALO
        oT = o.transpose(2, 1, 0).reshape(NT, D)
        out[b, tok[own]] = oT[own]
    return out


_NC_CACHE = {}


def kernel(**inputs):
    in_maps = prep_inputs(inputs)
    if "nc" not in _NC_CACHE:
        _NC_CACHE["nc"] = build_program()
    nc = _NC_CACHE["nc"]
    res = run_bass_kernel_spmd(nc, in_maps, core_ids=list(range(NCORE)))
    return assemble(res.results)
```

```python
import contextlib
import numpy as np
import concourse.bass as bass
import concourse.mybir as mybir
from concourse.bass_utils import run_bass_kernel_spmd

F32, BF16, I32 = mybir.dt.float32, mybir.dt.bfloat16, mybir.dt.int32
AF = mybir.ActivationFunctionType
ALU = mybir.AluOpType

D = 1024
S = 8192
NCORE = 8
CPB = 4
NG = 16
GW = 132
HALO = 4
NT = NG * GW
NH = 8
DFF = 2816
NPAIR = 22
SCALE = 96.0 ** -0.5
EPS = 1e-6
TWO_PI = 6.283185307179586

V_GMIX, V_GQ, V_GKV, V_GMPOST, V_GFPRE, V_GFPOST, V_GPPOST = 0, 8, 11, 13, 21, 29, 37
V_CAW, V_CFW, V_BF, V_INV, V_SGN, V_EPS = 45, 57, 189, 233, 234, 235
NV = 240
CFG = {"nh": 8, "mask": True, "pv_acc": True, "np": 4, "bevery": 5, "sbanks": [5, 6, 7], "la": 2, "mask_eng": "pool", "cool": 2, "junk": 128}


class Buf:
    __slots__ = ("name", "w", "readers", "dsem")

    def __init__(self, name):
        self.name = name
        self.w = None
        self.readers = {}
        self.dsem = None


class Ev:
    __slots__ = ("sem", "val", "eng")

    def __init__(self, sem, val, eng):
        self.sem = sem
        self.val = val
        self.eng = eng


class SemRec:
    __slots__ = ("h", "cnt", "key")

    def __init__(self, h, key):
        self.h = h
        self.cnt = 0
        self.key = key


class FW:
    EPOCH = 30000

    def __init__(self, nc, stack):
        self.nc = nc
        self.stack = stack
        self.engs = {"pe": nc.tensor, "act": nc.scalar, "dve": nc.vector,
                     "pool": nc.gpsimd, "sp": nc.sync}
        self.nsem = 0
        self.esem = {}
        for e in ("pe", "act", "dve", "pool"):
            self.esem[e] = self._newsem(e)
        self.seen = {e: {} for e in self.engs}
        self.pending = {e: False for e in self.engs}
        self.dsems = []
        self.nwaits = 0
        self.nops = {e: 0 for e in self.engs}

    def _newsem(self, name):
        self.nsem += 1
        h = self.stack.enter_context(self.nc.semaphore(f"s{self.nsem}_{name}"))
        return SemRec(h, self.nsem)

    def _deps(self, eng, reads, writes):
        out = []
        for b in reads:
            if b.w is not None:
                out.append(b.w)
        for b in writes:
            if b.w is not None:
                out.append(b.w)
            for ev in b.readers.values():
                if ev.eng == eng:
                    continue
                out.append(ev)
        return out

    def _wait(self, eng, evs):
        best = {}
        for ev in evs:
            if ev.eng == "pe" and eng == "pe":
                continue
            k = ev.sem.key
            if self.seen[eng].get(k, 0) >= ev.val:
                continue
            if k not in best or best[k].val < ev.val:
                best[k] = ev
        e = self.engs[eng]
        for k, ev in best.items():
            e.wait_ge(ev.sem.h, ev.val)
            self.seen[eng][k] = ev.val
            self.nwaits += 1

    def _record(self, ev, reads, writes):
        for b in reads:
            key = ev.eng if ev.eng != "dma" else ("dma", ev.sem.key)
            b.readers[key] = ev
        for b in writes:
            b.w = ev
            b.readers = {}

    def op(self, eng, fn, reads=(), writes=(), signal=True):
        self._wait(eng, self._deps(eng, reads, writes))
        ins = fn(self.engs[eng])
        self.nops[eng] += 1
        s = self.esem[eng]
        if signal:
            if s.cnt >= self.EPOCH:
                s = self.esem[eng] = self._newsem(eng)
            s.cnt += 1
            ins.then_inc(s.h, 1)
            ev = Ev(s, s.cnt, eng)
            self.pending[eng] = False
        else:
            ev = Ev(s, s.cnt + 1, eng)
            self.pending[eng] = True
        self._record(ev, reads, writes)
        return ev

    def dma(self, queue, out, in_, reads=(), writes=(), key=None):
        self._wait(queue, self._deps(queue, reads, writes))
        ins = self.engs[queue].dma_start(out=out, in_=in_)
        if key.dsem is None:
            key.dsem = self._newsem("d")
            self.dsems.append(key.dsem)
        s = key.dsem
        s.cnt += 16
        ins.then_inc(s.h, 16)
        ev = Ev(s, s.cnt, "dma")
        self._record(ev, reads, writes)
        return ev

    def barrier(self):
        assert not self.pending["pe"], "PE has unsignaled tail"
        evs = []
        for e in ("pe", "act", "dve", "pool"):
            s = self.esem[e]
            if s.cnt > 0:
                evs.append(Ev(s, s.cnt, e))
        for s in self.dsems:
            if s.cnt > 0:
                evs.append(Ev(s, s.cnt, "dma"))
        for e in self.engs:
            self._wait(e, [ev for ev in evs if ev.eng != e])


def MM(out, lhsT, rhs, start, stop):
    return lambda e: e.matmul(out, lhsT=lhsT, rhs=rhs, start=start, stop=stop)


def ACTF(out, in_, func, scale=1.0, bias=None):
    if bias is None:
        return lambda e: e.activation(out=out, in_=in_, func=func, scale=scale)
    return lambda e: e.activation(out=out, in_=in_, func=func, scale=scale, bias=bias)


def TT(out, a, b, op):
    return lambda e: e.tensor_tensor(out=out, in0=a, in1=b, op=op)


def TS(out, a, s1, op0, s2=None, op1=None):
    if op1 is None:
        return lambda e: e.tensor_scalar(out=out, in0=a, scalar1=s1, scalar2=None, op0=op0)
    return lambda e: e.tensor_scalar(out=out, in0=a, scalar1=s1, scalar2=s2, op0=op0, op1=op1)


def STT(out, in0, scalar, in1, op0, op1):
    return lambda e: e.scalar_tensor_tensor(out=out, in0=in0, scalar=scalar, in1=in1, op0=op0, op1=op1)


def CP(out, in_):
    return lambda e: e.tensor_copy(out=out, in_=in_)


def MSET(ap, val):
    return lambda e: e.memset(ap, val)


def RCP(out, in_):
    return lambda e: e.reciprocal(out=out, in_=in_)


def pieces_of(total, width):
    out = []
    a = 0
    while a < total:
        out.append((a, min(width, total - a)))
        a += width
    return out


def build_program(stage=99):
    nc = bass.Bass("TRN2", target_bir_lowering=False)

    def din(name, shape, dt=F32):
        return nc.dram_tensor(name, list(shape), dt, kind="ExternalInput").ap()

    xT_all = din("xT_all", [16, 128, 8 * 512])
    xT_loc = din("xT_loc", [128, 8, NT])
    pT_loc = din("pT_loc", [128, 2, NT])
    pos_all = din("pos_all", [1, S], I32)
    pos_loc = din("pos_loc", [1, NT], I32)
    vecs_d = din("vecs", [128, NV])
    mask_d = din("mask", [128, 4 * GW])
    wkv_d = din("wkv", [128, 8 * 448])
    wq_d = din("wq", [128, 8 * 384])
    wab_d = din("wab", [128, 8 * 1536])
    wg_d = din("wg", [128, 8 * 2048])
    wqu_d = din("wqu", [128, 3 * 768])
    wqs_d = din("wqs", [128, 3 * 768])
    wkvu_d = din("wkvu", [128, 2 * 1024])
    wao_d = din("wao", [128, 4 * 1024])
    wbo_d = din("wbo", [128, 4 * 1024])
    wo_d = din("wo", [128, 8 * 1024])
    wup_d = din("wup", [NPAIR, 128, 8 * 256])
    wdn_d = din("wdn", [128, NPAIR * 1024])
    wpp_d = din("wpp", [128, 2 * 1024])
    wpg_d = din("wpg", [128, 8 * 1024])
    out_d = nc.dram_tensor("out", [128, 8, NT], F32, kind="ExternalOutput").ap()
    dbg_d = None
    if stage < 99:
        dbg_d = nc.dram_tensor("dbg", [128, 8 * NT], F32, kind="ExternalOutput").ap()

    with contextlib.ExitStack() as st:
        fw = FW(nc, st)

        def sb(name, shape, dt):
            return st.enter_context(nc.sbuf_tensor("s_" + name, list(shape), dt))

        ps = st.enter_context(nc.psum_tensor("ps", [128, 4096], F32))
        PB = [Buf(f"pb{i}") for i in range(8)]

        def bank(i):
            return ps[:, 512 * i:512 * (i + 1)]

        class Rot:
            def __init__(self, ids):
                self.ids = list(ids)
                self.i = 0

            def next(self):
                b = self.ids[self.i % len(self.ids)]
                self.i += 1
                return b

        vec = sb("vec", [128, NV], F32)
        VEC = Buf("vec")
        fw.dma("sp", vec[:], vecs_d, writes=[VEC], key=VEC)
        ones = sb("ones", [128, 128], BF16)
        ONES = Buf("ones")
        fw.op("pool", MSET(ones[:], 1.0), writes=[ONES])
        mask = sb("mask", [128, 4 * GW], BF16)
        MASK = Buf("mask")
        fw.dma("pool", mask[:], mask_d, writes=[MASK], key=MASK)

        def vcol(i, lo=0, hi=128):
            return vec[lo:hi, i:i + 1]

        def rstd_from_bank(bk_ap, BK, out_ap, OUT, tmp_ap, TMP, npart, inv_d):
            fw.op("act", ACTF(tmp_ap, bk_ap, AF.Ln, scale=inv_d, bias=vcol(V_EPS, 0, npart)),
                  reads=[BK, VEC], writes=[TMP])
            fw.op("act", ACTF(out_ap, tmp_ap, AF.Exp, scale=-0.5), reads=[TMP], writes=[OUT])

        ob = sb("ob", [128, 4, NT], BF16)
        OB = [Buf(f"ob{i}") for i in range(4)]
        stA = contextlib.ExitStack()

        def sbA(name, shape, dt):
            return stA.enter_context(nc.sbuf_tensor("s_" + name, list(shape), dt))

        kvn = sbA("kvn", [128, 2, S], BF16)
        KVN = [Buf(f"kvn{t}") for t in range(16)]
        kbuf = [sbA(f"kbuf{i}", [96, S], BF16) for i in range(2)]
        KB_PE = [Buf("kpe0"), Buf("kpe1")]
        KB_NO = [[Buf(f"kno{i}_{t}") for t in range(16)] for i in range(2)]
        wqu = sbA("wqu", [128, 3, 768], BF16)
        wqs = sbA("wqs", [128, 3, 768], BF16)
        wkvu = sbA("wkvu", [128, 2, 1024], BF16)
        WQU, WQS, WKVU = Buf("wqu"), Buf("wqs"), Buf("wkvu")
        fw.dma("pool", wqu[:].rearrange("p a b -> p (a b)"), wqu_d, writes=[WQU], key=WQU)
        fw.dma("pool", wqs[:].rearrange("p a b -> p (a b)"), wqs_d, writes=[WQS], key=WQS)
        fw.dma("pool", wkvu[:].rearrange("p a b -> p (a b)"), wkvu_d, writes=[WKVU], key=WKVU)

        def rope_tables(posi_ap, POSI, n, cos_ap, COS, sin_ap, SIN, ta, TA, tb, TB, ti, TI):
            R = slice(64, 96)
            fw.op("dve", CP(ta[R, :n], posi_ap), reads=[POSI], writes=[TA])
            fw.op("dve", TS(ta[R, :n], ta[R, :n], vcol(V_INV, 64, 96), ALU.mult), reads=[TA, VEC], writes=[TA])
            fw.op("dve", CP(ti[R, :n], ta[R, :n]), reads=[TA], writes=[TI])
            fw.op("dve", CP(tb[R, :n], ti[R, :n]), reads=[TI], writes=[TB])
            fw.op("dve", TT(ta[R, :n], ta[R, :n], tb[R, :n], ALU.subtract), reads=[TA, TB], writes=[TA])
            fw.op("dve", TS(tb[R, :n], ta[R, :n], 0.5, ALU.is_gt), reads=[TA], writes=[TB])
            fw.op("dve", TT(ta[R, :n], ta[R, :n], tb[R, :n], ALU.subtract), reads=[TA, TB], writes=[TA])
            fw.op("dve", TS(tb[R, :n], ta[R, :n], -0.5, ALU.is_lt), reads=[TA], writes=[TB])
            fw.op("dve", TT(ta[R, :n], ta[R, :n], tb[R, :n], ALU.add), reads=[TA, TB], writes=[TA])
            yield
            fw.op("act", ACTF(sin_ap, ta[R, :n], AF.Sin, scale=vcol(V_SGN, 64, 96)), reads=[TA, VEC], writes=[SIN])
            fw.op("dve", TS(tb[R, :n], ta[R, :n], 0.25, ALU.add), reads=[TA], writes=[TB])
            fw.op("dve", TS(ti[R, :n].bitcast(F32), tb[R, :n], 0.5, ALU.is_gt), reads=[TB], writes=[TI])
            fw.op("dve", TT(tb[R, :n], tb[R, :n], ti[R, :n].bitcast(F32), ALU.subtract), reads=[TI, TB], writes=[TB])
            yield
            fw.op("act", ACTF(cos_ap, tb[R, :n], AF.Sin, scale=TWO_PI), reads=[TB], writes=[COS])
            yield

        with contextlib.ExitStack() as st1:
            def sb1(name, shape, dt):
                return st1.enter_context(nc.sbuf_tensor("s_" + name, list(shape), dt))

            TBK = 1024
            TPT = TBK // 512
            wkv = sb1("wkv_bf", [128, 8, 448], BF16)
            WKVST, WKV = Buf("wkvst"), Buf("wkv")
            with nc.sbuf_tensor("s_wkv_st", [128, 8, 448], F32) as wkv_st:
                fw.dma("sp", wkv_st[:].rearrange("p a b -> p (a b)"), wkv_d, writes=[WKVST], key=WKVST)
                for c in range(8):
                    fw.op("dve", TS(wkv[:, c, :], wkv_st[:, c, :], vcol(V_GMIX + c), ALU.mult),
                          reads=[WKVST, VEC], writes=[WKV])
                fw.barrier()
            xb = [sb1(f"xb{i}", [128, 8 * 512], BF16) for i in range(2)]
            XB = [Buf("xb0"), Buf("xb1")]
            sq = [sb1(f"sq{i}", [128, 8 * 512], BF16) for i in range(2)]
            SQ = [Buf("sq0"), Buf("sq1")]
            rs = [sb1(f"rs{i}", [128, 512], F32) for i in range(2)]
            RS = [Buf("rs0"), Buf("rs1")]
            lnt = [sb1(f"lnt{i}", [128, 512], F32) for i in range(2)]
            LNT = [Buf("lnt0"), Buf("lnt1")]
            kvl = [sb1(f"kvl{i}", [128, 2, 512], F32) for i in range(2)]
            KVL = [Buf("kvl0"), Buf("kvl1")]
            sq2 = [sb1(f"sq2{i}", [128, 1024], BF16) for i in range(2)]
            SQ2 = [Buf("sq20"), Buf("sq21")]
            rs2 = [sb1(f"rs2{i}", [128, 512], F32) for i in range(2)]
            RS2 = [Buf("rs20"), Buf("rs21")]
            tpa = [sb1(f"tpa{i}", [96, 512], F32) for i in range(2)]
            tpb = [sb1(f"tpb{i}", [96, 512], F32) for i in range(2)]
            TPA = [Buf("tpa0"), Buf("tpa1")]
            TPB = [Buf("tpb0"), Buf("tpb1")]
            posk = [sb1(f"posk{i}", [96, TBK], I32) for i in range(2)]
            POSK = [Buf("posk0"), Buf("posk1")]
            cosk = [sb1(f"cosk{i}", [96, TBK], F32) for i in range(2)]
            sink = [sb1(f"sink{i}", [96, TBK], F32) for i in range(2)]
            COSK = [Buf("cosk0"), Buf("cosk1")]
            SINK = [Buf("sink0"), Buf("sink1")]
            tta = sb1("tta", [96, TBK], F32)
            ttb = sb1("ttb", [96, TBK], F32)
            tti = sb1("tti", [96, TBK], I32)
            TTA, TTB, TTI = Buf("tta"), Buf("ttb"), Buf("tti")

            rot = Rot(range(8))

            def tables_batch(tb_i):
                i = tb_i % 2
                fw.dma("sp", posk[i][64:96, :],
                       pos_all[0:1, tb_i * TBK:(tb_i + 1) * TBK].partition_broadcast(32),
                       writes=[POSK[i]], key=POSK[i])
                return rope_tables(posk[i][64:96, :], POSK[i], TBK, cosk[i][64:96, :], COSK[i],
                                   sink[i][64:96, :], SINK[i], tta, TTA, ttb, TTB, tti, TTI)

            g0 = tables_batch(0)
            for _ in g0:
                pass
            gen = None
            for t in range(16):
                i = t % 2
                if t % TPT == 0 and t // TPT + 1 < 16 // TPT:
                    gen = tables_batch(t // TPT + 1)
                    next(gen)
                    gcnt = 0
                tbi = (t // TPT) % 2
                tcol = (t % TPT) * 512
                fw.dma("pool", xb[i][:], xT_all[t], writes=[XB[i]], key=XB[i])
                fw.op("act", ACTF(sq[i][:], xb[i][:], AF.Square), reads=[XB[i]], writes=[SQ[i]])
                if gen is not None and t % TPT == 0:
                    next(gen)
                bA = rot.next()
                for c in range(8):
                    fw.op("pe", MM(bank(bA), ones[:, :], sq[i][:, c * 512:(c + 1) * 512], c == 0, c == 7),
                          reads=[ONES, SQ[i]], writes=[PB[bA]], signal=(c == 7))
                rstd_from_bank(bank(bA), PB[bA], rs[i][:], RS[i], lnt[i][:], LNT[i], 128, 1.0 / D)
                for m in range(2):
                    bk = rot.next()
                    for c in range(8):
                        fw.op("pe", MM(bank(bk), wkv[:, c, m * 128:(m + 1) * 128], xb[i][:, c * 512:(c + 1) * 512],
                                       c == 0, c == 7), reads=[WKV, XB[i]], writes=[PB[bk]], signal=(c == 7))
                    fw.op("dve", TT(kvl[i][:, m, :], bank(bk), rs[i][:], ALU.mult),
                          reads=[PB[bk], RS[i]], writes=[KVL[i]])
                fw.op("act", ACTF(sq2[i][:], kvl[i][:].rearrange("p a b -> p (a b)"), AF.Square),
                      reads=[KVL[i]], writes=[SQ2[i]])
                bD = rot.next()
                for c in range(8):
                    fw.op("pe", MM(bank(bD)[0:96, :], wkv[:, c, 256:352], xb[i][:, c * 512:(c + 1) * 512],
                                   c == 0, c == 7), reads=[WKV, XB[i]], writes=[PB[bD]], signal=(c == 7))
                bE = rot.next()
                for c in range(8):
                    fw.op("pe", MM(bank(bE)[0:96, :], wkv[:, c, 352:448], xb[i][:, c * 512:(c + 1) * 512],
                                   c == 0, c == 7), reads=[WKV, XB[i]], writes=[PB[bE]], signal=(c == 7))
                bC = rot.next()
                for m in range(2):
                    fw.op("pe", MM(bank(bC), ones[:, :], sq2[i][:, m * 512:(m + 1) * 512], m == 0, m == 1),
                          reads=[ONES, SQ2[i]], writes=[PB[bC]], signal=(m == 1))
                rstd_from_bank(bank(bC), PB[bC], rs2[i][:], RS2[i], lnt[i][:], LNT[i], 128, 1.0 / 256)
                for m in range(2):
                    fw.op("dve", STT(kvn[:, m, t * 512:(t + 1) * 512], kvl[i][:, m, :], vcol(V_GKV + m),
                                     rs2[i][:], ALU.mult, ALU.mult),
                          reads=[KVL[i], RS2[i], VEC], writes=[KVN[t]])
                R = slice(64, 96)
                fw.op("dve", TT(tpa[i][R, :], bank(bD)[R, :], cosk[tbi][R, tcol:tcol + 512], ALU.mult),
                      reads=[PB[bD], COSK[tbi]], writes=[TPA[i]])
                fw.op("dve", TT(tpb[i][R, :], bank(bE)[R, :], sink[tbi][R, tcol:tcol + 512], ALU.mult),
                      reads=[PB[bE], SINK[tbi]], writes=[TPB[i]])
                fw.op("pool", TT(tpa[i][R, :], tpa[i][R, :], tpb[i][R, :], ALU.add),
                      reads=[TPA[i], TPB[i]], writes=[TPA[i]])
                fw.op("pool", TT(kbuf[0][R, t * 512:(t + 1) * 512], tpa[i][R, :], rs[i][R, :], ALU.mult),
                      reads=[TPA[i], RS[i]], writes=[KB_PE[0]])
                if gen is not None and t % TPT == 0:
                    next(gen)
                    gen = None
            fw.dma("sp", kbuf[1][64:96, :], kbuf[0][64:96, :], reads=[KB_PE[0]], writes=[KB_PE[1]], key=KB_PE[1])
            fw.barrier()

        if stage == 1:
            with contextlib.ExitStack() as std:
                dt_ = std.enter_context(nc.sbuf_tensor("dbgt", [128, 8 * NT], F32))
                DT = Buf("dbgt")
                fw.op("dve", MSET(dt_[:], 0.0), writes=[DT])
                fw.op("dve", CP(dt_[:, 0:8192], kvn[:, 0, :]), reads=KVN, writes=[DT])
                fw.op("dve", CP(dt_[64:96, 8192:16384], kbuf[1][64:96, :]), reads=[KB_PE[1]], writes=[DT])
                fw.dma("sp", dbg_d, dt_[:], reads=[DT], key=DT)
                fw.barrier()
            stA.close()
            return nc

        qn = sbA("qn", [128, 3, NT], BF16)
        QN = [Buf(f"qn{i}") for i in range(6)]
        cos_l = sbA("cos_l", [96, NT], F32)
        sin_l = sbA("sin_l", [96, NT], F32)
        COSL, SINL = Buf("cosl"), Buf("sinl")
        PW = 352
        PCS = pieces_of(NT, PW)

        with contextlib.ExitStack() as st2t:
            def sb2(name, shape, dt):
                return st2t.enter_context(nc.sbuf_tensor("s_" + name, list(shape), dt))
            posl = sb2("posl", [96, NT], I32)
            POSL = Buf("posl")
            lta = sb2("lta", [96, NT], F32)
            ltb = sb2("ltb", [96, NT], F32)
            lti = sb2("lti", [96, NT], I32)
            LTA, LTB, LTI = Buf("lta"), Buf("ltb"), Buf("lti")
            fw.dma("sp", posl[64:96, :], pos_loc[0:1, :].partition_broadcast(32), writes=[POSL], key=POSL)
            for _ in rope_tables(posl[64:96, :], POSL, NT, cos_l[64:96, :], COSL, sin_l[64:96, :], SINL,
                                 lta, LTA, ltb, LTB, lti, LTI):
                pass
            fw.barrier()
        with contextlib.ExitStack() as st2:
            def sb2(name, shape, dt):
                return st2.enter_context(nc.sbuf_tensor("s_" + name, list(shape), dt))

            wq = sb2("wq_bf", [128, 8, 384], BF16)
            WQ = Buf("wq")
            fw.dma("pool", wq[:].rearrange("p a b -> p (a b)"), wq_d, writes=[WQ], key=WQ)
            xl = [sb2(f"xl{i}", [128, 8, PW], F32) for i in range(2)]
            XL = [Buf("xl0"), Buf("xl1")]
            sqx = [sb2(f"sqx{i}", [128, 8, PW], BF16) for i in range(2)]
            SQX = [Buf("sqx0"), Buf("sqx1")]
            up = [sb2(f"up{i}", [128, 8, PW], BF16) for i in range(2)]
            UP = [Buf("up0"), Buf("up1")]
            rsl = [sb2(f"rsl{i}", [128, PW], F32) for i in range(2)]
            RSL = [Buf("rsl0"), Buf("rsl1")]
            lnl = [sb2(f"lnl{i}", [128, PW], F32) for i in range(2)]
            LNL = [Buf("lnl0"), Buf("lnl1")]
            sq3 = [sb2(f"sq3{i}", [128, 3, PW], BF16) for i in range(2)]
            SQ3 = [Buf("sq30"), Buf("sq31")]
            rsq = [sb2(f"rsq{i}", [128, PW], F32) for i in range(2)]
            RSQ = [Buf("rsq0"), Buf("rsq1")]
            rot = Rot(range(8))
            for pi, (a, w) in enumerate(PCS):
                i = pi % 2
                fw.dma("sp", xl[i][:, :, :w], xT_loc[:, :, a:a + w], writes=[XL[i]], key=XL[i])
                fw.op("act", ACTF(sqx[i][:, :, :w], xl[i][:, :, :w], AF.Square), reads=[XL[i]], writes=[SQX[i]])
                bA = rot.next()
                for c in range(8):
                    fw.op("pe", MM(bank(bA)[:, :w], ones[:, :], sqx[i][:, c, :w], c == 0, c == 7),
                          reads=[ONES, SQX[i]], writes=[PB[bA]], signal=(c == 7))
                rstd_from_bank(bank(bA)[:, :w], PB[bA], rsl[i][:, :w], RSL[i], lnl[i][:, :w], LNL[i], 128, 1.0 / D)
                for c in range(8):
                    fw.op("dve", STT(up[i][:, c, :w], xl[i][:, c, :w], vcol(V_GMIX + c), rsl[i][:, :w],
                                     ALU.mult, ALU.mult), reads=[XL[i], RSL[i], VEC], writes=[UP[i]])
                bq = []
                for m in range(3):
                    bk = rot.next()
                    bq.append(bk)
                    for c in range(8):
                        fw.op("pe", MM(bank(bk)[:, :w], wq[:, c, m * 128:(m + 1) * 128], up[i][:, c, :w],
                                       c == 0, c == 7), reads=[WQ, UP[i]], writes=[PB[bk]], signal=(c == 7))
                    fw.op("act", ACTF(sq3[i][:, m, :w], bank(bk)[:, :w], AF.Square), reads=[PB[bk]], writes=[SQ3[i]])
                bS = rot.next()
                for m in range(3):
                    fw.op("pe", MM(bank(bS)[:, :w], ones[:, :], sq3[i][:, m, :w], m == 0, m == 2),
                          reads=[ONES, SQ3[i]], writes=[PB[bS]], signal=(m == 2))
                rstd_from_bank(bank(bS)[:, :w], PB[bS], rsq[i][:, :w], RSQ[i], lnl[i][:, :w], LNL[i], 128, 1.0 / 384)
                for m in range(3):
                    fw.op("dve", STT(qn[:, m, a:a + w], bank(bq[m])[:, :w], vcol(V_GQ + m), rsq[i][:, :w],
                                     ALU.mult, ALU.mult), reads=[PB[bq[m]], RSQ[i], VEC], writes=[QN[pi]])
            fw.barrier()

        with contextlib.ExitStack() as st3:
            def sb3(name, shape, dt):
                return st3.enter_context(nc.sbuf_tensor("s_" + name, list(shape), dt))

            vbuf = [sb3(f"vbuf{i}", [128, 64, 128], BF16) for i in range(2)]
            VB = [[Buf(f"vb{i}_{u}") for u in range(8)] for i in range(2)]
            VONE = [Buf("vone0"), Buf("vone1")]
            fw.op("pool", MSET(vbuf[0][:, :, 64:128], 1.0), writes=[VONE[0]])
            fw.op("pool", MSET(vbuf[1][:, :, 0:64], 1.0), writes=[VONE[1]])
            qbuf = [sb3(f"qbuf{i}", [96, NT], BF16) for i in range(2)]
            QB = [Buf("qb0"), Buf("qb1")]
            NP = CFG["np"]
            pbuf = [sb3(f"pbuf{i}", [128, 512], BF16) for i in range(NP)]
            PBUF = [Buf(f"pbuf{i}") for i in range(NP)]
            dtmp = sb3("dtmp", [128, NT], F32)
            DTMP = Buf("dtmp")
            qta = [sb3(f"qta{i}", [96, 512], F32) for i in range(2)]
            qtb = sb3("qtb", [96, 512], F32)
            QTA, QTB = [Buf("qta0"), Buf("qta1")], Buf("qtb")
            APC = pieces_of(NT, 512)
            zt = sb3("zt", [128, 512], BF16)
            ZT = Buf("zt")
            fw.op("pool", MSET(zt[:], 0.0), writes=[ZT])

            def junk(n):
                if n > 0:
                    fw.op("pe", MM(ps[:, 2112:2112 + n], ones[:, :], zt[:, :n], False, False),
                          reads=[ONES, ZT], writes=[PB[4]], signal=False)

            def build_units(h, bank_rot):
                hb = h % 2
                us = []

                def k_unit(t):
                    def f():
                        bk = bank_rot.next()
                        for c in range(2):
                            fw.op("pe", MM(bank(bk)[0:64, :], wkvu[:, c, h * 128:h * 128 + 64],
                                           kvn[:, c, t * 512:(t + 1) * 512], c == 0, c == 1),
                                  reads=[WKVU, KVN[t]], writes=[PB[bk]], signal=(c == 1))
                        fw.op("dve", CP(kbuf[hb][0:64, t * 512:(t + 1) * 512], bank(bk)[0:64, :]),
                              reads=[PB[bk]], writes=[KB_NO[hb][t]])
                    return f

                def v_unit(u):
                    def f():
                        bk = bank_rot.next()
                        for tt in range(8):
                            tile = 8 * u + tt
                            for c in range(2):
                                fw.op("pe", MM(bank(bk)[:, tt * 64:(tt + 1) * 64],
                                               kvn[:, c, tile * 128:(tile + 1) * 128],
                                               wkvu[:, c, h * 128 + 64:h * 128 + 128], c == 0, c == 1),
                                      reads=[WKVU, KVN[tile // 4]], writes=[PB[bk]],
                                      signal=(c == 1 and tt == 7))
                        voff = 0 if hb == 0 else 64
                        fw.op("dve", CP(vbuf[hb][:, 8 * u:8 * u + 8, voff:voff + 64],
                                        bank(bk).rearrange("p (a b) -> p a b", b=64)),
                              reads=[PB[bk]], writes=[VB[hb][u]])
                    return f

                def qa_unit(pi, a, w):
                    def f():
                        bk = bank_rot.next()
                        for c in range(3):
                            fw.op("pe", MM(bank(bk)[0:96, :w], wqu[:, c, h * 96:(h + 1) * 96], qn[:, c, a:a + w],
                                           c == 0, c == 2), reads=[WQU] + QN, writes=[PB[bk]], signal=(c == 2))
                        fw.op("dve", CP(qbuf[hb][0:64, a:a + w], bank(bk)[0:64, :w]), reads=[PB[bk]], writes=[QB[hb]])
                        fw.op("dve", TT(qta[pi % 2][64:96, :w], bank(bk)[64:96, :w], cos_l[64:96, a:a + w], ALU.mult),
                              reads=[PB[bk], COSL], writes=[QTA[pi % 2]])
                    return f

                def qb_unit(pi, a, w):
                    def f():
                        bk2 = bank_rot.next()
                        for c in range(3):
                            fw.op("pe", MM(bank(bk2)[0:96, :w], wqs[:, c, h * 96:(h + 1) * 96], qn[:, c, a:a + w],
                                           c == 0, c == 2), reads=[WQS] + QN, writes=[PB[bk2]], signal=(c == 2))
                        fw.op("dve", TT(qtb[64:96, :w], bank(bk2)[64:96, :w], sin_l[64:96, a:a + w], ALU.mult),
                              reads=[PB[bk2], SINL], writes=[QTB])
                        fw.op("dve", TT(qbuf[hb][64:96, a:a + w], qta[pi % 2][64:96, :w], qtb[64:96, :w], ALU.add),
                              reads=[QTA[pi % 2], QTB], writes=[QB[hb]])
                    return f

                for pi, (a, w) in enumerate(APC):
                    us.append(qa_unit(pi, a, w))
                    us.append(qb_unit(pi, a, w))
                kk = [k_unit(t) for t in range(16)]
                vv = [v_unit(u) for u in range(8)]
                for u in range(8):
                    us.append(kk[2 * u])
                    us.append(kk[2 * u + 1])
                    us.append(vv[u])
                return us

            def main_units():
                out = []
                for kb in range(64):
                    G = kb // 4
                    a0 = GW * G
                    for p, (pa, pw) in enumerate(APC):
                        lo = max(pa, a0)
                        hi = pa + pw
                        if lo < hi:
                            out.append((kb, p, lo, hi - lo))
                return out

            LASTKB = {}
            for p, (pa, pw) in enumerate(APC):
                LASTKB[p] = 4 * min(15, (pa + pw - 1) // GW) + 3

            for un in build_units(0, Rot(range(8))):
                un()
            import collections as _c

            class FreeBanks:
                def __init__(self, ids):
                    self.q = _c.deque(ids)

                def next(self):
                    self.last = self.q.popleft()
                    return self.last

            fb = FreeBanks(CFG["sbanks"])
            LA = CFG["la"]
            for h in range(CFG["nh"]):
                hb = h % 2
                units = main_units()
                builds = build_units(h + 1, fb) if h + 1 < NH else []
                bi = 0
                sbank = {}
                nq = 0
                nun = len(units)
                cool = []
                BE = CFG["bevery"]
                for k in range(nun):
                    kb, p, a, w = units[k]
                    G, r = kb // 4, kb % 4
                    pb = k % NP
                    while cool and cool[0][0] <= k:
                        fb.q.append(cool.pop(0)[1])
                    want_build = (k % BE == BE - 1) and bi < len(builds)
                    la_eff = LA - 1 if want_build else LA
                    while nq < nun and nq <= k + la_eff and fb.q:
                        kb2, p2, a2, w2 = units[nq]
                        sbk = fb.next()
                        sbank[nq] = sbk
                        fw.op("pe", MM(bank(sbk)[:, :w2], kbuf[hb][0:96, kb2 * 128:(kb2 + 1) * 128],
                                       qbuf[hb][0:96, a2:a2 + w2], True, True),
                              reads=[KB_NO[hb][kb2 // 4], KB_PE[hb], QB[hb]], writes=[PB[sbk]])
                        nq += 1
                    if k not in sbank:
                        fb.q.append(cool.pop(0)[1])
                        kb2, p2, a2, w2 = units[nq]
                        assert nq == k
                        sbk = fb.next()
                        sbank[nq] = sbk
                        fw.op("pe", MM(bank(sbk)[:, :w2], kbuf[hb][0:96, kb2 * 128:(kb2 + 1) * 128],
                                       qbuf[hb][0:96, a2:a2 + w2], True, True),
                              reads=[KB_NO[hb][kb2 // 4], KB_PE[hb], QB[hb]], writes=[PB[sbk]])
                        nq += 1
                    sbk = sbank[k]
                    fw.op("act", ACTF(pbuf[pb][:, :w], bank(sbk)[:, :w], AF.Exp, scale=SCALE),
                          reads=[PB[sbk]], writes=[PBUF[pb]])
                    fb.q.append(sbk)
                    lo = max(a, GW * G)
                    hi = min(a + w, GW * G + GW)
                    if lo < hi and CFG["mask"]:
                        fw.op(CFG["mask_eng"], TT(pbuf[pb][:, lo - a:hi - a], pbuf[pb][:, lo - a:hi - a],
                                         mask[:, r * GW + lo - GW * G:r * GW + hi - GW * G], ALU.mult),
                              reads=[PBUF[pb], MASK], writes=[PBUF[pb]])
                    junk(CFG["junk"])
                    fw.op("pe", MM(ps[:, a:a + w], vbuf[hb][:, kb, :], pbuf[pb][:, :w], (kb == 0) or not CFG["pv_acc"], (kb == LASTKB[p]) or not CFG["pv_acc"]),
                          reads=[VB[hb][kb // 8], VONE[hb], PBUF[pb]], writes=[PB[p]])
                    if want_build and fb.q:
                        builds[bi]()
                        bi += 1
                        cool.append((k + 1 + CFG["cool"], fb.last))
                for _, bkc in cool:
                    fb.q.append(bkc)
                cool = []
                while bi < len(builds):
                    builds[bi]()
                    bi += 1
                    fb.q.append(fb.last)
                lo_r, hi_r = (slice(0, 64), slice(64, 128)) if hb == 0 else (slice(64, 128), slice(0, 64))
                for p, (pa, pw) in enumerate(APC):
                    cs = slice(pa, pa + pw)
                    fw.op("dve", TS(dtmp[lo_r, cs], ps[hi_r, cs], 1e-30, ALU.max), reads=[PB[p]], writes=[DTMP])
                    fw.op("dve", RCP(dtmp[lo_r, cs], dtmp[lo_r, cs]), reads=[DTMP], writes=[DTMP])
                    fw.op("dve", TT(ob[lo_r, h // 2, cs], ps[lo_r, cs], dtmp[lo_r, cs], ALU.mult),
                          reads=[PB[p], DTMP], writes=[OB[h // 2]])
            fw.barrier()

        if stage == 2:
            with contextlib.ExitStack() as std:
                dt_ = std.enter_context(nc.sbuf_tensor("dbgt", [128, 8 * NT], F32))
                DT = Buf("dbgt")
                fw.op("dve", MSET(dt_[:], 0.0), writes=[DT])
                fw.op("dve", CP(dt_[:, 0:4 * NT], ob[:].rearrange("p a b -> p (a b)")), reads=OB, writes=[DT])
                fw.op("dve", CP(dt_[:, 4 * NT:7 * NT], qn[:].rearrange("p a b -> p (a b)")), reads=QN, writes=[DT])
                fw.dma("sp", dbg_d, dt_[:], reads=[DT], key=DT)
                fw.barrier()
            stA.close()
            return nc

        stA.close()

        NTH = NT // 2
        HPC = pieces_of(NTH, PW)
        rot = Rot(range(8))
        OUTB = Buf("outdma")

        def sq_stat_rs(src_fn, SRC, nch, w, sqt, SQT, rst, RST, lnb, LNB, inv_d, sq_eng="pool"):
            for c in range(nch):
                if sq_eng == "act":
                    fw.op("act", ACTF(sqt[:, c, :w], src_fn(c), AF.Square), reads=SRC, writes=[SQT])
                else:
                    fw.op("pool", TT(sqt[:, c, :w], src_fn(c), src_fn(c), ALU.mult), reads=SRC, writes=[SQT])
            bS = rot.next()
            for c in range(nch):
                fw.op("pe", MM(bank(bS)[:, :w], ones[:, :], sqt[:, c, :w], c == 0, c == nch - 1),
                      reads=[ONES, SQT], writes=[PB[bS]], signal=(c == nch - 1))
            rstd_from_bank(bank(bS)[:, :w], PB[bS], rst[:, :w], RST, lnb[:, :w], LNB, 128, inv_d)

        for hf in range(2):
            c0 = hf * NTH
            with contextlib.ExitStack() as sth:
                def sbh(name, shape, dt):
                    return sth.enter_context(nc.sbuf_tensor(f"s_{name}_{hf}", list(shape), dt))

                xres = sbh("xres", [128, 8, NTH], F32)
                XRES = [Buf(f"xres{i}") for i in range(3)]
                mixed = sbh("mixed", [128, 8, NTH], BF16)
                MIX = [Buf(f"mix{i}") for i in range(3)]
                u2 = sbh("u2", [128, 8, NTH], BF16)
                U2 = [Buf(f"u2{i}") for i in range(3)]
                sqt = sbh("sqt", [128, 8, PW], BF16)
                SQT = Buf("sqt")
                rst = sbh("rst", [128, PW], F32)
                RST = Buf("rst")
                lnb = sbh("lnb", [128, PW], F32)
                LNB = Buf("lnb")
                for pi, (a, w) in enumerate(HPC):
                    fw.dma("sp", xres[:, :, a:a + w], xT_loc[:, :, c0 + a:c0 + a + w], writes=[XRES[pi]], key=XRES[pi])

                with contextlib.ExitStack() as sab:
                    def sbab(name, shape, dt):
                        return sab.enter_context(nc.sbuf_tensor(f"s_{name}_{hf}", list(shape), dt))

                    uh = sbab("uh", [128, 8, NTH], BF16)
                    UH = [Buf(f"uh{i}") for i in range(3)]
                    yap = sbab("yap", [128, 4, NTH], BF16)
                    YAP = [Buf(f"yap{m}") for m in range(4)]
                    for pi, (a, w) in enumerate(HPC):
                        sq_stat_rs(lambda c: xres[:, c, a:a + w], [XRES[pi]], 8, w, sqt, SQT, rst, RST, lnb, LNB,
                                   1.0 / D, sq_eng="act")
                        for c in range(8):
                            fw.op("dve", STT(uh[:, c, a:a + w], xres[:, c, a:a + w], vcol(V_GMIX + c), rst[:, :w],
                                             ALU.mult, ALU.mult), reads=[XRES[pi], RST, VEC], writes=[UH[pi]])
                    with contextlib.ExitStack() as sa:
                        def sba(name, shape, dt):
                            return sa.enter_context(nc.sbuf_tensor(f"s_{name}_{hf}", list(shape), dt))

                        wab = sba("wab", [128, 8, 1536], BF16)
                        WAB = Buf("wab")
                        fw.dma("pool", wab[:].rearrange("p a b -> p (a b)"), wab_d, writes=[WAB], key=WAB)
                        t1 = [sba(f"t1{i}", [128, PW], F32) for i in range(2)]
                        T1 = [Buf("t10"), Buf("t11")]
                        cx = [sba(f"cx{i}", [128, NTH], F32) for i in range(2)]
                        CX = [Buf("cx0"), Buf("cx1")]
                        cv = [sba(f"cv{i}", [128, NTH], F32) for i in range(2)]
                        CV = [Buf("cv0"), Buf("cv1")]
                        k = 0
                        for m in range(4):
                            i = m % 2
                            for pi, (a, w) in enumerate(HPC):
                                bc = rot.next()
                                for c in range(8):
                                    fw.op("pe", MM(bank(bc)[:, :w], wab[:, c, 512 + m * 128:512 + (m + 1) * 128],
                                                   uh[:, c, a:a + w], c == 0, c == 7),
                                          reads=[WAB, UH[pi]], writes=[PB[bc]], signal=(c == 7))
                                fw.op("act", ACTF(t1[k % 2][:, :w], bank(bc)[:, :w], AF.Copy),
                                      reads=[PB[bc]], writes=[T1[k % 2]])
                                bx = rot.next()
                                for c in range(8):
                                    fw.op("pe", MM(bank(bx)[:, :w], wab[:, c, 1024 + m * 128:1024 + (m + 1) * 128],
                                                   uh[:, c, a:a + w], c == 0, c == 7),
                                          reads=[WAB, UH[pi]], writes=[PB[bx]], signal=(c == 7))
                                fw.op("dve", TT(cx[i][:, a:a + w], bank(bx)[:, :w], t1[k % 2][:, :w], ALU.mult),
                                      reads=[PB[bx], T1[k % 2]], writes=[CX[i]])
                                k += 1
                            w0, w1, w2 = (vcol(V_CAW + 3 * m + kk) for kk in range(3))
                            fw.op("pool", TS(cv[i][:, :], cx[i][:, :], w2, ALU.mult, 0.0, ALU.add), reads=[CX[i], VEC], writes=[CV[i]])
                            fw.op("dve", STT(cv[i][:, 1:NTH], cx[i][:, 0:NTH - 1], w1, cv[i][:, 1:NTH], ALU.mult, ALU.add),
                                  reads=[CX[i], CV[i], VEC], writes=[CV[i]])
                            fw.op("dve", STT(cv[i][:, 2:NTH], cx[i][:, 0:NTH - 2], w0, cv[i][:, 2:NTH], ALU.mult, ALU.add),
                                  reads=[CX[i], CV[i], VEC], writes=[CV[i]])
                            for pi, (a, w) in enumerate(HPC):
                                bb = rot.next()
                                for c in range(8):
                                    fw.op("pe", MM(bank(bb)[:, :w], wab[:, c, m * 128:(m + 1) * 128],
                                                   uh[:, c, a:a + w], c == 0, c == 7),
                                          reads=[WAB, UH[pi]], writes=[PB[bb]], signal=(c == 7))
                                fw.op("dve", TT(yap[:, m, a:a + w], bank(bb)[:, :w], cv[i][:, a:a + w], ALU.mult),
                                      reads=[PB[bb], CV[i]], writes=[YAP[m]])
                        fw.barrier()
                    with contextlib.ExitStack() as sbb:
                        def sbb_(name, shape, dt):
                            return sbb.enter_context(nc.sbuf_tensor(f"s_{name}_{hf}", list(shape), dt))

                        wg = sbb_("wg", [128, 8, 2048], BF16)
                        wao = sbb_("wao", [128, 4, 1024], BF16)
                        wbo = sbb_("wbo", [128, 4, 1024], BF16)
                        WG, WAO, WBO = Buf("wg"), Buf("wao"), Buf("wbo")
                        fw.dma("pool", wao[:].rearrange("p a b -> p (a b)"), wao_d, writes=[WAO], key=WAO)
                        fw.dma("pool", wbo[:].rearrange("p a b -> p (a b)"), wbo_d, writes=[WBO], key=WBO)
                        fw.dma("pool", wg[:].rearrange("p a b -> p (a b)"), wg_d, writes=[WG], key=WG)
                        sga = [sbb_(f"sga{i}", [128, PW], F32) for i in range(2)]
                        sgb = [sbb_(f"sgb{i}", [128, PW], F32) for i in range(2)]
                        SGA = [Buf("sga0"), Buf("sga1")]
                        SGB = [Buf("sgb0"), Buf("sgb1")]
                        k = 0
                        for mo in range(8):
                            for pi, (a, w) in enumerate(HPC):
                                i = k % 2
                                k += 1
                                b1, b2, b3, b4 = rot.next(), rot.next(), rot.next(), rot.next()
                                for c in range(8):
                                    fw.op("pe", MM(bank(b1)[:, :w], wg[:, c, mo * 128:(mo + 1) * 128], uh[:, c, a:a + w],
                                                   c == 0, c == 7), reads=[WG, UH[pi]], writes=[PB[b1]], signal=(c == 7))
                                fw.op("act", ACTF(sga[i][:, :w], bank(b1)[:, :w], AF.Sigmoid), reads=[PB[b1]], writes=[SGA[i]])
                                for c in range(4):
                                    fw.op("pe", MM(bank(b2)[:, :w], wao[:, c, mo * 128:(mo + 1) * 128], yap[:, c, a:a + w],
                                                   c == 0, c == 3), reads=[WAO] + YAP, writes=[PB[b2]], signal=(c == 3))
                                fw.op("dve", TT(sga[i][:, :w], bank(b2)[:, :w], sga[i][:, :w], ALU.mult),
                                      reads=[PB[b2], SGA[i]], writes=[SGA[i]])
                                for c in range(8):
                                    fw.op("pe", MM(bank(b3)[:, :w], wg[:, c, 1024 + mo * 128:1024 + (mo + 1) * 128],
                                                   uh[:, c, a:a + w], c == 0, c == 7),
                                          reads=[WG, UH[pi]], writes=[PB[b3]], signal=(c == 7))
                                fw.op("act", ACTF(sgb[i][:, :w], bank(b3)[:, :w], AF.Sigmoid), reads=[PB[b3]], writes=[SGB[i]])
                                for c in range(4):
                                    fw.op("pe", MM(bank(b4)[:, :w], wbo[:, c, mo * 128:(mo + 1) * 128],
                                                   ob[:, c, c0 + a:c0 + a + w], c == 0, c == 3),
                                          reads=[WBO] + OB, writes=[PB[b4]], signal=(c == 3))
                                fw.op("dve", TT(sgb[i][:, :w], bank(b4)[:, :w], sgb[i][:, :w], ALU.mult),
                                      reads=[PB[b4], SGB[i]], writes=[SGB[i]])
                                fw.op("pool", TT(mixed[:, mo, a:a + w], sga[i][:, :w], sgb[i][:, :w], ALU.add),
                                      reads=[SGA[i], SGB[i]], writes=[MIX[pi]])
                        fw.barrier()

                with contextlib.ExitStack() as sc_:
                    def sbc(name, shape, dt):
                        return sc_.enter_context(nc.sbuf_tensor(f"s_{name}_{hf}", list(shape), dt))

                    wo = sbc("wo", [128, 8, 1024], BF16)
                    WO = Buf("wo")
                    fw.dma("pool", wo[:].rearrange("p a b -> p (a b)"), wo_d, writes=[WO], key=WO)
                    mos = sbc("mos", [128, 8, PW], F32)
                    MOS = Buf("mos")
                    tmpc = [sbc(f"tmpc{i}", [128, PW], F32) for i in range(2)]
                    TMPC = [Buf("tmpc0"), Buf("tmpc1")]
                    for pi, (a, w) in enumerate(HPC):
                        for m in range(8):
                            bk = rot.next()
                            for c in range(8):
                                fw.op("pe", MM(bank(bk)[:, :w], wo[:, c, m * 128:(m + 1) * 128], mixed[:, c, a:a + w],
                                               c == 0, c == 7), reads=[WO, MIX[pi]], writes=[PB[bk]], signal=(c == 7))
                            fw.op("act", ACTF(mos[:, m, :w], bank(bk)[:, :w], AF.Copy), reads=[PB[bk]], writes=[MOS])
                        sq_stat_rs(lambda c: mos[:, c, :w], [MOS], 8, w, sqt, SQT, rst, RST, lnb, LNB, 1.0 / D)
                        for m in range(8):
                            i = m % 2
                            fw.op("dve", STT(tmpc[i][:, :w], mos[:, m, :w], vcol(V_GMPOST + m), rst[:, :w],
                                             ALU.mult, ALU.mult), reads=[MOS, RST, VEC], writes=[TMPC[i]])
                            fw.op("pool", TT(xres[:, m, a:a + w], xres[:, m, a:a + w], tmpc[i][:, :w], ALU.add),
                                  reads=[XRES[pi], TMPC[i]], writes=[XRES[pi]])
                        sq_stat_rs(lambda c: xres[:, c, a:a + w], [XRES[pi]], 8, w, sqt, SQT, rst, RST, lnb, LNB,
                                   1.0 / D, sq_eng="act")
                        for c in range(8):
                            fw.op("dve", STT(u2[:, c, a:a + w], xres[:, c, a:a + w], vcol(V_GFPRE + c), rst[:, :w],
                                             ALU.mult, ALU.mult), reads=[XRES[pi], RST, VEC], writes=[U2[pi]])
                    fw.barrier()

                with contextlib.ExitStack() as sd:
                    def sbd(name, shape, dt):
                        return sd.enter_context(nc.sbuf_tensor(f"s_{name}_{hf}", list(shape), dt))

                    ff = sbd("ff", [128, NPAIR, NTH], BF16)
                    FF = [Buf(f"ff{m}") for m in range(NPAIR)]
                    with contextlib.ExitStack() as sdd:
                        def sbdd(name, shape, dt):
                            return sdd.enter_context(nc.sbuf_tensor(f"s_{name}_{hf}", list(shape), dt))

                        NWB = 3
                        wup = [sbdd(f"wup{i}", [128, 8, 256], BF16) for i in range(NWB)]
                        WUP = [Buf(f"wup{i}") for i in range(NWB)]
                        rows = [[sbdd(f"row{i}_{r}", [128, NTH], F32) for r in range(4)] for i in range(2)]
                        ROW = [[Buf(f"row{i}_{r}") for r in range(4)] for i in range(2)]

                        def load_wup(m):
                            fw.dma("pool", wup[m % NWB][:].rearrange("p a b -> p (a b)"), wup_d[m],
                                   writes=[WUP[m % NWB]], key=WUP[m % NWB])

                        for m in range(min(NWB - 1, NPAIR)):
                            load_wup(m)
                        for m in range(NPAIR):
                            if m + NWB - 1 < NPAIR:
                                load_wup(m + NWB - 1)
                            i = m % 2
                            wb = m % NWB
                            for half_i, chunk in enumerate((m, NPAIR + m)):
                                gs, t0 = rows[i][2 * half_i], rows[i][2 * half_i + 1]
                                GS, T0 = ROW[i][2 * half_i], ROW[i][2 * half_i + 1]
                                for pi, (a, w) in enumerate(HPC):
                                    bk = rot.next()
                                    for c in range(8):
                                        fw.op("pe", MM(bank(bk)[:, :w], wup[wb][:, c, half_i * 128:(half_i + 1) * 128],
                                                       u2[:, c, a:a + w], c == 0, c == 7),
                                              reads=[WUP[wb], U2[pi]], writes=[PB[bk]], signal=(c == 7))
                                    fw.op("act", ACTF(gs[:, a:a + w], bank(bk)[:, :w], AF.Copy), reads=[PB[bk]], writes=[GS])
                                cw = [vcol(V_CFW + 3 * chunk + kk) for kk in range(3)]
                                fw.op("pool", TS(t0[:, :], gs[:, :], cw[2], ALU.mult, vcol(V_BF + chunk), ALU.add),
                                      reads=[GS, VEC], writes=[T0])
                                fw.op("dve", STT(t0[:, 1:NTH], gs[:, 0:NTH - 1], cw[1], t0[:, 1:NTH], ALU.mult, ALU.add),
                                      reads=[GS, T0, VEC], writes=[T0])
                                fw.op("dve", STT(t0[:, 2:NTH], gs[:, 0:NTH - 2], cw[0], t0[:, 2:NTH], ALU.mult, ALU.add),
                                      reads=[GS, T0, VEC], writes=[T0])
                                if half_i == 0:
                                    fw.op("act", ACTF(t0[:, :], t0[:, :], AF.Gelu_apprx_tanh), reads=[T0], writes=[T0])
                            fw.op("dve", TT(ff[:, m, :], rows[i][1][:, :], rows[i][3][:, :], ALU.mult),
                                  reads=[ROW[i][1], ROW[i][3]], writes=[FF[m]])
                        fw.barrier()
                    with contextlib.ExitStack() as se:
                        def sbe(name, shape, dt):
                            return se.enter_context(nc.sbuf_tensor(f"s_{name}_{hf}", list(shape), dt))

                        wdn = [sbe(f"wdn{i}", [128, NPAIR, 512], BF16) for i in range(2)]
                        WDN = [Buf("wdn0"), Buf("wdn1")]
                        for i in range(2):
                            fw.dma("pool", wdn[i][:], wdn_d.rearrange("p (a b) -> p a b", b=1024)[:, :, i * 512:(i + 1) * 512],
                                   writes=[WDN[i]], key=WDN[i])
                        fds = sbe("fds", [128, 8, PW], F32)
                        FDS = Buf("fds")
                        tmpe = [sbe(f"tmpe{i}", [128, PW], F32) for i in range(2)]
                        TMPE = [Buf("tmpe0"), Buf("tmpe1")]
                        for pi, (a, w) in enumerate(HPC):
                            for m in range(8):
                                bk = rot.next()
                                for c in range(NPAIR):
                                    fw.op("pe", MM(bank(bk)[:, :w], wdn[m // 4][:, c, (m % 4) * 128:(m % 4 + 1) * 128],
                                                   ff[:, c, a:a + w], c == 0, c == NPAIR - 1),
                                          reads=[WDN[m // 4], FF[c]], writes=[PB[bk]], signal=(c == NPAIR - 1))
                                fw.op("act", ACTF(fds[:, m, :w], bank(bk)[:, :w], AF.Copy), reads=[PB[bk]], writes=[FDS])
                            sq_stat_rs(lambda c: fds[:, c, :w], [FDS], 8, w, sqt, SQT, rst, RST, lnb, LNB, 1.0 / D)
                            for m in range(8):
                                i = m % 2
                                fw.op("dve", STT(tmpe[i][:, :w], fds[:, m, :w], vcol(V_GFPOST + m), rst[:, :w],
                                                 ALU.mult, ALU.mult), reads=[FDS, RST, VEC], writes=[TMPE[i]])
                                fw.op("pool", TT(xres[:, m, a:a + w], xres[:, m, a:a + w], tmpe[i][:, :w], ALU.add),
                                      reads=[XRES[pi], TMPE[i]], writes=[XRES[pi]])
                        fw.barrier()

                with contextlib.ExitStack() as sf:
                    def sbf(name, shape, dt):
                        return sf.enter_context(nc.sbuf_tensor(f"s_{name}_{hf}", list(shape), dt))

                    wpg = sbf("wpg", [128, 8, 1024], BF16)
                    wpp = sbf("wpp", [128, 2, 1024], BF16)
                    ptb = sbf("ptb", [128, 2, NTH], BF16)
                    WPG, WPP, PTB = Buf("wpg"), Buf("wpp"), Buf("ptb")
                    fw.dma("pool", wpp[:].rearrange("p a b -> p (a b)"), wpp_d, writes=[WPP], key=WPP)
                    fw.dma("pool", ptb[:], pT_loc[:, :, c0:c0 + NTH], writes=[PTB], key=PTB)
                    fw.dma("pool", wpg[:].rearrange("p a b -> p (a b)"), wpg_d, writes=[WPG], key=WPG)
                    h2b = mixed
                    H2B = MIX
                    egs = sbf("egs", [128, 8, PW], F32)
                    EGS = Buf("egs")
                    sgp = [sbf(f"sgp{i}", [128, PW], F32) for i in range(2)]
                    SGP = [Buf("sgp0"), Buf("sgp1")]
                    tmpf = [sbf(f"tmpf{i}", [128, PW], F32) for i in range(2)]
                    TMPF = [Buf("tmpf0"), Buf("tmpf1")]
                    otile = [sbf(f"otile{i}", [128, 8, PW], F32) for i in range(2)]
                    OT = [Buf("ot0"), Buf("ot1")]
                    for pi, (a, w) in enumerate(HPC):
                        for c in range(8):
                            fw.op("act", ACTF(h2b[:, c, a:a + w], xres[:, c, a:a + w], AF.Copy),
                                  reads=[XRES[pi]], writes=[H2B[pi]])
                    for pi, (a, w) in enumerate(HPC):
                        oi = pi % 2
                        for m in range(8):
                            i = m % 2
                            b1, b2 = rot.next(), rot.next()
                            for c in range(8):
                                fw.op("pe", MM(bank(b1)[:, :w], wpg[:, c, m * 128:(m + 1) * 128], h2b[:, c, a:a + w],
                                               c == 0, c == 7), reads=[WPG, H2B[pi]], writes=[PB[b1]], signal=(c == 7))
                            fw.op("act", ACTF(sgp[i][:, :w], bank(b1)[:, :w], AF.Sigmoid), reads=[PB[b1]], writes=[SGP[i]])
                            for c in range(2):
                                fw.op("pe", MM(bank(b2)[:, :w], wpp[:, c, m * 128:(m + 1) * 128], ptb[:, c, a:a + w],
                                               c == 0, c == 1), reads=[WPP, PTB], writes=[PB[b2]], signal=(c == 1))
                            fw.op("dve", TT(egs[:, m, :w], bank(b2)[:, :w], sgp[i][:, :w], ALU.mult),
                                  reads=[PB[b2], SGP[i]], writes=[EGS])
                        sq_stat_rs(lambda c: egs[:, c, :w], [EGS], 8, w, sqt, SQT, rst, RST, lnb, LNB, 1.0 / D)
                        for m in range(8):
                            i = m % 2
                            fw.op("dve", STT(tmpf[i][:, :w], egs[:, m, :w], vcol(V_GPPOST + m), rst[:, :w],
                                             ALU.mult, ALU.mult), reads=[EGS, RST, VEC], writes=[TMPF[i]])
                            fw.op("pool", TT(otile[oi][:, m, :w], xres[:, m, a:a + w], tmpf[i][:, :w], ALU.add),
                                  reads=[XRES[pi], TMPF[i]], writes=[OT[oi]])
                        fw.dma("sp", out_d[:, :, c0 + a:c0 + a + w], otile[oi][:, :, :w], reads=[OT[oi]], key=OT[oi])
                    fw.barrier()
    return nc


def _chunks(w, kc):
    n = w.shape[1]
    return np.ascontiguousarray(w.reshape(kc, 128, n).transpose(1, 0, 2).reshape(128, kc * n))


def col_tokens(j):
    c = np.arange(NT)
    G = c // GW
    o = c % GW
    return 512 * G + 128 * j + (o - HALO)


def prep_inputs(inputs):
    f32 = np.float32
    x = np.asarray(inputs["x"], f32)
    p = np.asarray(inputs["p"], f32)[0]
    positions = np.asarray(inputs["positions"]).astype(np.int32)
    w_in = np.asarray(inputs["w_in"], f32)[0]

    def vec_cols(v, kc):
        return np.asarray(v, f32).reshape(kc, 128).T

    vecs = np.zeros((128, NV), f32)
    vecs[:, V_GMIX:V_GMIX + 8] = vec_cols(inputs["g_mix_pre"][0], 8)
    vecs[:, V_GQ:V_GQ + 3] = vec_cols(inputs["g_q_lat"][0], 3)
    vecs[:, V_GKV:V_GKV + 2] = vec_cols(inputs["g_kv_lat"][0], 2)
    vecs[:, V_GMPOST:V_GMPOST + 8] = vec_cols(inputs["g_mix_post"][0], 8)
    vecs[:, V_GFPRE:V_GFPRE + 8] = vec_cols(inputs["g_ffn_pre"][0], 8)
    vecs[:, V_GFPOST:V_GFPOST + 8] = vec_cols(inputs["g_ffn_post"][0], 8)
    vecs[:, V_GPPOST:V_GPPOST + 8] = vec_cols(inputs["g_ple_post"][0], 8)
    caw = np.asarray(inputs["conv_a_w"], f32)[0]
    for m in range(4):
        for k in range(3):
            vecs[:, V_CAW + 3 * m + k] = caw[k, m * 128:(m + 1) * 128]
    cfw = np.asarray(inputs["conv_ffn_w"], f32)[0]
    bfc = np.asarray(inputs["b_ffn_conv"], f32)[0]
    for m in range(44):
        for k in range(3):
            vecs[:, V_CFW + 3 * m + k] = cfw[k, m * 128:(m + 1) * 128]
        vecs[:, V_BF + m] = bfc[m * 128:(m + 1) * 128]
    inv = 1.0 / (10000.0 ** (np.arange(16, dtype=np.float64) * (2.0 / 32)))
    for i in range(32):
        vecs[64 + i, V_INV] = np.float32(inv[i % 16] / TWO_PI)
        vecs[64 + i, V_SGN] = np.float32(-TWO_PI if i < 16 else TWO_PI)
    vecs[:, V_EPS] = EPS

    kv_lat = w_in[:, 1920:2176]
    k_rope = w_in[:, 2176:2208]
    z64 = np.zeros((D, 64), f32)
    wkv = np.concatenate([kv_lat, z64, k_rope, z64, k_rope[:, 16:32], k_rope[:, 0:16]], axis=1)
    wq = w_in[:, 1536:1920]
    wab = w_in[:, 0:1536]
    wg = w_in[:, 2208:4256]
    wqu = np.asarray(inputs["w_q_up"], f32)[0]
    wqs = np.zeros_like(wqu)
    for h in range(NH):
        b0 = h * 96
        wqs[:, b0 + 64:b0 + 80] = wqu[:, b0 + 80:b0 + 96]
        wqs[:, b0 + 80:b0 + 96] = wqu[:, b0 + 64:b0 + 80]
    wup = np.asarray(inputs["w_ffn_up"], f32)[0]
    wup_p = np.empty((NPAIR, 128, 8 * 256), f32)
    for m in range(NPAIR):
        blk = np.concatenate([wup[:, m * 128:(m + 1) * 128], wup[:, DFF + m * 128:DFF + (m + 1) * 128]], axis=1)
        wup_p[m] = _chunks(blk, 8)
    shared = {
        "vecs": vecs,
        "wkv": _chunks(wkv, 8), "wq": _chunks(wq, 8), "wab": _chunks(wab, 8), "wg": _chunks(wg, 8),
        "wqu": _chunks(wqu, 3), "wqs": _chunks(wqs, 3),
        "wkvu": _chunks(np.asarray(inputs["w_kv_up"], f32)[0], 2),
        "wao": _chunks(np.asarray(inputs["w_a_out"], f32)[0], 4),
        "wbo": _chunks(np.asarray(inputs["w_b_out"], f32)[0], 4),
        "wo": _chunks(np.asarray(inputs["w_o"], f32)[0], 8),
        "wup": wup_p,
        "wdn": _chunks(np.asarray(inputs["w_ffn_down"], f32)[0], NPAIR),
        "wpp": _chunks(np.asarray(inputs["w_ple_proj"], f32)[0], 2),
        "wpg": _chunks(np.asarray(inputs["w_ple_gate"], f32)[0], 8),
    }
    in_maps = []
    per_batch = {}
    for b in range(2):
        xT = x[b].T
        xa = xT.reshape(8, 128, 16, 512).transpose(2, 1, 0, 3).reshape(16, 128, 8 * 512)
        per_batch[b] = (np.ascontiguousarray(xa), np.ascontiguousarray(positions[b][None, :]))
    for core in range(NCORE):
        b, j = core // CPB, core % CPB
        tok = col_tokens(j)
        valid = tok >= 0
        tk = np.where(valid, tok, 0)
        xl = x[b][tk] * valid[:, None].astype(f32)
        xl = np.ascontiguousarray(xl.T.reshape(8, 128, NT).transpose(1, 0, 2))
        pl = p[b][tk] * valid[:, None].astype(f32)
        pl = np.ascontiguousarray(pl.T.reshape(2, 128, NT).transpose(1, 0, 2))
        posl = np.where(valid, positions[b][tk], 0).astype(np.int32)[None, :]
        mk = np.zeros((128, 4, GW), f32)
        kk = np.arange(128)[:, None]
        qq = np.arange(128)[None, :]
        for r in range(4):
            if r < j:
                mk[:, r, :] = 1.0
            elif r == j:
                mk[:, r, HALO:] = ((kk // 64) <= (qq // 64)).astype(f32)
                mk[:, r, :HALO] = 0.0
        m = dict(shared)
        m.update({"xT_all": per_batch[b][0], "pos_all": per_batch[b][1], "xT_loc": xl, "pT_loc": pl,
                  "pos_loc": posl, "mask": mk.reshape(128, 4 * GW)})
        in_maps.append(m)
    return in_maps


def assemble(results):
    out = np.empty((2, S, D), np.float32)
    for core in range(NCORE):
        b, j = core // CPB, core % CPB
        o = results[core]["out"]
        tok = col_tokens(j)
        own = (np.arange(NT) % GW) >= HALO
        oT = o.transpose(2, 1, 0).reshape(NT, D)
        out[b, tok[own]] = oT[own]
    return out


_NC_CACHE = {}


def kernel(**inputs):
    in_maps = prep_inputs(inputs)
    if "nc" not in _NC_CACHE:
        _NC_CACHE["nc"] = build_program()
    nc = _NC_CACHE["nc"]
    res = run_bass_kernel_spmd(nc, in_maps, core_ids=list(range(NCORE)))
    return assemble(res.results)
```

```python
import contextlib
import numpy as np
import concourse.bass as bass
import concourse.mybir as mybir
from concourse.bass_utils import run_bass_kernel_spmd

F32, BF16, I32 = mybir.dt.float32, mybir.dt.bfloat16, mybir.dt.int32
AF = mybir.ActivationFunctionType
ALU = mybir.AluOpType

D = 1024
S = 8192
NCORE = 8
CPB = 4
NG = 16
GW = 132
HALO = 4
NT = NG * GW
NH = 8
DFF = 2816
NPAIR = 22
SCALE = 96.0 ** -0.5
EPS = 1e-6
TWO_PI = 6.283185307179586

V_GMIX, V_GQ, V_GKV, V_GMPOST, V_GFPRE, V_GFPOST, V_GPPOST = 0, 8, 11, 13, 21, 29, 37
V_CAW, V_CFW, V_BF, V_INV, V_SGN, V_EPS = 45, 57, 189, 233, 234, 235
NV = 240
CFG = {"nh": 8, "mask": True, "pv_acc": True, "np": 4, "bevery": 5, "sbanks": [5, 6, 7], "la": 2, "mask_eng": "pool", "cool": 2, "junk": 128}


class Buf:
    __slots__ = ("name", "w", "readers", "dsem")

    def __init__(self, name):
        self.name = name
        self.w = None
        self.readers = {}
        self.dsem = None


class Ev:
    __slots__ = ("sem", "val", "eng")

    def __init__(self, sem, val, eng):
        self.sem = sem
        self.val = val
        self.eng = eng


class SemRec:
    __slots__ = ("h", "cnt", "key")

    def __init__(self, h, key):
        self.h = h
        self.cnt = 0
        self.key = key


class FW:
    EPOCH = 30000

    def __init__(self, nc, stack):
        self.nc = nc
        self.stack = stack
        self.engs = {"pe": nc.tensor, "act": nc.scalar, "dve": nc.vector,
                     "pool": nc.gpsimd, "sp": nc.sync}
        self.nsem = 0
        self.esem = {}
        for e in ("pe", "act", "dve", "pool"):
            self.esem[e] = self._newsem(e)
        self.seen = {e: {} for e in self.engs}
        self.pending = {e: False for e in self.engs}
        self.dsems = []
        self.nwaits = 0
        self.nops = {e: 0 for e in self.engs}

    def _newsem(self, name):
        self.nsem += 1
        h = self.stack.enter_context(self.nc.semaphore(f"s{self.nsem}_{name}"))
        return SemRec(h, self.nsem)

    def _deps(self, eng, reads, writes):
        out = []
        for b in reads:
            if b.w is not None:
                out.append(b.w)
        for b in writes:
            if b.w is not None:
                out.append(b.w)
            for ev in b.readers.values():
                if ev.eng == eng:
                    continue
                out.append(ev)
        return out

    def _wait(self, eng, evs):
        best = {}
        for ev in evs:
            if ev.eng == "pe" and eng == "pe":
                continue
            k = ev.sem.key
            if self.seen[eng].get(k, 0) >= ev.val:
                continue
            if k not in best or best[k].val < ev.val:
                best[k] = ev
        e = self.engs[eng]
        for k, ev in best.items():
            e.wait_ge(ev.sem.h, ev.val)
            self.seen[eng][k] = ev.val
            self.nwaits += 1

    def _record(self, ev, reads, writes):
        for b in reads:
            key = ev.eng if ev.eng != "dma" else ("dma", ev.sem.key)
            b.readers[key] = ev
        for b in writes:
            b.w = ev
            b.readers = {}

    def op(self, eng, fn, reads=(), writes=(), signal=True):
        self._wait(eng, self._deps(eng, reads, writes))
        ins = fn(self.engs[eng])
        self.nops[eng] += 1
        s = self.esem[eng]
        if signal:
            if s.cnt >= self.EPOCH:
                s = self.esem[eng] = self._newsem(eng)
            s.cnt += 1
            ins.then_inc(s.h, 1)
            ev = Ev(s, s.cnt, eng)
            self.pending[eng] = False
        else:
            ev = Ev(s, s.cnt + 1, eng)
            self.pending[eng] = True
        self._record(ev, reads, writes)
        return ev

    def dma(self, queue, out, in_, reads=(), writes=(), key=None):
        self._wait(queue, self._deps(queue, reads, writes))
        ins = self.engs[queue].dma_start(out=out, in_=in_)
        if key.dsem is None:
            key.dsem = self._newsem("d")
            self.dsems.append(key.dsem)
        s = key.dsem
        s.cnt += 16
        ins.then_inc(s.h, 16)
        ev = Ev(s, s.cnt, "dma")
        self._record(ev, reads, writes)
        return ev

    def barrier(self):
        assert not self.pending["pe"], "PE has unsignaled tail"
        evs = []
        for e in ("pe", "act", "dve", "pool"):
            s = self.esem[e]
            if s.cnt > 0:
                evs.append(Ev(s, s.cnt, e))
        for s in self.dsems:
            if s.cnt > 0:
                evs.append(Ev(s, s.cnt, "dma"))
        for e in self.engs:
            self._wait(e, [ev for ev in evs if ev.eng != e])


def MM(out, lhsT, rhs, start, stop):
    return lambda e: e.matmul(out, lhsT=lhsT, rhs=rhs, start=start, stop=stop)


def ACTF(out, in_, func, scale=1.0, bias=None):
    if bias is None:
        return lambda e: e.activation(out=out, in_=in_, func=func, scale=scale)
    return lambda e: e.activation(out=out, in_=in_, func=func, scale=scale, bias=bias)


def TT(out, a, b, op):
    return lambda e: e.tensor_tensor(out=out, in0=a, in1=b, op=op)


def TS(out, a, s1, op0, s2=None, op1=None):
    if op1 is None:
        return lambda e: e.tensor_scalar(out=out, in0=a, scalar1=s1, scalar2=None, op0=op0)
    return lambda e: e.tensor_scalar(out=out, in0=a, scalar1=s1, scalar2=s2, op0=op0, op1=op1)


def STT(out, in0, scalar, in1, op0, op1):
    return lambda e: e.scalar_tensor_tensor(out=out, in0=in0, scalar=scalar, in1=in1, op0=op0, op1=op1)


def CP(out, in_):
    return lambda e: e.tensor_copy(out=out, in_=in_)


def MSET(ap, val):
    return lambda e: e.memset(ap, val)


def RCP(out, in_):
    return lambda e: e.reciprocal(out=out, in_=in_)


def pieces_of(total, width):
    out = []
    a = 0
    while a < total:
        out.append((a, min(width, total - a)))
        a += width
    return out


def build_program(stage=99):
    nc = bass.Bass("TRN2", target_bir_lowering=False)

    def din(name, shape, dt=F32):
        return nc.dram_tensor(name, list(shape), dt, kind="ExternalInput").ap()

    xT_all = din("xT_all", [16, 128, 8 * 512])
    xT_loc = din("xT_loc", [128, 8, NT])
    pT_loc = din("pT_loc", [128, 2, NT])
    pos_all = din("pos_all", [1, S], I32)
    pos_loc = din("pos_loc", [1, NT], I32)
    vecs_d = din("vecs", [128, NV])
    mask_d = din("mask", [128, 4 * GW])
    wkv_d = din("wkv", [128, 8 * 448])
    wq_d = din("wq", [128, 8 * 384])
    wab_d = din("wab", [4, 128, 8 * 384])
    wg_d = din("wg", [8, 128, 8 * 256])
    wqu_d = din("wqu", [128, 3 * 768])
    wqs_d = din("wqs", [128, 3 * 768])
    wkvu_d = din("wkvu", [128, 2 * 1024])
    wao_d = din("wao", [128, 4 * 1024])
    wbo_d = din("wbo", [128, 4 * 1024])
    wo_d = din("wo", [128, 8 * 1024])
    wup_d = din("wup", [NPAIR, 128, 8 * 256])
    wdn_d = din("wdn", [128, NPAIR * 1024])
    wpp_d = din("wpp", [128, 2 * 1024])
    wpg_d = din("wpg", [128, 8 * 1024])
    out_d = nc.dram_tensor("out", [128, 8, NT], F32, kind="ExternalOutput").ap()
    dbg_d = None
    if stage < 99:
        dbg_d = nc.dram_tensor("dbg", [128, 8 * NT], F32, kind="ExternalOutput").ap()

    with contextlib.ExitStack() as st:
        fw = FW(nc, st)

        def sb(name, shape, dt):
            return st.enter_context(nc.sbuf_tensor("s_" + name, list(shape), dt))

        ps = st.enter_context(nc.psum_tensor("ps", [128, 4096], F32))
        PB = [Buf(f"pb{i}") for i in range(8)]

        def bank(i):
            return ps[:, 512 * i:512 * (i + 1)]

        class Rot:
            def __init__(self, ids):
                self.ids = list(ids)
                self.i = 0

            def next(self):
                b = self.ids[self.i % len(self.ids)]
                self.i += 1
                return b

        vec = sb("vec", [128, NV], F32)
        VEC = Buf("vec")
        fw.dma("sp", vec[:], vecs_d, writes=[VEC], key=VEC)
        ones = sb("ones", [128, 128], BF16)
        ONES = Buf("ones")
        fw.op("pool", MSET(ones[:], 1.0), writes=[ONES])
        mask = sb("mask", [128, 4 * GW], BF16)
        MASK = Buf("mask")
        fw.dma("pool", mask[:], mask_d, writes=[MASK], key=MASK)

        def vcol(i, lo=0, hi=128):
            return vec[lo:hi, i:i + 1]

        def rstd_from_bank(bk_ap, BK, out_ap, OUT, tmp_ap, TMP, npart, inv_d):
            fw.op("act", ACTF(tmp_ap, bk_ap, AF.Ln, scale=inv_d, bias=vcol(V_EPS, 0, npart)),
                  reads=[BK, VEC], writes=[TMP])
            fw.op("act", ACTF(out_ap, tmp_ap, AF.Exp, scale=-0.5), reads=[TMP], writes=[OUT])

        ob = sb("ob", [128, 4, NT], BF16)
        OB = [Buf(f"ob{i}") for i in range(4)]
        stA = contextlib.ExitStack()

        def sbA(name, shape, dt):
            return stA.enter_context(nc.sbuf_tensor("s_" + name, list(shape), dt))

        kvn = sbA("kvn", [128, 2, S], BF16)
        KVN = [Buf(f"kvn{t}") for t in range(16)]
        kbuf = [sbA(f"kbuf{i}", [96, S], BF16) for i in range(2)]
        KB_PE = [Buf("kpe0"), Buf("kpe1")]
        KB_NO = [[Buf(f"kno{i}_{t}") for t in range(16)] for i in range(2)]
        wqu = sbA("wqu", [128, 3, 768], BF16)
        wqs = sbA("wqs", [128, 3, 768], BF16)
        wkvu = sbA("wkvu", [128, 2, 1024], BF16)
        WQU, WQS, WKVU = Buf("wqu"), Buf("wqs"), Buf("wkvu")
        fw.dma("pool", wqu[:].rearrange("p a b -> p (a b)"), wqu_d, writes=[WQU], key=WQU)
        fw.dma("pool", wqs[:].rearrange("p a b -> p (a b)"), wqs_d, writes=[WQS], key=WQS)
        fw.dma("pool", wkvu[:].rearrange("p a b -> p (a b)"), wkvu_d, writes=[WKVU], key=WKVU)

        def rope_tables(posi_ap, POSI, n, cos_ap, COS, sin_ap, SIN, ta, TA, tb, TB, ti, TI):
            R = slice(64, 96)
            fw.op("dve", CP(ta[R, :n], posi_ap), reads=[POSI], writes=[TA])
            fw.op("dve", TS(ta[R, :n], ta[R, :n], vcol(V_INV, 64, 96), ALU.mult), reads=[TA, VEC], writes=[TA])
            fw.op("dve", CP(ti[R, :n], ta[R, :n]), reads=[TA], writes=[TI])
            fw.op("dve", CP(tb[R, :n], ti[R, :n]), reads=[TI], writes=[TB])
            fw.op("dve", TT(ta[R, :n], ta[R, :n], tb[R, :n], ALU.subtract), reads=[TA, TB], writes=[TA])
            fw.op("dve", TS(tb[R, :n], ta[R, :n], 0.5, ALU.is_gt), reads=[TA], writes=[TB])
            fw.op("dve", TT(ta[R, :n], ta[R, :n], tb[R, :n], ALU.subtract), reads=[TA, TB], writes=[TA])
            fw.op("dve", TS(tb[R, :n], ta[R, :n], -0.5, ALU.is_lt), reads=[TA], writes=[TB])
            fw.op("dve", TT(ta[R, :n], ta[R, :n], tb[R, :n], ALU.add), reads=[TA, TB], writes=[TA])
            yield
            fw.op("act", ACTF(sin_ap, ta[R, :n], AF.Sin, scale=vcol(V_SGN, 64, 96)), reads=[TA, VEC], writes=[SIN])
            fw.op("dve", TS(tb[R, :n], ta[R, :n], 0.25, ALU.add), reads=[TA], writes=[TB])
            fw.op("dve", TS(ti[R, :n].bitcast(F32), tb[R, :n], 0.5, ALU.is_gt), reads=[TB], writes=[TI])
            fw.op("dve", TT(tb[R, :n], tb[R, :n], ti[R, :n].bitcast(F32), ALU.subtract), reads=[TI, TB], writes=[TB])
            yield
            fw.op("act", ACTF(cos_ap, tb[R, :n], AF.Sin, scale=TWO_PI), reads=[TB], writes=[COS])
            yield

        with contextlib.ExitStack() as st1:
            def sb1(name, shape, dt):
                return st1.enter_context(nc.sbuf_tensor("s_" + name, list(shape), dt))

            TBK = 1024
            TPT = TBK // 512
            wkv = sb1("wkv_bf", [128, 8, 448], BF16)
            WKVST, WKV = Buf("wkvst"), Buf("wkv")
            with nc.sbuf_tensor("s_wkv_st", [128, 8, 448], F32) as wkv_st:
                fw.dma("sp", wkv_st[:].rearrange("p a b -> p (a b)"), wkv_d, writes=[WKVST], key=WKVST)
                for c in range(8):
                    fw.op("dve", TS(wkv[:, c, :], wkv_st[:, c, :], vcol(V_GMIX + c), ALU.mult),
                          reads=[WKVST, VEC], writes=[WKV])
                fw.barrier()
            xb = [sb1(f"xb{i}", [128, 8 * 512], BF16) for i in range(2)]
            XB = [Buf("xb0"), Buf("xb1")]
            sq = [sb1(f"sq{i}", [128, 8 * 512], BF16) for i in range(2)]
            SQ = [Buf("sq0"), Buf("sq1")]
            rs = [sb1(f"rs{i}", [128, 512], F32) for i in range(2)]
            RS = [Buf("rs0"), Buf("rs1")]
            lnt = [sb1(f"lnt{i}", [128, 512], F32) for i in range(2)]
            LNT = [Buf("lnt0"), Buf("lnt1")]
            kvl = [sb1(f"kvl{i}", [128, 2, 512], F32) for i in range(2)]
            KVL = [Buf("kvl0"), Buf("kvl1")]
            sq2 = [sb1(f"sq2{i}", [128, 1024], BF16) for i in range(2)]
            SQ2 = [Buf("sq20"), Buf("sq21")]
            rs2 = [sb1(f"rs2{i}", [128, 512], F32) for i in range(2)]
            RS2 = [Buf("rs20"), Buf("rs21")]
            tpa = [sb1(f"tpa{i}", [96, 512], F32) for i in range(2)]
            tpb = [sb1(f"tpb{i}", [96, 512], F32) for i in range(2)]
            TPA = [Buf("tpa0"), Buf("tpa1")]
            TPB = [Buf("tpb0"), Buf("tpb1")]
            posk = [sb1(f"posk{i}", [96, TBK], I32) for i in range(2)]
            POSK = [Buf("posk0"), Buf("posk1")]
            cosk = [sb1(f"cosk{i}", [96, TBK], F32) for i in range(2)]
            sink = [sb1(f"sink{i}", [96, TBK], F32) for i in range(2)]
            COSK = [Buf("cosk0"), Buf("cosk1")]
            SINK = [Buf("sink0"), Buf("sink1")]
            tta = sb1("tta", [96, TBK], F32)
            ttb = sb1("ttb", [96, TBK], F32)
            tti = sb1("tti", [96, TBK], I32)
            TTA, TTB, TTI = Buf("tta"), Buf("ttb"), Buf("tti")

            rot = Rot(range(8))

            def tables_batch(tb_i):
                i = tb_i % 2
                fw.dma("sp", posk[i][64:96, :],
                       pos_all[0:1, tb_i * TBK:(tb_i + 1) * TBK].partition_broadcast(32),
                       writes=[POSK[i]], key=POSK[i])
                return rope_tables(posk[i][64:96, :], POSK[i], TBK, cosk[i][64:96, :], COSK[i],
                                   sink[i][64:96, :], SINK[i], tta, TTA, ttb, TTB, tti, TTI)

            g0 = tables_batch(0)
            for _ in g0:
                pass
            state = {"gen": None}
            tbanks = {}

            def part1(t):
                i = t % 2
                fw.dma("pool", xb[i][:], xT_all[t], writes=[XB[i]], key=XB[i])
                fw.op("act", ACTF(sq[i][:], xb[i][:], AF.Square), reads=[XB[i]], writes=[SQ[i]])
                bA = rot.next()
                for c in range(8):
                    fw.op("pe", MM(bank(bA), ones[:, :], sq[i][:, c * 512:(c + 1) * 512], c == 0, c == 7),
                          reads=[ONES, SQ[i]], writes=[PB[bA]], signal=(c == 7))
                rstd_from_bank(bank(bA), PB[bA], rs[i][:], RS[i], lnt[i][:], LNT[i], 128, 1.0 / D)
                for m in range(2):
                    bk = rot.next()
                    for c in range(8):
                        fw.op("pe", MM(bank(bk), wkv[:, c, m * 128:(m + 1) * 128], xb[i][:, c * 512:(c + 1) * 512],
                                       c == 0, c == 7), reads=[WKV, XB[i]], writes=[PB[bk]], signal=(c == 7))
                    fw.op("dve", TT(kvl[i][:, m, :], bank(bk), rs[i][:], ALU.mult),
                          reads=[PB[bk], RS[i]], writes=[KVL[i]])
                fw.op("act", ACTF(sq2[i][:], kvl[i][:].rearrange("p a b -> p (a b)"), AF.Square),
                      reads=[KVL[i]], writes=[SQ2[i]])
                bD = rot.next()
                for c in range(8):
                    fw.op("pe", MM(bank(bD)[0:96, :], wkv[:, c, 256:352], xb[i][:, c * 512:(c + 1) * 512],
                                   c == 0, c == 7), reads=[WKV, XB[i]], writes=[PB[bD]], signal=(c == 7))
                bE = rot.next()
                for c in range(8):
                    fw.op("pe", MM(bank(bE)[0:96, :], wkv[:, c, 352:448], xb[i][:, c * 512:(c + 1) * 512],
                                   c == 0, c == 7), reads=[WKV, XB[i]], writes=[PB[bE]], signal=(c == 7))
                tbanks[t] = (bD, bE)

            def part2(t):
                i = t % 2
                tbi = (t // TPT) % 2
                tcol = (t % TPT) * 512
                bD, bE = tbanks[t]
                R = slice(64, 96)
                fw.op("dve", TT(tpa[i][R, :], bank(bD)[R, :], cosk[tbi][R, tcol:tcol + 512], ALU.mult),
                      reads=[PB[bD], COSK[tbi]], writes=[TPA[i]])
                fw.op("dve", TT(tpb[i][R, :], bank(bE)[R, :], sink[tbi][R, tcol:tcol + 512], ALU.mult),
                      reads=[PB[bE], SINK[tbi]], writes=[TPB[i]])
                bC = rot.next()
                for m in range(2):
                    fw.op("pe", MM(bank(bC), ones[:, :], sq2[i][:, m * 512:(m + 1) * 512], m == 0, m == 1),
                          reads=[ONES, SQ2[i]], writes=[PB[bC]], signal=(m == 1))
                rstd_from_bank(bank(bC), PB[bC], rs2[i][:], RS2[i], lnt[i][:], LNT[i], 128, 1.0 / 256)
                for m in range(2):
                    fw.op("dve", STT(kvn[:, m, t * 512:(t + 1) * 512], kvl[i][:, m, :], vcol(V_GKV + m),
                                     rs2[i][:], ALU.mult, ALU.mult),
                          reads=[KVL[i], RS2[i], VEC], writes=[KVN[t]])
                fw.op("pool", TT(tpa[i][R, :], tpa[i][R, :], tpb[i][R, :], ALU.add),
                      reads=[TPA[i], TPB[i]], writes=[TPA[i]])
                fw.op("pool", TT(kbuf[0][R, t * 512:(t + 1) * 512], tpa[i][R, :], rs[i][R, :], ALU.mult),
                      reads=[TPA[i], RS[i]], writes=[KB_PE[0]])

            g1 = tables_batch(1)
            for _ in g1:
                pass
            part1(0)
            gens = []
            for t in range(16):
                if t + 1 < 16:
                    part1(t + 1)
                for g in list(gens):
                    try:
                        next(g)
                    except StopIteration:
                        gens.remove(g)
                part2(t)
                if (t + 1) % TPT == 0 and (t + 1) // TPT + 1 < 16 // TPT:
                    g = tables_batch((t + 1) // TPT + 1)
                    next(g)
                    gens.append(g)
            for g in gens:
                for _ in g:
                    pass
            fw.dma("sp", kbuf[1][64:96, :], kbuf[0][64:96, :], reads=[KB_PE[0]], writes=[KB_PE[1]], key=KB_PE[1])
            fw.barrier()

        if stage == 1:
            with contextlib.ExitStack() as std:
                dt_ = std.enter_context(nc.sbuf_tensor("dbgt", [128, 8 * NT], F32))
                DT = Buf("dbgt")
                fw.op("dve", MSET(dt_[:], 0.0), writes=[DT])
                fw.op("dve", CP(dt_[:, 0:8192], kvn[:, 0, :]), reads=KVN, writes=[DT])
                fw.op("dve", CP(dt_[64:96, 8192:16384], kbuf[1][64:96, :]), reads=[KB_PE[1]], writes=[DT])
                fw.dma("sp", dbg_d, dt_[:], reads=[DT], key=DT)
                fw.barrier()
            stA.close()
            return nc

        qn = sbA("qn", [128, 3, NT], BF16)
        QN = [Buf(f"qn{i}") for i in range(6)]
        cos_l = sbA("cos_l", [96, NT], F32)
        sin_l = sbA("sin_l", [96, NT], F32)
        COSL, SINL = Buf("cosl"), Buf("sinl")
        PW = 352
        PCS = pieces_of(NT, PW)

        with contextlib.ExitStack() as st2t:
            def sb2(name, shape, dt):
                return st2t.enter_context(nc.sbuf_tensor("s_" + name, list(shape), dt))
            posl = sb2("posl", [96, NT], I32)
            POSL = Buf("posl")
            lta = sb2("lta", [96, NT], F32)
            ltb = sb2("ltb", [96, NT], F32)
            lti = sb2("lti", [96, NT], I32)
            LTA, LTB, LTI = Buf("lta"), Buf("ltb"), Buf("lti")
            fw.dma("sp", posl[64:96, :], pos_loc[0:1, :].partition_broadcast(32), writes=[POSL], key=POSL)
            for _ in rope_tables(posl[64:96, :], POSL, NT, cos_l[64:96, :], COSL, sin_l[64:96, :], SINL,
                                 lta, LTA, ltb, LTB, lti, LTI):
                pass
            fw.barrier()
        with contextlib.ExitStack() as st2:
            def sb2(name, shape, dt):
                return st2.enter_context(nc.sbuf_tensor("s_" + name, list(shape), dt))

            wq = sb2("wq_bf", [128, 8, 384], BF16)
            WQ = Buf("wq")
            fw.dma("pool", wq[:].rearrange("p a b -> p (a b)"), wq_d, writes=[WQ], key=WQ)
            xl = [sb2(f"xl{i}", [128, 8, PW], F32) for i in range(2)]
            XL = [Buf("xl0"), Buf("xl1")]
            sqx = [sb2(f"sqx{i}", [128, 8, PW], BF16) for i in range(2)]
            SQX = [Buf("sqx0"), Buf("sqx1")]
            up = [sb2(f"up{i}", [128, 8, PW], BF16) for i in range(2)]
            UP = [Buf("up0"), Buf("up1")]
            rsl = [sb2(f"rsl{i}", [128, PW], F32) for i in range(2)]
            RSL = [Buf("rsl0"), Buf("rsl1")]
            lnl = [sb2(f"lnl{i}", [128, PW], F32) for i in range(2)]
            LNL = [Buf("lnl0"), Buf("lnl1")]
            sq3 = [sb2(f"sq3{i}", [128, 3, PW], BF16) for i in range(2)]
            SQ3 = [Buf("sq30"), Buf("sq31")]
            rsq = [sb2(f"rsq{i}", [128, PW], F32) for i in range(2)]
            RSQ = [Buf("rsq0"), Buf("rsq1")]
            rot = Rot(range(8))
            for pi, (a, w) in enumerate(PCS):
                i = pi % 2
                fw.dma("sp", xl[i][:, :, :w], xT_loc[:, :, a:a + w], writes=[XL[i]], key=XL[i])
                fw.op("act", ACTF(sqx[i][:, :, :w], xl[i][:, :, :w], AF.Square), reads=[XL[i]], writes=[SQX[i]])
                bA = rot.next()
                for c in range(8):
                    fw.op("pe", MM(bank(bA)[:, :w], ones[:, :], sqx[i][:, c, :w], c == 0, c == 7),
                          reads=[ONES, SQX[i]], writes=[PB[bA]], signal=(c == 7))
                rstd_from_bank(bank(bA)[:, :w], PB[bA], rsl[i][:, :w], RSL[i], lnl[i][:, :w], LNL[i], 128, 1.0 / D)
                for c in range(8):
                    fw.op("dve", STT(up[i][:, c, :w], xl[i][:, c, :w], vcol(V_GMIX + c), rsl[i][:, :w],
                                     ALU.mult, ALU.mult), reads=[XL[i], RSL[i], VEC], writes=[UP[i]])
                bq = []
                for m in range(3):
                    bk = rot.next()
                    bq.append(bk)
                    for c in range(8):
                        fw.op("pe", MM(bank(bk)[:, :w], wq[:, c, m * 128:(m + 1) * 128], up[i][:, c, :w],
                                       c == 0, c == 7), reads=[WQ, UP[i]], writes=[PB[bk]], signal=(c == 7))
                    fw.op("act", ACTF(sq3[i][:, m, :w], bank(bk)[:, :w], AF.Square), reads=[PB[bk]], writes=[SQ3[i]])
                bS = rot.next()
                for m in range(3):
                    fw.op("pe", MM(bank(bS)[:, :w], ones[:, :], sq3[i][:, m, :w], m == 0, m == 2),
                          reads=[ONES, SQ3[i]], writes=[PB[bS]], signal=(m == 2))
                rstd_from_bank(bank(bS)[:, :w], PB[bS], rsq[i][:, :w], RSQ[i], lnl[i][:, :w], LNL[i], 128, 1.0 / 384)
                for m in range(3):
                    fw.op("dve", STT(qn[:, m, a:a + w], bank(bq[m])[:, :w], vcol(V_GQ + m), rsq[i][:, :w],
                                     ALU.mult, ALU.mult), reads=[PB[bq[m]], RSQ[i], VEC], writes=[QN[pi]])
            fw.barrier()

        with contextlib.ExitStack() as st3:
            def sb3(name, shape, dt):
                return st3.enter_context(nc.sbuf_tensor("s_" + name, list(shape), dt))

            vbuf = [sb3(f"vbuf{i}", [128, 64, 128], BF16) for i in range(2)]
            VB = [[Buf(f"vb{i}_{u}") for u in range(8)] for i in range(2)]
            VONE = [Buf("vone0"), Buf("vone1")]
            fw.op("pool", MSET(vbuf[0][:, :, 64:128], 1.0), writes=[VONE[0]])
            fw.op("pool", MSET(vbuf[1][:, :, 0:64], 1.0), writes=[VONE[1]])
            qbuf = [sb3(f"qbuf{i}", [96, NT], BF16) for i in range(2)]
            QB = [Buf("qb0"), Buf("qb1")]
            NP = CFG["np"]
            pbuf = [sb3(f"pbuf{i}", [128, 512], BF16) for i in range(NP)]
            PBUF = [Buf(f"pbuf{i}") for i in range(NP)]
            dtmp = sb3("dtmp", [128, NT], F32)
            DTMP = Buf("dtmp")
            qta = [sb3(f"qta{i}", [96, 512], F32) for i in range(2)]
            qtb = sb3("qtb", [96, 512], F32)
            QTA, QTB = [Buf("qta0"), Buf("qta1")], Buf("qtb")
            APC = pieces_of(NT, 512)
            zt = sb3("zt", [128, 512], BF16)
            ZT = Buf("zt")
            fw.op("pool", MSET(zt[:], 0.0), writes=[ZT])

            def junk(n):
                if n > 0:
                    fw.op("pe", MM(ps[:, 2112:2112 + n], ones[:, :], zt[:, :n], False, False),
                          reads=[ONES, ZT], writes=[PB[4]], signal=False)

            def build_units(h, bank_rot):
                hb = h % 2
                us = []

                def k_unit(t):
                    def f():
                        bk = bank_rot.next()
                        for c in range(2):
                            fw.op("pe", MM(bank(bk)[0:64, :], wkvu[:, c, h * 128:h * 128 + 64],
                                           kvn[:, c, t * 512:(t + 1) * 512], c == 0, c == 1),
                                  reads=[WKVU, KVN[t]], writes=[PB[bk]], signal=(c == 1))
                        fw.op("dve", CP(kbuf[hb][0:64, t * 512:(t + 1) * 512], bank(bk)[0:64, :]),
                              reads=[PB[bk]], writes=[KB_NO[hb][t]])
                    return f

                def v_unit(u):
                    def f():
                        bk = bank_rot.next()
                        for tt in range(8):
                            tile = 8 * u + tt
                            for c in range(2):
                                fw.op("pe", MM(bank(bk)[:, tt * 64:(tt + 1) * 64],
                                               kvn[:, c, tile * 128:(tile + 1) * 128],
                                               wkvu[:, c, h * 128 + 64:h * 128 + 128], c == 0, c == 1),
                                      reads=[WKVU, KVN[tile // 4]], writes=[PB[bk]],
                                      signal=(c == 1 and tt == 7))
                        voff = 0 if hb == 0 else 64
                        fw.op("dve", CP(vbuf[hb][:, 8 * u:8 * u + 8, voff:voff + 64],
                                        bank(bk).rearrange("p (a b) -> p a b", b=64)),
                              reads=[PB[bk]], writes=[VB[hb][u]])
                    return f

                def qa_unit(pi, a, w):
                    def f():
                        bk = bank_rot.next()
                        for c in range(3):
                            fw.op("pe", MM(bank(bk)[0:96, :w], wqu[:, c, h * 96:(h + 1) * 96], qn[:, c, a:a + w],
                                           c == 0, c == 2), reads=[WQU] + QN, writes=[PB[bk]], signal=(c == 2))
                        fw.op("dve", CP(qbuf[hb][0:64, a:a + w], bank(bk)[0:64, :w]), reads=[PB[bk]], writes=[QB[hb]])
                        fw.op("dve", TT(qta[pi % 2][64:96, :w], bank(bk)[64:96, :w], cos_l[64:96, a:a + w], ALU.mult),
                              reads=[PB[bk], COSL], writes=[QTA[pi % 2]])
                    return f

                def qb_unit(pi, a, w):
                    def f():
                        bk2 = bank_rot.next()
                        for c in range(3):
                            fw.op("pe", MM(bank(bk2)[0:96, :w], wqs[:, c, h * 96:(h + 1) * 96], qn[:, c, a:a + w],
                                           c == 0, c == 2), reads=[WQS] + QN, writes=[PB[bk2]], signal=(c == 2))
                        fw.op("dve", TT(qtb[64:96, :w], bank(bk2)[64:96, :w], sin_l[64:96, a:a + w], ALU.mult),
                              reads=[PB[bk2], SINL], writes=[QTB])
                        fw.op("dve", TT(qbuf[hb][64:96, a:a + w], qta[pi % 2][64:96, :w], qtb[64:96, :w], ALU.add),
                              reads=[QTA[pi % 2], QTB], writes=[QB[hb]])
                    return f

                for pi, (a, w) in enumerate(APC):
                    us.append(qa_unit(pi, a, w))
                    us.append(qb_unit(pi, a, w))
                kk = [k_unit(t) for t in range(16)]
                vv = [v_unit(u) for u in range(8)]
                for u in range(8):
                    us.append(kk[2 * u])
                    us.append(kk[2 * u + 1])
                    us.append(vv[u])
                return us

            def main_units():
                out = []
                for kb in range(64):
                    G = kb // 4
                    a0 = GW * G
                    for p, (pa, pw) in enumerate(APC):
                        lo = max(pa, a0)
                        hi = pa + pw
                        if lo < hi:
                            out.append((kb, p, lo, hi - lo))
                return out

            LASTKB = {}
            for p, (pa, pw) in enumerate(APC):
                LASTKB[p] = 4 * min(15, (pa + pw - 1) // GW) + 3

            for un in build_units(0, Rot(range(8))):
                un()
            import collections as _c

            class FreeBanks:
                def __init__(self, ids):
                    self.q = _c.deque(ids)

                def next(self):
                    self.last = self.q.popleft()
                    return self.last

            fb = FreeBanks(CFG["sbanks"])
            LA = CFG["la"]
            for h in range(CFG["nh"]):
                hb = h % 2
                units = main_units()
                builds = build_units(h + 1, fb) if h + 1 < NH else []
                bi = 0
                sbank = {}
                nq = 0
                nun = len(units)
                cool = []
                BE = CFG["bevery"]
                for k in range(nun):
                    kb, p, a, w = units[k]
                    G, r = kb // 4, kb % 4
                    pb = k % NP
                    while cool and cool[0][0] <= k:
                        fb.q.append(cool.pop(0)[1])
                    want_build = (k % BE == BE - 1) and bi < len(builds)
                    la_eff = LA - 1 if want_build else LA
                    while nq < nun and nq <= k + la_eff and fb.q:
                        kb2, p2, a2, w2 = units[nq]
                        sbk = fb.next()
                        sbank[nq] = sbk
                        fw.op("pe", MM(bank(sbk)[:, :w2], kbuf[hb][0:96, kb2 * 128:(kb2 + 1) * 128],
                                       qbuf[hb][0:96, a2:a2 + w2], True, True),
                              reads=[KB_NO[hb][kb2 // 4], KB_PE[hb], QB[hb]], writes=[PB[sbk]])
                        nq += 1
                    if k not in sbank:
                        fb.q.append(cool.pop(0)[1])
                        kb2, p2, a2, w2 = units[nq]
                        assert nq == k
                        sbk = fb.next()
                        sbank[nq] = sbk
                        fw.op("pe", MM(bank(sbk)[:, :w2], kbuf[hb][0:96, kb2 * 128:(kb2 + 1) * 128],
                                       qbuf[hb][0:96, a2:a2 + w2], True, True),
                              reads=[KB_NO[hb][kb2 // 4], KB_PE[hb], QB[hb]], writes=[PB[sbk]])
                        nq += 1
                    sbk = sbank[k]
                    fw.op("act", ACTF(pbuf[pb][:, :w], bank(sbk)[:, :w], AF.Exp, scale=SCALE),
                          reads=[PB[sbk]], writes=[PBUF[pb]])
                    fb.q.append(sbk)
                    lo = max(a, GW * G)
                    hi = min(a + w, GW * G + GW)
                    if lo < hi and CFG["mask"]:
                        fw.op(CFG["mask_eng"], TT(pbuf[pb][:, lo - a:hi - a], pbuf[pb][:, lo - a:hi - a],
                                         mask[:, r * GW + lo - GW * G:r * GW + hi - GW * G], ALU.mult),
                              reads=[PBUF[pb], MASK], writes=[PBUF[pb]])
                    junk(CFG["junk"])
                    fw.op("pe", MM(ps[:, a:a + w], vbuf[hb][:, kb, :], pbuf[pb][:, :w], (kb == 0) or not CFG["pv_acc"], (kb == LASTKB[p]) or not CFG["pv_acc"]),
                          reads=[VB[hb][kb // 8], VONE[hb], PBUF[pb]], writes=[PB[p]])
                    if want_build and fb.q:
                        builds[bi]()
                        bi += 1
                        cool.append((k + 1 + CFG["cool"], fb.last))
                for _, bkc in cool:
                    fb.q.append(bkc)
                cool = []
                while bi < len(builds):
                    builds[bi]()
                    bi += 1
                    fb.q.append(fb.last)
                lo_r, hi_r = (slice(0, 64), slice(64, 128)) if hb == 0 else (slice(64, 128), slice(0, 64))
                for p, (pa, pw) in enumerate(APC):
                    cs = slice(pa, pa + pw)
                    fw.op("dve", TS(dtmp[lo_r, cs], ps[hi_r, cs], 1e-30, ALU.max), reads=[PB[p]], writes=[DTMP])
                    fw.op("dve", RCP(dtmp[lo_r, cs], dtmp[lo_r, cs]), reads=[DTMP], writes=[DTMP])
                    fw.op("dve", TT(ob[lo_r, h // 2, cs], ps[lo_r, cs], dtmp[lo_r, cs], ALU.mult),
                          reads=[PB[p], DTMP], writes=[OB[h // 2]])
            fw.barrier()

        if stage == 2:
            with contextlib.ExitStack() as std:
                dt_ = std.enter_context(nc.sbuf_tensor("dbgt", [128, 8 * NT], F32))
                DT = Buf("dbgt")
                fw.op("dve", MSET(dt_[:], 0.0), writes=[DT])
                fw.op("dve", CP(dt_[:, 0:4 * NT], ob[:].rearrange("p a b -> p (a b)")), reads=OB, writes=[DT])
                fw.op("dve", CP(dt_[:, 4 * NT:7 * NT], qn[:].rearrange("p a b -> p (a b)")), reads=QN, writes=[DT])
                fw.dma("sp", dbg_d, dt_[:], reads=[DT], key=DT)
                fw.barrier()
            stA.close()
            return nc

        stA.close()

        NTH = NT // 2
        HPC = pieces_of(NTH, PW)
        rot = Rot(range(8))
        OUTB = Buf("outdma")

        def sq_stat_rs(src_fn, SRC, nch, w, sqt, SQT, rst, RST, lnb, LNB, inv_d, sq_eng="pool"):
            for c in range(nch):
                if sq_eng == "act":
                    fw.op("act", ACTF(sqt[:, c, :w], src_fn(c), AF.Square), reads=SRC, writes=[SQT])
                else:
                    fw.op("pool", TT(sqt[:, c, :w], src_fn(c), src_fn(c), ALU.mult), reads=SRC, writes=[SQT])
            bS = rot.next()
            for c in range(nch):
                fw.op("pe", MM(bank(bS)[:, :w], ones[:, :], sqt[:, c, :w], c == 0, c == nch - 1),
                      reads=[ONES, SQT], writes=[PB[bS]], signal=(c == nch - 1))
            rstd_from_bank(bank(bS)[:, :w], PB[bS], rst[:, :w], RST, lnb[:, :w], LNB, 128, inv_d)

        for hf in range(2):
            c0 = hf * NTH
            with contextlib.ExitStack() as sth:
                def sbh(name, shape, dt):
                    return sth.enter_context(nc.sbuf_tensor(f"s_{name}_{hf}", list(shape), dt))

                xres = sbh("xres", [128, 8, NTH], F32)
                XRES = [Buf(f"xres{i}") for i in range(3)]
                u2 = sbh("u2", [128, 8, NTH], BF16)
                U2 = [Buf(f"u2{i}") for i in range(3)]
                sqt = sbh("sqt", [128, 8, PW], BF16)
                SQT = Buf("sqt")
                rst = sbh("rst", [128, PW], F32)
                RST = Buf("rst")
                lnb = sbh("lnb", [128, PW], F32)
                LNB = Buf("lnb")
                for pi, (a, w) in enumerate(HPC):
                    fw.dma("sp", xres[:, :, a:a + w], xT_loc[:, :, c0 + a:c0 + a + w], writes=[XRES[pi]], key=XRES[pi])

                with contextlib.ExitStack() as sabc:
                    mixed = sabc.enter_context(nc.sbuf_tensor(f"s_mixed_{hf}", [128, 8, NTH], BF16))
                    MIX = [Buf(f"mix{i}") for i in range(3)]
                    with contextlib.ExitStack() as sab:
                        def sbab(name, shape, dt):
                            return sab.enter_context(nc.sbuf_tensor(f"s_{name}_{hf}", list(shape), dt))

                        uh = sbab("uh", [128, 8, NTH], BF16)
                        UH = [Buf(f"uh{i}") for i in range(3)]
                        yap = sbab("yap", [128, 4, NTH], BF16)
                        YAP = [Buf(f"yap{m}") for m in range(4)]
                        with contextlib.ExitStack() as sa:
                            def sba(name, shape, dt):
                                return sa.enter_context(nc.sbuf_tensor(f"s_{name}_{hf}", list(shape), dt))

                            wab = [sba(f"wab{m}", [128, 8, 384], BF16) for m in range(4)]
                            WAB = [Buf(f"wab{m}") for m in range(4)]
                            for m in range(4):
                                fw.dma("pool", wab[m][:].rearrange("p a b -> p (a b)"), wab_d[m], writes=[WAB[m]], key=WAB[m])
                            for pi, (a, w) in enumerate(HPC):
                                sq_stat_rs(lambda c: xres[:, c, a:a + w], [XRES[pi]], 8, w, sqt, SQT, rst, RST, lnb, LNB,
                                           1.0 / D, sq_eng="act")
                                for c in range(8):
                                    fw.op("dve", STT(uh[:, c, a:a + w], xres[:, c, a:a + w], vcol(V_GMIX + c), rst[:, :w],
                                                     ALU.mult, ALU.mult), reads=[XRES[pi], RST, VEC], writes=[UH[pi]])
                            t1 = [sba(f"t1{i}", [128, PW], F32) for i in range(2)]
                            T1 = [Buf("t10"), Buf("t11")]
                            cx = [sba(f"cx{i}", [128, NTH], F32) for i in range(2)]
                            CX = [Buf("cx0"), Buf("cx1")]
                            cv = [sba(f"cv{i}", [128, NTH], F32) for i in range(2)]
                            CV = [Buf("cv0"), Buf("cv1")]
                            k = 0

                            def ab_part(m):
                                i = m % 2
                                for pi, (a, w) in enumerate(HPC):
                                    bb = rot.next()
                                    for c in range(8):
                                        fw.op("pe", MM(bank(bb)[:, :w], wab[m][:, c, 256:384],
                                                       uh[:, c, a:a + w], c == 0, c == 7),
                                              reads=[WAB[m], UH[pi]], writes=[PB[bb]], signal=(c == 7))
                                    fw.op("dve", TT(yap[:, m, a:a + w], bank(bb)[:, :w], cv[i][:, a:a + w], ALU.mult),
                                          reads=[PB[bb], CV[i]], writes=[YAP[m]])

                            for m in range(4):
                                i = m % 2
                                for pi, (a, w) in enumerate(HPC):
                                    bc = rot.next()
                                    for c in range(8):
                                        fw.op("pe", MM(bank(bc)[:, :w], wab[m][:, c, 0:128],
                                                       uh[:, c, a:a + w], c == 0, c == 7),
                                              reads=[WAB[m], UH[pi]], writes=[PB[bc]], signal=(c == 7))
                                    fw.op("act", ACTF(t1[k % 2][:, :w], bank(bc)[:, :w], AF.Copy),
                                          reads=[PB[bc]], writes=[T1[k % 2]])
                                    bx = rot.next()
                                    for c in range(8):
                                        fw.op("pe", MM(bank(bx)[:, :w], wab[m][:, c, 128:256],
                                                       uh[:, c, a:a + w], c == 0, c == 7),
                                              reads=[WAB[m], UH[pi]], writes=[PB[bx]], signal=(c == 7))
                                    fw.op("dve", TT(cx[i][:, a:a + w], bank(bx)[:, :w], t1[k % 2][:, :w], ALU.mult),
                                          reads=[PB[bx], T1[k % 2]], writes=[CX[i]])
                                    k += 1
                                w0, w1, w2 = (vcol(V_CAW + 3 * m + kk) for kk in range(3))
                                fw.op("pool", TS(cv[i][:, :], cx[i][:, :], w2, ALU.mult, 0.0, ALU.add),
                                      reads=[CX[i], VEC], writes=[CV[i]])
                                fw.op("dve", STT(cv[i][:, 1:NTH], cx[i][:, 0:NTH - 1], w1, cv[i][:, 1:NTH], ALU.mult, ALU.add),
                                      reads=[CX[i], CV[i], VEC], writes=[CV[i]])
                                fw.op("dve", STT(cv[i][:, 2:NTH], cx[i][:, 0:NTH - 2], w0, cv[i][:, 2:NTH], ALU.mult, ALU.add),
                                      reads=[CX[i], CV[i], VEC], writes=[CV[i]])
                                if m >= 1:
                                    ab_part(m - 1)
                            ab_part(3)
                            fw.barrier()
                        with contextlib.ExitStack() as sbb:
                            def sbb_(name, shape, dt):
                                return sbb.enter_context(nc.sbuf_tensor(f"s_{name}_{hf}", list(shape), dt))

                            wg = [sbb_(f"wg{m}", [128, 8, 256], BF16) for m in range(8)]
                            WG = [Buf(f"wg{m}") for m in range(8)]
                            wao = sbb_("wao", [128, 4, 1024], BF16)
                            wbo = sbb_("wbo", [128, 4, 1024], BF16)
                            WAO, WBO = Buf("wao"), Buf("wbo")
                            fw.dma("pool", wg[0][:].rearrange("p a b -> p (a b)"), wg_d[0], writes=[WG[0]], key=WG[0])
                            fw.dma("pool", wao[:].rearrange("p a b -> p (a b)"), wao_d, writes=[WAO], key=WAO)
                            fw.dma("pool", wbo[:].rearrange("p a b -> p (a b)"), wbo_d, writes=[WBO], key=WBO)
                            for m in range(1, 8):
                                fw.dma("pool", wg[m][:].rearrange("p a b -> p (a b)"), wg_d[m], writes=[WG[m]], key=WG[m])
                            sga = [sbb_(f"sga{i}", [128, PW], F32) for i in range(2)]
                            sgb = [sbb_(f"sgb{i}", [128, PW], F32) for i in range(2)]
                            SGA = [Buf("sga0"), Buf("sga1")]
                            SGB = [Buf("sgb0"), Buf("sgb1")]
                            k = 0
                            for mo in range(8):
                                for pi, (a, w) in enumerate(HPC):
                                    i = k % 2
                                    k += 1
                                    b1, b2, b3, b4 = rot.next(), rot.next(), rot.next(), rot.next()
                                    for c in range(8):
                                        fw.op("pe", MM(bank(b1)[:, :w], wg[mo][:, c, 0:128], uh[:, c, a:a + w],
                                                       c == 0, c == 7), reads=[WG[mo], UH[pi]], writes=[PB[b1]], signal=(c == 7))
                                    fw.op("act", ACTF(sga[i][:, :w], bank(b1)[:, :w], AF.Sigmoid), reads=[PB[b1]], writes=[SGA[i]])
                                    for c in range(4):
                                        fw.op("pe", MM(bank(b2)[:, :w], wao[:, c, mo * 128:(mo + 1) * 128], yap[:, c, a:a + w],
                                                       c == 0, c == 3), reads=[WAO] + YAP, writes=[PB[b2]], signal=(c == 3))
                                    fw.op("dve", TT(sga[i][:, :w], bank(b2)[:, :w], sga[i][:, :w], ALU.mult),
                                          reads=[PB[b2], SGA[i]], writes=[SGA[i]])
                                    for c in range(8):
                                        fw.op("pe", MM(bank(b3)[:, :w], wg[mo][:, c, 128:256],
                                                       uh[:, c, a:a + w], c == 0, c == 7),
                                              reads=[WG[mo], UH[pi]], writes=[PB[b3]], signal=(c == 7))
                                    fw.op("act", ACTF(sgb[i][:, :w], bank(b3)[:, :w], AF.Sigmoid), reads=[PB[b3]], writes=[SGB[i]])
                                    for c in range(4):
                                        fw.op("pe", MM(bank(b4)[:, :w], wbo[:, c, mo * 128:(mo + 1) * 128],
                                                       ob[:, c, c0 + a:c0 + a + w], c == 0, c == 3),
                                              reads=[WBO] + OB, writes=[PB[b4]], signal=(c == 3))
                                    fw.op("dve", TT(sgb[i][:, :w], bank(b4)[:, :w], sgb[i][:, :w], ALU.mult),
                                          reads=[PB[b4], SGB[i]], writes=[SGB[i]])
                                    fw.op("pool", TT(mixed[:, mo, a:a + w], sga[i][:, :w], sgb[i][:, :w], ALU.add),
                                          reads=[SGA[i], SGB[i]], writes=[MIX[pi]])
                            fw.barrier()

                    with contextlib.ExitStack() as sc_:
                        def sbc(name, shape, dt):
                            return sc_.enter_context(nc.sbuf_tensor(f"s_{name}_{hf}", list(shape), dt))

                        wo = [sbc(f"wo{i}", [128, 8, 512], BF16) for i in range(2)]
                        WO = [Buf("wo0"), Buf("wo1")]
                        for i in range(2):
                            fw.dma("pool", wo[i][:], wo_d.rearrange("p (a b) -> p a b", b=1024)[:, :, i * 512:(i + 1) * 512],
                                   writes=[WO[i]], key=WO[i])
                        mos = [sbc(f"mos{i}", [128, 8, PW], F32) for i in range(2)]
                        MOS = [Buf("mos0"), Buf("mos1")]
                        tmpc = [sbc(f"tmpc{i}", [128, PW], F32) for i in range(2)]
                        TMPC = [Buf("tmpc0"), Buf("tmpc1")]

                        def c_mm(pi):
                            a, w = HPC[pi]
                            for m in range(8):
                                bk = rot.next()
                                for c in range(8):
                                    fw.op("pe", MM(bank(bk)[:, :w], wo[m // 4][:, c, (m % 4) * 128:(m % 4 + 1) * 128],
                                                   mixed[:, c, a:a + w], c == 0, c == 7),
                                          reads=[WO[m // 4], MIX[pi]], writes=[PB[bk]], signal=(c == 7))
                                fw.op("act", ACTF(mos[pi % 2][:, m, :w], bank(bk)[:, :w], AF.Copy),
                                      reads=[PB[bk]], writes=[MOS[pi % 2]])

                        def c_post(pi):
                            a, w = HPC[pi]
                            mo_ = mos[pi % 2]
                            sq_stat_rs(lambda c: mo_[:, c, :w], [MOS[pi % 2]], 8, w, sqt, SQT, rst, RST, lnb, LNB, 1.0 / D)
                            for m in range(8):
                                i = m % 2
                                fw.op("dve", STT(tmpc[i][:, :w], mo_[:, m, :w], vcol(V_GMPOST + m), rst[:, :w],
                                                 ALU.mult, ALU.mult), reads=[MOS[pi % 2], RST, VEC], writes=[TMPC[i]])
                                fw.op("pool", TT(xres[:, m, a:a + w], xres[:, m, a:a + w], tmpc[i][:, :w], ALU.add),
                                      reads=[XRES[pi], TMPC[i]], writes=[XRES[pi]])
                            sq_stat_rs(lambda c: xres[:, c, a:a + w], [XRES[pi]], 8, w, sqt, SQT, rst, RST, lnb, LNB,
                                       1.0 / D, sq_eng="act")
                            for c in range(8):
                                fw.op("dve", STT(u2[:, c, a:a + w], xres[:, c, a:a + w], vcol(V_GFPRE + c), rst[:, :w],
                                                 ALU.mult, ALU.mult), reads=[XRES[pi], RST, VEC], writes=[U2[pi]])

                        c_mm(0)
                        for pi in range(3):
                            if pi + 1 < 3:
                                c_mm(pi + 1)
                            c_post(pi)
                        fw.barrier()

                with contextlib.ExitStack() as sd:
                    def sbd(name, shape, dt):
                        return sd.enter_context(nc.sbuf_tensor(f"s_{name}_{hf}", list(shape), dt))

                    ff = sbd("ff", [128, NPAIR, NTH], BF16)
                    FF = [Buf(f"ff{m}") for m in range(NPAIR)]
                    wdn0 = sbd("wdn0", [128, NPAIR, 512], BF16)
                    WDN = [Buf("wdn0"), Buf("wdn1")]
                    wdn_v = wdn_d.rearrange("p (a b) -> p a b", b=1024)
                    with contextlib.ExitStack() as sdd:
                        def sbdd(name, shape, dt):
                            return sdd.enter_context(nc.sbuf_tensor(f"s_{name}_{hf}", list(shape), dt))

                        NWB = 3
                        wup = [sbdd(f"wup{i}", [128, 8, 256], BF16) for i in range(NWB)]
                        WUP = [Buf(f"wup{i}") for i in range(NWB)]
                        rows = [[sbdd(f"row{i}_{r}", [128, NTH], F32) for r in range(4)] for i in range(2)]
                        ROW = [[Buf(f"row{i}_{r}") for r in range(4)] for i in range(2)]

                        def load_wup(m):
                            fw.dma("pool", wup[m % NWB][:].rearrange("p a b -> p (a b)"), wup_d[m],
                                   writes=[WUP[m % NWB]], key=WUP[m % NWB])

                        for m in range(min(NWB - 1, NPAIR)):
                            load_wup(m)

                        def d_mm(m):
                            i = m % 2
                            wb = m % NWB
                            for half_i in range(2):
                                gs = rows[i][2 * half_i]
                                GS = ROW[i][2 * half_i]
                                for pi, (a, w) in enumerate(HPC):
                                    bk = rot.next()
                                    for c in range(8):
                                        fw.op("pe", MM(bank(bk)[:, :w], wup[wb][:, c, half_i * 128:(half_i + 1) * 128],
                                                       u2[:, c, a:a + w], c == 0, c == 7),
                                              reads=[WUP[wb], U2[pi]], writes=[PB[bk]], signal=(c == 7))
                                    fw.op("act", ACTF(gs[:, a:a + w], bank(bk)[:, :w], AF.Copy), reads=[PB[bk]], writes=[GS])

                        def d_conv(m):
                            i = m % 2
                            for half_i, chunk in enumerate((m, NPAIR + m)):
                                gs, t0 = rows[i][2 * half_i], rows[i][2 * half_i + 1]
                                GS, T0 = ROW[i][2 * half_i], ROW[i][2 * half_i + 1]
                                cw = [vcol(V_CFW + 3 * chunk + kk) for kk in range(3)]
                                fw.op("pool", TS(t0[:, :], gs[:, :], cw[2], ALU.mult, vcol(V_BF + chunk), ALU.add),
                                      reads=[GS, VEC], writes=[T0])
                                fw.op("dve", STT(t0[:, 1:NTH], gs[:, 0:NTH - 1], cw[1], t0[:, 1:NTH], ALU.mult, ALU.add),
                                      reads=[GS, T0, VEC], writes=[T0])
                                fw.op("dve", STT(t0[:, 2:NTH], gs[:, 0:NTH - 2], cw[0], t0[:, 2:NTH], ALU.mult, ALU.add),
                                      reads=[GS, T0, VEC], writes=[T0])

                        def d_fin(m):
                            i = m % 2
                            fw.op("act", ACTF(rows[i][1][:, :], rows[i][1][:, :], AF.Gelu_apprx_tanh),
                                  reads=[ROW[i][1]], writes=[ROW[i][1]])
                            fw.op("dve", TT(ff[:, m, :], rows[i][1][:, :], rows[i][3][:, :], ALU.mult),
                                  reads=[ROW[i][1], ROW[i][3]], writes=[FF[m]])

                        for m in range(NPAIR):
                            if m + NWB - 1 < NPAIR:
                                load_wup(m + NWB - 1)
                            if m == 2:
                                fw.dma("pool", wdn0[:], wdn_v[:, :, 0:512], writes=[WDN[0]], key=WDN[0])
                            d_mm(m)
                            if m >= 1:
                                d_fin(m - 1)
                            d_conv(m)
                        d_fin(NPAIR - 1)
                        fw.barrier()
                    with contextlib.ExitStack() as se:
                        def sbe(name, shape, dt):
                            return se.enter_context(nc.sbuf_tensor(f"s_{name}_{hf}", list(shape), dt))

                        wdn1 = sbe("wdn1", [128, NPAIR, 512], BF16)
                        wdn = [wdn0, wdn1]
                        fw.dma("pool", wdn1[:], wdn_v[:, :, 512:1024], writes=[WDN[1]], key=WDN[1])
                        fds = [sbe(f"fds{i}", [128, 8, PW], F32) for i in range(2)]
                        FDS = [Buf("fds0"), Buf("fds1")]
                        tmpe = [sbe(f"tmpe{i}", [128, PW], F32) for i in range(2)]
                        TMPE = [Buf("tmpe0"), Buf("tmpe1")]

                        def e_mm(pi):
                            a, w = HPC[pi]
                            for m in range(8):
                                bk = rot.next()
                                for c in range(NPAIR):
                                    fw.op("pe", MM(bank(bk)[:, :w], wdn[m // 4][:, c, (m % 4) * 128:(m % 4 + 1) * 128],
                                                   ff[:, c, a:a + w], c == 0, c == NPAIR - 1),
                                          reads=[WDN[m // 4], FF[c]], writes=[PB[bk]], signal=(c == NPAIR - 1))
                                fw.op("act", ACTF(fds[pi % 2][:, m, :w], bank(bk)[:, :w], AF.Copy),
                                      reads=[PB[bk]], writes=[FDS[pi % 2]])

                        def e_post(pi):
                            a, w = HPC[pi]
                            fd_ = fds[pi % 2]
                            sq_stat_rs(lambda c: fd_[:, c, :w], [FDS[pi % 2]], 8, w, sqt, SQT, rst, RST, lnb, LNB, 1.0 / D)
                            for m in range(8):
                                i = m % 2
                                fw.op("dve", STT(tmpe[i][:, :w], fd_[:, m, :w], vcol(V_GFPOST + m), rst[:, :w],
                                                 ALU.mult, ALU.mult), reads=[FDS[pi % 2], RST, VEC], writes=[TMPE[i]])
                                fw.op("pool", TT(xres[:, m, a:a + w], xres[:, m, a:a + w], tmpe[i][:, :w], ALU.add),
                                      reads=[XRES[pi], TMPE[i]], writes=[XRES[pi]])

                        e_mm(0)
                        for pi in range(3):
                            if pi + 1 < 3:
                                e_mm(pi + 1)
                            e_post(pi)
                        fw.barrier()

                with contextlib.ExitStack() as sf:
                    def sbf(name, shape, dt):
                        return sf.enter_context(nc.sbuf_tensor(f"s_{name}_{hf}", list(shape), dt))

                    wpg = [sbf(f"wpg{i}", [128, 8, 512], BF16) for i in range(2)]
                    wpp = sbf("wpp", [128, 2, 1024], BF16)
                    ptb = sbf("ptb", [128, 2, NTH], BF16)
                    WPG, WPP, PTB = [Buf("wpg0"), Buf("wpg1")], Buf("wpp"), Buf("ptb")
                    fw.dma("pool", wpp[:].rearrange("p a b -> p (a b)"), wpp_d, writes=[WPP], key=WPP)
                    fw.dma("pool", ptb[:], pT_loc[:, :, c0:c0 + NTH], writes=[PTB], key=PTB)
                    for i in range(2):
                        fw.dma("pool", wpg[i][:], wpg_d.rearrange("p (a b) -> p a b", b=1024)[:, :, i * 512:(i + 1) * 512],
                               writes=[WPG[i]], key=WPG[i])
                    h2b = sbf("h2b", [128, 8, NTH], BF16)
                    H2B = [Buf(f"h2b{i}") for i in range(3)]
                    egs = [sbf(f"egs{i}", [128, 8, PW], F32) for i in range(2)]
                    EGS = [Buf("egs0"), Buf("egs1")]
                    sgp = [sbf(f"sgp{i}", [128, PW], F32) for i in range(2)]
                    SGP = [Buf("sgp0"), Buf("sgp1")]
                    tmpf = [sbf(f"tmpf{i}", [128, PW], F32) for i in range(2)]
                    TMPF = [Buf("tmpf0"), Buf("tmpf1")]
                    otile = [sbf(f"otile{i}", [128, 8, PW], F32) for i in range(2)]
                    OT = [Buf("ot0"), Buf("ot1")]
                    for pi, (a, w) in enumerate(HPC):
                        for c in range(8):
                            eng = "act" if c % 2 == 0 else "dve"
                            if eng == "act":
                                fw.op("act", ACTF(h2b[:, c, a:a + w], xres[:, c, a:a + w], AF.Copy),
                                      reads=[XRES[pi]], writes=[H2B[pi]])
                            else:
                                fw.op("dve", CP(h2b[:, c, a:a + w], xres[:, c, a:a + w]),
                                      reads=[XRES[pi]], writes=[H2B[pi]])

                    def f_mm(pi):
                        a, w = HPC[pi]
                        for m in range(8):
                            i = m % 2
                            b1, b2 = rot.next(), rot.next()
                            for c in range(8):
                                fw.op("pe", MM(bank(b1)[:, :w], wpg[m // 4][:, c, (m % 4) * 128:(m % 4 + 1) * 128],
                                               h2b[:, c, a:a + w], c == 0, c == 7),
                                      reads=[WPG[m // 4], H2B[pi]], writes=[PB[b1]], signal=(c == 7))
                            fw.op("act", ACTF(sgp[i][:, :w], bank(b1)[:, :w], AF.Sigmoid), reads=[PB[b1]], writes=[SGP[i]])
                            for c in range(2):
                                fw.op("pe", MM(bank(b2)[:, :w], wpp[:, c, m * 128:(m + 1) * 128], ptb[:, c, a:a + w],
                                               c == 0, c == 1), reads=[WPP, PTB], writes=[PB[b2]], signal=(c == 1))
                            fw.op("dve", TT(egs[pi % 2][:, m, :w], bank(b2)[:, :w], sgp[i][:, :w], ALU.mult),
                                  reads=[PB[b2], SGP[i]], writes=[EGS[pi % 2]])

                    def f_post(pi):
                        a, w = HPC[pi]
                        oi = pi % 2
                        eg_ = egs[pi % 2]
                        sq_stat_rs(lambda c: eg_[:, c, :w], [EGS[pi % 2]], 8, w, sqt, SQT, rst, RST, lnb, LNB, 1.0 / D)
                        for m in range(8):
                            i = m % 2
                            fw.op("dve", STT(tmpf[i][:, :w], eg_[:, m, :w], vcol(V_GPPOST + m), rst[:, :w],
                                             ALU.mult, ALU.mult), reads=[EGS[pi % 2], RST, VEC], writes=[TMPF[i]])
                            fw.op("pool", TT(otile[oi][:, m, :w], xres[:, m, a:a + w], tmpf[i][:, :w], ALU.add),
                                  reads=[XRES[pi], TMPF[i]], writes=[OT[oi]])
                        fw.dma("sp", out_d[:, :, c0 + a:c0 + a + w], otile[oi][:, :, :w], reads=[OT[oi]], key=OT[oi])

                    f_mm(0)
                    for pi in range(3):
                        if pi + 1 < 3:
                            f_mm(pi + 1)
                        f_post(pi)
                    fw.barrier()
    return nc


def _chunks(w, kc):
    n = w.shape[1]
    return np.ascontiguousarray(w.reshape(kc, 128, n).transpose(1, 0, 2).reshape(128, kc * n))


def col_tokens(j):
    c = np.arange(NT)
    G = c // GW
    o = c % GW
    return 512 * G + 128 * j + (o - HALO)


def prep_inputs(inputs):
    f32 = np.float32
    x = np.asarray(inputs["x"], f32)
    p = np.asarray(inputs["p"], f32)[0]
    positions = np.asarray(inputs["positions"]).astype(np.int32)
    w_in = np.asarray(inputs["w_in"], f32)[0]

    def vec_cols(v, kc):
        return np.asarray(v, f32).reshape(kc, 128).T

    vecs = np.zeros((128, NV), f32)
    vecs[:, V_GMIX:V_GMIX + 8] = vec_cols(inputs["g_mix_pre"][0], 8)
    vecs[:, V_GQ:V_GQ + 3] = vec_cols(inputs["g_q_lat"][0], 3)
    vecs[:, V_GKV:V_GKV + 2] = vec_cols(inputs["g_kv_lat"][0], 2)
    vecs[:, V_GMPOST:V_GMPOST + 8] = vec_cols(inputs["g_mix_post"][0], 8)
    vecs[:, V_GFPRE:V_GFPRE + 8] = vec_cols(inputs["g_ffn_pre"][0], 8)
    vecs[:, V_GFPOST:V_GFPOST + 8] = vec_cols(inputs["g_ffn_post"][0], 8)
    vecs[:, V_GPPOST:V_GPPOST + 8] = vec_cols(inputs["g_ple_post"][0], 8)
    caw = np.asarray(inputs["conv_a_w"], f32)[0]
    for m in range(4):
        for k in range(3):
            vecs[:, V_CAW + 3 * m + k] = caw[k, m * 128:(m + 1) * 128]
    cfw = np.asarray(inputs["conv_ffn_w"], f32)[0]
    bfc = np.asarray(inputs["b_ffn_conv"], f32)[0]
    for m in range(44):
        for k in range(3):
            vecs[:, V_CFW + 3 * m + k] = cfw[k, m * 128:(m + 1) * 128]
        vecs[:, V_BF + m] = bfc[m * 128:(m + 1) * 128]
    inv = 1.0 / (10000.0 ** (np.arange(16, dtype=np.float64) * (2.0 / 32)))
    for i in range(32):
        vecs[64 + i, V_INV] = np.float32(inv[i % 16] / TWO_PI)
        vecs[64 + i, V_SGN] = np.float32(-TWO_PI if i < 16 else TWO_PI)
    vecs[:, V_EPS] = EPS

    kv_lat = w_in[:, 1920:2176]
    k_rope = w_in[:, 2176:2208]
    z64 = np.zeros((D, 64), f32)
    wkv = np.concatenate([kv_lat, z64, k_rope, z64, k_rope[:, 16:32], k_rope[:, 0:16]], axis=1)
    wq = w_in[:, 1536:1920]
    wab = w_in[:, 0:1536]
    wg = w_in[:, 2208:4256]
    wqu = np.asarray(inputs["w_q_up"], f32)[0]
    wqs = np.zeros_like(wqu)
    for h in range(NH):
        b0 = h * 96
        wqs[:, b0 + 64:b0 + 80] = wqu[:, b0 + 80:b0 + 96]
        wqs[:, b0 + 80:b0 + 96] = wqu[:, b0 + 64:b0 + 80]
    wup = np.asarray(inputs["w_ffn_up"], f32)[0]
    wup_p = np.empty((NPAIR, 128, 8 * 256), f32)
    for m in range(NPAIR):
        blk = np.concatenate([wup[:, m * 128:(m + 1) * 128], wup[:, DFF + m * 128:DFF + (m + 1) * 128]], axis=1)
        wup_p[m] = _chunks(blk, 8)
    shared = {
        "vecs": vecs,
        "wkv": _chunks(wkv, 8), "wq": _chunks(wq, 8),
        "wab": np.stack([_chunks(np.concatenate([wab[:, 512 + m * 128:512 + (m + 1) * 128],
                                                 wab[:, 1024 + m * 128:1024 + (m + 1) * 128],
                                                 wab[:, m * 128:(m + 1) * 128]], axis=1), 8) for m in range(4)]),
        "wg": np.stack([_chunks(np.concatenate([wg[:, m * 128:(m + 1) * 128],
                                                wg[:, 1024 + m * 128:1024 + (m + 1) * 128]], axis=1), 8) for m in range(8)]),
        "wqu": _chunks(wqu, 3), "wqs": _chunks(wqs, 3),
        "wkvu": _chunks(np.asarray(inputs["w_kv_up"], f32)[0], 2),
        "wao": _chunks(np.asarray(inputs["w_a_out"], f32)[0], 4),
        "wbo": _chunks(np.asarray(inputs["w_b_out"], f32)[0], 4),
        "wo": _chunks(np.asarray(inputs["w_o"], f32)[0], 8),
        "wup": wup_p,
        "wdn": _chunks(np.asarray(inputs["w_ffn_down"], f32)[0], NPAIR),
        "wpp": _chunks(np.asarray(inputs["w_ple_proj"], f32)[0], 2),
        "wpg": _chunks(np.asarray(inputs["w_ple_gate"], f32)[0], 8),
    }
    in_maps = []
    per_batch = {}
    for b in range(2):
        xT = x[b].T
        xa = xT.reshape(8, 128, 16, 512).transpose(2, 1, 0, 3).reshape(16, 128, 8 * 512)
        per_batch[b] = (np.ascontiguousarray(xa), np.ascontiguousarray(positions[b][None, :]))
    for core in range(NCORE):
        b, j = core // CPB, core % CPB
        tok = col_tokens(j)
        valid = tok >= 0
        tk = np.where(valid, tok, 0)
        xl = x[b][tk] * valid[:, None].astype(f32)
        xl = np.ascontiguousarray(xl.T.reshape(8, 128, NT).transpose(1, 0, 2))
        pl = p[b][tk] * valid[:, None].astype(f32)
        pl = np.ascontiguousarray(pl.T.reshape(2, 128, NT).transpose(1, 0, 2))
        posl = np.where(valid, positions[b][tk], 0).astype(np.int32)[None, :]
        mk = np.zeros((128, 4, GW), f32)
        kk = np.arange(128)[:, None]
        qq = np.arange(128)[None, :]
        for r in range(4):
            if r < j:
                mk[:, r, :] = 1.0
            elif r == j:
                mk[:, r, HALO:] = ((kk // 64) <= (qq // 64)).astype(f32)
                mk[:, r, :HALO] = 0.0
        m = dict(shared)
        m.update({"xT_all": per_batch[b][0], "pos_all": per_batch[b][1], "xT_loc": xl, "pT_loc": pl,
                  "pos_loc": posl, "mask": mk.reshape(128, 4 * GW)})
        in_maps.append(m)
    return in_maps


def assemble(results):
    out = np.empty((2, S, D), np.float32)
    for core in range(NCORE):
        b, j = core // CPB, core % CPB
        o = results[core]["out"]
        tok = col_tokens(j)
        own = (np.arange(NT) % GW) >= HALO
        oT = o.transpose(2, 1, 0).reshape(NT, D)
        out[b, tok[own]] = oT[own]
    return out


_NC_CACHE = {}


def kernel(**inputs):
    in_maps = prep_inputs(inputs)
    if "nc" not in _NC_CACHE:
        _NC_CACHE["nc"] = build_program()
    nc = _NC_CACHE["nc"]
    res = run_bass_kernel_spmd(nc, in_maps, core_ids=list(range(NCORE)))
    return assemble(res.results)
```

```python
import contextlib
import numpy as np
import concourse.bass as bass
import concourse.mybir as mybir
from concourse.bass_utils import run_bass_kernel_spmd

F32, BF16, I32 = mybir.dt.float32, mybir.dt.bfloat16, mybir.dt.int32
AF = mybir.ActivationFunctionType
ALU = mybir.AluOpType

D = 1024
S = 8192
NCORE = 8
CPB = 4
NG = 16
GW = 132
HALO = 4
NT = NG * GW
NH = 8
DFF = 2816
NPAIR = 22
SCALE = 96.0 ** -0.5
EPS = 1e-6
TWO_PI = 6.283185307179586

V_GMIX, V_GQ, V_GKV, V_GMPOST, V_GFPRE, V_GFPOST, V_GPPOST = 0, 8, 11, 13, 21, 29, 37
V_CAW, V_CFW, V_BF, V_INV, V_SGN, V_EPS = 45, 57, 189, 233, 234, 235
NV = 240
CFG = {"nh": 8, "mask": True, "pv_acc": True, "np": 4, "bevery": 5, "sbanks": [5, 6, 7], "la": 2, "mask_eng": "pool", "cool": 2, "junk": 128}


class Buf:
    __slots__ = ("name", "w", "readers", "dsem")

    def __init__(self, name):
        self.name = name
        self.w = None
        self.readers = {}
        self.dsem = None


class Ev:
    __slots__ = ("sem", "val", "eng", "ord")

    def __init__(self, sem, val, eng, ord=None):
        self.sem = sem
        self.val = val
        self.eng = eng
        self.ord = ord


class SemRec:
    __slots__ = ("h", "cnt", "key")

    def __init__(self, h, key):
        self.h = h
        self.cnt = 0
        self.key = key


class FW:
    EPOCH = 30000

    def __init__(self, nc, stack, used=None):
        self.nc = nc
        self.stack = stack
        self.used_in = used
        self.used = set()
        self.ord = {"pe": 0, "act": 0, "dve": 0, "pool": 0}
        self.engs = {"pe": nc.tensor, "act": nc.scalar, "dve": nc.vector,
                     "pool": nc.gpsimd, "sp": nc.sync}
        self.nsem = 0
        self.esem = {}
        for e in ("pe", "act", "dve", "pool"):
            self.esem[e] = self._newsem(e)
        self.seen = {e: {} for e in self.engs}
        self.pending = {e: False for e in self.engs}
        self.dsems = []
        self.nwaits = 0
        self.nops = {e: 0 for e in self.engs}

    def _newsem(self, name):
        self.nsem += 1
        h = self.stack.enter_context(self.nc.semaphore(f"s{self.nsem}_{name}"))
        return SemRec(h, self.nsem)

    def _deps(self, eng, reads, writes):
        out = []
        for b in reads:
            if b.w is not None:
                out.append(b.w)
        for b in writes:
            if b.w is not None:
                out.append(b.w)
            for ev in b.readers.values():
                if ev.eng == eng:
                    continue
                out.append(ev)
        return out

    def _wait(self, eng, evs):
        best = {}
        for ev in evs:
            if ev.eng == "pe" and eng == "pe":
                continue
            k = ev.sem.key
            if self.seen[eng].get(k, 0) >= ev.val:
                continue
            if k not in best or best[k].val < ev.val:
                best[k] = ev
        e = self.engs[eng]
        for k, ev in best.items():
            e.wait_ge(ev.sem.h, ev.val)
            self.seen[eng][k] = ev.val
            self.nwaits += 1
            if ev.ord is not None:
                self.used.add((ev.eng, ev.ord))

    def _record(self, ev, reads, writes):
        for b in reads:
            key = ev.eng if ev.eng != "dma" else ("dma", ev.sem.key)
            b.readers[key] = ev
        for b in writes:
            b.w = ev
            b.readers = {}

    def op(self, eng, fn, reads=(), writes=(), signal=True):
        self._wait(eng, self._deps(eng, reads, writes))
        ins = fn(self.engs[eng])
        self.nops[eng] += 1
        s = self.esem[eng]
        my_ord = None
        if signal:
            self.ord[eng] += 1
            my_ord = self.ord[eng]
            if self.used_in is not None and (eng, my_ord) not in self.used_in:
                signal = False
        if signal:
            if s.cnt >= self.EPOCH:
                s = self.esem[eng] = self._newsem(eng)
            s.cnt += 1
            ins.then_inc(s.h, 1)
            ev = Ev(s, s.cnt, eng, my_ord)
            self.pending[eng] = False
        else:
            ev = Ev(s, s.cnt + 1, eng, my_ord if my_ord is not None else self.ord[eng] + 1)
            self.pending[eng] = True
        self._record(ev, reads, writes)
        return ev

    def dma(self, queue, out, in_, reads=(), writes=(), key=None):
        self._wait(queue, self._deps(queue, reads, writes))
        ins = self.engs[queue].dma_start(out=out, in_=in_)
        if key.dsem is None:
            key.dsem = self._newsem("d")
            self.dsems.append(key.dsem)
        s = key.dsem
        s.cnt += 16
        ins.then_inc(s.h, 16)
        ev = Ev(s, s.cnt, "dma")
        self._record(ev, reads, writes)
        return ev

    def barrier(self):
        if self.used_in is not None:
            for e_ in ("pe", "act", "dve", "pool"):
                assert not self.pending[e_], f"{e_} has unsignaled tail at barrier"
        evs = []
        for e in ("pe", "act", "dve", "pool"):
            s = self.esem[e]
            if s.cnt > 0:
                evs.append(Ev(s, s.cnt, e, self.ord[e]))
        for s in self.dsems:
            if s.cnt > 0:
                evs.append(Ev(s, s.cnt, "dma"))
        for e in self.engs:
            self._wait(e, [ev for ev in evs if ev.eng != e])


def MM(out, lhsT, rhs, start, stop):
    return lambda e: e.matmul(out, lhsT=lhsT, rhs=rhs, start=start, stop=stop)


def ACTF(out, in_, func, scale=1.0, bias=None):
    if bias is None:
        return lambda e: e.activation(out=out, in_=in_, func=func, scale=scale)
    return lambda e: e.activation(out=out, in_=in_, func=func, scale=scale, bias=bias)


def TT(out, a, b, op):
    return lambda e: e.tensor_tensor(out=out, in0=a, in1=b, op=op)


def TS(out, a, s1, op0, s2=None, op1=None):
    if op1 is None:
        return lambda e: e.tensor_scalar(out=out, in0=a, scalar1=s1, scalar2=None, op0=op0)
    return lambda e: e.tensor_scalar(out=out, in0=a, scalar1=s1, scalar2=s2, op0=op0, op1=op1)


def STT(out, in0, scalar, in1, op0, op1):
    return lambda e: e.scalar_tensor_tensor(out=out, in0=in0, scalar=scalar, in1=in1, op0=op0, op1=op1)


def CP(out, in_):
    return lambda e: e.tensor_copy(out=out, in_=in_)


def MSET(ap, val):
    return lambda e: e.memset(ap, val)


def RCP(out, in_):
    return lambda e: e.reciprocal(out=out, in_=in_)


def pieces_of(total, width):
    out = []
    a = 0
    while a < total:
        out.append((a, min(width, total - a)))
        a += width
    return out


def build_program(stage=99):
    used = _build(stage, None)[1]
    return _build(stage, used)[0]


def _build(stage, used):
    nc = bass.Bass("TRN2", target_bir_lowering=False)

    def din(name, shape, dt=F32):
        return nc.dram_tensor(name, list(shape), dt, kind="ExternalInput").ap()

    xT_all = din("xT_all", [16, 128, 8 * 512])
    xT_loc = din("xT_loc", [128, 8, NT])
    pT_loc = din("pT_loc", [128, 2, NT])
    pos_all = din("pos_all", [1, S], I32)
    pos_loc = din("pos_loc", [1, NT], I32)
    vecs_d = din("vecs", [128, NV])
    mask_d = din("mask", [128, 4 * GW])
    wkv_d = din("wkv", [128, 8 * 448])
    wq_d = din("wq", [128, 8 * 384])
    wab_d = din("wab", [4, 128, 8 * 384])
    wg_d = din("wg", [8, 128, 8 * 256])
    wqu_d = din("wqu", [128, 3 * 768])
    wqs_d = din("wqs", [128, 3 * 768])
    wkvu_d = din("wkvu", [128, 2 * 1024])
    wao_d = din("wao", [128, 4 * 1024])
    wbo_d = din("wbo", [128, 4 * 1024])
    wo_d = din("wo", [128, 8 * 1024])
    wup_d = din("wup", [NPAIR, 128, 8 * 256])
    wdn_d = din("wdn", [128, NPAIR * 1024])
    wpp_d = din("wpp", [128, 2 * 1024])
    wpg_d = din("wpg", [128, 8 * 1024])
    out_d = nc.dram_tensor("out", [128, 8, NT], F32, kind="ExternalOutput").ap()
    dbg_d = None
    if stage < 99:
        dbg_d = nc.dram_tensor("dbg", [128, 8 * NT], F32, kind="ExternalOutput").ap()

    with contextlib.ExitStack() as st:
        fw = FW(nc, st, used)

        def sb(name, shape, dt):
            return st.enter_context(nc.sbuf_tensor("s_" + name, list(shape), dt))

        ps = st.enter_context(nc.psum_tensor("ps", [128, 4096], F32))
        PB = [Buf(f"pb{i}") for i in range(8)]

        def bank(i):
            return ps[:, 512 * i:512 * (i + 1)]

        class Rot:
            def __init__(self, ids):
                self.ids = list(ids)
                self.i = 0

            def next(self):
                b = self.ids[self.i % len(self.ids)]
                self.i += 1
                return b

        vec = sb("vec", [128, NV], F32)
        VEC = Buf("vec")
        fw.dma("sp", vec[:], vecs_d, writes=[VEC], key=VEC)
        ones = sb("ones", [128, 128], BF16)
        ONES = Buf("ones")
        fw.op("pool", MSET(ones[:], 1.0), writes=[ONES])
        mask = sb("mask", [128, 4 * GW], BF16)
        MASK = Buf("mask")
        fw.dma("pool", mask[:], mask_d, writes=[MASK], key=MASK)

        def vcol(i, lo=0, hi=128):
            return vec[lo:hi, i:i + 1]

        def rstd_from_bank(bk_ap, BK, out_ap, OUT, tmp_ap, TMP, npart, inv_d):
            fw.op("act", ACTF(tmp_ap, bk_ap, AF.Ln, scale=inv_d, bias=vcol(V_EPS, 0, npart)),
                  reads=[BK, VEC], writes=[TMP])
            fw.op("act", ACTF(out_ap, tmp_ap, AF.Exp, scale=-0.5), reads=[TMP], writes=[OUT])

        ob = sb("ob", [128, 4, NT], BF16)
        OB = [Buf(f"ob{i}") for i in range(4)]
        stA = contextlib.ExitStack()

        def sbA(name, shape, dt):
            return stA.enter_context(nc.sbuf_tensor("s_" + name, list(shape), dt))

        kvn = sbA("kvn", [128, 2, S], BF16)
        KVN = [Buf(f"kvn{t}") for t in range(16)]
        kbuf = [sbA(f"kbuf{i}", [96, S], BF16) for i in range(2)]
        KB_PE = [Buf("kpe0"), Buf("kpe1")]
        KB_NO = [[Buf(f"kno{i}_{t}") for t in range(16)] for i in range(2)]
        wqu = sbA("wqu", [128, 3, 768], BF16)
        wqs = sbA("wqs", [128, 3, 768], BF16)
        wkvu = sbA("wkvu", [128, 2, 1024], BF16)
        WQU, WQS, WKVU = Buf("wqu"), Buf("wqs"), Buf("wkvu")
        fw.dma("pool", wqu[:].rearrange("p a b -> p (a b)"), wqu_d, writes=[WQU], key=WQU)
        fw.dma("pool", wqs[:].rearrange("p a b -> p (a b)"), wqs_d, writes=[WQS], key=WQS)
        fw.dma("pool", wkvu[:].rearrange("p a b -> p (a b)"), wkvu_d, writes=[WKVU], key=WKVU)

        def rope_tables(posi_ap, POSI, n, cos_ap, COS, sin_ap, SIN, ta, TA, tb, TB, ti, TI):
            R = slice(64, 96)
            fw.op("dve", CP(ta[R, :n], posi_ap), reads=[POSI], writes=[TA])
            fw.op("dve", TS(ta[R, :n], ta[R, :n], vcol(V_INV, 64, 96), ALU.mult), reads=[TA, VEC], writes=[TA])
            fw.op("dve", CP(ti[R, :n], ta[R, :n]), reads=[TA], writes=[TI])
            fw.op("dve", CP(tb[R, :n], ti[R, :n]), reads=[TI], writes=[TB])
            fw.op("dve", TT(ta[R, :n], ta[R, :n], tb[R, :n], ALU.subtract), reads=[TA, TB], writes=[TA])
            fw.op("dve", TS(tb[R, :n], ta[R, :n], 0.5, ALU.is_gt), reads=[TA], writes=[TB])
            fw.op("dve", TT(ta[R, :n], ta[R, :n], tb[R, :n], ALU.subtract), reads=[TA, TB], writes=[TA])
            fw.op("dve", TS(tb[R, :n], ta[R, :n], -0.5, ALU.is_lt), reads=[TA], writes=[TB])
            fw.op("dve", TT(ta[R, :n], ta[R, :n], tb[R, :n], ALU.add), reads=[TA, TB], writes=[TA])
            yield
            fw.op("act", ACTF(sin_ap, ta[R, :n], AF.Sin, scale=vcol(V_SGN, 64, 96)), reads=[TA, VEC], writes=[SIN])
            fw.op("dve", TS(tb[R, :n], ta[R, :n], 0.25, ALU.add), reads=[TA], writes=[TB])
            fw.op("dve", TS(ti[R, :n].bitcast(F32), tb[R, :n], 0.5, ALU.is_gt), reads=[TB], writes=[TI])
            fw.op("dve", TT(tb[R, :n], tb[R, :n], ti[R, :n].bitcast(F32), ALU.subtract), reads=[TI, TB], writes=[TB])
            yield
            fw.op("act", ACTF(cos_ap, tb[R, :n], AF.Sin, scale=TWO_PI), reads=[TB], writes=[COS])
            yield

        with contextlib.ExitStack() as st1:
            def sb1(name, shape, dt):
                return st1.enter_context(nc.sbuf_tensor("s_" + name, list(shape), dt))

            TBK = 1024
            TPT = TBK // 512
            wkv = sb1("wkv_bf", [128, 8, 448], BF16)
            WKVST, WKV = Buf("wkvst"), Buf("wkv")
            with nc.sbuf_tensor("s_wkv_st", [128, 8, 448], F32) as wkv_st:
                fw.dma("sp", wkv_st[:].rearrange("p a b -> p (a b)"), wkv_d, writes=[WKVST], key=WKVST)
                for c in range(8):
                    fw.op("dve", TS(wkv[:, c, :], wkv_st[:, c, :], vcol(V_GMIX + c), ALU.mult),
                          reads=[WKVST, VEC], writes=[WKV])
                fw.barrier()
            xb = [sb1(f"xb{i}", [128, 8 * 512], BF16) for i in range(2)]
            XB = [Buf("xb0"), Buf("xb1")]
            sq = [sb1(f"sq{i}", [128, 8 * 512], BF16) for i in range(2)]
            SQ = [Buf("sq0"), Buf("sq1")]
            rs = [sb1(f"rs{i}", [128, 512], F32) for i in range(2)]
            RS = [Buf("rs0"), Buf("rs1")]
            lnt = [sb1(f"lnt{i}", [128, 512], F32) for i in range(2)]
            LNT = [Buf("lnt0"), Buf("lnt1")]
            kvl = [sb1(f"kvl{i}", [128, 2, 512], F32) for i in range(2)]
            KVL = [Buf("kvl0"), Buf("kvl1")]
            sq2 = [sb1(f"sq2{i}", [128, 1024], BF16) for i in range(2)]
            SQ2 = [Buf("sq20"), Buf("sq21")]
            rs2 = [sb1(f"rs2{i}", [128, 512], F32) for i in range(2)]
            RS2 = [Buf("rs20"), Buf("rs21")]
            tpa = [sb1(f"tpa{i}", [96, 512], F32) for i in range(2)]
            tpb = [sb1(f"tpb{i}", [96, 512], F32) for i in range(2)]
            TPA = [Buf("tpa0"), Buf("tpa1")]
            TPB = [Buf("tpb0"), Buf("tpb1")]
            posk = [sb1(f"posk{i}", [96, TBK], I32) for i in range(2)]
            POSK = [Buf("posk0"), Buf("posk1")]
            cosk = [sb1(f"cosk{i}", [96, TBK], F32) for i in range(2)]
            sink = [sb1(f"sink{i}", [96, TBK], F32) for i in range(2)]
            COSK = [Buf("cosk0"), Buf("cosk1")]
            SINK = [Buf("sink0"), Buf("sink1")]
            tta = sb1("tta", [96, TBK], F32)
            ttb = sb1("ttb", [96, TBK], F32)
            tti = sb1("tti", [96, TBK], I32)
            TTA, TTB, TTI = Buf("tta"), Buf("ttb"), Buf("tti")

            rot = Rot(range(8))

            def tables_batch(tb_i):
                i = tb_i % 2
                fw.dma("sp", posk[i][64:96, :],
                       pos_all[0:1, tb_i * TBK:(tb_i + 1) * TBK].partition_broadcast(32),
                       writes=[POSK[i]], key=POSK[i])
                return rope_tables(posk[i][64:96, :], POSK[i], TBK, cosk[i][64:96, :], COSK[i],
                                   sink[i][64:96, :], SINK[i], tta, TTA, ttb, TTB, tti, TTI)

            g0 = tables_batch(0)
            for _ in g0:
                pass
            state = {"gen": None}
            tbanks = {}

            def part1(t):
                i = t % 2
                fw.op("act", ACTF(sq[i][:], xb[i][:], AF.Square), reads=[XB[i]], writes=[SQ[i]])
                bA = rot.next()
                for c in range(8):
                    fw.op("pe", MM(bank(bA), ones[:, :], sq[i][:, c * 512:(c + 1) * 512], c == 0, c == 7),
                          reads=[ONES, SQ[i]], writes=[PB[bA]], signal=(c == 7))
                rstd_from_bank(bank(bA), PB[bA], rs[i][:], RS[i], lnt[i][:], LNT[i], 128, 1.0 / D)
                for m in range(2):
                    bk = rot.next()
                    for c in range(8):
                        fw.op("pe", MM(bank(bk), wkv[:, c, m * 128:(m + 1) * 128], xb[i][:, c * 512:(c + 1) * 512],
                                       c == 0, c == 7), reads=[WKV, XB[i]], writes=[PB[bk]], signal=(c == 7))
                    fw.op("dve", TT(kvl[i][:, m, :], bank(bk), rs[i][:], ALU.mult),
                          reads=[PB[bk], RS[i]], writes=[KVL[i]])
                fw.op("act", ACTF(sq2[i][:], kvl[i][:].rearrange("p a b -> p (a b)"), AF.Square),
                      reads=[KVL[i]], writes=[SQ2[i]])
                bD = rot.next()
                for c in range(8):
                    fw.op("pe", MM(bank(bD)[0:96, :], wkv[:, c, 256:352], xb[i][:, c * 512:(c + 1) * 512],
                                   c == 0, c == 7), reads=[WKV, XB[i]], writes=[PB[bD]], signal=(c == 7))
                bE = rot.next()
                for c in range(8):
                    fw.op("pe", MM(bank(bE)[0:96, :], wkv[:, c, 352:448], xb[i][:, c * 512:(c + 1) * 512],
                                   c == 0, c == 7), reads=[WKV, XB[i]], writes=[PB[bE]], signal=(c == 7))
                tbanks[t] = (bD, bE)
                if t + 2 < 16:
                    fw.dma("pool", xb[i][:], xT_all[t + 2], reads=[SQ[i]], writes=[XB[i]], key=XB[i])

            def part2(t):
                i = t % 2
                tbi = (t // TPT) % 2
                tcol = (t % TPT) * 512
                bD, bE = tbanks[t]
                R = slice(64, 96)
                fw.op("dve", TT(tpa[i][R, :], bank(bD)[R, :], cosk[tbi][R, tcol:tcol + 512], ALU.mult),
                      reads=[PB[bD], COSK[tbi]], writes=[TPA[i]])
                fw.op("dve", TT(tpb[i][R, :], bank(bE)[R, :], sink[tbi][R, tcol:tcol + 512], ALU.mult),
                      reads=[PB[bE], SINK[tbi]], writes=[TPB[i]])
                bC = rot.next()
                for m in range(2):
                    fw.op("pe", MM(bank(bC), ones[:, :], sq2[i][:, m * 512:(m + 1) * 512], m == 0, m == 1),
                          reads=[ONES, SQ2[i]], writes=[PB[bC]], signal=(m == 1))
                rstd_from_bank(bank(bC), PB[bC], rs2[i][:], RS2[i], lnt[i][:], LNT[i], 128, 1.0 / 256)
                for m in range(2):
                    fw.op("dve", STT(kvn[:, m, t * 512:(t + 1) * 512], kvl[i][:, m, :], vcol(V_GKV + m),
                                     rs2[i][:], ALU.mult, ALU.mult),
                          reads=[KVL[i], RS2[i], VEC], writes=[KVN[t]])
                fw.op("dve", TT(tpa[i][R, :], tpa[i][R, :], tpb[i][R, :], ALU.add),
                      reads=[TPA[i], TPB[i]], writes=[TPA[i]])
                fw.op("dve", TT(kbuf[0][R, t * 512:(t + 1) * 512], tpa[i][R, :], rs[i][R, :], ALU.mult),
                      reads=[TPA[i], RS[i]], writes=[KB_PE[0]])

            fw.dma("pool", xb[0][:], xT_all[0], writes=[XB[0]], key=XB[0])
            fw.dma("pool", xb[1][:], xT_all[1], writes=[XB[1]], key=XB[1])
            g1 = tables_batch(1)
            for _ in g1:
                pass
            part1(0)
            gens = []
            for t in range(16):
                if t + 1 < 16:
                    part1(t + 1)
                for g in list(gens):
                    try:
                        next(g)
                    except StopIteration:
                        gens.remove(g)
                part2(t)
                if (t + 1) % TPT == 0 and (t + 1) // TPT + 1 < 16 // TPT:
                    g = tables_batch((t + 1) // TPT + 1)
                    next(g)
                    gens.append(g)
            for g in gens:
                for _ in g:
                    pass
            fw.dma("sp", kbuf[1][64:96, :], kbuf[0][64:96, :], reads=[KB_PE[0]], writes=[KB_PE[1]], key=KB_PE[1])
            fw.barrier()

        if stage == 1:
            with contextlib.ExitStack() as std:
                dt_ = std.enter_context(nc.sbuf_tensor("dbgt", [128, 8 * NT], F32))
                DT = Buf("dbgt")
                fw.op("dve", MSET(dt_[:], 0.0), writes=[DT])
                fw.op("dve", CP(dt_[:, 0:8192], kvn[:, 0, :]), reads=KVN, writes=[DT])
                fw.op("dve", CP(dt_[64:96, 8192:16384], kbuf[1][64:96, :]), reads=[KB_PE[1]], writes=[DT])
                fw.dma("sp", dbg_d, dt_[:], reads=[DT], key=DT)
                fw.barrier()
            stA.close()
            return nc, fw.used

        qn = sbA("qn", [128, 3, NT], BF16)
        QN = [Buf(f"qn{i}") for i in range(6)]
        cos_l = sbA("cos_l", [96, NT], F32)
        sin_l = sbA("sin_l", [96, NT], F32)
        COSL, SINL = Buf("cosl"), Buf("sinl")
        PW = 352
        PCS = pieces_of(NT, PW)

        with contextlib.ExitStack() as st2t:
            def sb2(name, shape, dt):
                return st2t.enter_context(nc.sbuf_tensor("s_" + name, list(shape), dt))
            posl = sb2("posl", [96, NT], I32)
            POSL = Buf("posl")
            lta = sb2("lta", [96, NT], F32)
            ltb = sb2("ltb", [96, NT], F32)
            lti = sb2("lti", [96, NT], I32)
            LTA, LTB, LTI = Buf("lta"), Buf("ltb"), Buf("lti")
            fw.dma("sp", posl[64:96, :], pos_loc[0:1, :].partition_broadcast(32), writes=[POSL], key=POSL)
            for _ in rope_tables(posl[64:96, :], POSL, NT, cos_l[64:96, :], COSL, sin_l[64:96, :], SINL,
                                 lta, LTA, ltb, LTB, lti, LTI):
                pass
            fw.barrier()
        with contextlib.ExitStack() as st2:
            def sb2(name, shape, dt):
                return st2.enter_context(nc.sbuf_tensor("s_" + name, list(shape), dt))

            wq = sb2("wq_bf", [128, 8, 384], BF16)
            WQ = Buf("wq")
            fw.dma("pool", wq[:].rearrange("p a b -> p (a b)"), wq_d, writes=[WQ], key=WQ)
            xl = [sb2(f"xl{i}", [128, 8, PW], F32) for i in range(2)]
            XL = [Buf("xl0"), Buf("xl1")]
            sqx = [sb2(f"sqx{i}", [128, 8, PW], BF16) for i in range(2)]
            SQX = [Buf("sqx0"), Buf("sqx1")]
            up = [sb2(f"up{i}", [128, 8, PW], BF16) for i in range(2)]
            UP = [Buf("up0"), Buf("up1")]
            rsl = [sb2(f"rsl{i}", [128, PW], F32) for i in range(2)]
            RSL = [Buf("rsl0"), Buf("rsl1")]
            lnl = [sb2(f"lnl{i}", [128, PW], F32) for i in range(2)]
            LNL = [Buf("lnl0"), Buf("lnl1")]
            sq3 = [sb2(f"sq3{i}", [128, 3, PW], BF16) for i in range(2)]
            SQ3 = [Buf("sq30"), Buf("sq31")]
            rsq = [sb2(f"rsq{i}", [128, PW], F32) for i in range(2)]
            RSQ = [Buf("rsq0"), Buf("rsq1")]
            rot = Rot(range(8))
            for pi, (a, w) in enumerate(PCS):
                i = pi % 2
                fw.dma("sp", xl[i][:, :, :w], xT_loc[:, :, a:a + w], writes=[XL[i]], key=XL[i])
                fw.op("act", ACTF(sqx[i][:, :, :w], xl[i][:, :, :w], AF.Square), reads=[XL[i]], writes=[SQX[i]])
                bA = rot.next()
                for c in range(8):
                    fw.op("pe", MM(bank(bA)[:, :w], ones[:, :], sqx[i][:, c, :w], c == 0, c == 7),
                          reads=[ONES, SQX[i]], writes=[PB[bA]], signal=(c == 7))
                rstd_from_bank(bank(bA)[:, :w], PB[bA], rsl[i][:, :w], RSL[i], lnl[i][:, :w], LNL[i], 128, 1.0 / D)
                for c in range(8):
                    fw.op("dve", STT(up[i][:, c, :w], xl[i][:, c, :w], vcol(V_GMIX + c), rsl[i][:, :w],
                                     ALU.mult, ALU.mult), reads=[XL[i], RSL[i], VEC], writes=[UP[i]])
                bq = []
                for m in range(3):
                    bk = rot.next()
                    bq.append(bk)
                    for c in range(8):
                        fw.op("pe", MM(bank(bk)[:, :w], wq[:, c, m * 128:(m + 1) * 128], up[i][:, c, :w],
                                       c == 0, c == 7), reads=[WQ, UP[i]], writes=[PB[bk]], signal=(c == 7))
                    fw.op("act", ACTF(sq3[i][:, m, :w], bank(bk)[:, :w], AF.Square), reads=[PB[bk]], writes=[SQ3[i]])
                bS = rot.next()
                for m in range(3):
                    fw.op("pe", MM(bank(bS)[:, :w], ones[:, :], sq3[i][:, m, :w], m == 0, m == 2),
                          reads=[ONES, SQ3[i]], writes=[PB[bS]], signal=(m == 2))
                rstd_from_bank(bank(bS)[:, :w], PB[bS], rsq[i][:, :w], RSQ[i], lnl[i][:, :w], LNL[i], 128, 1.0 / 384)
                for m in range(3):
                    fw.op("dve", STT(qn[:, m, a:a + w], bank(bq[m])[:, :w], vcol(V_GQ + m), rsq[i][:, :w],
                                     ALU.mult, ALU.mult), reads=[PB[bq[m]], RSQ[i], VEC], writes=[QN[pi]])
            fw.barrier()

        with contextlib.ExitStack() as st3:
            def sb3(name, shape, dt):
                return st3.enter_context(nc.sbuf_tensor("s_" + name, list(shape), dt))

            vbuf = [sb3(f"vbuf{i}", [128, 64, 128], BF16) for i in range(2)]
            VB = [[Buf(f"vb{i}_{u}") for u in range(8)] for i in range(2)]
            VONE = [Buf("vone0"), Buf("vone1")]
            fw.op("pool", MSET(vbuf[0][:, :, 64:128], 1.0), writes=[VONE[0]])
            fw.op("pool", MSET(vbuf[1][:, :, 0:64], 1.0), writes=[VONE[1]])
            qbuf = [sb3(f"qbuf{i}", [96, NT], BF16) for i in range(2)]
            QB = [Buf("qb0"), Buf("qb1")]
            NP = CFG["np"]
            pbuf = [sb3(f"pbuf{i}", [128, 512], BF16) for i in range(NP)]
            PBUF = [Buf(f"pbuf{i}") for i in range(NP)]
            dtmp = sb3("dtmp", [128, NT], F32)
            DTMP = Buf("dtmp")
            qta = [sb3(f"qta{i}", [96, 512], F32) for i in range(2)]
            qtb = sb3("qtb", [96, 512], F32)
            QTA, QTB = [Buf("qta0"), Buf("qta1")], Buf("qtb")
            APC = pieces_of(NT, 512)
            zt = sb3("zt", [128, 512], BF16)
            ZT = Buf("zt")
            fw.op("pool", MSET(zt[:], 0.0), writes=[ZT])

            def junk(n):
                if n > 0:
                    fw.op("pe", MM(ps[:, 2112:2112 + n], ones[:, :], zt[:, :n], False, False),
                          reads=[ONES, ZT], writes=[PB[4]], signal=False)

            def build_units(h, bank_rot):
                hb = h % 2
                us = []

                def k_unit(t):
                    def f():
                        bk = bank_rot.next()
                        for c in range(2):
                            fw.op("pe", MM(bank(bk)[0:64, :], wkvu[:, c, h * 128:h * 128 + 64],
                                           kvn[:, c, t * 512:(t + 1) * 512], c == 0, c == 1),
                                  reads=[WKVU, KVN[t]], writes=[PB[bk]], signal=(c == 1))
                        fw.op("dve", CP(kbuf[hb][0:64, t * 512:(t + 1) * 512], bank(bk)[0:64, :]),
                              reads=[PB[bk]], writes=[KB_NO[hb][t]])
                    return f

                def v_unit(u):
                    def f():
                        bk = bank_rot.next()
                        for tt in range(8):
                            tile = 8 * u + tt
                            for c in range(2):
                                fw.op("pe", MM(bank(bk)[:, tt * 64:(tt + 1) * 64],
                                               kvn[:, c, tile * 128:(tile + 1) * 128],
                                               wkvu[:, c, h * 128 + 64:h * 128 + 128], c == 0, c == 1),
                                      reads=[WKVU, KVN[tile // 4]], writes=[PB[bk]],
                                      signal=(c == 1 and tt == 7))
                        voff = 0 if hb == 0 else 64
                        fw.op("dve", CP(vbuf[hb][:, 8 * u:8 * u + 8, voff:voff + 64],
                                        bank(bk).rearrange("p (a b) -> p a b", b=64)),
                              reads=[PB[bk]], writes=[VB[hb][u]])
                    return f

                def qa_unit(pi, a, w):
                    def f():
                        bk = bank_rot.next()
                        for c in range(3):
                            fw.op("pe", MM(bank(bk)[0:96, :w], wqu[:, c, h * 96:(h + 1) * 96], qn[:, c, a:a + w],
                                           c == 0, c == 2), reads=[WQU] + QN, writes=[PB[bk]], signal=(c == 2))
                        fw.op("dve", CP(qbuf[hb][0:64, a:a + w], bank(bk)[0:64, :w]), reads=[PB[bk]], writes=[QB[hb]])
                        fw.op("dve", TT(qta[pi % 2][64:96, :w], bank(bk)[64:96, :w], cos_l[64:96, a:a + w], ALU.mult),
                              reads=[PB[bk], COSL], writes=[QTA[pi % 2]])
                    return f

                def qb_unit(pi, a, w):
                    def f():
                        bk2 = bank_rot.next()
                        for c in range(3):
                            fw.op("pe", MM(bank(bk2)[0:96, :w], wqs[:, c, h * 96:(h + 1) * 96], qn[:, c, a:a + w],
                                           c == 0, c == 2), reads=[WQS] + QN, writes=[PB[bk2]], signal=(c == 2))
                        fw.op("dve", TT(qtb[64:96, :w], bank(bk2)[64:96, :w], sin_l[64:96, a:a + w], ALU.mult),
                              reads=[PB[bk2], SINL], writes=[QTB])
                        fw.op("dve", TT(qbuf[hb][64:96, a:a + w], qta[pi % 2][64:96, :w], qtb[64:96, :w], ALU.add),
                              reads=[QTA[pi % 2], QTB], writes=[QB[hb]])
                    return f

                for pi, (a, w) in enumerate(APC):
                    us.append(qa_unit(pi, a, w))
                    us.append(qb_unit(pi, a, w))
                kk = [k_unit(t) for t in range(16)]
                vv = [v_unit(u) for u in range(8)]
                for u in range(8):
                    us.append(kk[2 * u])
                    us.append(kk[2 * u + 1])
                    us.append(vv[u])
                return us

            def main_units():
                out = []
                for kb in range(64):
                    G = kb // 4
                    a0 = GW * G
                    for p, (pa, pw) in enumerate(APC):
                        lo = max(pa, a0)
                        hi = pa + pw
                        if lo < hi:
                            out.append((kb, p, lo, hi - lo))
                return out

            LASTKB = {}
            for p, (pa, pw) in enumerate(APC):
                LASTKB[p] = 4 * min(15, (pa + pw - 1) // GW) + 3

            for un in build_units(0, Rot(range(8))):
                un()
            import collections as _c

            class FreeBanks:
                def __init__(self, ids):
                    self.q = _c.deque(ids)

                def next(self):
                    self.last = self.q.popleft()
                    return self.last

            fb = FreeBanks(CFG["sbanks"])
            LA = CFG["la"]
            for h in range(CFG["nh"]):
                hb = h % 2
                units = main_units()
                builds = build_units(h + 1, fb) if h + 1 < NH else []
                bi = 0
                sbank = {}
                nq = 0
                nun = len(units)
                cool = []
                BE = CFG["bevery"]
                for k in range(nun):
                    kb, p, a, w = units[k]
                    G, r = kb // 4, kb % 4
                    pb = k % NP
                    while cool and cool[0][0] <= k:
                        fb.q.append(cool.pop(0)[1])
                    want_build = (k % BE == BE - 1) and bi < len(builds)
                    la_eff = LA - 1 if want_build else LA
                    while nq < nun and nq <= k + la_eff and fb.q:
                        kb2, p2, a2, w2 = units[nq]
                        sbk = fb.next()
                        sbank[nq] = sbk
                        fw.op("pe", MM(bank(sbk)[:, :w2], kbuf[hb][0:96, kb2 * 128:(kb2 + 1) * 128],
                                       qbuf[hb][0:96, a2:a2 + w2], True, True),
                              reads=[KB_NO[hb][kb2 // 4], KB_PE[hb], QB[hb]], writes=[PB[sbk]])
                        nq += 1
                    if k not in sbank:
                        fb.q.append(cool.pop(0)[1])
                        kb2, p2, a2, w2 = units[nq]
                        assert nq == k
                        sbk = fb.next()
                        sbank[nq] = sbk
                        fw.op("pe", MM(bank(sbk)[:, :w2], kbuf[hb][0:96, kb2 * 128:(kb2 + 1) * 128],
                                       qbuf[hb][0:96, a2:a2 + w2], True, True),
                              reads=[KB_NO[hb][kb2 // 4], KB_PE[hb], QB[hb]], writes=[PB[sbk]])
                        nq += 1
                    sbk = sbank[k]
                    fw.op("act", ACTF(pbuf[pb][:, :w], bank(sbk)[:, :w], AF.Exp, scale=SCALE),
                          reads=[PB[sbk]], writes=[PBUF[pb]])
                    fb.q.append(sbk)
                    lo = max(a, GW * G)
                    hi = min(a + w, GW * G + GW)
                    if lo < hi and CFG["mask"]:
                        fw.op(CFG["mask_eng"], TT(pbuf[pb][:, lo - a:hi - a], pbuf[pb][:, lo - a:hi - a],
                                         mask[:, r * GW + lo - GW * G:r * GW + hi - GW * G], ALU.mult),
                              reads=[PBUF[pb], MASK], writes=[PBUF[pb]])
                    junk(CFG["junk"])
                    fw.op("pe", MM(ps[:, a:a + w], vbuf[hb][:, kb, :], pbuf[pb][:, :w], (kb == 0) or not CFG["pv_acc"], (kb == LASTKB[p]) or not CFG["pv_acc"]),
                          reads=[VB[hb][kb // 8], VONE[hb], PBUF[pb]], writes=[PB[p]])
                    if want_build and fb.q:
                        builds[bi]()
                        bi += 1
                        cool.append((k + 1 + CFG["cool"], fb.last))
                for _, bkc in cool:
                    fb.q.append(bkc)
                cool = []
                while bi < len(builds):
                    builds[bi]()
                    bi += 1
                    fb.q.append(fb.last)
                lo_r, hi_r = (slice(0, 64), slice(64, 128)) if hb == 0 else (slice(64, 128), slice(0, 64))
                for p, (pa, pw) in enumerate(APC):
                    cs = slice(pa, pa + pw)
                    fw.op("dve", TS(dtmp[lo_r, cs], ps[hi_r, cs], 1e-30, ALU.max), reads=[PB[p]], writes=[DTMP])
                    fw.op("dve", RCP(dtmp[lo_r, cs], dtmp[lo_r, cs]), reads=[DTMP], writes=[DTMP])
                    fw.op("dve", TT(ob[lo_r, h // 2, cs], ps[lo_r, cs], dtmp[lo_r, cs], ALU.mult),
                          reads=[PB[p], DTMP], writes=[OB[h // 2]])
            fw.barrier()

        if stage == 2:
            with contextlib.ExitStack() as std:
                dt_ = std.enter_context(nc.sbuf_tensor("dbgt", [128, 8 * NT], F32))
                DT = Buf("dbgt")
                fw.op("dve", MSET(dt_[:], 0.0), writes=[DT])
                fw.op("dve", CP(dt_[:, 0:4 * NT], ob[:].rearrange("p a b -> p (a b)")), reads=OB, writes=[DT])
                fw.op("dve", CP(dt_[:, 4 * NT:7 * NT], qn[:].rearrange("p a b -> p (a b)")), reads=QN, writes=[DT])
                fw.dma("sp", dbg_d, dt_[:], reads=[DT], key=DT)
                fw.barrier()
            stA.close()
            return nc, fw.used

        stA.close()

        NTH = NT // 2
        HPC = pieces_of(NTH, PW)
        rot = Rot(range(8))
        OUTB = Buf("outdma")

        def sq_stat_rs(src_fn, SRC, nch, w, sqt, SQT, rst, RST, lnb, LNB, inv_d, sq_eng="pool"):
            for c in range(nch):
                if sq_eng == "act":
                    fw.op("act", ACTF(sqt[:, c, :w], src_fn(c), AF.Square), reads=SRC, writes=[SQT])
                else:
                    fw.op("pool", TT(sqt[:, c, :w], src_fn(c), src_fn(c), ALU.mult), reads=SRC, writes=[SQT])
            bS = rot.next()
            for c in range(nch):
                fw.op("pe", MM(bank(bS)[:, :w], ones[:, :], sqt[:, c, :w], c == 0, c == nch - 1),
                      reads=[ONES, SQT], writes=[PB[bS]], signal=(c == nch - 1))
            rstd_from_bank(bank(bS)[:, :w], PB[bS], rst[:, :w], RST, lnb[:, :w], LNB, 128, inv_d)

        for hf in range(2):
            c0 = hf * NTH
            with contextlib.ExitStack() as sth:
                def sbh(name, shape, dt):
                    return sth.enter_context(nc.sbuf_tensor(f"s_{name}_{hf}", list(shape), dt))

                xres = sbh("xres", [128, 8, NTH], F32)
                XRES = [Buf(f"xres{i}") for i in range(3)]
                u2 = sbh("u2", [128, 8, NTH], BF16)
                U2 = [Buf(f"u2{i}") for i in range(3)]
                sqt = sbh("sqt", [128, 8, PW], BF16)
                SQT = Buf("sqt")
                rst = sbh("rst", [128, PW], F32)
                RST = Buf("rst")
                lnb = sbh("lnb", [128, PW], F32)
                LNB = Buf("lnb")
                for pi, (a, w) in enumerate(HPC):
                    fw.dma("sp", xres[:, :, a:a + w], xT_loc[:, :, c0 + a:c0 + a + w], writes=[XRES[pi]], key=XRES[pi])

                with contextlib.ExitStack() as sabc:
                    mixed = sabc.enter_context(nc.sbuf_tensor(f"s_mixed_{hf}", [128, 8, NTH], BF16))
                    MIX = [Buf(f"mix{i}") for i in range(3)]
                    with contextlib.ExitStack() as sab:
                        def sbab(name, shape, dt):
                            return sab.enter_context(nc.sbuf_tensor(f"s_{name}_{hf}", list(shape), dt))

                        uh = sbab("uh", [128, 8, NTH], BF16)
                        UH = [Buf(f"uh{i}") for i in range(3)]
                        yap = sbab("yap", [128, 4, NTH], BF16)
                        YAP = [Buf(f"yap{m}") for m in range(4)]
                        with contextlib.ExitStack() as sa:
                            def sba(name, shape, dt):
                                return sa.enter_context(nc.sbuf_tensor(f"s_{name}_{hf}", list(shape), dt))

                            wab = [sba(f"wab{m}", [128, 8, 384], BF16) for m in range(4)]
                            WAB = [Buf(f"wab{m}") for m in range(4)]
                            for m in range(4):
                                fw.dma("pool", wab[m][:].rearrange("p a b -> p (a b)"), wab_d[m], writes=[WAB[m]], key=WAB[m])
                            for pi, (a, w) in enumerate(HPC):
                                sq_stat_rs(lambda c: xres[:, c, a:a + w], [XRES[pi]], 8, w, sqt, SQT, rst, RST, lnb, LNB,
                                           1.0 / D, sq_eng="act")
                                for c in range(8):
                                    fw.op("dve", STT(uh[:, c, a:a + w], xres[:, c, a:a + w], vcol(V_GMIX + c), rst[:, :w],
                                                     ALU.mult, ALU.mult), reads=[XRES[pi], RST, VEC], writes=[UH[pi]])
                            t1 = [sba(f"t1{i}", [128, PW], F32) for i in range(2)]
                            T1 = [Buf("t10"), Buf("t11")]
                            cx = [sba(f"cx{i}", [128, NTH], F32) for i in range(2)]
                            CX = [Buf("cx0"), Buf("cx1")]
                            cv = [sba(f"cv{i}", [128, NTH], F32) for i in range(2)]
                            CV = [Buf("cv0"), Buf("cv1")]
                            k = 0

                            def ab_part(m):
                                i = m % 2
                                for pi, (a, w) in enumerate(HPC):
                                    bb = rot.next()
                                    for c in range(8):
                                        fw.op("pe", MM(bank(bb)[:, :w], wab[m][:, c, 256:384],
                                                       uh[:, c, a:a + w], c == 0, c == 7),
                                              reads=[WAB[m], UH[pi]], writes=[PB[bb]], signal=(c == 7))
                                    fw.op("dve", TT(yap[:, m, a:a + w], bank(bb)[:, :w], cv[i][:, a:a + w], ALU.mult),
                                          reads=[PB[bb], CV[i]], writes=[YAP[m]])

                            for m in range(4):
                                i = m % 2
                                for pi, (a, w) in enumerate(HPC):
                                    bc = rot.next()
                                    for c in range(8):
                                        fw.op("pe", MM(bank(bc)[:, :w], wab[m][:, c, 0:128],
                                                       uh[:, c, a:a + w], c == 0, c == 7),
                                              reads=[WAB[m], UH[pi]], writes=[PB[bc]], signal=(c == 7))
                                    fw.op("act", ACTF(t1[k % 2][:, :w], bank(bc)[:, :w], AF.Copy),
                                          reads=[PB[bc]], writes=[T1[k % 2]])
                                    bx = rot.next()
                                    for c in range(8):
                                        fw.op("pe", MM(bank(bx)[:, :w], wab[m][:, c, 128:256],
                                                       uh[:, c, a:a + w], c == 0, c == 7),
                                              reads=[WAB[m], UH[pi]], writes=[PB[bx]], signal=(c == 7))
                                    fw.op("dve", TT(cx[i][:, a:a + w], bank(bx)[:, :w], t1[k % 2][:, :w], ALU.mult),
                                          reads=[PB[bx], T1[k % 2]], writes=[CX[i]])
                                    k += 1
                                w0, w1, w2 = (vcol(V_CAW + 3 * m + kk) for kk in range(3))
                                fw.op("pool", TS(cv[i][:, :], cx[i][:, :], w2, ALU.mult, 0.0, ALU.add),
                                      reads=[CX[i], VEC], writes=[CV[i]])
                                fw.op("dve", STT(cv[i][:, 1:NTH], cx[i][:, 0:NTH - 1], w1, cv[i][:, 1:NTH], ALU.mult, ALU.add),
                                      reads=[CX[i], CV[i], VEC], writes=[CV[i]])
                                fw.op("dve", STT(cv[i][:, 2:NTH], cx[i][:, 0:NTH - 2], w0, cv[i][:, 2:NTH], ALU.mult, ALU.add),
                                      reads=[CX[i], CV[i], VEC], writes=[CV[i]])
                                if m >= 1:
                                    ab_part(m - 1)
                            ab_part(3)
                            fw.barrier()
                        with contextlib.ExitStack() as sbb:
                            def sbb_(name, shape, dt):
                                return sbb.enter_context(nc.sbuf_tensor(f"s_{name}_{hf}", list(shape), dt))

                            wg = [sbb_(f"wg{m}", [128, 8, 256], BF16) for m in range(8)]
                            WG = [Buf(f"wg{m}") for m in range(8)]
                            wao = sbb_("wao", [128, 4, 1024], BF16)
                            wbo = sbb_("wbo", [128, 4, 1024], BF16)
                            WAO, WBO = Buf("wao"), Buf("wbo")
                            fw.dma("pool", wg[0][:].rearrange("p a b -> p (a b)"), wg_d[0], writes=[WG[0]], key=WG[0])
                            fw.dma("pool", wao[:].rearrange("p a b -> p (a b)"), wao_d, writes=[WAO], key=WAO)
                            fw.dma("pool", wbo[:].rearrange("p a b -> p (a b)"), wbo_d, writes=[WBO], key=WBO)
                            for m in range(1, 8):
                                fw.dma("pool", wg[m][:].rearrange("p a b -> p (a b)"), wg_d[m], writes=[WG[m]], key=WG[m])
                            sga = [sbb_(f"sga{i}", [128, PW], F32) for i in range(2)]
                            sgb = [sbb_(f"sgb{i}", [128, PW], F32) for i in range(2)]
                            SGA = [Buf("sga0"), Buf("sga1")]
                            SGB = [Buf("sgb0"), Buf("sgb1")]
                            k = 0
                            for mo in range(8):
                                for pi, (a, w) in enumerate(HPC):
                                    i = k % 2
                                    k += 1
                                    b1, b2, b3, b4 = rot.next(), rot.next(), rot.next(), rot.next()
                                    for c in range(8):
                                        fw.op("pe", MM(bank(b1)[:, :w], wg[mo][:, c, 0:128], uh[:, c, a:a + w],
                                                       c == 0, c == 7), reads=[WG[mo], UH[pi]], writes=[PB[b1]], signal=(c == 7))
                                    fw.op("act", ACTF(sga[i][:, :w], bank(b1)[:, :w], AF.Sigmoid), reads=[PB[b1]], writes=[SGA[i]])
                                    for c in range(4):
                                        fw.op("pe", MM(bank(b2)[:, :w], wao[:, c, mo * 128:(mo + 1) * 128], yap[:, c, a:a + w],
                                                       c == 0, c == 3), reads=[WAO] + YAP, writes=[PB[b2]], signal=(c == 3))
                                    fw.op("dve", TT(sga[i][:, :w], bank(b2)[:, :w], sga[i][:, :w], ALU.mult),
                                          reads=[PB[b2], SGA[i]], writes=[SGA[i]])
                                    for c in range(8):
                                        fw.op("pe", MM(bank(b3)[:, :w], wg[mo][:, c, 128:256],
                                                       uh[:, c, a:a + w], c == 0, c == 7),
                                              reads=[WG[mo], UH[pi]], writes=[PB[b3]], signal=(c == 7))
                                    fw.op("act", ACTF(sgb[i][:, :w], bank(b3)[:, :w], AF.Sigmoid), reads=[PB[b3]], writes=[SGB[i]])
                                    for c in range(4):
                                        fw.op("pe", MM(bank(b4)[:, :w], wbo[:, c, mo * 128:(mo + 1) * 128],
                                                       ob[:, c, c0 + a:c0 + a + w], c == 0, c == 3),
                                              reads=[WBO] + OB, writes=[PB[b4]], signal=(c == 3))
                                    fw.op("dve", TT(sgb[i][:, :w], bank(b4)[:, :w], sgb[i][:, :w], ALU.mult),
                                          reads=[PB[b4], SGB[i]], writes=[SGB[i]])
                                    fw.op("pool", TT(mixed[:, mo, a:a + w], sga[i][:, :w], sgb[i][:, :w], ALU.add),
                                          reads=[SGA[i], SGB[i]], writes=[MIX[pi]])
                            fw.barrier()

                    with contextlib.ExitStack() as sc_:
                        def sbc(name, shape, dt):
                            return sc_.enter_context(nc.sbuf_tensor(f"s_{name}_{hf}", list(shape), dt))

                        wo = [sbc(f"wo{i}", [128, 8, 512], BF16) for i in range(2)]
                        WO = [Buf("wo0"), Buf("wo1")]
                        for i in range(2):
                            fw.dma("pool", wo[i][:], wo_d.rearrange("p (a b) -> p a b", b=1024)[:, :, i * 512:(i + 1) * 512],
                                   writes=[WO[i]], key=WO[i])
                        mos = [sbc(f"mos{i}", [128, 8, PW], F32) for i in range(2)]
                        MOS = [Buf("mos0"), Buf("mos1")]
                        tmpc = [sbc(f"tmpc{i}", [128, PW], F32) for i in range(2)]
                        TMPC = [Buf("tmpc0"), Buf("tmpc1")]

                        def c_mm(pi):
                            a, w = HPC[pi]
                            for m in range(8):
                                bk = rot.next()
                                for c in range(8):
                                    fw.op("pe", MM(bank(bk)[:, :w], wo[m // 4][:, c, (m % 4) * 128:(m % 4 + 1) * 128],
                                                   mixed[:, c, a:a + w], c == 0, c == 7),
                                          reads=[WO[m // 4], MIX[pi]], writes=[PB[bk]], signal=(c == 7))
                                fw.op("act", ACTF(mos[pi % 2][:, m, :w], bank(bk)[:, :w], AF.Copy),
                                      reads=[PB[bk]], writes=[MOS[pi % 2]])

                        def c_post(pi):
                            a, w = HPC[pi]
                            mo_ = mos[pi % 2]
                            sq_stat_rs(lambda c: mo_[:, c, :w], [MOS[pi % 2]], 8, w, sqt, SQT, rst, RST, lnb, LNB, 1.0 / D)
                            for m in range(8):
                                i = m % 2
                                fw.op("dve", STT(tmpc[i][:, :w], mo_[:, m, :w], vcol(V_GMPOST + m), rst[:, :w],
                                                 ALU.mult, ALU.mult), reads=[MOS[pi % 2], RST, VEC], writes=[TMPC[i]])
                                fw.op("pool", TT(xres[:, m, a:a + w], xres[:, m, a:a + w], tmpc[i][:, :w], ALU.add),
                                      reads=[XRES[pi], TMPC[i]], writes=[XRES[pi]])
                            sq_stat_rs(lambda c: xres[:, c, a:a + w], [XRES[pi]], 8, w, sqt, SQT, rst, RST, lnb, LNB,
                                       1.0 / D, sq_eng="act")
                            for c in range(8):
                                fw.op("dve", STT(u2[:, c, a:a + w], xres[:, c, a:a + w], vcol(V_GFPRE + c), rst[:, :w],
                                                 ALU.mult, ALU.mult), reads=[XRES[pi], RST, VEC], writes=[U2[pi]])

                        c_mm(0)
                        for pi in range(3):
                            if pi + 1 < 3:
                                c_mm(pi + 1)
                            c_post(pi)
                        fw.barrier()

                with contextlib.ExitStack() as sd:
                    def sbd(name, shape, dt):
                        return sd.enter_context(nc.sbuf_tensor(f"s_{name}_{hf}", list(shape), dt))

                    ff = sbd("ff", [128, NPAIR, NTH], BF16)
                    FF = [Buf(f"ff{m}") for m in range(NPAIR)]
                    wdn0 = sbd("wdn0", [128, NPAIR, 512], BF16)
                    WDN = [Buf("wdn0"), Buf("wdn1")]
                    wdn_v = wdn_d.rearrange("p (a b) -> p a b", b=1024)
                    with contextlib.ExitStack() as sdd:
                        def sbdd(name, shape, dt):
                            return sdd.enter_context(nc.sbuf_tensor(f"s_{name}_{hf}", list(shape), dt))

                        NWB = 3
                        wup = [sbdd(f"wup{i}", [128, 8, 256], BF16) for i in range(NWB)]
                        WUP = [Buf(f"wup{i}") for i in range(NWB)]
                        rows = [[sbdd(f"row{i}_{r}", [128, NTH], F32) for r in range(4)] for i in range(2)]
                        ROW = [[Buf(f"row{i}_{r}") for r in range(4)] for i in range(2)]

                        def load_wup(m):
                            fw.dma("pool", wup[m % NWB][:].rearrange("p a b -> p (a b)"), wup_d[m],
                                   writes=[WUP[m % NWB]], key=WUP[m % NWB])

                        for m in range(min(NWB - 1, NPAIR)):
                            load_wup(m)

                        def d_mm(m):
                            i = m % 2
                            wb = m % NWB
                            for half_i in range(2):
                                gs = rows[i][2 * half_i]
                                GS = ROW[i][2 * half_i]
                                for pi, (a, w) in enumerate(HPC):
                                    bk = rot.next()
                                    for c in range(8):
                                        fw.op("pe", MM(bank(bk)[:, :w], wup[wb][:, c, half_i * 128:(half_i + 1) * 128],
                                                       u2[:, c, a:a + w], c == 0, c == 7),
                                              reads=[WUP[wb], U2[pi]], writes=[PB[bk]], signal=(c == 7))
                                    fw.op("act", ACTF(gs[:, a:a + w], bank(bk)[:, :w], AF.Copy), reads=[PB[bk]], writes=[GS])

                        def d_conv(m):
                            i = m % 2
                            for half_i, chunk in enumerate((m, NPAIR + m)):
                                gs, t0 = rows[i][2 * half_i], rows[i][2 * half_i + 1]
                                GS, T0 = ROW[i][2 * half_i], ROW[i][2 * half_i + 1]
                                cw = [vcol(V_CFW + 3 * chunk + kk) for kk in range(3)]
                                fw.op("pool", TS(t0[:, :], gs[:, :], cw[2], ALU.mult, vcol(V_BF + chunk), ALU.add),
                                      reads=[GS, VEC], writes=[T0])
                                fw.op("dve", STT(t0[:, 1:NTH], gs[:, 0:NTH - 1], cw[1], t0[:, 1:NTH], ALU.mult, ALU.add),
                                      reads=[GS, T0, VEC], writes=[T0])
                                fw.op("dve", STT(t0[:, 2:NTH], gs[:, 0:NTH - 2], cw[0], t0[:, 2:NTH], ALU.mult, ALU.add),
                                      reads=[GS, T0, VEC], writes=[T0])

                        def d_fin(m):
                            i = m % 2
                            fw.op("act", ACTF(rows[i][1][:, :], rows[i][1][:, :], AF.Gelu_apprx_tanh),
                                  reads=[ROW[i][1]], writes=[ROW[i][1]])
                            fw.op("dve", TT(ff[:, m, :], rows[i][1][:, :], rows[i][3][:, :], ALU.mult),
                                  reads=[ROW[i][1], ROW[i][3]], writes=[FF[m]])

                        for m in range(NPAIR):
                            if m + NWB - 1 < NPAIR:
                                load_wup(m + NWB - 1)
                            if m == 2:
                                fw.dma("pool", wdn0[:], wdn_v[:, :, 0:512], writes=[WDN[0]], key=WDN[0])
                            d_mm(m)
                            if m >= 1:
                                d_fin(m - 1)
                            d_conv(m)
                        d_fin(NPAIR - 1)
                        fw.barrier()
                    with contextlib.ExitStack() as se:
                        def sbe(name, shape, dt):
                            return se.enter_context(nc.sbuf_tensor(f"s_{name}_{hf}", list(shape), dt))

                        wdn1 = sbe("wdn1", [128, NPAIR, 512], BF16)
                        wdn = [wdn0, wdn1]
                        fw.dma("pool", wdn1[:], wdn_v[:, :, 512:1024], writes=[WDN[1]], key=WDN[1])
                        fds = [sbe(f"fds{i}", [128, 8, PW], F32) for i in range(2)]
                        FDS = [Buf("fds0"), Buf("fds1")]
                        tmpe = [sbe(f"tmpe{i}", [128, PW], F32) for i in range(2)]
                        TMPE = [Buf("tmpe0"), Buf("tmpe1")]

                        def e_mm(pi):
                            a, w = HPC[pi]
                            for m in range(8):
                                bk = rot.next()
                                for c in range(NPAIR):
                                    fw.op("pe", MM(bank(bk)[:, :w], wdn[m // 4][:, c, (m % 4) * 128:(m % 4 + 1) * 128],
                                                   ff[:, c, a:a + w], c == 0, c == NPAIR - 1),
                                          reads=[WDN[m // 4], FF[c]], writes=[PB[bk]], signal=(c == NPAIR - 1))
                                fw.op("act", ACTF(fds[pi % 2][:, m, :w], bank(bk)[:, :w], AF.Copy),
                                      reads=[PB[bk]], writes=[FDS[pi % 2]])

                        def e_post(pi):
                            a, w = HPC[pi]
                            fd_ = fds[pi % 2]
                            sq_stat_rs(lambda c: fd_[:, c, :w], [FDS[pi % 2]], 8, w, sqt, SQT, rst, RST, lnb, LNB, 1.0 / D)
                            for m in range(8):
                                i = m % 2
                                fw.op("dve", STT(tmpe[i][:, :w], fd_[:, m, :w], vcol(V_GFPOST + m), rst[:, :w],
                                                 ALU.mult, ALU.mult), reads=[FDS[pi % 2], RST, VEC], writes=[TMPE[i]])
                                fw.op("pool", TT(xres[:, m, a:a + w], xres[:, m, a:a + w], tmpe[i][:, :w], ALU.add),
                                      reads=[XRES[pi], TMPE[i]], writes=[XRES[pi]])

                        e_mm(0)
                        for pi in range(3):
                            if pi + 1 < 3:
                                e_mm(pi + 1)
                            e_post(pi)
                        fw.barrier()

                with contextlib.ExitStack() as sf:
                    def sbf(name, shape, dt):
                        return sf.enter_context(nc.sbuf_tensor(f"s_{name}_{hf}", list(shape), dt))

                    wpg = [sbf(f"wpg{i}", [128, 8, 512], BF16) for i in range(2)]
                    wpp = sbf("wpp", [128, 2, 1024], BF16)
                    ptb = sbf("ptb", [128, 2, NTH], BF16)
                    WPG, WPP, PTB = [Buf("wpg0"), Buf("wpg1")], Buf("wpp"), Buf("ptb")
                    fw.dma("pool", wpp[:].rearrange("p a b -> p (a b)"), wpp_d, writes=[WPP], key=WPP)
                    fw.dma("pool", ptb[:], pT_loc[:, :, c0:c0 + NTH], writes=[PTB], key=PTB)
                    for i in range(2):
                        fw.dma("pool", wpg[i][:], wpg_d.rearrange("p (a b) -> p a b", b=1024)[:, :, i * 512:(i + 1) * 512],
                               writes=[WPG[i]], key=WPG[i])
                    h2b = sbf("h2b", [128, 8, NTH], BF16)
                    H2B = [Buf(f"h2b{i}") for i in range(3)]
                    egs = [sbf(f"egs{i}", [128, 8, PW], F32) for i in range(2)]
                    EGS = [Buf("egs0"), Buf("egs1")]
                    sgp = [sbf(f"sgp{i}", [128, PW], F32) for i in range(2)]
                    SGP = [Buf("sgp0"), Buf("sgp1")]
                    tmpf = [sbf(f"tmpf{i}", [128, PW], F32) for i in range(2)]
                    TMPF = [Buf("tmpf0"), Buf("tmpf1")]
                    otile = [sbf(f"otile{i}", [128, 8, PW], F32) for i in range(2)]
                    OT = [Buf("ot0"), Buf("ot1")]
                    for pi, (a, w) in enumerate(HPC):
                        for c in range(8):
                            eng = "act" if c % 2 == 0 else "dve"
                            if eng == "act":
                                fw.op("act", ACTF(h2b[:, c, a:a + w], xres[:, c, a:a + w], AF.Copy),
                                      reads=[XRES[pi]], writes=[H2B[pi]])
                            else:
                                fw.op("dve", CP(h2b[:, c, a:a + w], xres[:, c, a:a + w]),
                                      reads=[XRES[pi]], writes=[H2B[pi]])

                    def f_mm(pi):
                        a, w = HPC[pi]
                        for m in range(8):
                            i = m % 2
                            b1, b2 = rot.next(), rot.next()
                            for c in range(8):
                                fw.op("pe", MM(bank(b1)[:, :w], wpg[m // 4][:, c, (m % 4) * 128:(m % 4 + 1) * 128],
                                               h2b[:, c, a:a + w], c == 0, c == 7),
                                      reads=[WPG[m // 4], H2B[pi]], writes=[PB[b1]], signal=(c == 7))
                            fw.op("act", ACTF(sgp[i][:, :w], bank(b1)[:, :w], AF.Sigmoid), reads=[PB[b1]], writes=[SGP[i]])
                            for c in range(2):
                                fw.op("pe", MM(bank(b2)[:, :w], wpp[:, c, m * 128:(m + 1) * 128], ptb[:, c, a:a + w],
                                               c == 0, c == 1), reads=[WPP, PTB], writes=[PB[b2]], signal=(c == 1))
                            fw.op("dve", TT(egs[pi % 2][:, m, :w], bank(b2)[:, :w], sgp[i][:, :w], ALU.mult),
                                  reads=[PB[b2], SGP[i]], writes=[EGS[pi % 2]])

                    def f_post(pi):
                        a, w = HPC[pi]
                        oi = pi % 2
                        eg_ = egs[pi % 2]
                        sq_stat_rs(lambda c: eg_[:, c, :w], [EGS[pi % 2]], 8, w, sqt, SQT, rst, RST, lnb, LNB, 1.0 / D)
                        for m in range(8):
                            i = m % 2
                            fw.op("dve", STT(tmpf[i][:, :w], eg_[:, m, :w], vcol(V_GPPOST + m), rst[:, :w],
                                             ALU.mult, ALU.mult), reads=[EGS[pi % 2], RST, VEC], writes=[TMPF[i]])
                            fw.op("pool", TT(otile[oi][:, m, :w], xres[:, m, a:a + w], tmpf[i][:, :w], ALU.add),
                                  reads=[XRES[pi], TMPF[i]], writes=[OT[oi]])
                        fw.dma("sp", out_d[:, :, c0 + a:c0 + a + w], otile[oi][:, :, :w], reads=[OT[oi]], key=OT[oi])

                    f_mm(0)
                    for pi in range(3):
                        if pi + 1 < 3:
                            f_mm(pi + 1)
                        f_post(pi)
                    fw.barrier()
    return nc, fw.used


def _chunks(w, kc):
    n = w.shape[1]
    return np.ascontiguousarray(w.reshape(kc, 128, n).transpose(1, 0, 2).reshape(128, kc * n))


def col_tokens(j):
    c = np.arange(NT)
    G = c // GW
    o = c % GW
    return 512 * G + 128 * j + (o - HALO)


def prep_inputs(inputs):
    f32 = np.float32
    x = np.asarray(inputs["x"], f32)
    p = np.asarray(inputs["p"], f32)[0]
    positions = np.asarray(inputs["positions"]).astype(np.int32)
    w_in = np.asarray(inputs["w_in"], f32)[0]

    def vec_cols(v, kc):
        return np.asarray(v, f32).reshape(kc, 128).T

    vecs = np.zeros((128, NV), f32)
    vecs[:, V_GMIX:V_GMIX + 8] = vec_cols(inputs["g_mix_pre"][0], 8)
    vecs[:, V_GQ:V_GQ + 3] = vec_cols(inputs["g_q_lat"][0], 3)
    vecs[:, V_GKV:V_GKV + 2] = vec_cols(inputs["g_kv_lat"][0], 2)
    vecs[:, V_GMPOST:V_GMPOST + 8] = vec_cols(inputs["g_mix_post"][0], 8)
    vecs[:, V_GFPRE:V_GFPRE + 8] = vec_cols(inputs["g_ffn_pre"][0], 8)
    vecs[:, V_GFPOST:V_GFPOST + 8] = vec_cols(inputs["g_ffn_post"][0], 8)
    vecs[:, V_GPPOST:V_GPPOST + 8] = vec_cols(inputs["g_ple_post"][0], 8)
    caw = np.asarray(inputs["conv_a_w"], f32)[0]
    for m in range(4):
        for k in range(3):
            vecs[:, V_CAW + 3 * m + k] = caw[k, m * 128:(m + 1) * 128]
    cfw = np.asarray(inputs["conv_ffn_w"], f32)[0]
    bfc = np.asarray(inputs["b_ffn_conv"], f32)[0]
    for m in range(44):
        for k in range(3):
            vecs[:, V_CFW + 3 * m + k] = cfw[k, m * 128:(m + 1) * 128]
        vecs[:, V_BF + m] = bfc[m * 128:(m + 1) * 128]
    inv = 1.0 / (10000.0 ** (np.arange(16, dtype=np.float64) * (2.0 / 32)))
    for i in range(32):
        vecs[64 + i, V_INV] = np.float32(inv[i % 16] / TWO_PI)
        vecs[64 + i, V_SGN] = np.float32(-TWO_PI if i < 16 else TWO_PI)
    vecs[:, V_EPS] = EPS

    kv_lat = w_in[:, 1920:2176]
    k_rope = w_in[:, 2176:2208]
    z64 = np.zeros((D, 64), f32)
    wkv = np.concatenate([kv_lat, z64, k_rope, z64, k_rope[:, 16:32], k_rope[:, 0:16]], axis=1)
    wq = w_in[:, 1536:1920]
    wab = w_in[:, 0:1536]
    wg = w_in[:, 2208:4256]
    wqu = np.asarray(inputs["w_q_up"], f32)[0]
    wqs = np.zeros_like(wqu)
    for h in range(NH):
        b0 = h * 96
        wqs[:, b0 + 64:b0 + 80] = wqu[:, b0 + 80:b0 + 96]
        wqs[:, b0 + 80:b0 + 96] = wqu[:, b0 + 64:b0 + 80]
    wup = np.asarray(inputs["w_ffn_up"], f32)[0]
    wup_p = np.empty((NPAIR, 128, 8 * 256), f32)
    for m in range(NPAIR):
        blk = np.concatenate([wup[:, m * 128:(m + 1) * 128], wup[:, DFF + m * 128:DFF + (m + 1) * 128]], axis=1)
        wup_p[m] = _chunks(blk, 8)
    shared = {
        "vecs": vecs,
        "wkv": _chunks(wkv, 8), "wq": _chunks(wq, 8),
        "wab": np.stack([_chunks(np.concatenate([wab[:, 512 + m * 128:512 + (m + 1) * 128],
                                                 wab[:, 1024 + m * 128:1024 + (m + 1) * 128],
                                                 wab[:, m * 128:(m + 1) * 128]], axis=1), 8) for m in range(4)]),
        "wg": np.stack([_chunks(np.concatenate([wg[:, m * 128:(m + 1) * 128],
                                                wg[:, 1024 + m * 128:1024 + (m + 1) * 128]], axis=1), 8) for m in range(8)]),
        "wqu": _chunks(wqu, 3), "wqs": _chunks(wqs, 3),
        "wkvu": _chunks(np.asarray(inputs["w_kv_up"], f32)[0], 2),
        "wao": _chunks(np.asarray(inputs["w_a_out"], f32)[0], 4),
        "wbo": _chunks(np.asarray(inputs["w_b_out"], f32)[0], 4),
        "wo": _chunks(np.asarray(inputs["w_o"], f32)[0], 8),
        "wup": wup_p,
        "wdn": _chunks(np.asarray(inputs["w_ffn_down"], f32)[0], NPAIR),
        "wpp": _chunks(np.asarray(inputs["w_ple_proj"], f32)[0], 2),
        "wpg": _chunks(np.asarray(inputs["w_ple_gate"], f32)[0], 8),
    }
    in_maps = []
    per_batch = {}
    for b in range(2):
        xT = x[b].T
        xa = xT.reshape(8, 128, 16, 512).transpose(2, 1, 0, 3).reshape(16, 128, 8 * 512)
        per_batch[b] = (np.ascontiguousarray(xa), np.ascontiguousarray(positions[b][None, :]))
    for core in range(NCORE):
        b, j = core // CPB, core % CPB
        tok = col_tokens(j)
        valid = tok >= 0
        tk = np.where(valid, tok, 0)
        xl = x[b][tk] * valid[:, None].astype(f32)
        xl = np.ascontiguousarray(xl.T.reshape(8, 128, NT).transpose(1, 0, 2))
        pl = p[b][tk] * valid[:, None].astype(f32)
        pl = np.ascontiguousarray(pl.T.reshape(2, 128, NT).transpose(1, 0, 2))
        posl = np.where(valid, positions[b][tk], 0).astype(np.int32)[None, :]
        mk = np.zeros((128, 4, GW), f32)
        kk = np.arange(128)[:, None]
        qq = np.arange(128)[None, :]
        for r in range(4):
            if r < j:
                mk[:, r, :] = 1.0
            elif r == j:
                mk[:, r, HALO:] = ((kk // 64) <= (qq // 64)).astype(f32)
                mk[:, r, :HALO] = 0.0
        m = dict(shared)
        m.update({"xT_all": per_batch[b][0], "pos_all": per_batch[b][1], "xT_loc": xl, "pT_loc": pl,
                  "pos_loc": posl, "mask": mk.reshape(128, 4 * GW)})
        in_maps.append(m)
    return in_maps


def assemble(results):
    out = np.empty((2, S, D), np.float32)
    for core in range(NCORE):
        b, j = core // CPB, core % CPB
        o = results[core]["out"]
        tok = col_tokens(j)
        own = (np.arange(NT) % GW) >= HALO
        oT = o.transpose(2, 1, 0).reshape(NT, D)
        out[b, tok[own]] = oT[own]
    return out


_NC_CACHE = {}


def kernel(**inputs):
    in_maps = prep_inputs(inputs)
    if "nc" not in _NC_CACHE:
        _NC_CACHE["nc"] = build_program()
    nc = _NC_CACHE["nc"]
    res = run_bass_kernel_spmd(nc, in_maps, core_ids=list(range(NCORE)))
    return assemble(res.results)
```

```python
import contextlib
import numpy as np
import concourse.bass as bass
import concourse.mybir as mybir
from concourse.bass_utils import run_bass_kernel_spmd

F32, BF16, I32 = mybir.dt.float32, mybir.dt.bfloat16, mybir.dt.int32
AF = mybir.ActivationFunctionType
ALU = mybir.AluOpType

D = 1024
S = 8192
NCORE = 8
CPB = 4
NG = 16
GW = 132
HALO = 4
NT = NG * GW
NH = 8
DFF = 2816
NPAIR = 22
SCALE = 96.0 ** -0.5
EPS = 1e-6
TWO_PI = 6.283185307179586

V_GMIX, V_GQ, V_GKV, V_GMPOST, V_GFPRE, V_GFPOST, V_GPPOST = 0, 8, 11, 13, 21, 29, 37
V_CAW, V_CFW, V_BF, V_INV, V_SGN, V_EPS = 45, 57, 189, 233, 234, 235
NV = 240
CFG = {"nh": 8, "mask": True, "pv_acc": True, "np": 4, "bevery": 5, "sbanks": [5, 6, 7], "la": 2, "mask_eng": "pool", "cool": 2, "junk": 128}


class Buf:
    __slots__ = ("name", "w", "readers", "dsem")

    def __init__(self, name):
        self.name = name
        self.w = None
        self.readers = {}
        self.dsem = None


class Ev:
    __slots__ = ("sem", "val", "eng", "ord")

    def __init__(self, sem, val, eng, ord=None):
        self.sem = sem
        self.val = val
        self.eng = eng
        self.ord = ord


class SemRec:
    __slots__ = ("h", "cnt", "key")

    def __init__(self, h, key):
        self.h = h
        self.cnt = 0
        self.key = key


class FW:
    EPOCH = 30000

    def __init__(self, nc, stack, used=None):
        self.nc = nc
        self.stack = stack
        self.used_in = used
        self.used = set()
        self.ord = {"pe": 0, "act": 0, "dve": 0, "pool": 0}
        self.engs = {"pe": nc.tensor, "act": nc.scalar, "dve": nc.vector,
                     "pool": nc.gpsimd, "sp": nc.sync}
        self.nsem = 0
        self.esem = {}
        for e in ("pe", "act", "dve", "pool"):
            self.esem[e] = self._newsem(e)
        self.seen = {e: {} for e in self.engs}
        self.pending = {e: False for e in self.engs}
        self.dsems = []
        self.nwaits = 0
        self.nops = {e: 0 for e in self.engs}

    def _newsem(self, name):
        self.nsem += 1
        h = self.stack.enter_context(self.nc.semaphore(f"s{self.nsem}_{name}"))
        return SemRec(h, self.nsem)

    def _deps(self, eng, reads, writes):
        out = []
        for b in reads:
            if b.w is not None:
                out.append(b.w)
        for b in writes:
            if b.w is not None:
                out.append(b.w)
            for ev in b.readers.values():
                if ev.eng == eng:
                    continue
                out.append(ev)
        return out

    def _wait(self, eng, evs):
        best = {}
        for ev in evs:
            if ev.eng == "pe" and eng == "pe":
                continue
            k = ev.sem.key
            if self.seen[eng].get(k, 0) >= ev.val:
                continue
            if k not in best or best[k].val < ev.val:
                best[k] = ev
        e = self.engs[eng]
        for k, ev in best.items():
            e.wait_ge(ev.sem.h, ev.val)
            self.seen[eng][k] = ev.val
            self.nwaits += 1
            if ev.ord is not None:
                self.used.add((ev.eng, ev.ord))

    def _record(self, ev, reads, writes):
        for b in reads:
            key = ev.eng if ev.eng != "dma" else ("dma", ev.sem.key)
            b.readers[key] = ev
        for b in writes:
            b.w = ev
            b.readers = {}

    def op(self, eng, fn, reads=(), writes=(), signal=True):
        self._wait(eng, self._deps(eng, reads, writes))
        ins = fn(self.engs[eng])
        self.nops[eng] += 1
        s = self.esem[eng]
        my_ord = None
        if signal:
            self.ord[eng] += 1
            my_ord = self.ord[eng]
            if self.used_in is not None and (eng, my_ord) not in self.used_in:
                signal = False
        if signal:
            if s.cnt >= self.EPOCH:
                s = self.esem[eng] = self._newsem(eng)
            s.cnt += 1
            ins.then_inc(s.h, 1)
            ev = Ev(s, s.cnt, eng, my_ord)
            self.pending[eng] = False
        else:
            ev = Ev(s, s.cnt + 1, eng, my_ord if my_ord is not None else self.ord[eng] + 1)
            self.pending[eng] = True
        self._record(ev, reads, writes)
        return ev

    def dma(self, queue, out, in_, reads=(), writes=(), key=None):
        self._wait(queue, self._deps(queue, reads, writes))
        ins = self.engs[queue].dma_start(out=out, in_=in_)
        if key.dsem is None:
            key.dsem = self._newsem("d")
            self.dsems.append(key.dsem)
        s = key.dsem
        s.cnt += 16
        ins.then_inc(s.h, 16)
        ev = Ev(s, s.cnt, "dma")
        self._record(ev, reads, writes)
        return ev

    def barrier(self):
        if self.used_in is not None:
            for e_ in ("pe", "act", "dve", "pool"):
                assert not self.pending[e_], f"{e_} has unsignaled tail at barrier"
        evs = []
        for e in ("pe", "act", "dve", "pool"):
            s = self.esem[e]
            if s.cnt > 0:
                evs.append(Ev(s, s.cnt, e, self.ord[e]))
        for s in self.dsems:
            if s.cnt > 0:
                evs.append(Ev(s, s.cnt, "dma"))
        for e in self.engs:
            self._wait(e, [ev for ev in evs if ev.eng != e])


def MM(out, lhsT, rhs, start, stop):
    return lambda e: e.matmul(out, lhsT=lhsT, rhs=rhs, start=start, stop=stop)


def ACTF(out, in_, func, scale=1.0, bias=None):
    if bias is None:
        return lambda e: e.activation(out=out, in_=in_, func=func, scale=scale)
    return lambda e: e.activation(out=out, in_=in_, func=func, scale=scale, bias=bias)


def TT(out, a, b, op):
    return lambda e: e.tensor_tensor(out=out, in0=a, in1=b, op=op)


def TS(out, a, s1, op0, s2=None, op1=None):
    if op1 is None:
        return lambda e: e.tensor_scalar(out=out, in0=a, scalar1=s1, scalar2=None, op0=op0)
    return lambda e: e.tensor_scalar(out=out, in0=a, scalar1=s1, scalar2=s2, op0=op0, op1=op1)


def STT(out, in0, scalar, in1, op0, op1):
    return lambda e: e.scalar_tensor_tensor(out=out, in0=in0, scalar=scalar, in1=in1, op0=op0, op1=op1)


def CP(out, in_):
    return lambda e: e.tensor_copy(out=out, in_=in_)


def MSET(ap, val):
    return lambda e: e.memset(ap, val)


def RCP(out, in_):
    return lambda e: e.reciprocal(out=out, in_=in_)


def pieces_of(total, width):
    out = []
    a = 0
    while a < total:
        out.append((a, min(width, total - a)))
        a += width
    return out


def build_program(stage=99):
    used = _build(stage, None)[1]
    return _build(stage, used)[0]


def _build(stage, used):
    nc = bass.Bass("TRN2", target_bir_lowering=False)

    def din(name, shape, dt=F32):
        return nc.dram_tensor(name, list(shape), dt, kind="ExternalInput").ap()

    xT_all = din("xT_all", [16, 128, 8 * 512])
    xT_loc = din("xT_loc", [128, 8, NT])
    pT_loc = din("pT_loc", [128, 2, NT])
    pos_all = din("pos_all", [1, S], I32)
    pos_loc = din("pos_loc", [1, NT], I32)
    vecs_d = din("vecs", [128, NV])
    mask_d = din("mask", [128, 4 * GW])
    wkv_d = din("wkv", [128, 8 * 448])
    wq_d = din("wq", [128, 8 * 384])
    wab_d = din("wab", [4, 128, 8 * 384])
    wg_d = din("wg", [8, 128, 8 * 256])
    wqu_d = din("wqu", [128, 3 * 768])
    wqs_d = din("wqs", [128, 3 * 768])
    wkvu_d = din("wkvu", [128, 2 * 1024])
    wao_d = din("wao", [128, 4 * 1024])
    wbo_d = din("wbo", [128, 4 * 1024])
    wo_d = din("wo", [128, 8 * 1024])
    wup_d = din("wup", [NPAIR, 128, 8 * 256])
    wdn_d = din("wdn", [128, NPAIR * 1024])
    wpp_d = din("wpp", [128, 2 * 1024])
    wpg_d = din("wpg", [128, 8 * 1024])
    out_d = nc.dram_tensor("out", [128, 8, NT], F32, kind="ExternalOutput").ap()
    dbg_d = None
    if stage < 99:
        dbg_d = nc.dram_tensor("dbg", [128, 8 * NT], F32, kind="ExternalOutput").ap()

    with contextlib.ExitStack() as st:
        fw = FW(nc, st, used)

        def sb(name, shape, dt):
            return st.enter_context(nc.sbuf_tensor("s_" + name, list(shape), dt))

        ps = st.enter_context(nc.psum_tensor("ps", [128, 4096], F32))
        PB = [Buf(f"pb{i}") for i in range(8)]

        def bank(i):
            return ps[:, 512 * i:512 * (i + 1)]

        class Rot:
            def __init__(self, ids):
                self.ids = list(ids)
                self.i = 0

            def next(self):
                b = self.ids[self.i % len(self.ids)]
                self.i += 1
                return b

        vec = sb("vec", [128, NV], F32)
        VEC = Buf("vec")
        fw.dma("sp", vec[:], vecs_d, writes=[VEC], key=VEC)
        ones = sb("ones", [128, 128], BF16)
        ONES = Buf("ones")
        fw.op("pool", MSET(ones[:], 1.0), writes=[ONES])
        mask = sb("mask", [128, 4 * GW], BF16)
        MASK = Buf("mask")
        fw.dma("pool", mask[:], mask_d, writes=[MASK], key=MASK)

        def vcol(i, lo=0, hi=128):
            return vec[lo:hi, i:i + 1]

        def rstd_from_bank(bk_ap, BK, out_ap, OUT, tmp_ap, TMP, npart, inv_d):
            fw.op("act", ACTF(tmp_ap, bk_ap, AF.Ln, scale=inv_d, bias=vcol(V_EPS, 0, npart)),
                  reads=[BK, VEC], writes=[TMP])
            fw.op("act", ACTF(out_ap, tmp_ap, AF.Exp, scale=-0.5), reads=[TMP], writes=[OUT])

        ob = sb("ob", [128, 4, NT], BF16)
        OB = [Buf(f"ob{i}") for i in range(4)]
        stA = contextlib.ExitStack()

        def sbA(name, shape, dt):
            return stA.enter_context(nc.sbuf_tensor("s_" + name, list(shape), dt))

        kvn = sbA("kvn", [128, 2, S], BF16)
        KVN = [Buf(f"kvn{t}") for t in range(16)]
        kbuf = [sbA(f"kbuf{i}", [96, S], BF16) for i in range(2)]
        KB_PE = [Buf("kpe0"), Buf("kpe1")]
        KB_NO = [[Buf(f"kno{i}_{t}") for t in range(16)] for i in range(2)]
        wqu = sbA("wqu", [128, 3, 768], BF16)
        wqs = sbA("wqs", [128, 3, 768], BF16)
        wkvu = sbA("wkvu", [128, 2, 1024], BF16)
        WQU, WQS, WKVU = Buf("wqu"), Buf("wqs"), Buf("wkvu")
        fw.dma("pool", wqu[:].rearrange("p a b -> p (a b)"), wqu_d, writes=[WQU], key=WQU)
        fw.dma("pool", wqs[:].rearrange("p a b -> p (a b)"), wqs_d, writes=[WQS], key=WQS)
        fw.dma("pool", wkvu[:].rearrange("p a b -> p (a b)"), wkvu_d, writes=[WKVU], key=WKVU)

        def rope_tables(posi_ap, POSI, n, cos_ap, COS, sin_ap, SIN, ta, TA, tb, TB, ti, TI):
            R = slice(64, 96)
            fw.op("dve", CP(ta[R, :n], posi_ap), reads=[POSI], writes=[TA])
            fw.op("dve", TS(ta[R, :n], ta[R, :n], vcol(V_INV, 64, 96), ALU.mult), reads=[TA, VEC], writes=[TA])
            fw.op("dve", CP(ti[R, :n], ta[R, :n]), reads=[TA], writes=[TI])
            fw.op("dve", CP(tb[R, :n], ti[R, :n]), reads=[TI], writes=[TB])
            fw.op("dve", TT(ta[R, :n], ta[R, :n], tb[R, :n], ALU.subtract), reads=[TA, TB], writes=[TA])
            fw.op("dve", TS(tb[R, :n], ta[R, :n], 0.5, ALU.is_gt), reads=[TA], writes=[TB])
            fw.op("dve", TT(ta[R, :n], ta[R, :n], tb[R, :n], ALU.subtract), reads=[TA, TB], writes=[TA])
            fw.op("dve", TS(tb[R, :n], ta[R, :n], -0.5, ALU.is_lt), reads=[TA], writes=[TB])
            fw.op("dve", TT(ta[R, :n], ta[R, :n], tb[R, :n], ALU.add), reads=[TA, TB], writes=[TA])
            yield
            fw.op("act", ACTF(sin_ap, ta[R, :n], AF.Sin, scale=vcol(V_SGN, 64, 96)), reads=[TA, VEC], writes=[SIN])
            fw.op("dve", TS(tb[R, :n], ta[R, :n], 0.25, ALU.add), reads=[TA], writes=[TB])
            fw.op("dve", TS(ti[R, :n].bitcast(F32), tb[R, :n], 0.5, ALU.is_gt), reads=[TB], writes=[TI])
            fw.op("dve", TT(tb[R, :n], tb[R, :n], ti[R, :n].bitcast(F32), ALU.subtract), reads=[TI, TB], writes=[TB])
            yield
            fw.op("act", ACTF(cos_ap, tb[R, :n], AF.Sin, scale=TWO_PI), reads=[TB], writes=[COS])
            yield

        with contextlib.ExitStack() as st1:
            def sb1(name, shape, dt):
                return st1.enter_context(nc.sbuf_tensor("s_" + name, list(shape), dt))

            TBK = 1024
            TPT = TBK // 512
            wkv = sb1("wkv_bf", [128, 8, 448], BF16)
            WKVST, WKV = Buf("wkvst"), Buf("wkv")
            with nc.sbuf_tensor("s_wkv_st", [128, 8, 448], F32) as wkv_st:
                fw.dma("sp", wkv_st[:].rearrange("p a b -> p (a b)"), wkv_d, writes=[WKVST], key=WKVST)
                for c in range(8):
                    fw.op("dve", TS(wkv[:, c, :], wkv_st[:, c, :], vcol(V_GMIX + c), ALU.mult),
                          reads=[WKVST, VEC], writes=[WKV])
                fw.barrier()
            xb = [sb1(f"xb{i}", [128, 8 * 512], BF16) for i in range(2)]
            XB = [Buf("xb0"), Buf("xb1")]
            sq = [sb1(f"sq{i}", [128, 8 * 512], BF16) for i in range(2)]
            SQ = [Buf("sq0"), Buf("sq1")]
            rs = [sb1(f"rs{i}", [128, 512], F32) for i in range(2)]
            RS = [Buf("rs0"), Buf("rs1")]
            lnt = [sb1(f"lnt{i}", [128, 512], F32) for i in range(2)]
            LNT = [Buf("lnt0"), Buf("lnt1")]
            kvl = [sb1(f"kvl{i}", [128, 2, 512], F32) for i in range(2)]
            KVL = [Buf("kvl0"), Buf("kvl1")]
            sq2 = [sb1(f"sq2{i}", [128, 1024], BF16) for i in range(2)]
            SQ2 = [Buf("sq20"), Buf("sq21")]
            rs2 = [sb1(f"rs2{i}", [128, 512], F32) for i in range(2)]
            RS2 = [Buf("rs20"), Buf("rs21")]
            tpa = [sb1(f"tpa{i}", [96, 512], F32) for i in range(2)]
            tpb = [sb1(f"tpb{i}", [96, 512], F32) for i in range(2)]
            TPA = [Buf("tpa0"), Buf("tpa1")]
            TPB = [Buf("tpb0"), Buf("tpb1")]
            posk = [sb1(f"posk{i}", [96, TBK], I32) for i in range(2)]
            POSK = [Buf("posk0"), Buf("posk1")]
            cosk = [sb1(f"cosk{i}", [96, TBK], F32) for i in range(2)]
            sink = [sb1(f"sink{i}", [96, TBK], F32) for i in range(2)]
            COSK = [Buf("cosk0"), Buf("cosk1")]
            SINK = [Buf("sink0"), Buf("sink1")]
            tta = sb1("tta", [96, TBK], F32)
            ttb = sb1("ttb", [96, TBK], F32)
            tti = sb1("tti", [96, TBK], I32)
            TTA, TTB, TTI = Buf("tta"), Buf("ttb"), Buf("tti")

            rot = Rot(range(8))

            def tables_batch(tb_i):
                i = tb_i % 2
                fw.dma("sp", posk[i][64:96, :],
                       pos_all[0:1, tb_i * TBK:(tb_i + 1) * TBK].partition_broadcast(32),
                       writes=[POSK[i]], key=POSK[i])
                return rope_tables(posk[i][64:96, :], POSK[i], TBK, cosk[i][64:96, :], COSK[i],
                                   sink[i][64:96, :], SINK[i], tta, TTA, ttb, TTB, tti, TTI)

            g0 = tables_batch(0)
            for _ in g0:
                pass
            state = {"gen": None}
            tbanks = {}

            def part1(t):
                i = t % 2
                bA = rot.next()
                for c in range(8):
                    fw.op("pe", MM(bank(bA), ones[:, :], sq[i][:, c * 512:(c + 1) * 512], c == 0, c == 7),
                          reads=[ONES, SQ[i]], writes=[PB[bA]], signal=(c == 7))
                rstd_from_bank(bank(bA), PB[bA], rs[i][:], RS[i], lnt[i][:], LNT[i], 128, 1.0 / D)
                for m in range(2):
                    bk = rot.next()
                    for c in range(8):
                        fw.op("pe", MM(bank(bk), wkv[:, c, m * 128:(m + 1) * 128], xb[i][:, c * 512:(c + 1) * 512],
                                       c == 0, c == 7), reads=[WKV, XB[i]], writes=[PB[bk]], signal=(c == 7))
                    fw.op("dve", TT(kvl[i][:, m, :], bank(bk), rs[i][:], ALU.mult),
                          reads=[PB[bk], RS[i]], writes=[KVL[i]])
                fw.op("act", ACTF(sq2[i][:], kvl[i][:].rearrange("p a b -> p (a b)"), AF.Square),
                      reads=[KVL[i]], writes=[SQ2[i]])
                bD = rot.next()
                for c in range(8):
                    fw.op("pe", MM(bank(bD)[0:96, :], wkv[:, c, 256:352], xb[i][:, c * 512:(c + 1) * 512],
                                   c == 0, c == 7), reads=[WKV, XB[i]], writes=[PB[bD]], signal=(c == 7))
                bE = rot.next()
                for c in range(8):
                    fw.op("pe", MM(bank(bE)[0:96, :], wkv[:, c, 352:448], xb[i][:, c * 512:(c + 1) * 512],
                                   c == 0, c == 7), reads=[WKV, XB[i]], writes=[PB[bE]], signal=(c == 7))
                tbanks[t] = (bD, bE)
                if t + 2 < 16:
                    fw.dma("pool", xb[i][:], xT_all[t + 2], reads=[SQ[i]], writes=[XB[i]], key=XB[i])
                if t + 1 < 16:
                    j = (t + 1) % 2
                    fw.op("act", ACTF(sq[j][:], xb[j][:], AF.Square), reads=[XB[j]], writes=[SQ[j]])

            def part2(t):
                i = t % 2
                tbi = (t // TPT) % 2
                tcol = (t % TPT) * 512
                bD, bE = tbanks[t]
                R = slice(64, 96)
                fw.op("dve", TT(tpa[i][R, :], bank(bD)[R, :], cosk[tbi][R, tcol:tcol + 512], ALU.mult),
                      reads=[PB[bD], COSK[tbi]], writes=[TPA[i]])
                fw.op("dve", TT(tpb[i][R, :], bank(bE)[R, :], sink[tbi][R, tcol:tcol + 512], ALU.mult),
                      reads=[PB[bE], SINK[tbi]], writes=[TPB[i]])
                bC = rot.next()
                for m in range(2):
                    fw.op("pe", MM(bank(bC), ones[:, :], sq2[i][:, m * 512:(m + 1) * 512], m == 0, m == 1),
                          reads=[ONES, SQ2[i]], writes=[PB[bC]], signal=(m == 1))
                rstd_from_bank(bank(bC), PB[bC], rs2[i][:], RS2[i], lnt[i][:], LNT[i], 128, 1.0 / 256)
                for m in range(2):
                    fw.op("dve", STT(kvn[:, m, t * 512:(t + 1) * 512], kvl[i][:, m, :], vcol(V_GKV + m),
                                     rs2[i][:], ALU.mult, ALU.mult),
                          reads=[KVL[i], RS2[i], VEC], writes=[KVN[t]])
                fw.op("dve", TT(tpa[i][R, :], tpa[i][R, :], tpb[i][R, :], ALU.add),
                      reads=[TPA[i], TPB[i]], writes=[TPA[i]])
                fw.op("dve", TT(kbuf[0][R, t * 512:(t + 1) * 512], tpa[i][R, :], rs[i][R, :], ALU.mult),
                      reads=[TPA[i], RS[i]], writes=[KB_PE[0]])

            fw.dma("pool", xb[0][:], xT_all[0], writes=[XB[0]], key=XB[0])
            fw.dma("pool", xb[1][:], xT_all[1], writes=[XB[1]], key=XB[1])
            g1 = tables_batch(1)
            for _ in g1:
                pass
            fw.op("act", ACTF(sq[0][:], xb[0][:], AF.Square), reads=[XB[0]], writes=[SQ[0]])
            part1(0)
            gens = []
            for t in range(16):
                if t + 1 < 16:
                    part1(t + 1)
                for g in list(gens):
                    try:
                        next(g)
                    except StopIteration:
                        gens.remove(g)
                part2(t)
                if (t + 1) % TPT == 0 and (t + 1) // TPT + 1 < 16 // TPT:
                    g = tables_batch((t + 1) // TPT + 1)
                    next(g)
                    gens.append(g)
            for g in gens:
                for _ in g:
                    pass
            fw.dma("sp", kbuf[1][64:96, :], kbuf[0][64:96, :], reads=[KB_PE[0]], writes=[KB_PE[1]], key=KB_PE[1])
            fw.barrier()

        if stage == 1:
            with contextlib.ExitStack() as std:
                dt_ = std.enter_context(nc.sbuf_tensor("dbgt", [128, 8 * NT], F32))
                DT = Buf("dbgt")
                fw.op("dve", MSET(dt_[:], 0.0), writes=[DT])
                fw.op("dve", CP(dt_[:, 0:8192], kvn[:, 0, :]), reads=KVN, writes=[DT])
                fw.op("dve", CP(dt_[64:96, 8192:16384], kbuf[1][64:96, :]), reads=[KB_PE[1]], writes=[DT])
                fw.dma("sp", dbg_d, dt_[:], reads=[DT], key=DT)
                fw.barrier()
            stA.close()
            return nc, fw.used

        qn = sbA("qn", [128, 3, NT], BF16)
        QN = [Buf(f"qn{i}") for i in range(6)]
        cos_l = sbA("cos_l", [96, NT], F32)
        sin_l = sbA("sin_l", [96, NT], F32)
        COSL, SINL = Buf("cosl"), Buf("sinl")
        PW = 352
        PCS = pieces_of(NT, PW)

        with contextlib.ExitStack() as st2t:
            def sb2(name, shape, dt):
                return st2t.enter_context(nc.sbuf_tensor("s_" + name, list(shape), dt))
            posl = sb2("posl", [96, NT], I32)
            POSL = Buf("posl")
            lta = sb2("lta", [96, NT], F32)
            ltb = sb2("ltb", [96, NT], F32)
            lti = sb2("lti", [96, NT], I32)
            LTA, LTB, LTI = Buf("lta"), Buf("ltb"), Buf("lti")
            fw.dma("sp", posl[64:96, :], pos_loc[0:1, :].partition_broadcast(32), writes=[POSL], key=POSL)
            for _ in rope_tables(posl[64:96, :], POSL, NT, cos_l[64:96, :], COSL, sin_l[64:96, :], SINL,
                                 lta, LTA, ltb, LTB, lti, LTI):
                pass
            fw.barrier()
        with contextlib.ExitStack() as st2:
            def sb2(name, shape, dt):
                return st2.enter_context(nc.sbuf_tensor("s_" + name, list(shape), dt))

            wq = sb2("wq_bf", [128, 8, 384], BF16)
            WQ = Buf("wq")
            fw.dma("pool", wq[:].rearrange("p a b -> p (a b)"), wq_d, writes=[WQ], key=WQ)
            xl = [sb2(f"xl{i}", [128, 8, PW], F32) for i in range(2)]
            XL = [Buf("xl0"), Buf("xl1")]
            sqx = [sb2(f"sqx{i}", [128, 8, PW], BF16) for i in range(2)]
            SQX = [Buf("sqx0"), Buf("sqx1")]
            up = [sb2(f"up{i}", [128, 8, PW], BF16) for i in range(2)]
            UP = [Buf("up0"), Buf("up1")]
            rsl = [sb2(f"rsl{i}", [128, PW], F32) for i in range(2)]
            RSL = [Buf("rsl0"), Buf("rsl1")]
            lnl = [sb2(f"lnl{i}", [128, PW], F32) for i in range(2)]
            LNL = [Buf("lnl0"), Buf("lnl1")]
            sq3 = [sb2(f"sq3{i}", [128, 3, PW], BF16) for i in range(2)]
            SQ3 = [Buf("sq30"), Buf("sq31")]
            rsq = [sb2(f"rsq{i}", [128, PW], F32) for i in range(2)]
            RSQ = [Buf("rsq0"), Buf("rsq1")]
            rot = Rot(range(8))
            for pi, (a, w) in enumerate(PCS):
                i = pi % 2
                fw.dma("sp", xl[i][:, :, :w], xT_loc[:, :, a:a + w], writes=[XL[i]], key=XL[i])
                fw.op("act", ACTF(sqx[i][:, :, :w], xl[i][:, :, :w], AF.Square), reads=[XL[i]], writes=[SQX[i]])
                bA = rot.next()
                for c in range(8):
                    fw.op("pe", MM(bank(bA)[:, :w], ones[:, :], sqx[i][:, c, :w], c == 0, c == 7),
                          reads=[ONES, SQX[i]], writes=[PB[bA]], signal=(c == 7))
                rstd_from_bank(bank(bA)[:, :w], PB[bA], rsl[i][:, :w], RSL[i], lnl[i][:, :w], LNL[i], 128, 1.0 / D)
                for c in range(8):
                    fw.op("dve", STT(up[i][:, c, :w], xl[i][:, c, :w], vcol(V_GMIX + c), rsl[i][:, :w],
                                     ALU.mult, ALU.mult), reads=[XL[i], RSL[i], VEC], writes=[UP[i]])
                bq = []
                for m in range(3):
                    bk = rot.next()
                    bq.append(bk)
                    for c in range(8):
                        fw.op("pe", MM(bank(bk)[:, :w], wq[:, c, m * 128:(m + 1) * 128], up[i][:, c, :w],
                                       c == 0, c == 7), reads=[WQ, UP[i]], writes=[PB[bk]], signal=(c == 7))
                    fw.op("act", ACTF(sq3[i][:, m, :w], bank(bk)[:, :w], AF.Square), reads=[PB[bk]], writes=[SQ3[i]])
                bS = rot.next()
                for m in range(3):
                    fw.op("pe", MM(bank(bS)[:, :w], ones[:, :], sq3[i][:, m, :w], m == 0, m == 2),
                          reads=[ONES, SQ3[i]], writes=[PB[bS]], signal=(m == 2))
                rstd_from_bank(bank(bS)[:, :w], PB[bS], rsq[i][:, :w], RSQ[i], lnl[i][:, :w], LNL[i], 128, 1.0 / 384)
                for m in range(3):
                    fw.op("dve", STT(qn[:, m, a:a + w], bank(bq[m])[:, :w], vcol(V_GQ + m), rsq[i][:, :w],
                                     ALU.mult, ALU.mult), reads=[PB[bq[m]], RSQ[i], VEC], writes=[QN[pi]])
            fw.barrier()

        with contextlib.ExitStack() as st3:
            def sb3(name, shape, dt):
                return st3.enter_context(nc.sbuf_tensor("s_" + name, list(shape), dt))

            vbuf = [sb3(f"vbuf{i}", [128, 64, 128], BF16) for i in range(2)]
            VB = [[Buf(f"vb{i}_{u}") for u in range(8)] for i in range(2)]
            VONE = [Buf("vone0"), Buf("vone1")]
            fw.op("pool", MSET(vbuf[0][:, :, 64:128], 1.0), writes=[VONE[0]])
            fw.op("pool", MSET(vbuf[1][:, :, 0:64], 1.0), writes=[VONE[1]])
            qbuf = [sb3(f"qbuf{i}", [96, NT], BF16) for i in range(2)]
            QB = [Buf("qb0"), Buf("qb1")]
            NP = CFG["np"]
            pbuf = [sb3(f"pbuf{i}", [128, 512], BF16) for i in range(NP)]
            PBUF = [Buf(f"pbuf{i}") for i in range(NP)]
            dtmp = sb3("dtmp", [128, NT], F32)
            DTMP = Buf("dtmp")
            qta = [sb3(f"qta{i}", [96, 512], F32) for i in range(2)]
            qtb = sb3("qtb", [96, 512], F32)
            QTA, QTB = [Buf("qta0"), Buf("qta1")], Buf("qtb")
            APC = pieces_of(NT, 512)
            zt = sb3("zt", [128, 512], BF16)
            ZT = Buf("zt")
            fw.op("pool", MSET(zt[:], 0.0), writes=[ZT])

            def junk(n):
                if n > 0:
                    fw.op("pe", MM(ps[:, 2112:2112 + n], ones[:, :], zt[:, :n], False, False),
                          reads=[ONES, ZT], writes=[PB[4]], signal=False)

            def build_units(h, bank_rot):
                hb = h % 2
                us = []

                def k_unit(t):
                    def f():
                        bk = bank_rot.next()
                        for c in range(2):
                            fw.op("pe", MM(bank(bk)[0:64, :], wkvu[:, c, h * 128:h * 128 + 64],
                                           kvn[:, c, t * 512:(t + 1) * 512], c == 0, c == 1),
                                  reads=[WKVU, KVN[t]], writes=[PB[bk]], signal=(c == 1))
                        fw.op("dve", CP(kbuf[hb][0:64, t * 512:(t + 1) * 512], bank(bk)[0:64, :]),
                              reads=[PB[bk]], writes=[KB_NO[hb][t]])
                    return f

                def v_unit(u):
                    def f():
                        bk = bank_rot.next()
                        for tt in range(8):
                            tile = 8 * u + tt
                            for c in range(2):
                                fw.op("pe", MM(bank(bk)[:, tt * 64:(tt + 1) * 64],
                                               kvn[:, c, tile * 128:(tile + 1) * 128],
                                               wkvu[:, c, h * 128 + 64:h * 128 + 128], c == 0, c == 1),
                                      reads=[WKVU, KVN[tile // 4]], writes=[PB[bk]],
                                      signal=(c == 1 and tt == 7))
                        voff = 0 if hb == 0 else 64
                        fw.op("dve", CP(vbuf[hb][:, 8 * u:8 * u + 8, voff:voff + 64],
                                        bank(bk).rearrange("p (a b) -> p a b", b=64)),
                              reads=[PB[bk]], writes=[VB[hb][u]])
                    return f

                def qa_unit(pi, a, w):
                    def f():
                        bk = bank_rot.next()
                        for c in range(3):
                            fw.op("pe", MM(bank(bk)[0:96, :w], wqu[:, c, h * 96:(h + 1) * 96], qn[:, c, a:a + w],
                                           c == 0, c == 2), reads=[WQU] + QN, writes=[PB[bk]], signal=(c == 2))
                        fw.op("dve", CP(qbuf[hb][0:64, a:a + w], bank(bk)[0:64, :w]), reads=[PB[bk]], writes=[QB[hb]])
                        fw.op("dve", TT(qta[pi % 2][64:96, :w], bank(bk)[64:96, :w], cos_l[64:96, a:a + w], ALU.mult),
                              reads=[PB[bk], COSL], writes=[QTA[pi % 2]])
                    return f

                def qb_unit(pi, a, w):
                    def f():
                        bk2 = bank_rot.next()
                        for c in range(3):
                            fw.op("pe", MM(bank(bk2)[0:96, :w], wqs[:, c, h * 96:(h + 1) * 96], qn[:, c, a:a + w],
                                           c == 0, c == 2), reads=[WQS] + QN, writes=[PB[bk2]], signal=(c == 2))
                        fw.op("dve", TT(qtb[64:96, :w], bank(bk2)[64:96, :w], sin_l[64:96, a:a + w], ALU.mult),
                              reads=[PB[bk2], SINL], writes=[QTB])
                        fw.op("dve", TT(qbuf[hb][64:96, a:a + w], qta[pi % 2][64:96, :w], qtb[64:96, :w], ALU.add),
                              reads=[QTA[pi % 2], QTB], writes=[QB[hb]])
                    return f

                for pi, (a, w) in enumerate(APC):
                    us.append(qa_unit(pi, a, w))
                    us.append(qb_unit(pi, a, w))
                kk = [k_unit(t) for t in range(16)]
                vv = [v_unit(u) for u in range(8)]
                for u in range(8):
                    us.append(kk[2 * u])
                    us.append(kk[2 * u + 1])
                    us.append(vv[u])
                return us

            def main_units():
                out = []
                for kb in range(64):
                    G = kb // 4
                    a0 = GW * G
                    for p, (pa, pw) in enumerate(APC):
                        lo = max(pa, a0)
                        hi = pa + pw
                        if lo < hi:
                            out.append((kb, p, lo, hi - lo))
                return out

            LASTKB = {}
            for p, (pa, pw) in enumerate(APC):
                LASTKB[p] = 4 * min(15, (pa + pw - 1) // GW) + 3

            for un in build_units(0, Rot(range(8))):
                un()
            import collections as _c

            class FreeBanks:
                def __init__(self, ids):
                    self.q = _c.deque(ids)

                def next(self):
                    self.last = self.q.popleft()
                    return self.last

            fb = FreeBanks(CFG["sbanks"])
            LA = CFG["la"]
            for h in range(CFG["nh"]):
                hb = h % 2
                units = main_units()
                builds = build_units(h + 1, fb) if h + 1 < NH else []
                bi = 0
                sbank = {}
                nq = 0
                nun = len(units)
                cool = []
                BE = CFG["bevery"]
                for k in range(nun):
                    kb, p, a, w = units[k]
                    G, r = kb // 4, kb % 4
                    pb = k % NP
                    while cool and cool[0][0] <= k:
                        fb.q.append(cool.pop(0)[1])
                    want_build = (k % BE == BE - 1) and bi < len(builds)
                    la_eff = LA - 1 if want_build else LA
                    while nq < nun and nq <= k + la_eff and fb.q:
                        kb2, p2, a2, w2 = units[nq]
                        sbk = fb.next()
                        sbank[nq] = sbk
                        fw.op("pe", MM(bank(sbk)[:, :w2], kbuf[hb][0:96, kb2 * 128:(kb2 + 1) * 128],
                                       qbuf[hb][0:96, a2:a2 + w2], True, True),
                              reads=[KB_NO[hb][kb2 // 4], KB_PE[hb], QB[hb]], writes=[PB[sbk]])
                        nq += 1
                    if k not in sbank:
                        fb.q.append(cool.pop(0)[1])
                        kb2, p2, a2, w2 = units[nq]
                        assert nq == k
                        sbk = fb.next()
                        sbank[nq] = sbk
                        fw.op("pe", MM(bank(sbk)[:, :w2], kbuf[hb][0:96, kb2 * 128:(kb2 + 1) * 128],
                                       qbuf[hb][0:96, a2:a2 + w2], True, True),
                              reads=[KB_NO[hb][kb2 // 4], KB_PE[hb], QB[hb]], writes=[PB[sbk]])
                        nq += 1
                    sbk = sbank[k]
                    fw.op("act", ACTF(pbuf[pb][:, :w], bank(sbk)[:, :w], AF.Exp, scale=SCALE),
                          reads=[PB[sbk]], writes=[PBUF[pb]])
                    fb.q.append(sbk)
                    lo = max(a, GW * G)
                    hi = min(a + w, GW * G + GW)
                    if lo < hi and CFG["mask"]:
                        fw.op(CFG["mask_eng"], TT(pbuf[pb][:, lo - a:hi - a], pbuf[pb][:, lo - a:hi - a],
                                         mask[:, r * GW + lo - GW * G:r * GW + hi - GW * G], ALU.mult),
                              reads=[PBUF[pb], MASK], writes=[PBUF[pb]])
                    junk(CFG["junk"])
                    fw.op("pe", MM(ps[:, a:a + w], vbuf[hb][:, kb, :], pbuf[pb][:, :w], (kb == 0) or not CFG["pv_acc"], (kb == LASTKB[p]) or not CFG["pv_acc"]),
                          reads=[VB[hb][kb // 8], VONE[hb], PBUF[pb]], writes=[PB[p]])
                    if want_build and fb.q:
                        builds[bi]()
                        bi += 1
                        cool.append((k + 1 + CFG["cool"], fb.last))
                for _, bkc in cool:
                    fb.q.append(bkc)
                cool = []
                while bi < len(builds):
                    builds[bi]()
                    bi += 1
                    fb.q.append(fb.last)
                lo_r, hi_r = (slice(0, 64), slice(64, 128)) if hb == 0 else (slice(64, 128), slice(0, 64))
                for p, (pa, pw) in enumerate(APC):
                    cs = slice(pa, pa + pw)
                    fw.op("dve", TS(dtmp[lo_r, cs], ps[hi_r, cs], 1e-30, ALU.max), reads=[PB[p]], writes=[DTMP])
                    fw.op("dve", RCP(dtmp[lo_r, cs], dtmp[lo_r, cs]), reads=[DTMP], writes=[DTMP])
                    fw.op("dve", TT(ob[lo_r, h // 2, cs], ps[lo_r, cs], dtmp[lo_r, cs], ALU.mult),
                          reads=[PB[p], DTMP], writes=[OB[h // 2]])
            fw.barrier()

        if stage == 2:
            with contextlib.ExitStack() as std:
                dt_ = std.enter_context(nc.sbuf_tensor("dbgt", [128, 8 * NT], F32))
                DT = Buf("dbgt")
                fw.op("dve", MSET(dt_[:], 0.0), writes=[DT])
                fw.op("dve", CP(dt_[:, 0:4 * NT], ob[:].rearrange("p a b -> p (a b)")), reads=OB, writes=[DT])
                fw.op("dve", CP(dt_[:, 4 * NT:7 * NT], qn[:].rearrange("p a b -> p (a b)")), reads=QN, writes=[DT])
                fw.dma("sp", dbg_d, dt_[:], reads=[DT], key=DT)
                fw.barrier()
            stA.close()
            return nc, fw.used

        stA.close()

        NTH = NT // 2
        HPC = pieces_of(NTH, PW)
        rot = Rot(range(8))
        OUTB = Buf("outdma")

        def sq_stat_rs(src_fn, SRC, nch, w, sqt, SQT, rst, RST, lnb, LNB, inv_d, sq_eng="pool"):
            for c in range(nch):
                if sq_eng is None:
                    continue
                if sq_eng == "act":
                    fw.op("act", ACTF(sqt[:, c, :w], src_fn(c), AF.Square), reads=SRC, writes=[SQT])
                else:
                    fw.op("pool", TT(sqt[:, c, :w], src_fn(c), src_fn(c), ALU.mult), reads=SRC, writes=[SQT])
            bS = rot.next()
            for c in range(nch):
                fw.op("pe", MM(bank(bS)[:, :w], ones[:, :], sqt[:, c, :w], c == 0, c == nch - 1),
                      reads=[ONES, SQT], writes=[PB[bS]], signal=(c == nch - 1))
            rstd_from_bank(bank(bS)[:, :w], PB[bS], rst[:, :w], RST, lnb[:, :w], LNB, 128, inv_d)

        for hf in range(2):
            c0 = hf * NTH
            with contextlib.ExitStack() as sth:
                def sbh(name, shape, dt):
                    return sth.enter_context(nc.sbuf_tensor(f"s_{name}_{hf}", list(shape), dt))

                xres = sbh("xres", [128, 8, NTH], F32)
                XRES = [Buf(f"xres{i}") for i in range(3)]
                mixed = sbh("mixed", [128, 8, NTH], BF16)
                MIX = [Buf(f"mix{i}") for i in range(3)]
                u2 = mixed
                U2 = [Buf(f"u2{i}") for i in range(3)]
                wpp = sbh("wpp", [128, 2, 1024], BF16)
                ptb = sbh("ptb", [128, 2, NTH], BF16)
                WPG, WPP, PTB = [Buf("wpg0"), Buf("wpg1")], Buf("wpp"), Buf("ptb")
                sqt = sbh("sqt", [128, 8, PW], BF16)
                SQT = Buf("sqt")
                rst = sbh("rst", [128, PW], F32)
                RST = Buf("rst")
                lnb = sbh("lnb", [128, PW], F32)
                LNB = Buf("lnb")
                for pi, (a, w) in enumerate(HPC):
                    fw.dma("sp", xres[:, :, a:a + w], xT_loc[:, :, c0 + a:c0 + a + w], writes=[XRES[pi]], key=XRES[pi])

                with contextlib.ExitStack() as sabc:
                    with contextlib.ExitStack() as sab:
                        def sbab(name, shape, dt):
                            return sab.enter_context(nc.sbuf_tensor(f"s_{name}_{hf}", list(shape), dt))

                        uh = sbab("uh", [128, 8, NTH], BF16)
                        UH = [Buf(f"uh{i}") for i in range(3)]
                        yap = sbab("yap", [128, 4, NTH], BF16)
                        YAP = [Buf(f"yap{m}") for m in range(4)]
                        with contextlib.ExitStack() as sa:
                            def sba(name, shape, dt):
                                return sa.enter_context(nc.sbuf_tensor(f"s_{name}_{hf}", list(shape), dt))

                            wab = [sba(f"wab{m}", [128, 8, 384], BF16) for m in range(4)]
                            WAB = [Buf(f"wab{m}") for m in range(4)]
                            for m in range(4):
                                fw.dma("pool", wab[m][:].rearrange("p a b -> p (a b)"), wab_d[m], writes=[WAB[m]], key=WAB[m])
                            for pi, (a, w) in enumerate(HPC):
                                sq_stat_rs(lambda c: xres[:, c, a:a + w], [XRES[pi]], 8, w, sqt, SQT, rst, RST, lnb, LNB,
                                           1.0 / D, sq_eng="act")
                                for c in range(8):
                                    fw.op("dve", STT(uh[:, c, a:a + w], xres[:, c, a:a + w], vcol(V_GMIX + c), rst[:, :w],
                                                     ALU.mult, ALU.mult), reads=[XRES[pi], RST, VEC], writes=[UH[pi]])
                            t1 = [sba(f"t1{i}", [128, PW], F32) for i in range(2)]
                            T1 = [Buf("t10"), Buf("t11")]
                            cx = [sba(f"cx{i}", [128, NTH], F32) for i in range(2)]
                            CX = [Buf("cx0"), Buf("cx1")]
                            cv = [sba(f"cv{i}", [128, NTH], F32) for i in range(2)]
                            CV = [Buf("cv0"), Buf("cv1")]
                            k = 0

                            def ab_part(m):
                                i = m % 2
                                for pi, (a, w) in enumerate(HPC):
                                    bb = rot.next()
                                    for c in range(8):
                                        fw.op("pe", MM(bank(bb)[:, :w], wab[m][:, c, 256:384],
                                                       uh[:, c, a:a + w], c == 0, c == 7),
                                              reads=[WAB[m], UH[pi]], writes=[PB[bb]], signal=(c == 7))
                                    fw.op("dve", TT(yap[:, m, a:a + w], bank(bb)[:, :w], cv[i][:, a:a + w], ALU.mult),
                                          reads=[PB[bb], CV[i]], writes=[YAP[m]])

                            for m in range(4):
                                i = m % 2
                                for pi, (a, w) in enumerate(HPC):
                                    bc = rot.next()
                                    for c in range(8):
                                        fw.op("pe", MM(bank(bc)[:, :w], wab[m][:, c, 0:128],
                                                       uh[:, c, a:a + w], c == 0, c == 7),
                                              reads=[WAB[m], UH[pi]], writes=[PB[bc]], signal=(c == 7))
                                    fw.op("act", ACTF(t1[k % 2][:, :w], bank(bc)[:, :w], AF.Copy),
                                          reads=[PB[bc]], writes=[T1[k % 2]])
                                    bx = rot.next()
                                    for c in range(8):
                                        fw.op("pe", MM(bank(bx)[:, :w], wab[m][:, c, 128:256],
                                                       uh[:, c, a:a + w], c == 0, c == 7),
                                              reads=[WAB[m], UH[pi]], writes=[PB[bx]], signal=(c == 7))
                                    fw.op("dve", TT(cx[i][:, a:a + w], bank(bx)[:, :w], t1[k % 2][:, :w], ALU.mult),
                                          reads=[PB[bx], T1[k % 2]], writes=[CX[i]])
                                    k += 1
                                w0, w1, w2 = (vcol(V_CAW + 3 * m + kk) for kk in range(3))
                                fw.op("pool", TS(cv[i][:, :], cx[i][:, :], w2, ALU.mult, 0.0, ALU.add),
                                      reads=[CX[i], VEC], writes=[CV[i]])
                                fw.op("dve", STT(cv[i][:, 1:NTH], cx[i][:, 0:NTH - 1], w1, cv[i][:, 1:NTH], ALU.mult, ALU.add),
                                      reads=[CX[i], CV[i], VEC], writes=[CV[i]])
                                fw.op("dve", STT(cv[i][:, 2:NTH], cx[i][:, 0:NTH - 2], w0, cv[i][:, 2:NTH], ALU.mult, ALU.add),
                                      reads=[CX[i], CV[i], VEC], writes=[CV[i]])
                                if m >= 1:
                                    ab_part(m - 1)
                            ab_part(3)
                            fw.barrier()
                        with contextlib.ExitStack() as sbb:
                            def sbb_(name, shape, dt):
                                return sbb.enter_context(nc.sbuf_tensor(f"s_{name}_{hf}", list(shape), dt))

                            wg = [sbb_(f"wg{m}", [128, 8, 256], BF16) for m in range(8)]
                            WG = [Buf(f"wg{m}") for m in range(8)]
                            wao = sbb_("wao", [128, 4, 1024], BF16)
                            wbo = sbb_("wbo", [128, 4, 1024], BF16)
                            WAO, WBO = Buf("wao"), Buf("wbo")
                            fw.dma("pool", wg[0][:].rearrange("p a b -> p (a b)"), wg_d[0], writes=[WG[0]], key=WG[0])
                            fw.dma("pool", wao[:].rearrange("p a b -> p (a b)"), wao_d, writes=[WAO], key=WAO)
                            fw.dma("pool", wbo[:].rearrange("p a b -> p (a b)"), wbo_d, writes=[WBO], key=WBO)
                            for m in range(1, 8):
                                fw.dma("pool", wg[m][:].rearrange("p a b -> p (a b)"), wg_d[m], writes=[WG[m]], key=WG[m])
                            sga = [sbb_(f"sga{i}", [128, PW], F32) for i in range(2)]
                            sgb = [sbb_(f"sgb{i}", [128, PW], F32) for i in range(2)]
                            SGA = [Buf("sga0"), Buf("sga1")]
                            SGB = [Buf("sgb0"), Buf("sgb1")]
                            k = 0
                            for mo in range(8):
                                for pi, (a, w) in enumerate(HPC):
                                    i = k % 2
                                    k += 1
                                    b1, b2, b3, b4 = rot.next(), rot.next(), rot.next(), rot.next()
                                    for c in range(8):
                                        fw.op("pe", MM(bank(b1)[:, :w], wg[mo][:, c, 0:128], uh[:, c, a:a + w],
                                                       c == 0, c == 7), reads=[WG[mo], UH[pi]], writes=[PB[b1]], signal=(c == 7))
                                    fw.op("act", ACTF(sga[i][:, :w], bank(b1)[:, :w], AF.Sigmoid), reads=[PB[b1]], writes=[SGA[i]])
                                    for c in range(4):
                                        fw.op("pe", MM(bank(b2)[:, :w], wao[:, c, mo * 128:(mo + 1) * 128], yap[:, c, a:a + w],
                                                       c == 0, c == 3), reads=[WAO] + YAP, writes=[PB[b2]], signal=(c == 3))
                                    fw.op("dve", TT(sga[i][:, :w], bank(b2)[:, :w], sga[i][:, :w], ALU.mult),
                                          reads=[PB[b2], SGA[i]], writes=[SGA[i]])
                                    for c in range(8):
                                        fw.op("pe", MM(bank(b3)[:, :w], wg[mo][:, c, 128:256],
                                                       uh[:, c, a:a + w], c == 0, c == 7),
                                              reads=[WG[mo], UH[pi]], writes=[PB[b3]], signal=(c == 7))
                                    fw.op("act", ACTF(sgb[i][:, :w], bank(b3)[:, :w], AF.Sigmoid), reads=[PB[b3]], writes=[SGB[i]])
                                    for c in range(4):
                                        fw.op("pe", MM(bank(b4)[:, :w], wbo[:, c, mo * 128:(mo + 1) * 128],
                                                       ob[:, c, c0 + a:c0 + a + w], c == 0, c == 3),
                                              reads=[WBO] + OB, writes=[PB[b4]], signal=(c == 3))
                                    fw.op("dve", TT(sgb[i][:, :w], bank(b4)[:, :w], sgb[i][:, :w], ALU.mult),
                                          reads=[PB[b4], SGB[i]], writes=[SGB[i]])
                                    fw.op("pool", TT(mixed[:, mo, a:a + w], sga[i][:, :w], sgb[i][:, :w], ALU.add),
                                          reads=[SGA[i], SGB[i]], writes=[MIX[pi]])
                            fw.barrier()

                    with contextlib.ExitStack() as sc_:
                        def sbc(name, shape, dt):
                            return sc_.enter_context(nc.sbuf_tensor(f"s_{name}_{hf}", list(shape), dt))

                        wo = [sbc(f"wo{i}", [128, 8, 512], BF16) for i in range(2)]
                        WO = [Buf("wo0"), Buf("wo1")]
                        for i in range(2):
                            fw.dma("pool", wo[i][:], wo_d.rearrange("p (a b) -> p a b", b=1024)[:, :, i * 512:(i + 1) * 512],
                                   writes=[WO[i]], key=WO[i])
                        mos = [sbc(f"mos{i}", [128, 8, PW], F32) for i in range(2)]
                        MOS = [Buf("mos0"), Buf("mos1")]
                        tmpc = [sbc(f"tmpc{i}", [128, PW], F32) for i in range(4)]
                        TMPC = [Buf(f"tmpc{i}") for i in range(4)]
                        sqc = [sbc(f"sqc{i}", [128, 8, PW], BF16) for i in range(2)]
                        SQC = [Buf("sqc0"), Buf("sqc1")]

                        def c_mm(pi):
                            a, w = HPC[pi]
                            for m in range(8):
                                bk = rot.next()
                                for c in range(8):
                                    fw.op("pe", MM(bank(bk)[:, :w], wo[m // 4][:, c, (m % 4) * 128:(m % 4 + 1) * 128],
                                                   mixed[:, c, a:a + w], c == 0, c == 7),
                                          reads=[WO[m // 4], MIX[pi]], writes=[PB[bk]], signal=(c == 7))
                                fw.op("act", ACTF(mos[pi % 2][:, m, :w], bank(bk)[:, :w], AF.Copy),
                                      reads=[PB[bk]], writes=[MOS[pi % 2]])
                                fw.op("act", ACTF(sqc[pi % 2][:, m, :w], bank(bk)[:, :w], AF.Square),
                                      reads=[PB[bk]], writes=[SQC[pi % 2]])

                        def c_post(pi):
                            a, w = HPC[pi]
                            mo_ = mos[pi % 2]
                            sq_stat_rs(None, None, 8, w, sqc[pi % 2], SQC[pi % 2], rst, RST, lnb, LNB, 1.0 / D, sq_eng=None)
                            for m in range(8):
                                i = m % 4
                                fw.op("dve", STT(tmpc[i][:, :w], mo_[:, m, :w], vcol(V_GMPOST + m), rst[:, :w],
                                                 ALU.mult, ALU.mult), reads=[MOS[pi % 2], RST, VEC], writes=[TMPC[i]])
                                fw.op("pool" if m % 2 == 0 else "dve",
                                      TT(xres[:, m, a:a + w], xres[:, m, a:a + w], tmpc[i][:, :w], ALU.add),
                                      reads=[XRES[pi], TMPC[i]], writes=[XRES[pi]])

                        def c_post2(pi):
                            a, w = HPC[pi]
                            sq_stat_rs(lambda c: xres[:, c, a:a + w], [XRES[pi]], 8, w, sqt, SQT, rst, RST, lnb, LNB,
                                       1.0 / D, sq_eng="act")
                            for c in range(8):
                                fw.op("dve", STT(u2[:, c, a:a + w], xres[:, c, a:a + w], vcol(V_GFPRE + c), rst[:, :w],
                                                 ALU.mult, ALU.mult), reads=[XRES[pi], RST, VEC], writes=[U2[pi], MIX[pi]])

                        c_mm(0)
                        c_mm(1)
                        c_post(0)
                        c_mm(2)
                        c_post(1)
                        c_post(2)
                        for pi in range(3):
                            c_post2(pi)
                        fw.barrier()

                with contextlib.ExitStack() as sd:
                    def sbd(name, shape, dt):
                        return sd.enter_context(nc.sbuf_tensor(f"s_{name}_{hf}", list(shape), dt))

                    ff = sbd("ff", [128, NPAIR, NTH], BF16)
                    FF = [Buf(f"ff{m}") for m in range(NPAIR)]
                    wdn0 = sbd("wdn0", [128, NPAIR, 512], BF16)
                    WDN = [Buf("wdn0"), Buf("wdn1")]
                    wdn_v = wdn_d.rearrange("p (a b) -> p a b", b=1024)
                    with contextlib.ExitStack() as sdd:
                        def sbdd(name, shape, dt):
                            return sdd.enter_context(nc.sbuf_tensor(f"s_{name}_{hf}", list(shape), dt))

                        NWB = 3
                        wup = [sbdd(f"wup{i}", [128, 8, 256], BF16) for i in range(NWB)]
                        WUP = [Buf(f"wup{i}") for i in range(NWB)]
                        rows = [[sbdd(f"row{i}_{r}", [128, NTH], F32) for r in range(4)] for i in range(2)]
                        ROW = [[Buf(f"row{i}_{r}") for r in range(4)] for i in range(2)]

                        def load_wup(m):
                            fw.dma("pool", wup[m % NWB][:].rearrange("p a b -> p (a b)"), wup_d[m],
                                   writes=[WUP[m % NWB]], key=WUP[m % NWB])

                        for m in range(min(NWB - 1, NPAIR)):
                            load_wup(m)

                        def d_mm(m):
                            i = m % 2
                            wb = m % NWB
                            for half_i in range(2):
                                gs = rows[i][2 * half_i]
                                GS = ROW[i][2 * half_i]
                                for pi, (a, w) in enumerate(HPC):
                                    bk = rot.next()
                                    for c in range(8):
                                        fw.op("pe", MM(bank(bk)[:, :w], wup[wb][:, c, half_i * 128:(half_i + 1) * 128],
                                                       u2[:, c, a:a + w], c == 0, c == 7),
                                              reads=[WUP[wb], U2[pi]], writes=[PB[bk]], signal=(c == 7))
                                    fw.op("act", ACTF(gs[:, a:a + w], bank(bk)[:, :w], AF.Copy), reads=[PB[bk]], writes=[GS])

                        def d_conv(m):
                            i = m % 2
                            for half_i, chunk in enumerate((m, NPAIR + m)):
                                gs, t0 = rows[i][2 * half_i], rows[i][2 * half_i + 1]
                                GS, T0 = ROW[i][2 * half_i], ROW[i][2 * half_i + 1]
                                cw = [vcol(V_CFW + 3 * chunk + kk) for kk in range(3)]
                                fw.op("pool", TS(t0[:, :], gs[:, :], cw[2], ALU.mult, vcol(V_BF + chunk), ALU.add),
                                      reads=[GS, VEC], writes=[T0])
                                fw.op("dve", STT(t0[:, 1:NTH], gs[:, 0:NTH - 1], cw[1], t0[:, 1:NTH], ALU.mult, ALU.add),
                                      reads=[GS, T0, VEC], writes=[T0])
                                fw.op("dve", STT(t0[:, 2:NTH], gs[:, 0:NTH - 2], cw[0], t0[:, 2:NTH], ALU.mult, ALU.add),
                                      reads=[GS, T0, VEC], writes=[T0])

                        def d_fin(m):
                            i = m % 2
                            fw.op("act", ACTF(rows[i][1][:, :], rows[i][1][:, :], AF.Gelu_apprx_tanh),
                                  reads=[ROW[i][1]], writes=[ROW[i][1]])
                            fw.op("dve", TT(ff[:, m, :], rows[i][1][:, :], rows[i][3][:, :], ALU.mult),
                                  reads=[ROW[i][1], ROW[i][3]], writes=[FF[m]])

                        for m in range(NPAIR):
                            if m + NWB - 1 < NPAIR:
                                load_wup(m + NWB - 1)
                            if m == 2:
                                fw.dma("pool", wdn0[:], wdn_v[:, :, 0:512], writes=[WDN[0]], key=WDN[0])
                            d_mm(m)
                            if m >= 1:
                                d_fin(m - 1)
                            d_conv(m)
                        d_fin(NPAIR - 1)
                        fw.barrier()
                    with contextlib.ExitStack() as se:
                        def sbe(name, shape, dt):
                            return se.enter_context(nc.sbuf_tensor(f"s_{name}_{hf}", list(shape), dt))

                        wdn1 = sbe("wdn1", [128, NPAIR, 512], BF16)
                        wdn = [wdn0, wdn1]
                        fw.dma("pool", wdn1[:], wdn_v[:, :, 512:1024], writes=[WDN[1]], key=WDN[1])
                        fw.dma("pool", wpp[:].rearrange("p a b -> p (a b)"), wpp_d, writes=[WPP], key=WPP)
                        fw.dma("pool", ptb[:], pT_loc[:, :, c0:c0 + NTH], writes=[PTB], key=PTB)
                        fds = [sbe(f"fds{i}", [128, 8, PW], F32) for i in range(2)]
                        FDS = [Buf("fds0"), Buf("fds1")]
                        tmpe = [sbe(f"tmpe{i}", [128, PW], F32) for i in range(2)]
                        TMPE = [Buf(f"tmpe{i}") for i in range(2)]
                        sqe = [sqt, sbe("sqe1", [128, 8, PW], BF16)]
                        SQE = [SQT, Buf("sqe1")]

                        def e_mm(pi):
                            a, w = HPC[pi]
                            for m in range(8):
                                bk = rot.next()
                                for c in range(NPAIR):
                                    fw.op("pe", MM(bank(bk)[:, :w], wdn[m // 4][:, c, (m % 4) * 128:(m % 4 + 1) * 128],
                                                   ff[:, c, a:a + w], c == 0, c == NPAIR - 1),
                                          reads=[WDN[m // 4], FF[c]], writes=[PB[bk]], signal=(c == NPAIR - 1))
                                fw.op("act", ACTF(fds[pi % 2][:, m, :w], bank(bk)[:, :w], AF.Copy),
                                      reads=[PB[bk]], writes=[FDS[pi % 2]])
                                fw.op("act", ACTF(sqe[pi % 2][:, m, :w], bank(bk)[:, :w], AF.Square),
                                      reads=[PB[bk]], writes=[SQE[pi % 2]])

                        def e_post(pi):
                            a, w = HPC[pi]
                            fd_ = fds[pi % 2]
                            sq_stat_rs(None, None, 8, w, sqe[pi % 2], SQE[pi % 2], rst, RST, lnb, LNB, 1.0 / D, sq_eng=None)
                            for m in range(8):
                                i = m % 2
                                fw.op("dve", STT(tmpe[i][:, :w], fd_[:, m, :w], vcol(V_GFPOST + m), rst[:, :w],
                                                 ALU.mult, ALU.mult), reads=[FDS[pi % 2], RST, VEC], writes=[TMPE[i]])
                                fw.op("pool" if m % 2 == 0 else "dve",
                                      TT(xres[:, m, a:a + w], xres[:, m, a:a + w], tmpe[i][:, :w], ALU.add),
                                      reads=[XRES[pi], TMPE[i]], writes=[XRES[pi]])

                        e_mm(0)
                        for pi in range(3):
                            if pi + 1 < 3:
                                e_mm(pi + 1)
                            e_post(pi)
                        fw.barrier()

                with contextlib.ExitStack() as sf:
                    def sbf(name, shape, dt):
                        return sf.enter_context(nc.sbuf_tensor(f"s_{name}_{hf}", list(shape), dt))

                    wpg = [sbf(f"wpg{i}", [128, 8, 512], BF16) for i in range(2)]
                    for i in range(2):
                        fw.dma("pool", wpg[i][:], wpg_d.rearrange("p (a b) -> p a b", b=1024)[:, :, i * 512:(i + 1) * 512],
                               writes=[WPG[i]], key=WPG[i])
                    h2b = sbf("h2b", [128, 8, NTH], BF16)
                    H2B = [Buf(f"h2b{i}") for i in range(3)]
                    egs = [sbf(f"egs{i}", [128, 8, PW], F32) for i in range(2)]
                    EGS = [Buf("egs0"), Buf("egs1")]
                    sgp = [sbf(f"sgp{i}", [128, PW], F32) for i in range(2)]
                    SGP = [Buf("sgp0"), Buf("sgp1")]
                    tmpf = [sbf(f"tmpf{i}", [128, PW], F32) for i in range(2)]
                    TMPF = [Buf("tmpf0"), Buf("tmpf1")]
                    otile = [sbf(f"otile{i}", [128, 8, PW], F32) for i in range(2)]
                    OT = [Buf("ot0"), Buf("ot1")]
                    for pi, (a, w) in enumerate(HPC):
                        for c in range(8):
                            eng = "act" if c % 2 == 0 else "dve"
                            if eng == "act":
                                fw.op("act", ACTF(h2b[:, c, a:a + w], xres[:, c, a:a + w], AF.Copy),
                                      reads=[XRES[pi]], writes=[H2B[pi]])
                            else:
                                fw.op("dve", CP(h2b[:, c, a:a + w], xres[:, c, a:a + w]),
                                      reads=[XRES[pi]], writes=[H2B[pi]])

                    def f_mm(pi):
                        a, w = HPC[pi]
                        for m in range(8):
                            i = m % 2
                            b1, b2 = rot.next(), rot.next()
                            for c in range(8):
                                fw.op("pe", MM(bank(b1)[:, :w], wpg[m // 4][:, c, (m % 4) * 128:(m % 4 + 1) * 128],
                                               h2b[:, c, a:a + w], c == 0, c == 7),
                                      reads=[WPG[m // 4], H2B[pi]], writes=[PB[b1]], signal=(c == 7))
                            fw.op("act", ACTF(sgp[i][:, :w], bank(b1)[:, :w], AF.Sigmoid), reads=[PB[b1]], writes=[SGP[i]])
                            for c in range(2):
                                fw.op("pe", MM(bank(b2)[:, :w], wpp[:, c, m * 128:(m + 1) * 128], ptb[:, c, a:a + w],
                                               c == 0, c == 1), reads=[WPP, PTB], writes=[PB[b2]], signal=(c == 1))
                            fw.op("dve", TT(egs[pi % 2][:, m, :w], bank(b2)[:, :w], sgp[i][:, :w], ALU.mult),
                                  reads=[PB[b2], SGP[i]], writes=[EGS[pi % 2]])

                    def f_post(pi):
                        a, w = HPC[pi]
                        oi = pi % 2
                        eg_ = egs[pi % 2]
                        sq_stat_rs(lambda c: eg_[:, c, :w], [EGS[pi % 2]], 8, w, sqt, SQT, rst, RST, lnb, LNB, 1.0 / D)
                        for m in range(8):
                            i = m % 2
                            fw.op("dve", STT(tmpf[i][:, :w], eg_[:, m, :w], vcol(V_GPPOST + m), rst[:, :w],
                                             ALU.mult, ALU.mult), reads=[EGS[pi % 2], RST, VEC], writes=[TMPF[i]])
                            fw.op("pool", TT(otile[oi][:, m, :w], xres[:, m, a:a + w], tmpf[i][:, :w], ALU.add),
                                  reads=[XRES[pi], TMPF[i]], writes=[OT[oi]])
                        fw.dma("sp", out_d[:, :, c0 + a:c0 + a + w], otile[oi][:, :, :w], reads=[OT[oi]], key=OT[oi])

                    f_mm(0)
                    for pi in range(3):
                        if pi + 1 < 3:
                            f_mm(pi + 1)
                        f_post(pi)
                    fw.barrier()
    return nc, fw.used


def _chunks(w, kc):
    n = w.shape[1]
    return np.ascontiguousarray(w.reshape(kc, 128, n).transpose(1, 0, 2).reshape(128, kc * n))


def col_tokens(j):
    c = np.arange(NT)
    G = c // GW
    o = c % GW
    return 512 * G + 128 * j + (o - HALO)


def prep_inputs(inputs):
    f32 = np.float32
    x = np.asarray(inputs["x"], f32)
    p = np.asarray(inputs["p"], f32)[0]
    positions = np.asarray(inputs["positions"]).astype(np.int32)
    w_in = np.asarray(inputs["w_in"], f32)[0]

    def vec_cols(v, kc):
        return np.asarray(v, f32).reshape(kc, 128).T

    vecs = np.zeros((128, NV), f32)
    vecs[:, V_GMIX:V_GMIX + 8] = vec_cols(inputs["g_mix_pre"][0], 8)
    vecs[:, V_GQ:V_GQ + 3] = vec_cols(inputs["g_q_lat"][0], 3)
    vecs[:, V_GKV:V_GKV + 2] = vec_cols(inputs["g_kv_lat"][0], 2)
    vecs[:, V_GMPOST:V_GMPOST + 8] = vec_cols(inputs["g_mix_post"][0], 8)
    vecs[:, V_GFPRE:V_GFPRE + 8] = vec_cols(inputs["g_ffn_pre"][0], 8)
    vecs[:, V_GFPOST:V_GFPOST + 8] = vec_cols(inputs["g_ffn_post"][0], 8)
    vecs[:, V_GPPOST:V_GPPOST + 8] = vec_cols(inputs["g_ple_post"][0], 8)
    caw = np.asarray(inputs["conv_a_w"], f32)[0]
    for m in range(4):
        for k in range(3):
            vecs[:, V_CAW + 3 * m + k] = caw[k, m * 128:(m + 1) * 128]
    cfw = np.asarray(inputs["conv_ffn_w"], f32)[0]
    bfc = np.asarray(inputs["b_ffn_conv"], f32)[0]
    for m in range(44):
        for k in range(3):
            vecs[:, V_CFW + 3 * m + k] = cfw[k, m * 128:(m + 1) * 128]
        vecs[:, V_BF + m] = bfc[m * 128:(m + 1) * 128]
    inv = 1.0 / (10000.0 ** (np.arange(16, dtype=np.float64) * (2.0 / 32)))
    for i in range(32):
        vecs[64 + i, V_INV] = np.float32(inv[i % 16] / TWO_PI)
        vecs[64 + i, V_SGN] = np.float32(-TWO_PI if i < 16 else TWO_PI)
    vecs[:, V_EPS] = EPS

    kv_lat = w_in[:, 1920:2176]
    k_rope = w_in[:, 2176:2208]
    z64 = np.zeros((D, 64), f32)
    wkv = np.concatenate([kv_lat, z64, k_rope, z64, k_rope[:, 16:32], k_rope[:, 0:16]], axis=1)
    wq = w_in[:, 1536:1920]
    wab = w_in[:, 0:1536]
    wg = w_in[:, 2208:4256]
    wqu = np.asarray(inputs["w_q_up"], f32)[0]
    wqs = np.zeros_like(wqu)
    for h in range(NH):
        b0 = h * 96
        wqs[:, b0 + 64:b0 + 80] = wqu[:, b0 + 80:b0 + 96]
        wqs[:, b0 + 80:b0 + 96] = wqu[:, b0 + 64:b0 + 80]
    wup = np.asarray(inputs["w_ffn_up"], f32)[0]
    wup_p = np.empty((NPAIR, 128, 8 * 256), f32)
    for m in range(NPAIR):
        blk = np.concatenate([wup[:, m * 128:(m + 1) * 128], wup[:, DFF + m * 128:DFF + (m + 1) * 128]], axis=1)
        wup_p[m] = _chunks(blk, 8)
    shared = {
        "vecs": vecs,
        "wkv": _chunks(wkv, 8), "wq": _chunks(wq, 8),
        "wab": np.stack([_chunks(np.concatenate([wab[:, 512 + m * 128:512 + (m + 1) * 128],
                                                 wab[:, 1024 + m * 128:1024 + (m + 1) * 128],
                                                 wab[:, m * 128:(m + 1) * 128]], axis=1), 8) for m in range(4)]),
        "wg": np.stack([_chunks(np.concatenate([wg[:, m * 128:(m + 1) * 128],
                                                wg[:, 1024 + m * 128:1024 + (m + 1) * 128]], axis=1), 8) for m in range(8)]),
        "wqu": _chunks(wqu, 3), "wqs": _chunks(wqs, 3),
        "wkvu": _chunks(np.asarray(inputs["w_kv_up"], f32)[0], 2),
        "wao": _chunks(np.asarray(inputs["w_a_out"], f32)[0], 4),
        "wbo": _chunks(np.asarray(inputs["w_b_out"], f32)[0], 4),
        "wo": _chunks(np.asarray(inputs["w_o"], f32)[0], 8),
        "wup": wup_p,
        "wdn": _chunks(np.asarray(inputs["w_ffn_down"], f32)[0], NPAIR),
        "wpp": _chunks(np.asarray(inputs["w_ple_proj"], f32)[0], 2),
        "wpg": _chunks(np.asarray(inputs["w_ple_gate"], f32)[0], 8),
    }
    in_maps = []
    per_batch = {}
    for b in range(2):
        xT = x[b].T
        xa = xT.reshape(8, 128, 16, 512).transpose(2, 1, 0, 3).reshape(16, 128, 8 * 512)
        per_batch[b] = (np.ascontiguousarray(xa), np.ascontiguousarray(positions[b][None, :]))
    for core in range(NCORE):
        b, j = core // CPB, core % CPB
        tok = col_tokens(j)
        valid = tok >= 0
        tk = np.where(valid, tok, 0)
        xl = x[b][tk] * valid[:, None].astype(f32)
        xl = np.ascontiguousarray(xl.T.reshape(8, 128, NT).transpose(1, 0, 2))
        pl = p[b][tk] * valid[:, None].astype(f32)
        pl = np.ascontiguousarray(pl.T.reshape(2, 128, NT).transpose(1, 0, 2))
        posl = np.where(valid, positions[b][tk], 0).astype(np.int32)[None, :]
        mk = np.zeros((128, 4, GW), f32)
        kk = np.arange(128)[:, None]
        qq = np.arange(128)[None, :]
        for r in range(4):
            if r < j:
                mk[:, r, :] = 1.0
            elif r == j:
                mk[:, r, HALO:] = ((kk // 64) <= (qq // 64)).astype(f32)
                mk[:, r, :HALO] = 0.0
        m = dict(shared)
        m.update({"xT_all": per_batch[b][0], "pos_all": per_batch[b][1], "xT_loc": xl, "pT_loc": pl,
                  "pos_loc": posl, "mask": mk.reshape(128, 4 * GW)})
        in_maps.append(m)
    return in_maps


def assemble(results):
    out = np.empty((2, S, D), np.float32)
    for core in range(NCORE):
        b, j = core // CPB, core % CPB
        o = results[core]["out"]
        tok = col_tokens(j)
        own = (np.arange(NT) % GW) >= HALO
        oT = o.transpose(2, 1, 0).reshape(NT, D)
        out[b, tok[own]] = oT[own]
    return out


_NC_CACHE = {}


def kernel(**inputs):
    in_maps = prep_inputs(inputs)
    if "nc" not in _NC_CACHE:
        _NC_CACHE["nc"] = build_program()
    nc = _NC_CACHE["nc"]
    res = run_bass_kernel_spmd(nc, in_maps, core_ids=list(range(NCORE)))
    return assemble(res.results)
```
